# Optimizing a Trainium2 kernel written in Bass

```python
import math
import jax, jax.numpy as jnp
from jax import lax
import numpy as np

D_MODEL = 1024
BATCH = 4
SEQ = 8192
DEPTH = 1

N_META = 16
CHUNK = 128
PAD = CHUNK - N_META
RET_HEADS = 4
RET_DK = D_MODEL // RET_HEADS
RET_DV = 2 * RET_DK
RET_THETA = 10000.0
DIFF_HEADS = 8
DIFF_DH = D_MODEL // (2 * DIFF_HEADS)
DIFF_DV = 2 * DIFF_DH
ROPE_THETA = 500000.0
ROPE_DIM = DIFF_DH // 4
FFN_DIM = 2816
CONV_W = 3
EPS = 1e-6
NEG = -1e30

RQ = RET_HEADS * RET_DK
RK = RET_HEADS * RET_DK
RV = RET_HEADS * RET_DV
RG = RET_HEADS * RET_DV
DQ = DIFF_HEADS * 2 * DIFF_DH
DK = DIFF_HEADS * 2 * DIFF_DH
DV = DIFF_HEADS * DIFF_DV
GATE_COLS = 2 * D_MODEL
IN_COLS = RQ + RK + RV + RG + DQ + DK + DV + GATE_COLS
SPLIT_IDX = [int(s) for s in np.cumsum([RQ, RK, RV, RG, DQ, DK, DV])]

kernel_name = "hybrid_retention_diffattn_convffn"


def rmsnorm(x, w):
    xf = x.astype(jnp.float32)
    y = xf * lax.rsqrt(jnp.mean(xf * xf, axis=-1, keepdims=True) + EPS)
    return (y * w.astype(jnp.float32)).astype(x.dtype)


def rope(x, pos, rot_dim, theta):
    half = rot_dim // 2
    inv = jnp.power(theta, -jnp.arange(half, dtype=jnp.float32) / half)
    ang = pos[:, None] * inv[None, :]
    shape = (1, pos.shape[0]) + (1,) * (x.ndim - 3) + (half,)
    cos = jnp.cos(ang).reshape(shape)
    sin = jnp.sin(ang).reshape(shape)
    xr = x[..., :rot_dim].astype(jnp.float32)
    x1, x2 = xr[..., :half], xr[..., half:]
    rot = jnp.concatenate([x1 * cos - x2 * sin, x2 * cos + x1 * sin], axis=-1).astype(x.dtype)
    return jnp.concatenate([rot, x[..., rot_dim:]], axis=-1)


def retention(q, k, v):
    B, Lp, H, dk = q.shape
    dv = v.shape[-1]
    n = Lp // CHUNK

    def to_chunks(t):
        return t.reshape(B, n, CHUNK, H, t.shape[-1]).transpose(1, 0, 3, 2, 4)

    lg = jnp.log(1.0 - jnp.power(2.0, -5.0 - jnp.arange(H, dtype=jnp.float32)))
    idx = jnp.arange(CHUNK, dtype=jnp.float32)
    dist = idx[:, None] - idx[None, :]
    intra = jnp.where(dist[None] >= 0, jnp.exp(lg[:, None, None] * jnp.maximum(dist, 0.0)[None]), 0.0)
    q_dec = jnp.exp(lg[:, None] * (idx[None] + 1.0))[:, :, None]
    k_dec = jnp.exp(lg[:, None] * (CHUNK - 1.0 - idx[None]))[:, :, None]
    chunk_dec = jnp.exp(lg * CHUNK)[:, None, None]

    def step(R, qkv):
        qc, kc, vc = qkv
        s = jnp.einsum('bhid,bhjd->bhij', qc, kc) * intra
        o = jnp.einsum('bhij,bhje->bhie', s, vc) + jnp.einsum('bhid,bhde->bhie', qc, R) * q_dec
        R = R * chunk_dec + jnp.einsum('bhjd,bhje->bhde', kc * k_dec, vc)
        return R, o

    R0 = jnp.zeros((B, H, dk, dv), jnp.float32)
    _, o = lax.scan(step, R0, (to_chunks(q), to_chunks(k), to_chunks(v)))
    return o.transpose(1, 0, 3, 2, 4).reshape(B, Lp, H, dv)


def diff_attention(q, k, v, lam):
    B, Lp, H, _, dh = q.shape
    nb = Lp // CHUNK
    qb = q.reshape(B, nb, CHUNK, H, 2, dh).transpose(1, 0, 3, 4, 2, 5)
    kt = k.transpose(0, 2, 3, 1, 4)
    vt = v.transpose(0, 2, 1, 3)
    kpos = jnp.arange(Lp)
    scale = dh ** -0.5

    def block(args):
        qblk, i = args
        qpos = i * CHUNK + jnp.arange(CHUNK)
        mask = (kpos[None, :] <= qpos[:, None]) & (kpos[None, :] >= PAD)
        s = jnp.einsum('bhcqd,bhckd->bhcqk', qblk, kt) * scale
        p = jax.nn.softmax(jnp.where(mask, s, NEG), axis=-1)
        a = p[:, :, 0] - lam * p[:, :, 1]
        return jnp.einsum('bhqk,bhke->bhqe', a, vt)

    o = lax.map(block, (qb, jnp.arange(nb)))
    return o.transpose(1, 0, 3, 2, 4).reshape(B, Lp, H, v.shape[-1])


def causal_dwconv(x, w, b):
    C = x.shape[-1]
    y = lax.conv_general_dilated(x, w.astype(x.dtype)[:, None, :], window_strides=(1,),
                                 padding=[(CONV_W - 1, 0)], dimension_numbers=('NWC', 'WIO', 'NWC'),
                                 feature_group_count=C)
    return y + b.astype(x.dtype)


def setup_inputs(seed: int = 0) -> dict:
    key = jax.random.key(seed)
    ks = jax.random.split(key, 20)
    f = jnp.float32
    nrm = lambda k, shape, s: jax.random.normal(k, shape, f) * s
    return {
        'x': nrm(ks[0], (BATCH, SEQ, D_MODEL), 1.0),
        'meta_tokens': nrm(ks[1], (N_META, D_MODEL), 1.0),
        'norm1_w': 1.0 + nrm(ks[2], (DEPTH, D_MODEL), 0.01),
        'w_in': nrm(ks[3], (DEPTH, D_MODEL, IN_COLS), D_MODEL ** -0.5),
        'w_ret_o': nrm(ks[4], (DEPTH, RV, D_MODEL), RV ** -0.5),
        'q_norm_w': 1.0 + nrm(ks[5], (DEPTH, DIFF_DH), 0.01),
        'k_norm_w': 1.0 + nrm(ks[6], (DEPTH, DIFF_DH), 0.01),
        'lambda_q1': nrm(ks[7], (DEPTH, DIFF_DH), 0.1),
        'lambda_k1': nrm(ks[8], (DEPTH, DIFF_DH), 0.1),
        'lambda_q2': nrm(ks[9], (DEPTH, DIFF_DH), 0.1),
        'lambda_k2': nrm(ks[10], (DEPTH, DIFF_DH), 0.1),
        'diff_subln_w': 1.0 + nrm(ks[11], (DEPTH, DIFF_DV), 0.01),
        'w_diff_o': nrm(ks[12], (DEPTH, DV, D_MODEL), DV ** -0.5),
        'w_out': nrm(ks[13], (DEPTH, D_MODEL, D_MODEL), D_MODEL ** -0.5),
        'norm2_w': 1.0 + nrm(ks[14], (DEPTH, D_MODEL), 0.01),
        'w_up': nrm(ks[15], (DEPTH, D_MODEL, 2 * FFN_DIM), D_MODEL ** -0.5),
        'conv_w': nrm(ks[16], (DEPTH, CONV_W, 2 * FFN_DIM), CONV_W ** -0.5),
        'conv_b': nrm(ks[17], (DEPTH, 2 * FFN_DIM), 0.01),
        'w_down': nrm(ks[18], (DEPTH, FFN_DIM, D_MODEL), FFN_DIM ** -0.5),
    }


def reference(x, meta_tokens, norm1_w, w_in, w_ret_o, q_norm_w, k_norm_w, lambda_q1, lambda_k1,
              lambda_q2, lambda_k2, diff_subln_w, w_diff_o, w_out, norm2_w, w_up, conv_w, conv_b, w_down):
    B, S, D = x.shape
    L = N_META + S
    Lp = PAD + L
    dt = x.dtype
    h = jnp.concatenate([jnp.broadcast_to(meta_tokens.astype(dt)[None], (B, N_META, D)), x], axis=1)
    pos = jnp.arange(Lp, dtype=jnp.float32) - PAD

    for l in range(DEPTH):
        lam_init = 0.8 - 0.6 * math.exp(-0.3 * l)
        u = rmsnorm(h, norm1_w[l])
        u_pad = jnp.pad(u, ((0, 0), (PAD, 0), (0, 0)))
        proj = u_pad @ w_in[l].astype(dt)
        rq, rk, rv, rg, dq, dk, dv, gates = jnp.split(proj, SPLIT_IDX, axis=-1)

        rq = rope(rq.reshape(B, Lp, RET_HEADS, RET_DK), pos, RET_DK, RET_THETA)
        rk = rope(rk.reshape(B, Lp, RET_HEADS, RET_DK), pos, RET_DK, RET_THETA) * (RET_DK ** -0.5)
        rv = rv.reshape(B, Lp, RET_HEADS, RET_DV)
        ro = retention(rq.astype(jnp.float32), rk.astype(jnp.float32), rv.astype(jnp.float32))
        mu = jnp.mean(ro, axis=-1, keepdims=True)
        var = jnp.mean(jnp.square(ro - mu), axis=-1, keepdims=True)
        ro = ((ro - mu) * lax.rsqrt(var + EPS)).astype(dt).reshape(B, Lp, RV)
        ro = (jax.nn.silu(rg) * ro) @ w_ret_o[l].astype(dt)

        dq = rope(rmsnorm(dq.reshape(B, Lp, DIFF_HEADS, 2, DIFF_DH), q_norm_w[l]), pos, ROPE_DIM, ROPE_THETA)
        dk = rope(rmsnorm(dk.reshape(B, Lp, DIFF_HEADS, 2, DIFF_DH), k_norm_w[l]), pos, ROPE_DIM, ROPE_THETA)
        dv = dv.reshape(B, Lp, DIFF_HEADS, DIFF_DV)
        lam = (jnp.exp(jnp.sum(lambda_q1[l].astype(jnp.float32) * lambda_k1[l].astype(jnp.float32)))
               - jnp.exp(jnp.sum(lambda_q2[l].astype(jnp.float32) * lambda_k2[l].astype(jnp.float32))) + lam_init)
        do = diff_attention(dq.astype(jnp.float32), dk.astype(jnp.float32), dv.astype(jnp.float32), lam)
        do = (rmsnorm(do, diff_subln_w[l]) * (1.0 - lam_init)).astype(dt).reshape(B, Lp, DV)
        do = do @ w_diff_o[l].astype(dt)

        g_ret, g_diff = jnp.split(gates, 2, axis=-1)
        mix = (jax.nn.sigmoid(g_ret) * ro + jax.nn.sigmoid(g_diff) * do) @ w_out[l].astype(dt)
        h = h + mix[:, PAD:]

        u = rmsnorm(h, norm2_w[l])
        up = causal_dwconv(u @ w_up[l].astype(dt), conv_w[l], conv_b[l])
        a, b = jnp.split(up, 2, axis=-1)
        h = h + (jax.nn.silu(a) * b) @ w_down[l].astype(dt)

    return h[:, N_META:]
```

```python
import math
import numpy as np
from contextlib import ExitStack
import concourse.bass as bass
import concourse.mybir as mybir
from concourse.bass_utils import run_bass_kernel_spmd

F32 = mybir.dt.float32
BF16 = mybir.dt.bfloat16
AF = mybir.ActivationFunctionType
ALU = mybir.AluOpType
AX = mybir.AxisListType

D = 1024
N_META = 16
PAD = 112
FFN = 2816
IN_COLS = 11264
EPS = 1e-6
C_RQ, C_RK, C_RV, C_RG, C_DQ, C_DK, C_DV, C_GT = 0, 1024, 2048, 4096, 6144, 7168, 8192, 9216
NEGB = -30000.0
LAM_INIT = 0.8 - 0.6 * math.exp(-0.3 * 0)
GAM = [1.0 - 2.0 ** (-5.0 - h) for h in range(4)]

SAME_ENGINE_SYNC = True


class Buf:
    __slots__ = ("name", "last_write", "reads", "dsem", "dcount")

    def __init__(self, name=""):
        self.name = name
        self.last_write = None
        self.reads = {}
        self.dsem = None
        self.dcount = 0


class T:
    def __init__(self, t, name):
        self.t = t
        self.b = Buf(name)


class Prog:
    ENGS = ("pe", "act", "dve", "pool", "sp")
    ENGOBJ = {"pe": "tensor", "act": "scalar", "dve": "vector", "pool": "gpsimd", "sp": "sync"}

    def __init__(self, nc, stack):
        self.nc = nc
        self.stack = stack
        self.q = {e: [] for e in self.ENGS}
        self.ecount = {e: 0 for e in self.ENGS}
        self.sems = {}
        for e in self.ENGS:
            self.sems[("e", e)] = stack.enter_context(nc.semaphore("s_" + e))
        self.waited = {e: {} for e in self.ENGS}
        self.ndsem = 0
        self.n_inst = 0
        self.n_wait = 0

    def _dsem(self, buf):
        if buf.dsem is None:
            buf.dsem = ("d", self.ndsem)
            self.sems[buf.dsem] = self.stack.enter_context(self.nc.semaphore("d%d" % self.ndsem))
            self.ndsem += 1
        return buf.dsem

    def _deps(self, eng, reads, writes):
        deps = {}

        def add(t):
            if t is None:
                return
            k, v = t
            if deps.get(k, -1) < v:
                deps[k] = v
        for b in reads:
            add(b.last_write)
        for b in writes:
            add(b.last_write)
            for k, v in b.reads.items():
                add((k, v))
        out = []
        w = self.waited[eng]
        for k, v in deps.items():
            if k == ("e", eng) and (eng == "pe" or not SAME_ENGINE_SYNC):
                continue
            if w.get(k, -1) >= v:
                continue
            w[k] = v
            out.append((k, v))
        return out

    def _commit(self, tok, reads, writes):
        k, v = tok
        for b in writes:
            b.last_write = tok
            b.reads = {}
        for b in reads:
            if b.reads.get(k, -1) < v:
                b.reads[k] = v

    def op(self, eng, fn, reads=(), writes=()):
        waits = self._deps(eng, reads, writes)
        self.ecount[eng] += 1
        tok = (("e", eng), self.ecount[eng])
        self.q[eng].append((waits, fn, tok[0], 1))
        self._commit(tok, reads, writes)
        self.n_inst += 1
        self.n_wait += len(waits)
        return tok

    def dma(self, eng, fn, sb, reads=(), writes=()):
        waits = self._deps(eng, reads, writes)
        k = self._dsem(sb)
        sb.dcount += 16
        tok = (k, sb.dcount)
        self.q[eng].append((waits, fn, k, 16))
        self._commit(tok, reads, writes)
        self.n_inst += 1
        self.n_wait += len(waits)
        return tok

    def wait_all(self, eng, bufs):
        waits = self._deps(eng, bufs, bufs)
        self.q[eng].append((waits, None, None, 0))

    def emit(self):
        nc = self.nc
        sems = self.sems
        with nc.Block() as block:
            for e in self.ENGS:
                lst = self.q[e]

                def body(eo, lst=lst):
                    for waits, fn, sk, inc in lst:
                        for k, v in waits:
                            eo.wait_ge(sems[k], v)
                        if fn is not None:
                            fn(eo).then_inc(sems[sk], inc)
                getattr(block, self.ENGOBJ[e])(body)
        self.q = {e: [] for e in self.ENGS}


def bc_mid(ap, n):
    return bass.AP(ap.tensor, ap.offset, [list(ap.ap[0]), [0, n], list(ap.ap[1])])


def bc_last(ap, k):
    return bass.AP(ap.tensor, ap.offset, [list(ap.ap[0]), list(ap.ap[1]), [0, k]])


def bc_part(dram_ap_1d, n):
    return bass.AP(dram_ap_1d.tensor, dram_ap_1d.offset, [[0, 128], [1, n]])


def build(NC, NO, debug=False):
    NB = NC + NO
    nc = bass.Bass("TRN2", target_bir_lowering=False)

    def din(name, shape, dt=F32):
        return nc.dram_tensor(name, list(shape), dt, kind="ExternalInput").ap()

    okind = "ExternalOutput" if debug else "Internal"

    def dscr(name, shape, dt):
        return nc.dram_tensor(name, list(shape), dt, kind=okind).ap()

    xc = din("xc", [NC * 128, D])
    xo = din("xo", [NO * 128, D])
    w_in = din("w_in", [D, IN_COLS])
    w_ret_o = din("w_ret_o", [2048, D])
    w_diff_o = din("w_diff_o", [D, D])
    w_out = din("w_out", [D, D])
    w_up = din("w_up", [D, 2 * FFN])
    w_down = din("w_down", [FFN, D])
    norm1_w = din("norm1_w", [D])
    norm2_w = din("norm2_w", [D])
    qk_norm_w = din("qk_norm_w", [256])
    lambdas = din("lambdas", [256])
    subln_w = din("subln_w", [128])
    conv_w = din("conv_w", [3, 2 * FFN])
    conv_b = din("conv_b", [2 * FFN])
    rope_r = din("rope_r", [NB * 128, 512])
    rope_d = din("rope_d", [NB * 128, 32])
    kbias_d = din("kbias", [128, NB])
    cmask_d = din("cmask", [128, 128])
    rdec_d = din("rdec", [128, 8])
    y = nc.dram_tensor("y", [NO * 128, D], F32, kind="ExternalOutput").ap()

    UT = dscr("UT", [NB, 128, 1024], BF16)
    GT = dscr("GT", [NO, 128, 2048], BF16)
    DOT = dscr("DOT", [NO, 128, 1024], BF16)
    H2 = dscr("H2", [NO * 128, D], F32)
    U2T = dscr("U2T", [NO, 128, 1024], BF16)
    b_UT = [Buf("UT%d" % i) for i in range(NB)]
    b_GT = [Buf("GT%d" % i) for i in range(NO)]
    b_DOT = [Buf("DOT%d" % i) for i in range(NO)]
    b_H2 = [Buf("H2%d" % i) for i in range(NO)]
    b_U2T = [Buf("U2T%d" % i) for i in range(NO)]
    b_y = Buf("y")

    w_in_v = w_in.rearrange("(k p) n -> p k n", p=128)

    with ExitStack() as gst:
        P = Prog(nc, gst)

        def sbt(st, name, shape, dt):
            return T(st.enter_context(nc.sbuf_tensor("sb_" + name, list(shape), dt)), name)

        def pbank(st, name, dt=F32):
            n = 512 if dt == F32 else 1024
            return T(st.enter_context(nc.psum_tensor("ps_" + name, [128, n], dt)), name)

        ident = sbt(gst, "ident", [128, 128], BF16)
        identf = sbt(gst, "identf", [128, 128], F32)
        cmask = sbt(gst, "cmask", [128, 128], F32)
        cmask2 = sbt(gst, "cmask2", [128, 2, 128], BF16)
        kbias = sbt(gst, "kbias", [128, NB], F32)
        rdec = sbt(gst, "rdec", [128, 8], F32)
        lam = sbt(gst, "lam", [128, 4], F32)
        lamv = sbt(gst, "lamv", [128, 256], F32)
        lamt = sbt(gst, "lamt", [128, 128], F32)
        lams = sbt(gst, "lams", [128, 2], F32)
        sublnw = sbt(gst, "sublnw", [128, 128], F32)
        wqk = sbt(gst, "wqk", [128, 4, 64], F32)

        P.op("pool", lambda e: e.iota(identf.t[:], pattern=[[1, 128]], base=0, channel_multiplier=-1,
                                      allow_small_or_imprecise_dtypes=True), writes=[identf.b])
        P.op("dve", lambda e: e.tensor_scalar(ident.t[:], identf.t[:], 0.0, None, op0=ALU.is_equal),
             reads=[identf.b], writes=[ident.b])
        P.dma("sp", lambda e: e.dma_start(out=cmask.t[:], in_=cmask_d), cmask.b, writes=[cmask.b])
        P.dma("sp", lambda e: e.dma_start(out=kbias.t[:], in_=kbias_d), kbias.b, writes=[kbias.b])
        P.dma("sp", lambda e: e.dma_start(out=rdec.t[:], in_=rdec_d), rdec.b, writes=[rdec.b])
        P.dma("sp", lambda e: e.dma_start(out=lamv.t[:], in_=bc_part(lambdas, 256)), lamv.b, writes=[lamv.b])
        P.dma("sp", lambda e: e.dma_start(out=sublnw.t[:], in_=bc_part(subln_w, 128)), sublnw.b, writes=[sublnw.b])
        P.dma("sp", lambda e: e.dma_start(out=wqk.t[:].rearrange("p a b -> p (a b)"), in_=bc_part(qk_norm_w, 256)),
              wqk.b, writes=[wqk.b])
        P.op("dve", lambda e: e.tensor_copy(cmask2.t[:, 0, :], cmask.t[:]), reads=[cmask.b], writes=[cmask2.b])
        P.op("dve", lambda e: e.tensor_copy(cmask2.t[:, 1, :], cmask.t[:]), reads=[cmask.b], writes=[cmask2.b])
        P.op("dve", lambda e: e.tensor_tensor(out=lamt.t[:, 0:64], in0=lamv.t[:, 0:64], in1=lamv.t[:, 64:128], op=ALU.mult),
             reads=[lamv.b], writes=[lamt.b])
        P.op("dve", lambda e: e.tensor_tensor(out=lamt.t[:, 64:128], in0=lamv.t[:, 128:192], in1=lamv.t[:, 192:256], op=ALU.mult),
             reads=[lamv.b], writes=[lamt.b])
        P.op("dve", lambda e: e.tensor_reduce(out=lams.t[:, 0:2], in_=lamt.t[:].rearrange("p (a b) -> p a b", a=2),
                                              axis=AX.X, op=ALU.add), reads=[lamt.b], writes=[lams.b])
        P.op("act", lambda e: e.activation(out=lams.t[:], in_=lams.t[:], func=AF.Exp), reads=[lams.b], writes=[lams.b])
        P.op("dve", lambda e: e.tensor_tensor(out=lam.t[:, 0:1], in0=lams.t[:, 0:1], in1=lams.t[:, 1:2], op=ALU.subtract),
             reads=[lams.b], writes=[lam.b])
        P.op("dve", lambda e: e.tensor_scalar(lam.t[:, 0:1], lam.t[:, 0:1], LAM_INIT, None, op0=ALU.add),
             reads=[lam.b], writes=[lam.b])
        P.op("dve", lambda e: e.tensor_scalar(sublnw.t[:], sublnw.t[:], 1.0 - LAM_INIT, None, op0=ALU.mult),
             reads=[sublnw.b], writes=[sublnw.b])

        def norm_part(xt, nw, sq, ss, rs, ub):
            P.op("act", lambda e: e.activation(out=sq.t[:], in_=xt.t[:], func=AF.Square, accum_out=ss.t[:, 0:1]),
                 reads=[xt.b], writes=[sq.b, ss.b])
            P.op("act", lambda e: e.activation(out=rs.t[:, 0:1], in_=ss.t[:, 0:1], func=AF.Sqrt, scale=1.0 / D, bias=eps_t.t[:, 0:1]),
                 reads=[ss.b, eps_t.b], writes=[rs.b])
            P.op("dve", lambda e: e.reciprocal(rs.t[:, 0:1], rs.t[:, 0:1]), reads=[rs.b], writes=[rs.b])
            P.op("dve", lambda e: e.scalar_tensor_tensor(out=ub.t[:], in0=xt.t[:], scalar=rs.t[:, 0:1], in1=nw.t[:],
                                                         op0=ALU.mult, op1=ALU.mult),
                 reads=[xt.b, rs.b, nw.b], writes=[ub.b])

        def tr_part(ub, pT, uT):
            for half in range(2):
                pt = pT[half]
                for j in range(4):
                    kc = half * 4 + j
                    P.op("pe", lambda e, kc=kc, j=j, pt=pt: e.transpose(pt.t[:, j * 128:(j + 1) * 128],
                                                                        ub.t[:, kc * 128:(kc + 1) * 128], ident.t[:]),
                         reads=[ub.b, ident.b], writes=[pt.b])
                if half == 0:
                    P.op("dve", lambda e, pt=pt: e.tensor_copy(uT.t[:, 0:4, :].rearrange("p a b -> p (a b)"), pt.t[:, 0:512]),
                         reads=[pt.b], writes=[uT.b])
                else:
                    P.op("act", lambda e, pt=pt: e.copy(uT.t[:, 4:8, :].rearrange("p a b -> p (a b)"), pt.t[:, 0:512]),
                         reads=[pt.b], writes=[uT.b])

        def norm_transpose(xt, nw, sq, ss, rs, ub, pT, uT):
            norm_part(xt, nw, sq, ss, rs, ub)
            tr_part(ub, pT, uT)

        eps_t = sbt(gst, "eps_t", [128, 1], F32)
        P.op("pool", lambda e: e.memset(eps_t.t[:], EPS), writes=[eps_t.b])
        mhalf = sbt(gst, "mhalf", [128, 1], F32)
        P.op("pool", lambda e: e.memset(mhalf.t[:], -0.5), writes=[mhalf.b])

        with ExitStack() as st:
            n1w = sbt(st, "n1w", [128, D], F32)
            P.dma("sp", lambda e: e.dma_start(out=n1w.t[:], in_=bc_part(norm1_w, D)), n1w.b, writes=[n1w.b])
            xb = [sbt(st, "x%d" % i, [128, D], F32) for i in range(3)]
            sq = [sbt(st, "sq%d" % i, [128, D], F32) for i in range(2)]
            ss = [sbt(st, "ss%d" % i, [128, 1], F32) for i in range(2)]
            rs = [sbt(st, "rs%d" % i, [128, 1], F32) for i in range(2)]
            ub = [sbt(st, "ub%d" % i, [128, D], BF16) for i in range(2)]
            uT = [sbt(st, "uT%d" % i, [128, 8, 128], BF16) for i in range(2)]
            pT = [pbank(st, "pT%d" % i, BF16) for i in range(4)]
            def p0_x(blk):
                src = xc[blk * 128:(blk + 1) * 128, :] if blk < NC else xo[(blk - NC) * 128:(blk - NC + 1) * 128, :]
                x_ = xb[blk % 3]
                P.dma("sp", lambda e: e.dma_start(out=x_.t[:], in_=src), x_.b, writes=[x_.b])
                norm_part(x_, n1w, sq[blk % 2], ss[blk % 2], rs[blk % 2], ub[blk % 2])

            def p0_y(blk):
                u_ = uT[blk % 2]
                tr_part(ub[blk % 2], pT[(blk % 2) * 2:(blk % 2) * 2 + 2], u_)
                P.dma("pool", lambda e: e.dma_start(out=UT[blk], in_=u_.t[:].rearrange("p a b -> p (a b)")),
                      u_.b, reads=[u_.b], writes=[b_UT[blk]])
            p0_x(0)
            for blk in range(NB):
                if blk + 1 < NB:
                    p0_x(blk + 1)
                p0_y(blk)
            P.emit()

        with ExitStack() as st:
            WR = [sbt(st, "WR%d" % i, [128, 8, 1536], BF16) for i in range(2)]
            uTb = [sbt(st, "ruT%d" % i, [128, 8, 128], BF16) for i in range(3)]
            RT = [sbt(st, "RT%d" % i, [128, 512], F32) for i in range(3)]
            Rf = sbt(st, "Rf", [128, 2, 512], F32)
            Rb = sbt(st, "Rb", [128, 2, 512], BF16)
            Aq = sbt(st, "Aq", [128, 256], F32)
            Bq = sbt(st, "Bq", [128, 256], F32)
            Ak = sbt(st, "Ak", [128, 256], F32)
            Bk = sbt(st, "Bk", [128, 256], F32)
            qr = [sbt(st, "qr%d" % i, [128, 256], BF16) for i in range(2)]
            kr = [sbt(st, "kr%d" % i, [128, 256], BF16) for i in range(2)]
            vb = [sbt(st, "vb%d" % i, [128, 512], BF16) for i in range(2)]
            sg = [sbt(st, "sg%d" % i, [128, 512], F32) for i in range(2)]
            qkT = sbt(st, "qkT", [128, 4, 128], BF16)
            Sm = sbt(st, "Sm", [128, 128], BF16)
            bst = sbt(st, "bst", [128, 6], F32)
            mv = sbt(st, "mv", [128, 2], F32)
            grs = sbt(st, "grs", [128, 1], F32)
            on = sbt(st, "on", [128, 512], F32)
            gtd = sbt(st, "gtd", [128, 512], BF16)
            gT = [sbt(st, "gT%d" % i, [128, 4, 128], BF16) for i in range(2)]
            pQK = pbank(st, "pQK")
            pV = pbank(st, "pV")
            pG = pbank(st, "pG")
            pTq = pbank(st, "pTq", BF16)
            pS = pbank(st, "pS")
            pO = pbank(st, "pO")
            pR = [pbank(st, "pR%d" % i) for i in range(2)]

            def load_WR(h):
                w = WR[h % 2]
                for (c0, n, o0) in ((C_RQ + h * 256, 256, 0), (C_RK + h * 256, 256, 256),
                                    (C_RV + h * 512, 512, 512), (C_RG + h * 512, 512, 1024)):
                    P.dma("pool", lambda e, w=w, c0=c0, n=n, o0=o0: e.dma_start(out=w.t[:, :, o0:o0 + n],
                                                                                   in_=w_in_v[:, :, c0:c0 + n]),
                          w.b, writes=[w.b])
            load_WR(0)
            for h in range(4):
                if h + 1 < 4:
                    load_WR(h + 1)
                w = WR[h % 2]
                g = GAM[h]
                P.op("pool", lambda e: e.memset(Rf.t[:], 0.0), writes=[Rf.b])
                P.op("pool", lambda e: e.memset(Rb.t[:], 0.0), writes=[Rb.b])

                def A1(blk):
                    own = blk >= NC
                    u_ = uTb[blk % 3]
                    rt = RT[blk % 3]
                    k_ = kr[blk % 2]
                    q_ = qr[blk % 2]
                    P.dma("sp", lambda e: e.dma_start(out=u_.t[:].rearrange("p a b -> p (a b)"), in_=UT[blk]),
                          u_.b, reads=[b_UT[blk]], writes=[u_.b])
                    P.dma("sp", lambda e: e.dma_start(out=rt.t[:], in_=rope_r[blk * 128:(blk + 1) * 128, :]),
                          rt.b, writes=[rt.b])
                    c0 = 0 if own else 256
                    for kc in range(8):
                        P.op("pe", lambda e, kc=kc: e.matmul(pQK.t[:, c0:512], lhsT=u_.t[:, kc, :], rhs=w.t[:, kc, c0:512],
                                                             start=(kc == 0), stop=(kc == 7)),
                             reads=[u_.b, w.b], writes=[pQK.b])
                    P.op("dve", lambda e: e.scalar_tensor_tensor(out=Ak.t[:], in0=pQK.t[:, 256:512], scalar=rdec.t[:, 4 + h:5 + h],
                                                                 in1=rt.t[:, 0:256], op0=ALU.mult, op1=ALU.mult),
                         reads=[pQK.b, rdec.b, rt.b], writes=[Ak.b])
                    P.op("dve", lambda e: e.scalar_tensor_tensor(out=Bk.t[:], in0=pQK.t[:, 256:512], scalar=rdec.t[:, 4 + h:5 + h],
                                                                 in1=rt.t[:, 256:512], op0=ALU.mult, op1=ALU.mult),
                         reads=[pQK.b, rdec.b, rt.b], writes=[Bk.b])
                    if own:
                        P.op("dve", lambda e: e.scalar_tensor_tensor(out=Aq.t[:], in0=pQK.t[:, 0:256], scalar=rdec.t[:, h:h + 1],
                                                                     in1=rt.t[:, 0:256], op0=ALU.mult, op1=ALU.mult),
                             reads=[pQK.b, rdec.b, rt.b], writes=[Aq.b])
                        P.op("dve", lambda e: e.scalar_tensor_tensor(out=Bq.t[:], in0=pQK.t[:, 0:256], scalar=rdec.t[:, h:h + 1],
                                                                     in1=rt.t[:, 256:512], op0=ALU.mult, op1=ALU.mult),
                             reads=[pQK.b, rdec.b, rt.b], writes=[Bq.b])
                    P.op("pool", lambda e: e.tensor_tensor(out=k_.t[:, 0:128], in0=Ak.t[:, 0:128], in1=Bk.t[:, 128:256], op=ALU.subtract),
                         reads=[Ak.b, Bk.b], writes=[k_.b])
                    P.op("pool", lambda e: e.tensor_tensor(out=k_.t[:, 128:256], in0=Ak.t[:, 128:256], in1=Bk.t[:, 0:128], op=ALU.add),
                         reads=[Ak.b, Bk.b], writes=[k_.b])
                    if own:
                        P.op("pool", lambda e: e.tensor_tensor(out=q_.t[:, 0:128], in0=Aq.t[:, 0:128], in1=Bq.t[:, 128:256], op=ALU.subtract),
                             reads=[Aq.b, Bq.b], writes=[q_.b])
                        P.op("pool", lambda e: e.tensor_tensor(out=q_.t[:, 128:256], in0=Aq.t[:, 128:256], in1=Bq.t[:, 0:128], op=ALU.add),
                             reads=[Aq.b, Bq.b], writes=[q_.b])

                def A2(blk):
                    u_ = uTb[blk % 3]
                    v_ = vb[blk % 2]
                    for kc in range(8):
                        P.op("pe", lambda e, kc=kc: e.matmul(pV.t[:, 0:512], lhsT=u_.t[:, kc, :], rhs=w.t[:, kc, 512:1024],
                                                             start=(kc == 0), stop=(kc == 7)),
                             reads=[u_.b, w.b], writes=[pV.b])
                    P.op("act", lambda e: e.copy(v_.t[:], pV.t[:, 0:512]), reads=[pV.b], writes=[v_.b])

                def A3(blk):
                    if blk < NC:
                        return
                    u_ = uTb[blk % 3]
                    s_ = sg[blk % 2]
                    for kc in range(8):
                        P.op("pe", lambda e, kc=kc: e.matmul(pG.t[:, 0:512], lhsT=u_.t[:, kc, :], rhs=w.t[:, kc, 1024:1536],
                                                             start=(kc == 0), stop=(kc == 7)),
                             reads=[u_.b, w.b], writes=[pG.b])
                    P.op("act", lambda e: e.activation(out=s_.t[:], in_=pG.t[:, 0:512], func=AF.Silu), reads=[pG.b], writes=[s_.b])

                def B1(blk):
                    if blk < NC:
                        return
                    k_ = kr[blk % 2]
                    q_ = qr[blk % 2]
                    for j in range(4):
                        srcT = q_ if j < 2 else k_
                        c = j % 2
                        P.op("pe", lambda e, j=j, c=c, srcT=srcT: e.transpose(pTq.t[:, j * 128:(j + 1) * 128],
                                                                              srcT.t[:, c * 128:(c + 1) * 128], ident.t[:]),
                             reads=[srcT.b, ident.b], writes=[pTq.b])
                    P.op("act", lambda e: e.copy(qkT.t[:].rearrange("p a b -> p (a b)"), pTq.t[:, 0:512]),
                         reads=[pTq.b], writes=[qkT.b])

                def B2(blk):
                    if blk < NC:
                        return
                    for c in range(2):
                        P.op("pe", lambda e, c=c: e.matmul(pS.t[:, 0:128], lhsT=qkT.t[:, 2 + c, :], rhs=qkT.t[:, c, :],
                                                           start=(c == 0), stop=(c == 1)),
                             reads=[qkT.b], writes=[pS.b])
                    P.op("dve", lambda e: e.scalar_tensor_tensor(out=Sm.t[:], in0=pS.t[:, 0:128], scalar=float(g ** -128.0),
                                                                 in1=cmask.t[:], op0=ALU.mult, op1=ALU.mult),
                         reads=[pS.b, cmask.b], writes=[Sm.b])

                def B3(blk):
                    if blk < NC:
                        return
                    ob = blk - NC
                    v_ = vb[blk % 2]
                    s_ = sg[blk % 2]
                    P.op("pe", lambda e: e.matmul(pO.t[:, 0:512], lhsT=Sm.t[:], rhs=v_.t[:], start=True, stop=False),
                         reads=[Sm.b, v_.b], writes=[pO.b])
                    for c in range(2):
                        P.op("pe", lambda e, c=c: e.matmul(pO.t[:, 0:512], lhsT=qkT.t[:, c, :], rhs=Rb.t[:, c, :],
                                                           start=False, stop=(c == 1)),
                             reads=[qkT.b, Rb.b], writes=[pO.b])
                    P.op("dve", lambda e: e.bn_stats(bst.t[:], pO.t[:, 0:512]), reads=[pO.b], writes=[bst.b])
                    P.op("dve", lambda e: e.bn_aggr(mv.t[:], bst.t[:]), reads=[bst.b], writes=[mv.b])
                    P.op("act", lambda e: e.activation(out=grs.t[:], in_=mv.t[:, 1:2], func=AF.Sqrt, bias=eps_t.t[:, 0:1]),
                         reads=[mv.b, eps_t.b], writes=[grs.b])
                    P.op("dve", lambda e: e.reciprocal(grs.t[:], grs.t[:]), reads=[grs.b], writes=[grs.b])
                    P.op("dve", lambda e: e.tensor_scalar(on.t[:], pO.t[:, 0:512], mv.t[:, 0:1], grs.t[:, 0:1],
                                                          op0=ALU.subtract, op1=ALU.mult),
                         reads=[pO.b, mv.b, grs.b], writes=[on.b])
                    P.op("pool", lambda e: e.tensor_tensor(out=gtd.t[:], in0=on.t[:], in1=s_.t[:], op=ALU.mult),
                         reads=[on.b, s_.b], writes=[gtd.b])

                def B4(blk):
                    k_ = kr[blk % 2]
                    v_ = vb[blk % 2]
                    if blk >= NC:
                        ob = blk - NC
                        g_ = gT[ob % 2]
                        for j in range(4):
                            P.op("pe", lambda e, j=j: e.transpose(pTq.t[:, 512 + j * 128:512 + (j + 1) * 128],
                                                                  gtd.t[:, j * 128:(j + 1) * 128], ident.t[:]),
                                 reads=[gtd.b, ident.b], writes=[pTq.b])
                        P.op("act", lambda e: e.copy(g_.t[:].rearrange("p a b -> p (a b)"), pTq.t[:, 512:1024]),
                             reads=[pTq.b], writes=[g_.b])
                        P.dma("pool", lambda e: e.dma_start(out=GT[ob][:, h * 512:(h + 1) * 512],
                                                            in_=g_.t[:].rearrange("p a b -> p (a b)")),
                              g_.b, reads=[g_.b], writes=[b_GT[ob]])
                    if blk < NB - 1:
                        for c in range(2):
                            P.op("pe", lambda e, c=c: e.matmul(pR[c].t[:, 0:512], lhsT=k_.t[:, c * 128:(c + 1) * 128], rhs=v_.t[:],
                                                               start=True, stop=True),
                                 reads=[k_.b, v_.b], writes=[pR[c].b])
                            P.op("dve", lambda e, c=c: e.scalar_tensor_tensor(out=Rf.t[:, c, :], in0=Rf.t[:, c, :], scalar=float(g ** 128.0),
                                                                              in1=pR[c].t[:, 0:512], op0=ALU.mult, op1=ALU.add),
                                 reads=[Rf.b, pR[c].b], writes=[Rf.b])
                        P.op("pool", lambda e: e.tensor_copy(Rb.t[:], Rf.t[:]), reads=[Rf.b], writes=[Rb.b])

                A1(0); A2(0); A3(0)
                for blk in range(NB):
                    nx = blk + 1
                    B1(blk)
                    if nx < NB:
                        A1(nx)
                    B2(blk)
                    if nx < NB:
                        A2(nx)
                    B3(blk)
                    if nx < NB:
                        A3(nx)
                    B4(blk)
                P.emit()

        KTs = dscr("KTs", [8, 128, NB * 128], BF16)
        QTs = dscr("QTs", [8, 128, NO * 128], BF16)
        VVs = dscr("VVs", [8, 128, NB, 128], BF16)
        b_KTs = Buf("KTs"); b_QTs = Buf("QTs"); b_VVs = Buf("VVs")
        KTs_w = KTs.rearrange("h p (b t) -> p h b t", t=128)
        QTs_w = QTs.rearrange("h p (b t) -> p h b t", t=128)
        VVs_w = VVs.rearrange("h p b e -> p h b e")
        with ExitStack() as st:
            WDa = sbt(st, "WDa", [128, 8, 3072], BF16)
            for k0 in range(0, 8, 2):
                P.dma("pool", lambda e, k0=k0: e.dma_start(out=WDa.t[:, k0:k0 + 2, :], in_=w_in_v[:, k0:k0 + 2, C_DQ:C_DQ + 3072]),
                      WDa.b, writes=[WDa.b])
            ropd = sbt(st, "ropd", [128, NB, 32], F32)
            ropd_v = rope_d.rearrange("(b p) c -> p b c", p=128)
            for b0 in range(0, NB, 16):
                b1 = min(NB, b0 + 16)
                P.dma("sp", lambda e, b0=b0, b1=b1: e.dma_start(out=ropd.t[:, b0:b1, :], in_=ropd_v[:, b0:b1, :]),
                      ropd.b, writes=[ropd.b])
            wq8 = sbt(st, "wq8", [128, 8, 64], F32)
            wk8 = sbt(st, "wk8", [128, 8, 64], F32)
            for g8 in range(8):
                P.op("pool", lambda e, g8=g8: e.tensor_copy(wq8.t[:, g8, :], wqk.t[:, 0, :]), reads=[wqk.b], writes=[wq8.b])
                P.op("pool", lambda e, g8=g8: e.tensor_copy(wk8.t[:, g8, :], wqk.t[:, 2, :]), reads=[wqk.b], writes=[wk8.b])
            uTb = [sbt(st, "duT%d" % i, [128, 8, 128], BF16) for i in range(3)]
            NCH = 4
            sqd = [sbt(st, "sqd%d" % i, [128, 8, 64], F32) for i in range(NCH)]
            ssd = [sbt(st, "ssd%d" % i, [128, 8], F32) for i in range(NCH)]
            rsd = [sbt(st, "rsd%d" % i, [128, 8], F32) for i in range(NCH)]
            xn = [[sbt(st, "xn%d_%d" % (pp, i), [128, 8, 64], F32) for i in range(NCH)] for pp in range(2)]
            xbq = [[sbt(st, "xbq%d_%d" % (pp, i), [128, 8, 64], BF16) for i in range(NCH)] for pp in range(2)]
            rc = [[sbt(st, "rc%d_%d" % (pp, i), [128, 8, 16], F32) for i in range(NCH)] for pp in range(2)]
            rsn = [[sbt(st, "rsn%d_%d" % (pp, i), [128, 8, 16], F32) for i in range(NCH)] for pp in range(2)]
            kst = [sbt(st, "kst%d" % i, [128, 8, 128], BF16) for i in range(2)]
            qst = [sbt(st, "qst%d" % i, [128, 8, 128], BF16) for i in range(2)]
            vst = [sbt(st, "vst%d" % i, [128, 8, 128], BF16) for i in range(2)]
            pq = [pbank(st, "pq%d" % i) for i in range(2)]
            pk = [pbank(st, "pk%d" % i) for i in range(2)]
            pvv = [pbank(st, "pvv%d" % i) for i in range(2)]
            pTk = pbank(st, "pTk", BF16)
            pTq = pbank(st, "pTq1", BF16)
            def mk_chains(blk):
                chains = []
                for half in range(2):
                    chains.append((pk[half], wk8, pTk, half, 1024 + half * 512))
                if blk >= NC:
                    for half in range(2):
                        chains.append((pq[half], wq8, pTq, half, half * 512))
                return chains

            def d1_early(blk):
                u_ = uTb[blk % 3]
                par = blk % 2
                P.dma("sp", lambda e: e.dma_start(out=u_.t[:].rearrange("p a b -> p (a b)"), in_=UT[blk]),
                      u_.b, reads=[b_UT[blk]], writes=[u_.b])
                chains = mk_chains(blk)
                for (pb, wt, ptT, half, c0) in chains:
                    for kc in range(8):
                        P.op("pe", lambda e, kc=kc, pb=pb, c0=c0: e.matmul(pb.t[:, 0:512], lhsT=u_.t[:, kc, :], rhs=WDa.t[:, kc, c0:c0 + 512],
                                                                           start=(kc == 0), stop=(kc == 7)),
                             reads=[u_.b, WDa.b], writes=[pb.b])
                for half in range(2):
                    for kc in range(8):
                        P.op("pe", lambda e, kc=kc, half=half: e.matmul(pvv[half].t[:, 0:512], lhsT=u_.t[:, kc, :],
                                                                       rhs=WDa.t[:, kc, 2048 + half * 512:2048 + (half + 1) * 512],
                                                                       start=(kc == 0), stop=(kc == 7)),
                             reads=[u_.b, WDa.b], writes=[pvv[half].b])
                nch = len(chains)
                pvw = [ch[0].t[:, 0:512].rearrange("p (a b) -> p a b", b=64) for ch in chains]
                for ci in range(nch):
                    P.op("act", lambda e, ci=ci: e.activation(out=sqd[ci].t[:], in_=pvw[ci], func=AF.Square),
                         reads=[chains[ci][0].b], writes=[sqd[ci].b])
                for ci in range(nch):
                    P.op("dve", lambda e, ci=ci: e.tensor_reduce(out=ssd[ci].t[:], in_=sqd[ci].t[:], axis=AX.X, op=ALU.add),
                         reads=[sqd[ci].b], writes=[ssd[ci].b])
                for ci in range(nch):
                    P.op("act", lambda e, ci=ci: e.activation(out=rsd[ci].t[:], in_=ssd[ci].t[:], func=AF.Sqrt, scale=1.0 / 64, bias=eps_t.t[:, 0:1]),
                         reads=[ssd[ci].b, eps_t.b], writes=[rsd[ci].b])
                v_ = vst[par]
                for half in range(2):
                    P.op("act", lambda e, half=half: e.copy(v_.t[:, half * 4:half * 4 + 4, :].rearrange("p a b -> p (a b)"), pvv[half].t[:, 0:512]),
                         reads=[pvv[half].b], writes=[v_.b])
                P.dma("sp", lambda e: e.dma_start(out=VVs_w[:, :, blk, :], in_=v_.t[:]), v_.b, reads=[v_.b], writes=[b_VVs])
                for ci in range(nch):
                    P.op("dve", lambda e, ci=ci: e.reciprocal(rsd[ci].t[:], rsd[ci].t[:]), reads=[rsd[ci].b], writes=[rsd[ci].b])
                for ci in range(nch):
                    P.op("dve", lambda e, ci=ci: e.tensor_tensor(out=xn[par][ci].t[:], in0=pvw[ci], in1=bc_last(rsd[ci].t[:], 64), op=ALU.mult),
                         reads=[chains[ci][0].b, rsd[ci].b], writes=[xn[par][ci].b])

            def d1_late(blk):
                par = blk % 2
                own = blk >= NC
                ob = blk - NC
                chains = mk_chains(blk)
                nch = len(chains)
                xn_, xb_, rc_, rsn_ = xn[par], xbq[par], rc[par], rsn[par]
                for ci in range(nch):
                    P.op("dve", lambda e, ci=ci: e.tensor_tensor(out=xn_[ci].t[:], in0=xn_[ci].t[:], in1=chains[ci][1].t[:], op=ALU.mult),
                         reads=[xn_[ci].b, chains[ci][1].b], writes=[xn_[ci].b])
                for ci in range(nch):
                    P.op("act", lambda e, ci=ci: e.copy(xb_[ci].t[:], xn_[ci].t[:]), reads=[xn_[ci].b], writes=[xb_[ci].b])
                    P.op("pool", lambda e, ci=ci: e.tensor_tensor(out=rc_[ci].t[:], in0=xn_[ci].t[:, :, 0:16],
                                                                 in1=bc_mid(ropd.t[:, blk, 0:16], 8), op=ALU.mult),
                         reads=[xn_[ci].b, ropd.b], writes=[rc_[ci].b])
                    P.op("pool", lambda e, ci=ci: e.tensor_tensor(out=rsn_[ci].t[:], in0=xn_[ci].t[:, :, 0:16],
                                                                 in1=bc_mid(ropd.t[:, blk, 16:32], 8), op=ALU.mult),
                         reads=[xn_[ci].b, ropd.b], writes=[rsn_[ci].b])
                    P.op("pool", lambda e, ci=ci: e.tensor_tensor(out=xb_[ci].t[:, :, 0:8], in0=rc_[ci].t[:, :, 0:8], in1=rsn_[ci].t[:, :, 8:16], op=ALU.subtract),
                         reads=[rc_[ci].b, rsn_[ci].b], writes=[xb_[ci].b])
                    P.op("pool", lambda e, ci=ci: e.tensor_tensor(out=xb_[ci].t[:, :, 8:16], in0=rc_[ci].t[:, :, 8:16], in1=rsn_[ci].t[:, :, 0:8], op=ALU.add),
                         reads=[rc_[ci].b, rsn_[ci].b], writes=[xb_[ci].b])
                for ci in range(nch):
                    ptT, half = chains[ci][2], chains[ci][3]
                    for hh in range(4):
                        P.op("pe", lambda e, ci=ci, hh=hh, ptT=ptT, half=half: e.transpose(ptT.t[:, (half * 4 + hh) * 128:(half * 4 + hh + 1) * 128],
                                                                                          xb_[ci].t[:, 2 * hh:2 * hh + 2, :].rearrange("p a b -> p (a b)"),
                                                                                          ident.t[:]),
                             reads=[xb_[ci].b, ident.b], writes=[ptT.b])
                k_ = kst[par]
                P.op("dve", lambda e: e.tensor_copy(k_.t[:].rearrange("p a b -> p (a b)"), pTk.t[:, 0:1024]), reads=[pTk.b], writes=[k_.b])
                P.dma("sp", lambda e: e.dma_start(out=KTs_w[:, :, blk, :], in_=k_.t[:]), k_.b, reads=[k_.b], writes=[b_KTs])
                if own:
                    q_ = qst[par]
                    P.op("act", lambda e: e.copy(q_.t[:].rearrange("p a b -> p (a b)"), pTq.t[:, 0:1024]), reads=[pTq.b], writes=[q_.b])
                    P.dma("sp", lambda e: e.dma_start(out=QTs_w[:, :, ob, :], in_=q_.t[:]), q_.b, reads=[q_.b], writes=[b_QTs])

            d1_early(0)
            for blk in range(NB):
                if blk + 1 < NB:
                    d1_early(blk + 1)
                d1_late(blk)
            P.emit()

        with ExitStack() as st:
            KTb = [sbt(st, "KT%d" % i, [128, NB * 128], BF16) for i in range(2)]
            VVb = [sbt(st, "VV%d" % i, [128, NB, 130], BF16) for i in range(2)]
            QT2b = [sbt(st, "QT2%d" % i, [128, NO, 256], BF16) for i in range(2)]
            NPT = 6
            PT = [sbt(st, "PT%d" % i, [128, 4, 128], BF16) for i in range(NPT)]
            zz = sbt(st, "zz", [128, 2], F32)
            a1 = sbt(st, "a1", [128, 128], F32)
            aa = sbt(st, "aa", [128, 128], F32)
            asq = sbt(st, "asq", [128, 128], F32)
            ass = sbt(st, "ass", [128, 1], F32)
            ars = sbt(st, "ars", [128, 1], F32)
            dob = sbt(st, "dob", [128, 128], BF16)
            doT = [sbt(st, "doT%d" % i, [128, 128], BF16) for i in range(2)]
            pP = pbank(st, "pP")
            pTd = pbank(st, "pTd", BF16)
            pSd = [pbank(st, "pSd%d" % i) for i in range(2)]
            pO0 = [pbank(st, "pO0%d" % i) for i in range(2)]
            pO1 = [pbank(st, "pO1%d" % i) for i in range(2)]
            assert NC % 2 == 0
            for i2 in range(2):
                P.op("pool", lambda e, i2=i2: e.memset(VVb[i2].t[:], 0.0), writes=[VVb[i2].b])
                P.op("dve", lambda e, i2=i2: e.tensor_copy(VVb[i2].t[:, :, 128:129], kbias.t[:].rearrange("p (a b) -> p a b", b=1)),
                     reads=[kbias.b], writes=[VVb[i2].b])
                P.op("pool", lambda e, i2=i2: e.memset(QT2b[i2].t[:], 0.0), writes=[QT2b[i2].b])

            def load_head(h):
                kt, vv, q2 = KTb[h % 2], VVb[h % 2], QT2b[h % 2]
                P.dma("sp", lambda e: e.dma_start(out=kt.t[:], in_=KTs[h]), kt.b, reads=[b_KTs], writes=[kt.b])
                P.dma("sp", lambda e: e.dma_start(out=vv.t[:, :, 0:128], in_=VVs[h]), vv.b, reads=[b_VVs], writes=[vv.b])
                P.dma("sp", lambda e: e.dma_start(out=q2.t[0:64, :, 0:128], in_=QTs[h][0:64, :].rearrange("p (i t) -> p i t", t=128)),
                      q2.b, reads=[b_QTs], writes=[q2.b])
                P.dma("sp", lambda e: e.dma_start(out=q2.t[64:128, :, 128:256], in_=QTs[h][64:128, :].rearrange("p (i t) -> p i t", t=128)),
                      q2.b, reads=[b_QTs], writes=[q2.b])
            load_head(0)
            for h in range(8):
                if h + 1 < 8:
                    load_head(h + 1)
                KT, VV, QT2 = KTb[h % 2], VVb[h % 2], QT2b[h % 2]
                b_K = [KT.b] * NB
                b_Kv = VV.b
                b_Q = [QT2.b] * NO
                items = []
                for i in range(NO):
                    nk = NC + i + 1
                    for kb0 in range(0, nk, 2):
                        items.append((i, kb0, min(2, nk - kb0)))
                SKEW = 2
                pS3 = [pSd[0], pSd[1], pP]

                def qk_exp(n):
                    i, kb0, nb = items[n]
                    nk = NC + i + 1
                    ps = pS3[n % 3]
                    pt = PT[n % NPT]
                    for j in range(nb):
                        kb = kb0 + j
                        P.op("pe", lambda e, j=j, kb=kb: e.matmul(ps.t[:, j * 256:(j + 1) * 256], lhsT=KT.t[:, kb * 128:(kb + 1) * 128],
                                                                  rhs=QT2.t[:, i, :], start=True, stop=True),
                             reads=[b_K[kb], b_Q[i]], writes=[ps.b])
                    P.op("act", lambda e: e.activation(out=pt.t[:, 0:2 * nb, :].rearrange("p a b -> p (a b)"),
                                                       in_=ps.t[:, 0:256 * nb], func=AF.Exp, scale=0.125),
                         reads=[ps.b], writes=[pt.b])
                    if kb0 + nb == nk:
                        jl = nb - 1
                        P.op("pool", lambda e: e.tensor_tensor(out=pt.t[:, 2 * jl:2 * jl + 2, :], in0=pt.t[:, 2 * jl:2 * jl + 2, :],
                                                               in1=cmask2.t[:], op=ALU.mult),
                             reads=[pt.b, cmask2.b], writes=[pt.b])

                def pv(n):
                    i, kb0, nb = items[n]
                    nk = NC + i + 1
                    o0 = pO0[i % 2]
                    o1 = pO1[i % 2]
                    pt = PT[n % NPT]
                    for j in range(nb):
                        kb = kb0 + j
                        P.op("pe", lambda e, j=j, kb=kb: e.matmul(o0.t[:, 0:129], lhsT=pt.t[:, 2 * j, :], rhs=VV.t[:, kb, 0:129],
                                                                  start=(kb == 0), stop=(kb == nk - 1)),
                             reads=[pt.b, b_Kv], writes=[o0.b])
                        P.op("pe", lambda e, j=j, kb=kb: e.matmul(o1.t[:, 0:129], lhsT=pt.t[:, 2 * j + 1, :], rhs=VV.t[:, kb, 0:129],
                                                                  start=(kb == 0), stop=(kb == nk - 1)),
                             reads=[pt.b, b_Kv], writes=[o1.b])
                    if kb0 + nb == nk:
                        finalize(i, o0, o1)

                def finalize(i, o0, o1):
                    while pend_fin:
                        pend_fin.pop(0)[1]()
                    P.op("dve", lambda e, o0=o0: e.reciprocal(zz.t[:, 0:1], o0.t[:, 128:129]), reads=[o0.b], writes=[zz.b])
                    P.op("dve", lambda e, o1=o1: e.reciprocal(zz.t[:, 1:2], o1.t[:, 128:129]), reads=[o1.b], writes=[zz.b])
                    P.op("dve", lambda e: e.tensor_tensor(out=zz.t[:, 1:2], in0=zz.t[:, 1:2], in1=lam.t[:, 0:1], op=ALU.mult),
                         reads=[zz.b, lam.b], writes=[zz.b])
                    P.op("dve", lambda e, o1=o1: e.tensor_scalar(a1.t[:], o1.t[:, 0:128], zz.t[:, 1:2], None, op0=ALU.mult),
                         reads=[o1.b, zz.b], writes=[a1.b])
                    P.op("dve", lambda e, o0=o0: e.scalar_tensor_tensor(out=aa.t[:], in0=o0.t[:, 0:128], scalar=zz.t[:, 0:1], in1=a1.t[:],
                                                                        op0=ALU.mult, op1=ALU.subtract),
                         reads=[o0.b, zz.b, a1.b], writes=[aa.b])
                    pend_fin.append((n_now[0] + 3, lambda: finalize_b(i)))

                def finalize_b(i):
                    P.op("act", lambda e: e.activation(out=asq.t[:], in_=aa.t[:], func=AF.Square, accum_out=ass.t[:, 0:1]),
                         reads=[aa.b], writes=[asq.b, ass.b])
                    P.op("act", lambda e: e.activation(out=ars.t[:], in_=ass.t[:], func=AF.Sqrt, scale=1.0 / 128, bias=eps_t.t[:, 0:1]),
                         reads=[ass.b, eps_t.b], writes=[ars.b])
                    P.op("dve", lambda e: e.reciprocal(ars.t[:], ars.t[:]), reads=[ars.b], writes=[ars.b])
                    P.op("dve", lambda e: e.scalar_tensor_tensor(out=dob.t[:], in0=aa.t[:], scalar=ars.t[:, 0:1], in1=sublnw.t[:],
                                                                 op0=ALU.mult, op1=ALU.mult),
                         reads=[aa.b, ars.b, sublnw.b], writes=[dob.b])
                    P.op("pe", lambda e: e.transpose(pTd.t[:, 256:384], dob.t[:], ident.t[:]), reads=[dob.b, ident.b], writes=[pTd.b])
                    d_ = doT[i % 2]
                    P.op("dve", lambda e, d_=d_: e.tensor_copy(d_.t[:], pTd.t[:, 256:384]), reads=[pTd.b], writes=[d_.b])
                    P.dma("pool", lambda e, d_=d_, i=i: e.dma_start(out=DOT[i][:, h * 128:(h + 1) * 128], in_=d_.t[:]),
                          d_.b, reads=[d_.b], writes=[b_DOT[i]])
                pend_fin = []
                n_now = [0]
                for n in range(len(items) + SKEW + 4):
                    n_now[0] = n
                    if n < len(items):
                        qk_exp(n)
                    if 0 <= n - SKEW < len(items):
                        pv(n - SKEW)
                    while pend_fin and pend_fin[0][0] <= n:
                        pend_fin.pop(0)[1]()
                assert not pend_fin
                P.emit()

        with ExitStack() as st:
            Wro = sbt(st, "Wro", [128, 16, 1024], BF16)
            Wdo = sbt(st, "Wdo", [128, 8, 1024], BF16)
            Wou = sbt(st, "Wou", [128, 8, 1024], BF16)
            Wg = sbt(st, "Wg", [128, 8, 2048], BF16)
            n2w = sbt(st, "n2w", [128, D], F32)
            P.dma("sp", lambda e: e.dma_start(out=n2w.t[:], in_=bc_part(norm2_w, D)), n2w.b, writes=[n2w.b])
            wro_v = w_ret_o.rearrange("(k p) n -> p k n", p=128)
            for k0 in range(0, 16, 4):
                P.dma("pool", lambda e, k0=k0: e.dma_start(out=Wro.t[:, k0:k0 + 4, :], in_=wro_v[:, k0:k0 + 4, :]), Wro.b, writes=[Wro.b])
            wdo_v = w_diff_o.rearrange("(k p) n -> p k n", p=128)
            wou_v = w_out.rearrange("(k p) n -> p k n", p=128)
            for k0 in range(0, 8, 4):
                P.dma("pool", lambda e, k0=k0: e.dma_start(out=Wdo.t[:, k0:k0 + 4, :], in_=wdo_v[:, k0:k0 + 4, :]), Wdo.b, writes=[Wdo.b])
                P.dma("pool", lambda e, k0=k0: e.dma_start(out=Wou.t[:, k0:k0 + 4, :], in_=wou_v[:, k0:k0 + 4, :]), Wou.b, writes=[Wou.b])
            for k0 in range(0, 8, 2):
                P.dma("pool", lambda e, k0=k0: e.dma_start(out=Wg.t[:, k0:k0 + 2, :], in_=w_in_v[:, k0:k0 + 2, C_GT:C_GT + 2048]), Wg.b, writes=[Wg.b])
            gTb = [sbt(st, "mgT%d" % i, [128, 16, 128], BF16) for i in range(2)]
            dTb = [sbt(st, "mdT%d" % i, [128, 8, 128], BF16) for i in range(2)]
            uTb = [sbt(st, "muT%d" % i, [128, 8, 128], BF16) for i in range(2)]
            xb = [sbt(st, "mx%d" % i, [128, D], F32) for i in range(2)]
            sig = sbt(st, "sig", [128, 2048], F32)
            m1 = sbt(st, "m1", [128, D], F32)
            m2 = sbt(st, "m2", [128, D], F32)
            mb = [sbt(st, "mb%d" % i, [128, D], BF16) for i in range(2)]
            mT = sbt(st, "mT", [128, 8, 128], BF16)
            h2 = [sbt(st, "h2%d" % i, [128, D], F32) for i in range(2)]
            sq = sbt(st, "msq", [128, D], F32)
            ss = sbt(st, "mss", [128, 1], F32)
            rs = sbt(st, "mrs", [128, 1], F32)
            ub = sbt(st, "mub", [128, D], BF16)
            u2T = [sbt(st, "mu2T%d" % i, [128, 8, 128], BF16) for i in range(2)]
            pA = [pbank(st, "pA%d" % i) for i in range(4)]
            pB = [pbank(st, "pB%d" % i) for i in range(2)]
            pTm = [pbank(st, "pTm%d" % i, BF16) for i in range(2)]
            def MA1(ob):
                blk = NC + ob
                g_ = gTb[ob % 2]; d_ = dTb[ob % 2]; u_ = uTb[ob % 2]; x_ = xb[ob % 2]
                P.dma("sp", lambda e: e.dma_start(out=g_.t[:].rearrange("p a b -> p (a b)"), in_=GT[ob]),
                      g_.b, reads=[b_GT[ob]], writes=[g_.b])
                P.dma("sp", lambda e: e.dma_start(out=d_.t[:].rearrange("p a b -> p (a b)"), in_=DOT[ob]),
                      d_.b, reads=[b_DOT[ob]], writes=[d_.b])
                P.dma("sp", lambda e: e.dma_start(out=u_.t[:].rearrange("p a b -> p (a b)"), in_=UT[blk]),
                      u_.b, reads=[b_UT[blk]], writes=[u_.b])
                P.dma("sp", lambda e: e.dma_start(out=x_.t[:], in_=xo[ob * 128:(ob + 1) * 128, :]), x_.b, writes=[x_.b])
                for j in range(4):
                    for kc in range(8):
                        P.op("pe", lambda e, j=j, kc=kc: e.matmul(pA[j].t[:, 0:512], lhsT=u_.t[:, kc, :], rhs=Wg.t[:, kc, j * 512:(j + 1) * 512],
                                                                  start=(kc == 0), stop=(kc == 7)),
                             reads=[u_.b, Wg.b], writes=[pA[j].b])
                    P.op("act", lambda e, j=j: e.activation(out=sig.t[:, j * 512:(j + 1) * 512], in_=pA[j].t[:, 0:512], func=AF.Sigmoid),
                         reads=[pA[j].b], writes=[sig.b])

            def MA2(ob):
                g_ = gTb[ob % 2]
                for j in range(2):
                    for kc in range(16):
                        P.op("pe", lambda e, j=j, kc=kc: e.matmul(pA[j].t[:, 0:512], lhsT=g_.t[:, kc, :], rhs=Wro.t[:, kc, j * 512:(j + 1) * 512],
                                                                  start=(kc == 0), stop=(kc == 15)),
                             reads=[g_.b, Wro.b], writes=[pA[j].b])
                    P.op("dve", lambda e, j=j: e.tensor_tensor(out=m1.t[:, j * 512:(j + 1) * 512], in0=pA[j].t[:, 0:512],
                                                               in1=sig.t[:, j * 512:(j + 1) * 512], op=ALU.mult),
                         reads=[pA[j].b, sig.b], writes=[m1.b])

            def MA3(ob):
                d_ = dTb[ob % 2]
                mb_ = mb[ob % 2]
                for j in range(2):
                    for kc in range(8):
                        P.op("pe", lambda e, j=j, kc=kc: e.matmul(pA[2 + j].t[:, 0:512], lhsT=d_.t[:, kc, :], rhs=Wdo.t[:, kc, j * 512:(j + 1) * 512],
                                                                  start=(kc == 0), stop=(kc == 7)),
                             reads=[d_.b, Wdo.b], writes=[pA[2 + j].b])
                    P.op("dve", lambda e, j=j: e.tensor_tensor(out=m2.t[:, j * 512:(j + 1) * 512], in0=pA[2 + j].t[:, 0:512],
                                                               in1=sig.t[:, 1024 + j * 512:1024 + (j + 1) * 512], op=ALU.mult),
                         reads=[pA[2 + j].b, sig.b], writes=[m2.b])
                P.op("pool", lambda e: e.tensor_tensor(out=mb_.t[:], in0=m1.t[:], in1=m2.t[:], op=ALU.add), reads=[m1.b, m2.b], writes=[mb_.b])

            def MB1(ob):
                mb_ = mb[ob % 2]
                for half in range(2):
                    pt = pTm[half]
                    for j in range(4):
                        kc = half * 4 + j
                        P.op("pe", lambda e, kc=kc, j=j, pt=pt: e.transpose(pt.t[:, j * 128:(j + 1) * 128], mb_.t[:, kc * 128:(kc + 1) * 128], ident.t[:]),
                             reads=[mb_.b, ident.b], writes=[pt.b])
                    if half == 0:
                        P.op("dve", lambda e, pt=pt: e.tensor_copy(mT.t[:, 0:4, :].rearrange("p a b -> p (a b)"), pt.t[:, 0:512]),
                             reads=[pt.b], writes=[mT.b])
                    else:
                        P.op("act", lambda e, pt=pt: e.copy(mT.t[:, 4:8, :].rearrange("p a b -> p (a b)"), pt.t[:, 0:512]),
                             reads=[pt.b], writes=[mT.b])

            def MB2(ob):
                h_ = h2[ob % 2]
                x_ = xb[ob % 2]
                for j in range(2):
                    for kc in range(8):
                        P.op("pe", lambda e, j=j, kc=kc: e.matmul(pB[j].t[:, 0:512], lhsT=mT.t[:, kc, :], rhs=Wou.t[:, kc, j * 512:(j + 1) * 512],
                                                                  start=(kc == 0), stop=(kc == 7)),
                             reads=[mT.b, Wou.b], writes=[pB[j].b])
                    P.op("dve", lambda e, j=j: e.tensor_tensor(out=h_.t[:, j * 512:(j + 1) * 512], in0=pB[j].t[:, 0:512],
                                                               in1=x_.t[:, j * 512:(j + 1) * 512], op=ALU.add),
                         reads=[pB[j].b, x_.b], writes=[h_.b])
                P.dma("pool", lambda e: e.dma_start(out=H2[ob * 128:(ob + 1) * 128, :], in_=h_.t[:]),
                      h_.b, reads=[h_.b], writes=[b_H2[ob]])
                norm_part(h_, n2w, sq, ss, rs, ub)

            def MB3(ob):
                t_ = u2T[ob % 2]
                tr_part(ub, pTm, t_)
                P.dma("pool", lambda e: e.dma_start(out=U2T[ob], in_=t_.t[:].rearrange("p a b -> p (a b)")),
                      t_.b, reads=[t_.b], writes=[b_U2T[ob]])

            MA1(0); MA2(0); MA3(0)
            for ob in range(NO):
                nx = ob + 1
                MB1(ob)
                if nx < NO:
                    MA1(nx)
                MB2(ob)
                if nx < NO:
                    MA2(nx)
                MB3(ob)
                if nx < NO:
                    MA3(nx)
            P.emit()

        GB = 3
        NG = (NO + GB - 1) // GB
        with ExitStack() as st:
            Wup = sbt(st, "Wup", [128, 8, 2 * FFN], BF16)
            Wdn = sbt(st, "Wdn", [128, 22, D], BF16)
            cw = sbt(st, "cw", [128, 3, 44], F32)
            cb = sbt(st, "cb", [128, 44], F32)
            wup_v = w_up.rearrange("(k p) n -> p k n", p=128)
            for kc in range(8):
                for c0 in range(0, 2 * FFN, 1408):
                    P.dma("pool", lambda e, kc=kc, c0=c0: e.dma_start(out=Wup.t[:, kc, c0:c0 + 1408], in_=wup_v[:, kc, c0:c0 + 1408]),
                          Wup.b, writes=[Wup.b])
            wdn_v = w_down.rearrange("(k p) n -> p k n", p=128)
            for k0 in range(0, 22, 2):
                P.dma("pool", lambda e, k0=k0: e.dma_start(out=Wdn.t[:, k0:k0 + 2, :], in_=wdn_v[:, k0:k0 + 2, :]), Wdn.b, writes=[Wdn.b])
            for t0 in range(0, 44, 11):
                for k in range(3):
                    P.dma("sp", lambda e, k=k, t0=t0: e.dma_start(out=cw.t[:, k, t0:t0 + 11],
                                                                  in_=conv_w[k].rearrange("(t p) -> p t", p=128)[:, t0:t0 + 11],
                                                                  allow_slow_non_contiguous=True), cw.b, writes=[cw.b])
                P.dma("sp", lambda e, t0=t0: e.dma_start(out=cb.t[:, t0:t0 + 11], in_=conv_b.rearrange("(t p) -> p t", p=128)[:, t0:t0 + 11],
                                                         allow_slow_non_contiguous=True), cb.b, writes=[cb.b])
            NT = GB * 128
            u2g = [sbt(st, "u2g%d" % i, [128, 8, 2 + NT], BF16) for i in range(2)]
            ya = [sbt(st, "ya%d" % i, [128, NT], F32) for i in range(2)]
            yb = [sbt(st, "yb%d" % i, [128, NT], F32) for i in range(2)]
            sa = [sbt(st, "sa%d" % i, [128, NT], F32) for i in range(2)]
            gTt = sbt(st, "gTt", [128, 22, NT], BF16)
            hb = [sbt(st, "fh%d" % i, [128, D], F32) for i in range(2)]
            ob_ = [sbt(st, "fo%d" % i, [128, D], F32) for i in range(2)]
            pU = [pbank(st, "pU%d" % i) for i in range(4)]
            pD = [pbank(st, "pD%d" % i) for i in range(4)]
            P.op("pool", lambda e: e.memset(u2g[0].t[:], 0.0), writes=[u2g[0].b])
            P.op("pool", lambda e: e.memset(u2g[1].t[:], 0.0), writes=[u2g[1].b])
            ui = 0
            for gi in range(NG):
                blks = list(range(gi * GB, min(NO, (gi + 1) * GB)))
                nt = len(blks) * 128
                ug = u2g[gi % 2]
                up_ = u2g[(gi + 1) % 2]
                for j, ob in enumerate(blks):
                    P.dma("sp", lambda e, ug=ug, j=j, ob=ob: e.dma_start(out=ug.t[:, :, 2 + j * 128:2 + (j + 1) * 128],
                                                                        in_=U2T[ob].rearrange("p (a b) -> p a b", a=8)),
                          ug.b, reads=[b_U2T[ob]], writes=[ug.b])
                if gi > 0:
                    P.op("pool", lambda e, ug=ug, up_=up_: e.tensor_copy(ug.t[:, :, 0:2], up_.t[:, :, NT:NT + 2]),
                         reads=[up_.b], writes=[ug.b])
                for ft in range(22):
                    tiles = []
                    for which, fi in ((0, ft), (1, ft + 22)):
                        pu = pU[ui % 4]
                        ui += 1
                        for kc in range(8):
                            P.op("pe", lambda e, pu=pu, kc=kc, fi=fi, ug=ug, nt=nt: e.matmul(pu.t[:, 0:nt + 2], lhsT=Wup.t[:, kc, fi * 128:(fi + 1) * 128],
                                                                                        rhs=ug.t[:, kc, 0:nt + 2], start=(kc == 0), stop=(kc == 7)),
                                 reads=[Wup.b, ug.b], writes=[pu.b])
                        yt = (ya if which == 0 else yb)[ft % 2]
                        P.op("dve", lambda e, pu=pu, yt=yt, fi=fi, nt=nt: e.tensor_scalar(yt.t[:, 0:nt], pu.t[:, 2:nt + 2], cw.t[:, 2, fi:fi + 1], cb.t[:, fi:fi + 1],
                                                                                       op0=ALU.mult, op1=ALU.add),
                             reads=[pu.b, cw.b, cb.b], writes=[yt.b])
                        P.op("dve", lambda e, pu=pu, yt=yt, fi=fi, nt=nt: e.scalar_tensor_tensor(out=yt.t[:, 0:nt], in0=pu.t[:, 1:nt + 1], scalar=cw.t[:, 1, fi:fi + 1],
                                                                                              in1=yt.t[:, 0:nt], op0=ALU.mult, op1=ALU.add),
                             reads=[pu.b, cw.b, yt.b], writes=[yt.b])
                        P.op("dve", lambda e, pu=pu, yt=yt, fi=fi, nt=nt: e.scalar_tensor_tensor(out=yt.t[:, 0:nt], in0=pu.t[:, 0:nt], scalar=cw.t[:, 0, fi:fi + 1],
                                                                                              in1=yt.t[:, 0:nt], op0=ALU.mult, op1=ALU.add),
                             reads=[pu.b, cw.b, yt.b], writes=[yt.b])
                        tiles.append(yt)
                    s_ = sa[ft % 2]
                    P.op("act", lambda e, s_=s_, yt=tiles[0], nt=nt: e.activation(out=s_.t[:, 0:nt], in_=yt.t[:, 0:nt], func=AF.Silu),
                         reads=[tiles[0].b], writes=[s_.b])
                    P.op("pool", lambda e, s_=s_, yt=tiles[1], ft=ft, nt=nt: e.tensor_tensor(out=gTt.t[:, ft, 0:nt], in0=s_.t[:, 0:nt], in1=yt.t[:, 0:nt], op=ALU.mult),
                         reads=[s_.b, tiles[1].b], writes=[gTt.b])
                for j, ob in enumerate(blks):
                    h_ = hb[ob % 2]
                    o_ = ob_[ob % 2]
                    P.dma("sp", lambda e, h_=h_, ob=ob: e.dma_start(out=h_.t[:], in_=H2[ob * 128:(ob + 1) * 128, :]),
                          h_.b, reads=[b_H2[ob]], writes=[h_.b])
                    for half in range(2):
                        pd = pD[(ob * 2 + half) % 4]
                        for ft in range(22):
                            P.op("pe", lambda e, pd=pd, ft=ft, j=j, half=half: e.matmul(pd.t[:, 0:512], lhsT=gTt.t[:, ft, j * 128:(j + 1) * 128],
                                                                                    rhs=Wdn.t[:, ft, half * 512:(half + 1) * 512],
                                                                                    start=(ft == 0), stop=(ft == 21)),
                                 reads=[gTt.b, Wdn.b], writes=[pd.b])
                        P.op("dve", lambda e, pd=pd, half=half, h_=h_, o_=o_: e.tensor_tensor(out=o_.t[:, half * 512:(half + 1) * 512], in0=pd.t[:, 0:512],
                                                                                          in1=h_.t[:, half * 512:(half + 1) * 512], op=ALU.add),
                             reads=[pd.b, h_.b], writes=[o_.b])
                    P.dma("pool", lambda e, o_=o_, ob=ob: e.dma_start(out=y[ob * 128:(ob + 1) * 128, :], in_=o_.t[:]),
                          o_.b, reads=[o_.b], writes=[b_y])
            P.wait_all("pool", [b_y])
            P.emit()
        print("n_inst", P.n_inst, "n_wait", P.n_wait, "ndsem", P.ndsem)
    return nc


def make_tables(NC, NO, p, S):
    NB = NC + NO
    L = N_META + S
    if p == 0:
        ctx_pos = np.full(NC * 128, -1, np.int64)
        own_pos = np.arange(NO * 128)
    else:
        ctx_pos = np.arange(NC * 128) - PAD
        own_pos = L - NO * 128 + np.arange(NO * 128)
    pos = np.concatenate([ctx_pos, own_pos])
    valid = pos >= 0
    posf = np.where(valid, pos, 0).astype(np.float32)
    inv_r = np.power(np.float32(10000.0), -np.arange(128, dtype=np.float32) / np.float32(128))
    ang = posf[:, None] * inv_r[None, :]
    c, s = np.cos(ang), np.sin(ang)
    rope_r = np.concatenate([c, c, s, s], axis=1).astype(np.float32)
    inv_d = np.power(np.float32(500000.0), -np.arange(8, dtype=np.float32) / np.float32(8))
    ang = posf[:, None] * inv_d[None, :]
    c, s = np.cos(ang), np.sin(ang)
    rope_d = np.concatenate([c, c, s, s], axis=1).astype(np.float32)
    kb = np.where(valid, 1.0, 0.0).astype(np.float32).reshape(NB, 128).T.copy()
    idx = np.arange(128)
    cm = (idx[:, None] <= idx[None, :]).astype(np.float32)
    rdec = np.zeros((128, 8), np.float32)
    for h in range(4):
        rdec[:, h] = GAM[h] ** (idx + 1.0)
        rdec[:, 4 + h] = (256 ** -0.5) * GAM[h] ** (127.0 - idx)
    return rope_r, rope_d, kb, cm, rdec


_NC_CACHE = {}


def run(inputs, NC, NO, debug=False, trace=False):
    x = np.asarray(inputs["x"], np.float32)
    B, S, _ = x.shape
    assert S == 128 * (NC + NO - 1)
    L = N_META + S
    meta = np.asarray(inputs["meta_tokens"], np.float32)
    key = (NC, NO, debug)
    if key not in _NC_CACHE:
        _NC_CACHE[key] = build(NC, NO, debug)
    nc = _NC_CACHE[key]
    f = lambda k: np.ascontiguousarray(np.asarray(inputs[k], np.float32)[0])
    common = {
        "w_in": f("w_in"), "w_ret_o": f("w_ret_o"), "w_diff_o": f("w_diff_o"), "w_out": f("w_out"),
        "w_up": f("w_up"), "w_down": f("w_down"), "norm1_w": f("norm1_w"), "norm2_w": f("norm2_w"),
        "qk_norm_w": np.concatenate([f("q_norm_w"), f("q_norm_w"), f("k_norm_w"), f("k_norm_w")]),
        "lambdas": np.concatenate([f("lambda_q1"), f("lambda_k1"), f("lambda_q2"), f("lambda_k2")]),
        "subln_w": f("diff_subln_w"), "conv_w": f("conv_w"), "conv_b": f("conv_b"),
    }
    tabs = [make_tables(NC, NO, p, S) for p in range(2)]
    in_maps = []
    for b in range(B):
        seq = np.concatenate([meta, x[b]], axis=0)
        for p in range(2):
            if p == 0:
                xc_ = np.zeros((NC * 128, D), np.float32)
                xo_ = seq[0:NO * 128]
            else:
                xc_ = np.concatenate([np.zeros((PAD, D), np.float32), seq[0:NC * 128 - PAD]], axis=0)
                xo_ = seq[L - NO * 128:L]
            rr, rd, kb, cm, rdec = tabs[p]
            m = dict(common)
            m.update({"xc": np.ascontiguousarray(xc_), "xo": np.ascontiguousarray(xo_), "rope_r": rr, "rope_d": rd,
                      "kbias": kb, "cmask": cm, "rdec": rdec})
            in_maps.append(m)
    res = run_bass_kernel_spmd(nc, in_maps, core_ids=list(range(len(in_maps))), trace=trace)
    out = np.empty((B, S, D), np.float32)
    split = (NO * 128 - N_META) - 64
    for b in range(B):
        y0 = res.results[2 * b]["y"]
        y1 = res.results[2 * b + 1]["y"]
        out[b, :split] = y0[N_META:N_META + split]
        off1 = L - NO * 128
        out[b, split:] = y1[N_META + split - off1:]
    return out, res


def kernel(**inputs):
    out, _ = run(inputs, 32, 33)
    return out
```

```python
import math
import numpy as np
from contextlib import ExitStack
import concourse.bass as bass
import concourse.mybir as mybir
from concourse.bass_utils import run_bass_kernel_spmd

F32 = mybir.dt.float32
BF16 = mybir.dt.bfloat16
AF = mybir.ActivationFunctionType
ALU = mybir.AluOpType
AX = mybir.AxisListType

D = 1024
N_META = 16
PAD = 112
FFN = 2816
IN_COLS = 11264
EPS = 1e-6
C_RQ, C_RK, C_RV, C_RG, C_DQ, C_DK, C_DV, C_GT = 0, 1024, 2048, 4096, 6144, 7168, 8192, 9216
NEGB = -30000.0
LAM_INIT = 0.8 - 0.6 * math.exp(-0.3 * 0)
GAM = [1.0 - 2.0 ** (-5.0 - h) for h in range(4)]

SAME_ENGINE_SYNC = True


class Buf:
    __slots__ = ("name", "last_write", "reads", "dsem", "dcount")

    def __init__(self, name=""):
        self.name = name
        self.last_write = None
        self.reads = {}
        self.dsem = None
        self.dcount = 0


class T:
    def __init__(self, t, name):
        self.t = t
        self.b = Buf(name)


class Prog:
    ENGS = ("pe", "act", "dve", "pool", "sp")
    ENGOBJ = {"pe": "tensor", "act": "scalar", "dve": "vector", "pool": "gpsimd", "sp": "sync"}

    def __init__(self, nc, stack):
        self.nc = nc
        self.stack = stack
        self.q = {e: [] for e in self.ENGS}
        self.ecount = {e: 0 for e in self.ENGS}
        self.sems = {}
        for e in self.ENGS:
            self.sems[("e", e)] = stack.enter_context(nc.semaphore("s_" + e))
        self.waited = {e: {} for e in self.ENGS}
        self.ndsem = 0
        self.n_inst = 0
        self.n_wait = 0

    def _dsem(self, buf):
        if buf.dsem is None:
            buf.dsem = ("d", self.ndsem)
            self.sems[buf.dsem] = self.stack.enter_context(self.nc.semaphore("d%d" % self.ndsem))
            self.ndsem += 1
        return buf.dsem

    def _deps(self, eng, reads, writes):
        deps = {}

        def add(t):
            if t is None:
                return
            k, v = t
            if deps.get(k, -1) < v:
                deps[k] = v
        for b in reads:
            add(b.last_write)
        for b in writes:
            add(b.last_write)
            for k, v in b.reads.items():
                add((k, v))
        out = []
        w = self.waited[eng]
        for k, v in deps.items():
            if k == ("e", eng) and (eng == "pe" or not SAME_ENGINE_SYNC):
                continue
            if w.get(k, -1) >= v:
                continue
            w[k] = v
            out.append((k, v))
        return out

    def _commit(self, tok, reads, writes):
        k, v = tok
        for b in writes:
            b.last_write = tok
            b.reads = {}
        for b in reads:
            if b.reads.get(k, -1) < v:
                b.reads[k] = v

    def op(self, eng, fn, reads=(), writes=()):
        waits = self._deps(eng, reads, writes)
        self.ecount[eng] += 1
        tok = (("e", eng), self.ecount[eng])
        self.q[eng].append((waits, fn, tok[0], 1))
        self._commit(tok, reads, writes)
        self.n_inst += 1
        self.n_wait += len(waits)
        return tok

    def dma(self, eng, fn, sb, reads=(), writes=()):
        waits = self._deps(eng, reads, writes)
        k = self._dsem(sb)
        sb.dcount += 16
        tok = (k, sb.dcount)
        self.q[eng].append((waits, fn, k, 16))
        self._commit(tok, reads, writes)
        self.n_inst += 1
        self.n_wait += len(waits)
        return tok

    def wait_all(self, eng, bufs):
        waits = self._deps(eng, bufs, bufs)
        self.q[eng].append((waits, None, None, 0))

    def emit(self):
        nc = self.nc
        sems = self.sems
        with nc.Block() as block:
            for e in self.ENGS:
                lst = self.q[e]

                def body(eo, lst=lst):
                    for waits, fn, sk, inc in lst:
                        for k, v in waits:
                            eo.wait_ge(sems[k], v)
                        if fn is not None:
                            fn(eo).then_inc(sems[sk], inc)
                getattr(block, self.ENGOBJ[e])(body)
        self.q = {e: [] for e in self.ENGS}


def bc_mid(ap, n):
    return bass.AP(ap.tensor, ap.offset, [list(ap.ap[0]), [0, n], list(ap.ap[1])])


def bc_last(ap, k):
    return bass.AP(ap.tensor, ap.offset, [list(ap.ap[0]), list(ap.ap[1]), [0, k]])


def bc_part(dram_ap_1d, n):
    return bass.AP(dram_ap_1d.tensor, dram_ap_1d.offset, [[0, 128], [1, n]])


def build(NC, NO, debug=False):
    NB = NC + NO
    nc = bass.Bass("TRN2", target_bir_lowering=False)

    def din(name, shape, dt=F32):
        return nc.dram_tensor(name, list(shape), dt, kind="ExternalInput").ap()

    okind = "ExternalOutput" if debug else "Internal"

    def dscr(name, shape, dt):
        return nc.dram_tensor(name, list(shape), dt, kind=okind).ap()

    xc = din("xc", [NC * 128, D])
    xo = din("xo", [NO * 128, D])
    w_in = din("w_in", [D, IN_COLS])
    w_ret_o = din("w_ret_o", [2048, D])
    w_diff_o = din("w_diff_o", [D, D])
    w_out = din("w_out", [D, D])
    w_up = din("w_up", [D, 2 * FFN])
    w_down = din("w_down", [FFN, D])
    norm1_w = din("norm1_w", [D])
    norm2_w = din("norm2_w", [D])
    qk_norm_w = din("qk_norm_w", [256])
    lambdas = din("lambdas", [256])
    subln_w = din("subln_w", [128])
    conv_w = din("conv_w", [3, 2 * FFN])
    conv_b = din("conv_b", [2 * FFN])
    rope_r = din("rope_r", [NB * 128, 512])
    rope_d = din("rope_d", [NB * 128, 32])
    kbias_d = din("kbias", [128, NB])
    cmask_d = din("cmask", [128, 128])
    rdec_d = din("rdec", [128, 8])
    y = nc.dram_tensor("y", [NO * 128, D], F32, kind="ExternalOutput").ap()

    UT = dscr("UT", [NB, 128, 1024], BF16)
    GT = dscr("GT", [NO, 128, 2048], BF16)
    DOT = dscr("DOT", [NO, 128, 1024], BF16)
    H2 = dscr("H2", [NO * 128, D], F32)
    U2T = dscr("U2T", [NO, 128, 1024], BF16)
    b_UT = [Buf("UT%d" % i) for i in range(NB)]
    b_GT = [Buf("GT%d" % i) for i in range(NO)]
    b_DOT = [Buf("DOT%d" % i) for i in range(NO)]
    b_H2 = [Buf("H2%d" % i) for i in range(NO)]
    b_U2T = [Buf("U2T%d" % i) for i in range(NO)]
    b_y = Buf("y")

    w_in_v = w_in.rearrange("(k p) n -> p k n", p=128)

    with ExitStack() as gst:
        P = Prog(nc, gst)

        def sbt(st, name, shape, dt):
            return T(st.enter_context(nc.sbuf_tensor("sb_" + name, list(shape), dt)), name)

        def pbank(st, name, dt=F32):
            n = 512 if dt == F32 else 1024
            return T(st.enter_context(nc.psum_tensor("ps_" + name, [128, n], dt)), name)

        ident = sbt(gst, "ident", [128, 128], BF16)
        identf = sbt(gst, "identf", [128, 128], F32)
        cmask = sbt(gst, "cmask", [128, 128], F32)
        cmask2 = sbt(gst, "cmask2", [128, 2, 128], BF16)
        kbias = sbt(gst, "kbias", [128, NB], F32)
        rdec = sbt(gst, "rdec", [128, 8], F32)
        lam = sbt(gst, "lam", [128, 4], F32)
        lamv = sbt(gst, "lamv", [128, 256], F32)
        lamt = sbt(gst, "lamt", [128, 128], F32)
        lams = sbt(gst, "lams", [128, 2], F32)
        sublnw = sbt(gst, "sublnw", [128, 128], F32)
        wqk = sbt(gst, "wqk", [128, 4, 64], F32)

        P.op("pool", lambda e: e.iota(identf.t[:], pattern=[[1, 128]], base=0, channel_multiplier=-1,
                                      allow_small_or_imprecise_dtypes=True), writes=[identf.b])
        P.op("dve", lambda e: e.tensor_scalar(ident.t[:], identf.t[:], 0.0, None, op0=ALU.is_equal),
             reads=[identf.b], writes=[ident.b])
        P.dma("sp", lambda e: e.dma_start(out=cmask.t[:], in_=cmask_d), cmask.b, writes=[cmask.b])
        P.dma("sp", lambda e: e.dma_start(out=kbias.t[:], in_=kbias_d), kbias.b, writes=[kbias.b])
        P.dma("sp", lambda e: e.dma_start(out=rdec.t[:], in_=rdec_d), rdec.b, writes=[rdec.b])
        P.dma("sp", lambda e: e.dma_start(out=lamv.t[:], in_=bc_part(lambdas, 256)), lamv.b, writes=[lamv.b])
        P.dma("sp", lambda e: e.dma_start(out=sublnw.t[:], in_=bc_part(subln_w, 128)), sublnw.b, writes=[sublnw.b])
        P.dma("sp", lambda e: e.dma_start(out=wqk.t[:].rearrange("p a b -> p (a b)"), in_=bc_part(qk_norm_w, 256)),
              wqk.b, writes=[wqk.b])
        P.op("dve", lambda e: e.tensor_copy(cmask2.t[:, 0, :], cmask.t[:]), reads=[cmask.b], writes=[cmask2.b])
        P.op("dve", lambda e: e.tensor_copy(cmask2.t[:, 1, :], cmask.t[:]), reads=[cmask.b], writes=[cmask2.b])
        P.op("dve", lambda e: e.tensor_tensor(out=lamt.t[:, 0:64], in0=lamv.t[:, 0:64], in1=lamv.t[:, 64:128], op=ALU.mult),
             reads=[lamv.b], writes=[lamt.b])
        P.op("dve", lambda e: e.tensor_tensor(out=lamt.t[:, 64:128], in0=lamv.t[:, 128:192], in1=lamv.t[:, 192:256], op=ALU.mult),
             reads=[lamv.b], writes=[lamt.b])
        P.op("dve", lambda e: e.tensor_reduce(out=lams.t[:, 0:2], in_=lamt.t[:].rearrange("p (a b) -> p a b", a=2),
                                              axis=AX.X, op=ALU.add), reads=[lamt.b], writes=[lams.b])
        P.op("act", lambda e: e.activation(out=lams.t[:], in_=lams.t[:], func=AF.Exp), reads=[lams.b], writes=[lams.b])
        P.op("dve", lambda e: e.tensor_tensor(out=lam.t[:, 0:1], in0=lams.t[:, 0:1], in1=lams.t[:, 1:2], op=ALU.subtract),
             reads=[lams.b], writes=[lam.b])
        P.op("dve", lambda e: e.tensor_scalar(lam.t[:, 0:1], lam.t[:, 0:1], LAM_INIT, None, op0=ALU.add),
             reads=[lam.b], writes=[lam.b])
        P.op("dve", lambda e: e.tensor_scalar(sublnw.t[:], sublnw.t[:], 1.0 - LAM_INIT, None, op0=ALU.mult),
             reads=[sublnw.b], writes=[sublnw.b])

        def norm_part(xt, nw, sq, ss, rs, ub, use_pow=False):
            P.op("act", lambda e: e.activation(out=sq.t[:], in_=xt.t[:], func=AF.Square, accum_out=ss.t[:, 0:1]),
                 reads=[xt.b], writes=[sq.b, ss.b])
            if use_pow:
                P.op("dve", lambda e: e.tensor_scalar(rs.t[:, 0:1], ss.t[:, 0:1], 1.0 / D, EPS, op0=ALU.mult, op1=ALU.add),
                     reads=[ss.b], writes=[rs.b])
                P.op("pool", lambda e: e.tensor_tensor(out=rs.t[:, 0:1], in0=rs.t[:, 0:1], in1=mhalf.t[:, 0:1], op=ALU.pow),
                     reads=[rs.b, mhalf.b], writes=[rs.b])
            else:
                P.op("act", lambda e: e.activation(out=rs.t[:, 0:1], in_=ss.t[:, 0:1], func=AF.Sqrt, scale=1.0 / D, bias=eps_t.t[:, 0:1]),
                     reads=[ss.b, eps_t.b], writes=[rs.b])
                P.op("dve", lambda e: e.reciprocal(rs.t[:, 0:1], rs.t[:, 0:1]), reads=[rs.b], writes=[rs.b])
            P.op("dve", lambda e: e.scalar_tensor_tensor(out=ub.t[:], in0=xt.t[:], scalar=rs.t[:, 0:1], in1=nw.t[:],
                                                         op0=ALU.mult, op1=ALU.mult),
                 reads=[xt.b, rs.b, nw.b], writes=[ub.b])

        def tr_part(ub, pT, uT):
            for half in range(2):
                pt = pT[half]
                for j in range(4):
                    kc = half * 4 + j
                    P.op("pe", lambda e, kc=kc, j=j, pt=pt: e.transpose(pt.t[:, j * 128:(j + 1) * 128],
                                                                        ub.t[:, kc * 128:(kc + 1) * 128], ident.t[:]),
                         reads=[ub.b, ident.b], writes=[pt.b])
                if half == 0:
                    P.op("dve", lambda e, pt=pt: e.tensor_copy(uT.t[:, 0:4, :].rearrange("p a b -> p (a b)"), pt.t[:, 0:512]),
                         reads=[pt.b], writes=[uT.b])
                else:
                    P.op("act", lambda e, pt=pt: e.copy(uT.t[:, 4:8, :].rearrange("p a b -> p (a b)"), pt.t[:, 0:512]),
                         reads=[pt.b], writes=[uT.b])

        def norm_transpose(xt, nw, sq, ss, rs, ub, pT, uT):
            norm_part(xt, nw, sq, ss, rs, ub)
            tr_part(ub, pT, uT)

        eps_t = sbt(gst, "eps_t", [128, 1], F32)
        P.op("pool", lambda e: e.memset(eps_t.t[:], EPS), writes=[eps_t.b])
        mhalf = sbt(gst, "mhalf", [128, 8], F32)
        P.op("pool", lambda e: e.memset(mhalf.t[:], -0.5), writes=[mhalf.b])

        with ExitStack() as st:
            n1w = sbt(st, "n1w", [128, D], F32)
            P.dma("sp", lambda e: e.dma_start(out=n1w.t[:], in_=bc_part(norm1_w, D)), n1w.b, writes=[n1w.b])
            xb = [sbt(st, "x%d" % i, [128, D], F32) for i in range(3)]
            sq = [sbt(st, "sq%d" % i, [128, D], F32) for i in range(2)]
            ss = [sbt(st, "ss%d" % i, [128, 1], F32) for i in range(2)]
            rs = [sbt(st, "rs%d" % i, [128, 1], F32) for i in range(2)]
            ub = [sbt(st, "ub%d" % i, [128, D], BF16) for i in range(2)]
            uT = [sbt(st, "uT%d" % i, [128, 8, 128], BF16) for i in range(2)]
            pT = [pbank(st, "pT%d" % i, BF16) for i in range(4)]
            def p0_x(blk):
                src = xc[blk * 128:(blk + 1) * 128, :] if blk < NC else xo[(blk - NC) * 128:(blk - NC + 1) * 128, :]
                x_ = xb[blk % 3]
                P.dma("sp", lambda e: e.dma_start(out=x_.t[:], in_=src), x_.b, writes=[x_.b])
                norm_part(x_, n1w, sq[blk % 2], ss[blk % 2], rs[blk % 2], ub[blk % 2])

            def p0_y(blk):
                u_ = uT[blk % 2]
                tr_part(ub[blk % 2], pT[(blk % 2) * 2:(blk % 2) * 2 + 2], u_)
                P.dma("pool", lambda e: e.dma_start(out=UT[blk], in_=u_.t[:].rearrange("p a b -> p (a b)")),
                      u_.b, reads=[u_.b], writes=[b_UT[blk]])
            p0_x(0)
            for blk in range(NB):
                if blk + 1 < NB:
                    p0_x(blk + 1)
                p0_y(blk)
            P.emit()

        with ExitStack() as st:
            WR = [sbt(st, "WR%d" % i, [128, 8, 1536], BF16) for i in range(2)]
            uTb = [sbt(st, "ruT%d" % i, [128, 8, 128], BF16) for i in range(3)]
            RT = [sbt(st, "RT%d" % i, [128, 512], F32) for i in range(3)]
            Rf = sbt(st, "Rf", [128, 2, 512], F32)
            Rb = sbt(st, "Rb", [128, 2, 512], BF16)
            Aq = sbt(st, "Aq", [128, 256], F32)
            Bq = sbt(st, "Bq", [128, 256], F32)
            Ak = sbt(st, "Ak", [128, 256], F32)
            Bk = sbt(st, "Bk", [128, 256], F32)
            qr = [sbt(st, "qr%d" % i, [128, 256], BF16) for i in range(2)]
            kr = [sbt(st, "kr%d" % i, [128, 256], BF16) for i in range(2)]
            vb = [sbt(st, "vb%d" % i, [128, 512], BF16) for i in range(2)]
            sg = [sbt(st, "sg%d" % i, [128, 512], F32) for i in range(2)]
            qkT = sbt(st, "qkT", [128, 4, 128], BF16)
            Sm = sbt(st, "Sm", [128, 128], BF16)
            bst = sbt(st, "bst", [128, 6], F32)
            mv = sbt(st, "mv", [128, 2], F32)
            grs = sbt(st, "grs", [128, 1], F32)
            on = sbt(st, "on", [128, 512], F32)
            gtd = sbt(st, "gtd", [128, 512], BF16)
            gT = [sbt(st, "gT%d" % i, [128, 4, 128], BF16) for i in range(2)]
            pQK = pbank(st, "pQK")
            pV = pbank(st, "pV")
            pG = pbank(st, "pG")
            pTq = pbank(st, "pTq", BF16)
            pS = pbank(st, "pS")
            pO = pbank(st, "pO")
            pR = [pbank(st, "pR%d" % i) for i in range(2)]

            def load_WR(h):
                w = WR[h % 2]
                for (c0, n, o0) in ((C_RQ + h * 256, 256, 0), (C_RK + h * 256, 256, 256),
                                    (C_RV + h * 512, 512, 512), (C_RG + h * 512, 512, 1024)):
                    P.dma("pool", lambda e, w=w, c0=c0, n=n, o0=o0: e.dma_start(out=w.t[:, :, o0:o0 + n],
                                                                                   in_=w_in_v[:, :, c0:c0 + n]),
                          w.b, writes=[w.b])
            load_WR(0)
            for h in range(4):
                if h + 1 < 4:
                    load_WR(h + 1)
                w = WR[h % 2]
                g = GAM[h]
                P.op("pool", lambda e: e.memset(Rf.t[:], 0.0), writes=[Rf.b])
                P.op("pool", lambda e: e.memset(Rb.t[:], 0.0), writes=[Rb.b])

                def A1(blk):
                    own = blk >= NC
                    u_ = uTb[blk % 3]
                    rt = RT[blk % 3]
                    k_ = kr[blk % 2]
                    q_ = qr[blk % 2]
                    P.dma("sp", lambda e: e.dma_start(out=u_.t[:].rearrange("p a b -> p (a b)"), in_=UT[blk]),
                          u_.b, reads=[b_UT[blk]], writes=[u_.b])
                    P.dma("sp", lambda e: e.dma_start(out=rt.t[:], in_=rope_r[blk * 128:(blk + 1) * 128, :]),
                          rt.b, writes=[rt.b])
                    c0 = 0 if own else 256
                    for kc in range(8):
                        P.op("pe", lambda e, kc=kc: e.matmul(pQK.t[:, c0:512], lhsT=u_.t[:, kc, :], rhs=w.t[:, kc, c0:512],
                                                             start=(kc == 0), stop=(kc == 7)),
                             reads=[u_.b, w.b], writes=[pQK.b])
                    P.op("dve", lambda e: e.scalar_tensor_tensor(out=Ak.t[:], in0=pQK.t[:, 256:512], scalar=rdec.t[:, 4 + h:5 + h],
                                                                 in1=rt.t[:, 0:256], op0=ALU.mult, op1=ALU.mult),
                         reads=[pQK.b, rdec.b, rt.b], writes=[Ak.b])
                    P.op("dve", lambda e: e.scalar_tensor_tensor(out=Bk.t[:], in0=pQK.t[:, 256:512], scalar=rdec.t[:, 4 + h:5 + h],
                                                                 in1=rt.t[:, 256:512], op0=ALU.mult, op1=ALU.mult),
                         reads=[pQK.b, rdec.b, rt.b], writes=[Bk.b])
                    if own:
                        P.op("dve", lambda e: e.scalar_tensor_tensor(out=Aq.t[:], in0=pQK.t[:, 0:256], scalar=rdec.t[:, h:h + 1],
                                                                     in1=rt.t[:, 0:256], op0=ALU.mult, op1=ALU.mult),
                             reads=[pQK.b, rdec.b, rt.b], writes=[Aq.b])
                        P.op("dve", lambda e: e.scalar_tensor_tensor(out=Bq.t[:], in0=pQK.t[:, 0:256], scalar=rdec.t[:, h:h + 1],
                                                                     in1=rt.t[:, 256:512], op0=ALU.mult, op1=ALU.mult),
                             reads=[pQK.b, rdec.b, rt.b], writes=[Bq.b])
                    P.op("dve", lambda e: e.tensor_tensor(out=k_.t[:, 0:128], in0=Ak.t[:, 0:128], in1=Bk.t[:, 128:256], op=ALU.subtract),
                         reads=[Ak.b, Bk.b], writes=[k_.b])
                    P.op("dve", lambda e: e.tensor_tensor(out=k_.t[:, 128:256], in0=Ak.t[:, 128:256], in1=Bk.t[:, 0:128], op=ALU.add),
                         reads=[Ak.b, Bk.b], writes=[k_.b])
                    if own:
                        P.op("dve", lambda e: e.tensor_tensor(out=q_.t[:, 0:128], in0=Aq.t[:, 0:128], in1=Bq.t[:, 128:256], op=ALU.subtract),
                             reads=[Aq.b, Bq.b], writes=[q_.b])
                        P.op("dve", lambda e: e.tensor_tensor(out=q_.t[:, 128:256], in0=Aq.t[:, 128:256], in1=Bq.t[:, 0:128], op=ALU.add),
                             reads=[Aq.b, Bq.b], writes=[q_.b])

                def A2(blk):
                    u_ = uTb[blk % 3]
                    v_ = vb[blk % 2]
                    for kc in range(8):
                        P.op("pe", lambda e, kc=kc: e.matmul(pV.t[:, 0:512], lhsT=u_.t[:, kc, :], rhs=w.t[:, kc, 512:1024],
                                                             start=(kc == 0), stop=(kc == 7)),
                             reads=[u_.b, w.b], writes=[pV.b])
                    P.op("act", lambda e: e.copy(v_.t[:], pV.t[:, 0:512]), reads=[pV.b], writes=[v_.b])

                def A3(blk):
                    if blk < NC:
                        return
                    u_ = uTb[blk % 3]
                    s_ = sg[blk % 2]
                    for kc in range(8):
                        P.op("pe", lambda e, kc=kc: e.matmul(pG.t[:, 0:512], lhsT=u_.t[:, kc, :], rhs=w.t[:, kc, 1024:1536],
                                                             start=(kc == 0), stop=(kc == 7)),
                             reads=[u_.b, w.b], writes=[pG.b])
                    P.op("act", lambda e: e.activation(out=s_.t[:], in_=pG.t[:, 0:512], func=AF.Silu), reads=[pG.b], writes=[s_.b])

                def B1(blk):
                    if blk < NC:
                        return
                    k_ = kr[blk % 2]
                    q_ = qr[blk % 2]
                    for j in range(4):
                        srcT = q_ if j < 2 else k_
                        c = j % 2
                        P.op("pe", lambda e, j=j, c=c, srcT=srcT: e.transpose(pTq.t[:, j * 128:(j + 1) * 128],
                                                                              srcT.t[:, c * 128:(c + 1) * 128], ident.t[:]),
                             reads=[srcT.b, ident.b], writes=[pTq.b])
                    P.op("act", lambda e: e.copy(qkT.t[:].rearrange("p a b -> p (a b)"), pTq.t[:, 0:512]),
                         reads=[pTq.b], writes=[qkT.b])

                def B2(blk):
                    if blk < NC:
                        return
                    for c in range(2):
                        P.op("pe", lambda e, c=c: e.matmul(pS.t[:, 0:128], lhsT=qkT.t[:, 2 + c, :], rhs=qkT.t[:, c, :],
                                                           start=(c == 0), stop=(c == 1)),
                             reads=[qkT.b], writes=[pS.b])
                    P.op("dve", lambda e: e.scalar_tensor_tensor(out=Sm.t[:], in0=pS.t[:, 0:128], scalar=float(g ** -128.0),
                                                                 in1=cmask.t[:], op0=ALU.mult, op1=ALU.mult),
                         reads=[pS.b, cmask.b], writes=[Sm.b])

                def B3(blk):
                    if blk < NC:
                        return
                    ob = blk - NC
                    v_ = vb[blk % 2]
                    s_ = sg[blk % 2]
                    P.op("pe", lambda e: e.matmul(pO.t[:, 0:512], lhsT=Sm.t[:], rhs=v_.t[:], start=True, stop=False),
                         reads=[Sm.b, v_.b], writes=[pO.b])
                    for c in range(2):
                        P.op("pe", lambda e, c=c: e.matmul(pO.t[:, 0:512], lhsT=qkT.t[:, c, :], rhs=Rb.t[:, c, :],
                                                           start=False, stop=(c == 1)),
                             reads=[qkT.b, Rb.b], writes=[pO.b])
                    P.op("dve", lambda e: e.bn_stats(bst.t[:], pO.t[:, 0:512]), reads=[pO.b], writes=[bst.b])
                    P.op("dve", lambda e: e.bn_aggr(mv.t[:], bst.t[:]), reads=[bst.b], writes=[mv.b])
                    P.op("dve", lambda e: e.tensor_scalar(grs.t[:], mv.t[:, 1:2], EPS, None, op0=ALU.add),
                         reads=[mv.b], writes=[grs.b])
                    P.op("pool", lambda e: e.tensor_tensor(out=grs.t[:], in0=grs.t[:], in1=mhalf.t[:, 0:1], op=ALU.pow),
                         reads=[grs.b, mhalf.b], writes=[grs.b])
                    P.op("dve", lambda e: e.tensor_scalar(on.t[:], pO.t[:, 0:512], mv.t[:, 0:1], grs.t[:, 0:1],
                                                          op0=ALU.subtract, op1=ALU.mult),
                         reads=[pO.b, mv.b, grs.b], writes=[on.b])
                    P.op("dve", lambda e: e.tensor_tensor(out=gtd.t[:], in0=on.t[:], in1=s_.t[:], op=ALU.mult),
                         reads=[on.b, s_.b], writes=[gtd.b])

                def B4a(blk):
                    k_ = kr[blk % 2]
                    v_ = vb[blk % 2]
                    if blk < NB - 1:
                        for c in range(2):
                            P.op("pe", lambda e, c=c: e.matmul(pR[c].t[:, 0:512], lhsT=k_.t[:, c * 128:(c + 1) * 128], rhs=v_.t[:],
                                                               start=True, stop=True),
                                 reads=[k_.b, v_.b], writes=[pR[c].b])
                        for c in range(2):
                            P.op("dve", lambda e, c=c: e.scalar_tensor_tensor(out=Rf.t[:, c, :], in0=Rf.t[:, c, :], scalar=float(g ** 128.0),
                                                                              in1=pR[c].t[:, 0:512], op0=ALU.mult, op1=ALU.add),
                                 reads=[Rf.b, pR[c].b], writes=[Rf.b])
                        P.op("act", lambda e: e.copy(Rb.t[:].rearrange("p a b -> p (a b)"), Rf.t[:].rearrange("p a b -> p (a b)")),
                             reads=[Rf.b], writes=[Rb.b])

                def B4b(blk):
                    if blk >= NC:
                        ob = blk - NC
                        g_ = gT[ob % 2]
                        for j in range(4):
                            P.op("pe", lambda e, j=j: e.transpose(pTq.t[:, 512 + j * 128:512 + (j + 1) * 128],
                                                                  gtd.t[:, j * 128:(j + 1) * 128], ident.t[:]),
                                 reads=[gtd.b, ident.b], writes=[pTq.b])
                        P.op("act", lambda e: e.copy(g_.t[:].rearrange("p a b -> p (a b)"), pTq.t[:, 512:1024]),
                             reads=[pTq.b], writes=[g_.b])
                        P.dma("pool", lambda e: e.dma_start(out=GT[ob][:, h * 512:(h + 1) * 512],
                                                            in_=g_.t[:].rearrange("p a b -> p (a b)")),
                              g_.b, reads=[g_.b], writes=[b_GT[ob]])

                A1(0); A2(0); A3(0)
                for blk in range(NB):
                    nx = blk + 1
                    B1(blk)
                    if nx < NB:
                        A1(nx)
                    B2(blk)
                    if nx < NB:
                        A2(nx)
                    B3(blk)
                    B4a(blk)
                    if nx < NB:
                        A3(nx)
                    B4b(blk)
                P.emit()

        KTs = dscr("KTs", [8, 128, NB * 128], BF16)
        QTs = dscr("QTs", [8, 128, NO * 128], BF16)
        VVs = dscr("VVs", [8, 128, NB, 128], BF16)
        b_KTs = Buf("KTs"); b_QTs = Buf("QTs"); b_VVs = Buf("VVs")
        KTs_w = KTs.rearrange("h p (b t) -> p h b t", t=128)
        QTs_w = QTs.rearrange("h p (b t) -> p h b t", t=128)
        VVs_w = VVs.rearrange("h p b e -> p h b e")
        with ExitStack() as st:
            WDa = sbt(st, "WDa", [128, 8, 3072], BF16)
            for k0 in range(0, 8, 2):
                P.dma("pool", lambda e, k0=k0: e.dma_start(out=WDa.t[:, k0:k0 + 2, :], in_=w_in_v[:, k0:k0 + 2, C_DQ:C_DQ + 3072]),
                      WDa.b, writes=[WDa.b])
            ropd = sbt(st, "ropd", [128, NB, 32], F32)
            ropd_v = rope_d.rearrange("(b p) c -> p b c", p=128)
            for b0 in range(0, NB, 16):
                b1 = min(NB, b0 + 16)
                P.dma("sp", lambda e, b0=b0, b1=b1: e.dma_start(out=ropd.t[:, b0:b1, :], in_=ropd_v[:, b0:b1, :]),
                      ropd.b, writes=[ropd.b])
            wq8 = sbt(st, "wq8", [128, 8, 64], F32)
            wk8 = sbt(st, "wk8", [128, 8, 64], F32)
            for g8 in range(8):
                P.op("pool", lambda e, g8=g8: e.tensor_copy(wq8.t[:, g8, :], wqk.t[:, 0, :]), reads=[wqk.b], writes=[wq8.b])
                P.op("pool", lambda e, g8=g8: e.tensor_copy(wk8.t[:, g8, :], wqk.t[:, 2, :]), reads=[wqk.b], writes=[wk8.b])
            uTb = [sbt(st, "duT%d" % i, [128, 8, 128], BF16) for i in range(3)]
            NCH = 4
            sqd = [sbt(st, "sqd%d" % i, [128, 8, 64], F32) for i in range(NCH)]
            ssd = [sbt(st, "ssd%d" % i, [128, 8], F32) for i in range(NCH)]
            rsd = [sbt(st, "rsd%d" % i, [128, 8], F32) for i in range(NCH)]
            xn = [[sbt(st, "xn%d_%d" % (pp, i), [128, 8, 64], F32) for i in range(NCH)] for pp in range(2)]
            xbq = [[sbt(st, "xbq%d_%d" % (pp, i), [128, 8, 64], BF16) for i in range(NCH)] for pp in range(2)]
            rc = [[sbt(st, "rc%d_%d" % (pp, i), [128, 8, 16], F32) for i in range(NCH)] for pp in range(2)]
            rsn = [[sbt(st, "rsn%d_%d" % (pp, i), [128, 8, 16], F32) for i in range(NCH)] for pp in range(2)]
            kst = [sbt(st, "kst%d" % i, [128, 8, 128], BF16) for i in range(2)]
            qst = [sbt(st, "qst%d" % i, [128, 8, 128], BF16) for i in range(2)]
            vst = [sbt(st, "vst%d" % i, [128, 8, 128], BF16) for i in range(2)]
            pq = [pbank(st, "pq%d" % i) for i in range(2)]
            pk = [pbank(st, "pk%d" % i) for i in range(2)]
            pvv = [pbank(st, "pvv%d" % i) for i in range(2)]
            pTk = pbank(st, "pTk", BF16)
            pTq = pbank(st, "pTq1", BF16)
            def mk_chains(blk):
                chains = []
                for half in range(2):
                    chains.append((pk[half], wk8, pTk, half, 1024 + half * 512))
                if blk >= NC:
                    for half in range(2):
                        chains.append((pq[half], wq8, pTq, half, half * 512))
                return chains

            def d1_early(blk):
                u_ = uTb[blk % 3]
                par = blk % 2
                P.dma("sp", lambda e: e.dma_start(out=u_.t[:].rearrange("p a b -> p (a b)"), in_=UT[blk]),
                      u_.b, reads=[b_UT[blk]], writes=[u_.b])
                chains = mk_chains(blk)
                for (pb, wt, ptT, half, c0) in chains:
                    for kc in range(8):
                        P.op("pe", lambda e, kc=kc, pb=pb, c0=c0: e.matmul(pb.t[:, 0:512], lhsT=u_.t[:, kc, :], rhs=WDa.t[:, kc, c0:c0 + 512],
                                                                           start=(kc == 0), stop=(kc == 7)),
                             reads=[u_.b, WDa.b], writes=[pb.b])
                for half in range(2):
                    for kc in range(8):
                        P.op("pe", lambda e, kc=kc, half=half: e.matmul(pvv[half].t[:, 0:512], lhsT=u_.t[:, kc, :],
                                                                       rhs=WDa.t[:, kc, 2048 + half * 512:2048 + (half + 1) * 512],
                                                                       start=(kc == 0), stop=(kc == 7)),
                             reads=[u_.b, WDa.b], writes=[pvv[half].b])
                nch = len(chains)
                pvw = [ch[0].t[:, 0:512].rearrange("p (a b) -> p a b", b=64) for ch in chains]
                for ci in range(nch):
                    P.op("act", lambda e, ci=ci: e.activation(out=sqd[ci].t[:], in_=pvw[ci], func=AF.Square),
                         reads=[chains[ci][0].b], writes=[sqd[ci].b])
                for ci in range(nch):
                    P.op("dve", lambda e, ci=ci: e.tensor_reduce(out=ssd[ci].t[:], in_=sqd[ci].t[:], axis=AX.X, op=ALU.add),
                         reads=[sqd[ci].b], writes=[ssd[ci].b])
                for ci in range(nch):
                    P.op("act", lambda e, ci=ci: e.activation(out=rsd[ci].t[:], in_=ssd[ci].t[:], func=AF.Sqrt, scale=1.0 / 64, bias=eps_t.t[:, 0:1]),
                         reads=[ssd[ci].b, eps_t.b], writes=[rsd[ci].b])
                v_ = vst[par]
                for half in range(2):
                    P.op("act", lambda e, half=half: e.copy(v_.t[:, half * 4:half * 4 + 4, :].rearrange("p a b -> p (a b)"), pvv[half].t[:, 0:512]),
                         reads=[pvv[half].b], writes=[v_.b])
                P.dma("sp", lambda e: e.dma_start(out=VVs_w[:, :, blk, :], in_=v_.t[:]), v_.b, reads=[v_.b], writes=[b_VVs])
                for ci in range(nch):
                    P.op("dve", lambda e, ci=ci: e.reciprocal(rsd[ci].t[:], rsd[ci].t[:]), reads=[rsd[ci].b], writes=[rsd[ci].b])
                for ci in range(nch):
                    P.op("dve", lambda e, ci=ci: e.tensor_tensor(out=xn[par][ci].t[:], in0=pvw[ci], in1=bc_last(rsd[ci].t[:], 64), op=ALU.mult),
                         reads=[chains[ci][0].b, rsd[ci].b], writes=[xn[par][ci].b])

            def d1_late(blk):
                par = blk % 2
                own = blk >= NC
                ob = blk - NC
                chains = mk_chains(blk)
                nch = len(chains)
                xn_, xb_, rc_, rsn_ = xn[par], xbq[par], rc[par], rsn[par]
                for ci in range(nch):
                    eng = "pool" if ci % 2 == 0 else "dve"
                    P.op(eng, lambda e, ci=ci: e.tensor_tensor(out=xn_[ci].t[:], in0=xn_[ci].t[:], in1=chains[ci][1].t[:], op=ALU.mult),
                         reads=[xn_[ci].b, chains[ci][1].b], writes=[xn_[ci].b])
                for ci in range(nch):
                    eng = "pool" if ci % 2 == 0 else "dve"
                    P.op(eng, lambda e, ci=ci: e.tensor_copy(xb_[ci].t[:], xn_[ci].t[:]), reads=[xn_[ci].b], writes=[xb_[ci].b])
                    P.op(eng, lambda e, ci=ci: e.tensor_tensor(out=rc_[ci].t[:], in0=xn_[ci].t[:, :, 0:16],
                                                              in1=bc_mid(ropd.t[:, blk, 0:16], 8), op=ALU.mult),
                         reads=[xn_[ci].b, ropd.b], writes=[rc_[ci].b])
                    P.op(eng, lambda e, ci=ci: e.tensor_tensor(out=rsn_[ci].t[:], in0=xn_[ci].t[:, :, 0:16],
                                                              in1=bc_mid(ropd.t[:, blk, 16:32], 8), op=ALU.mult),
                         reads=[xn_[ci].b, ropd.b], writes=[rsn_[ci].b])
                    P.op(eng, lambda e, ci=ci: e.tensor_tensor(out=xb_[ci].t[:, :, 0:8], in0=rc_[ci].t[:, :, 0:8], in1=rsn_[ci].t[:, :, 8:16], op=ALU.subtract),
                         reads=[rc_[ci].b, rsn_[ci].b], writes=[xb_[ci].b])
                    P.op(eng, lambda e, ci=ci: e.tensor_tensor(out=xb_[ci].t[:, :, 8:16], in0=rc_[ci].t[:, :, 8:16], in1=rsn_[ci].t[:, :, 0:8], op=ALU.add),
                         reads=[rc_[ci].b, rsn_[ci].b], writes=[xb_[ci].b])
                for ci in range(nch):
                    ptT, half = chains[ci][2], chains[ci][3]
                    for hh in range(4):
                        P.op("pe", lambda e, ci=ci, hh=hh, ptT=ptT, half=half: e.transpose(ptT.t[:, (half * 4 + hh) * 128:(half * 4 + hh + 1) * 128],
                                                                                          xb_[ci].t[:, 2 * hh:2 * hh + 2, :].rearrange("p a b -> p (a b)"),
                                                                                          ident.t[:]),
                             reads=[xb_[ci].b, ident.b], writes=[ptT.b])
                k_ = kst[par]
                P.op("dve", lambda e: e.tensor_copy(k_.t[:].rearrange("p a b -> p (a b)"), pTk.t[:, 0:1024]), reads=[pTk.b], writes=[k_.b])
                P.dma("sp", lambda e: e.dma_start(out=KTs_w[:, :, blk, :], in_=k_.t[:]), k_.b, reads=[k_.b], writes=[b_KTs])
                if own:
                    q_ = qst[par]
                    P.op("act", lambda e: e.copy(q_.t[:].rearrange("p a b -> p (a b)"), pTq.t[:, 0:1024]), reads=[pTq.b], writes=[q_.b])
                    P.dma("sp", lambda e: e.dma_start(out=QTs_w[:, :, ob, :], in_=q_.t[:]), q_.b, reads=[q_.b], writes=[b_QTs])

            d1_early(0)
            for blk in range(NB):
                if blk + 1 < NB:
                    d1_early(blk + 1)
                d1_late(blk)
            P.emit()

        with ExitStack() as st:
            KTb = [sbt(st, "KT%d" % i, [128, NB * 128], BF16) for i in range(2)]
            VVb = [sbt(st, "VV%d" % i, [128, NB, 130], BF16) for i in range(2)]
            QT2b = [sbt(st, "QT2%d" % i, [128, NO, 256], BF16) for i in range(2)]
            NPT = 6
            PT = [sbt(st, "PT%d" % i, [128, 4, 128], BF16) for i in range(NPT)]
            zz = sbt(st, "zz", [128, 2], F32)
            a1 = sbt(st, "a1", [128, 128], F32)
            aa = sbt(st, "aa", [128, 128], F32)
            asq = sbt(st, "asq", [128, 128], F32)
            ass = sbt(st, "ass", [128, 1], F32)
            ars = sbt(st, "ars", [128, 1], F32)
            dob = sbt(st, "dob", [128, 128], BF16)
            doT = [sbt(st, "doT%d" % i, [128, 128], BF16) for i in range(2)]
            pP = pbank(st, "pP")
            pTd = pbank(st, "pTd", BF16)
            pSd = [pbank(st, "pSd%d" % i) for i in range(2)]
            pO0 = [pbank(st, "pO0%d" % i) for i in range(2)]
            pO1 = [pbank(st, "pO1%d" % i) for i in range(2)]
            assert NC % 2 == 0
            for i2 in range(2):
                P.op("pool", lambda e, i2=i2: e.memset(VVb[i2].t[:], 0.0), writes=[VVb[i2].b])
                P.op("dve", lambda e, i2=i2: e.tensor_copy(VVb[i2].t[:, :, 128:129], kbias.t[:].rearrange("p (a b) -> p a b", b=1)),
                     reads=[kbias.b], writes=[VVb[i2].b])
                P.op("pool", lambda e, i2=i2: e.memset(QT2b[i2].t[:], 0.0), writes=[QT2b[i2].b])

            def load_head(h):
                kt, vv, q2 = KTb[h % 2], VVb[h % 2], QT2b[h % 2]
                P.dma("sp", lambda e: e.dma_start(out=kt.t[:], in_=KTs[h]), kt.b, reads=[b_KTs], writes=[kt.b])
                P.dma("sp", lambda e: e.dma_start(out=vv.t[:, :, 0:128], in_=VVs[h]), vv.b, reads=[b_VVs], writes=[vv.b])
                P.dma("sp", lambda e: e.dma_start(out=q2.t[0:64, :, 0:128], in_=QTs[h][0:64, :].rearrange("p (i t) -> p i t", t=128)),
                      q2.b, reads=[b_QTs], writes=[q2.b])
                P.dma("sp", lambda e: e.dma_start(out=q2.t[64:128, :, 128:256], in_=QTs[h][64:128, :].rearrange("p (i t) -> p i t", t=128)),
                      q2.b, reads=[b_QTs], writes=[q2.b])
            load_head(0)
            for h in range(8):
                if h + 1 < 8:
                    load_head(h + 1)
                KT, VV, QT2 = KTb[h % 2], VVb[h % 2], QT2b[h % 2]
                b_K = [KT.b] * NB
                b_Kv = VV.b
                b_Q = [QT2.b] * NO
                items = []
                for i in range(NO):
                    nk = NC + i + 1
                    for kb0 in range(0, nk, 2):
                        items.append((i, kb0, min(2, nk - kb0)))
                SKEW = 2
                pS3 = [pSd[0], pSd[1], pP]

                def qk_exp(n):
                    i, kb0, nb = items[n]
                    nk = NC + i + 1
                    ps = pS3[n % 3]
                    pt = PT[n % NPT]
                    for j in range(nb):
                        kb = kb0 + j
                        P.op("pe", lambda e, j=j, kb=kb: e.matmul(ps.t[:, j * 256:(j + 1) * 256], lhsT=KT.t[:, kb * 128:(kb + 1) * 128],
                                                                  rhs=QT2.t[:, i, :], start=True, stop=True),
                             reads=[b_K[kb], b_Q[i]], writes=[ps.b])
                    P.op("act", lambda e: e.activation(out=pt.t[:, 0:2 * nb, :].rearrange("p a b -> p (a b)"),
                                                       in_=ps.t[:, 0:256 * nb], func=AF.Exp, scale=0.125),
                         reads=[ps.b], writes=[pt.b])
                    if kb0 + nb == nk:
                        jl = nb - 1
                        P.op("pool", lambda e: e.tensor_tensor(out=pt.t[:, 2 * jl:2 * jl + 2, :], in0=pt.t[:, 2 * jl:2 * jl + 2, :],
                                                               in1=cmask2.t[:], op=ALU.mult),
                             reads=[pt.b, cmask2.b], writes=[pt.b])

                def pv(n):
                    i, kb0, nb = items[n]
                    nk = NC + i + 1
                    o0 = pO0[i % 2]
                    o1 = pO1[i % 2]
                    pt = PT[n % NPT]
                    for j in range(nb):
                        kb = kb0 + j
                        P.op("pe", lambda e, j=j, kb=kb: e.matmul(o0.t[:, 0:129], lhsT=pt.t[:, 2 * j, :], rhs=VV.t[:, kb, 0:129],
                                                                  start=(kb == 0), stop=(kb == nk - 1)),
                             reads=[pt.b, b_Kv], writes=[o0.b])
                        P.op("pe", lambda e, j=j, kb=kb: e.matmul(o1.t[:, 0:129], lhsT=pt.t[:, 2 * j + 1, :], rhs=VV.t[:, kb, 0:129],
                                                                  start=(kb == 0), stop=(kb == nk - 1)),
                             reads=[pt.b, b_Kv], writes=[o1.b])
                    if kb0 + nb == nk:
                        finalize(i, o0, o1)

                def finalize(i, o0, o1):
                    while pend_fin:
                        pend_fin.pop(0)[1]()
                    P.op("dve", lambda e, o0=o0: e.reciprocal(zz.t[:, 0:1], o0.t[:, 128:129]), reads=[o0.b], writes=[zz.b])
                    P.op("dve", lambda e, o1=o1: e.reciprocal(zz.t[:, 1:2], o1.t[:, 128:129]), reads=[o1.b], writes=[zz.b])
                    P.op("dve", lambda e: e.tensor_tensor(out=zz.t[:, 1:2], in0=zz.t[:, 1:2], in1=lam.t[:, 0:1], op=ALU.mult),
                         reads=[zz.b, lam.b], writes=[zz.b])
                    P.op("dve", lambda e, o1=o1: e.tensor_scalar(a1.t[:], o1.t[:, 0:128], zz.t[:, 1:2], None, op0=ALU.mult),
                         reads=[o1.b, zz.b], writes=[a1.b])
                    P.op("dve", lambda e, o0=o0: e.scalar_tensor_tensor(out=aa.t[:], in0=o0.t[:, 0:128], scalar=zz.t[:, 0:1], in1=a1.t[:],
                                                                        op0=ALU.mult, op1=ALU.subtract),
                         reads=[o0.b, zz.b, a1.b], writes=[aa.b])
                    pend_fin.append((n_now[0] + 3, lambda: finalize_b(i)))

                def finalize_b(i):
                    P.op("dve", lambda e: e.tensor_tensor(out=asq.t[:], in0=aa.t[:], in1=aa.t[:], op=ALU.mult), reads=[aa.b], writes=[asq.b])
                    P.op("dve", lambda e: e.tensor_reduce(out=ass.t[:, 0:1], in_=asq.t[:], axis=AX.X, op=ALU.add), reads=[asq.b], writes=[ass.b])
                    P.op("dve", lambda e: e.tensor_scalar(ars.t[:], ass.t[:], 1.0 / 128, EPS, op0=ALU.mult, op1=ALU.add),
                         reads=[ass.b], writes=[ars.b])
                    P.op("pool", lambda e: e.tensor_tensor(out=ars.t[:], in0=ars.t[:], in1=mhalf.t[:, 0:1], op=ALU.pow),
                         reads=[ars.b, mhalf.b], writes=[ars.b])
                    P.op("dve", lambda e: e.scalar_tensor_tensor(out=dob.t[:], in0=aa.t[:], scalar=ars.t[:, 0:1], in1=sublnw.t[:],
                                                                 op0=ALU.mult, op1=ALU.mult),
                         reads=[aa.b, ars.b, sublnw.b], writes=[dob.b])
                    P.op("pe", lambda e: e.transpose(pTd.t[:, 256:384], dob.t[:], ident.t[:]), reads=[dob.b, ident.b], writes=[pTd.b])
                    d_ = doT[i % 2]
                    P.op("dve", lambda e, d_=d_: e.tensor_copy(d_.t[:], pTd.t[:, 256:384]), reads=[pTd.b], writes=[d_.b])
                    P.dma("pool", lambda e, d_=d_, i=i: e.dma_start(out=DOT[i][:, h * 128:(h + 1) * 128], in_=d_.t[:]),
                          d_.b, reads=[d_.b], writes=[b_DOT[i]])
                pend_fin = []
                n_now = [0]
                for n in range(len(items) + SKEW + 4):
                    n_now[0] = n
                    if n < len(items):
                        qk_exp(n)
                    if 0 <= n - SKEW < len(items):
                        pv(n - SKEW)
                    while pend_fin and pend_fin[0][0] <= n:
                        pend_fin.pop(0)[1]()
                assert not pend_fin
                P.emit()

        with ExitStack() as st:
            Wro = sbt(st, "Wro", [128, 16, 1024], BF16)
            Wdo = sbt(st, "Wdo", [128, 8, 1024], BF16)
            Wou = sbt(st, "Wou", [128, 8, 1024], BF16)
            Wg = sbt(st, "Wg", [128, 8, 2048], BF16)
            n2w = sbt(st, "n2w", [128, D], F32)
            P.dma("sp", lambda e: e.dma_start(out=n2w.t[:], in_=bc_part(norm2_w, D)), n2w.b, writes=[n2w.b])
            wro_v = w_ret_o.rearrange("(k p) n -> p k n", p=128)
            for k0 in range(0, 16, 4):
                P.dma("pool", lambda e, k0=k0: e.dma_start(out=Wro.t[:, k0:k0 + 4, :], in_=wro_v[:, k0:k0 + 4, :]), Wro.b, writes=[Wro.b])
            wdo_v = w_diff_o.rearrange("(k p) n -> p k n", p=128)
            wou_v = w_out.rearrange("(k p) n -> p k n", p=128)
            for k0 in range(0, 8, 4):
                P.dma("pool", lambda e, k0=k0: e.dma_start(out=Wdo.t[:, k0:k0 + 4, :], in_=wdo_v[:, k0:k0 + 4, :]), Wdo.b, writes=[Wdo.b])
                P.dma("pool", lambda e, k0=k0: e.dma_start(out=Wou.t[:, k0:k0 + 4, :], in_=wou_v[:, k0:k0 + 4, :]), Wou.b, writes=[Wou.b])
            for k0 in range(0, 8, 2):
                P.dma("pool", lambda e, k0=k0: e.dma_start(out=Wg.t[:, k0:k0 + 2, :], in_=w_in_v[:, k0:k0 + 2, C_GT:C_GT + 2048]), Wg.b, writes=[Wg.b])
            gTb = [sbt(st, "mgT%d" % i, [128, 16, 128], BF16) for i in range(2)]
            dTb = [sbt(st, "mdT%d" % i, [128, 8, 128], BF16) for i in range(2)]
            uTb = [sbt(st, "muT%d" % i, [128, 8, 128], BF16) for i in range(2)]
            xb = [sbt(st, "mx%d" % i, [128, D], F32) for i in range(2)]
            sig = sbt(st, "sig", [128, 2048], F32)
            m1 = sbt(st, "m1", [128, D], F32)
            m2 = sbt(st, "m2", [128, D], F32)
            mb = [sbt(st, "mb%d" % i, [128, D], BF16) for i in range(2)]
            mT = sbt(st, "mT", [128, 8, 128], BF16)
            h2 = [sbt(st, "h2%d" % i, [128, D], F32) for i in range(2)]
            sq = sbt(st, "msq", [128, D], F32)
            ss = sbt(st, "mss", [128, 1], F32)
            rs = sbt(st, "mrs", [128, 1], F32)
            ub = sbt(st, "mub", [128, D], BF16)
            u2T = [sbt(st, "mu2T%d" % i, [128, 8, 128], BF16) for i in range(2)]
            pA = [pbank(st, "pA%d" % i) for i in range(4)]
            pB = [pbank(st, "pB%d" % i) for i in range(2)]
            pTm = [pbank(st, "pTm%d" % i, BF16) for i in range(2)]
            def MA1(ob):
                blk = NC + ob
                g_ = gTb[ob % 2]; d_ = dTb[ob % 2]; u_ = uTb[ob % 2]; x_ = xb[ob % 2]
                P.dma("sp", lambda e: e.dma_start(out=g_.t[:].rearrange("p a b -> p (a b)"), in_=GT[ob]),
                      g_.b, reads=[b_GT[ob]], writes=[g_.b])
                P.dma("sp", lambda e: e.dma_start(out=d_.t[:].rearrange("p a b -> p (a b)"), in_=DOT[ob]),
                      d_.b, reads=[b_DOT[ob]], writes=[d_.b])
                P.dma("sp", lambda e: e.dma_start(out=u_.t[:].rearrange("p a b -> p (a b)"), in_=UT[blk]),
                      u_.b, reads=[b_UT[blk]], writes=[u_.b])
                P.dma("sp", lambda e: e.dma_start(out=x_.t[:], in_=xo[ob * 128:(ob + 1) * 128, :]), x_.b, writes=[x_.b])
                for j in range(4):
                    for kc in range(8):
                        P.op("pe", lambda e, j=j, kc=kc: e.matmul(pA[j].t[:, 0:512], lhsT=u_.t[:, kc, :], rhs=Wg.t[:, kc, j * 512:(j + 1) * 512],
                                                                  start=(kc == 0), stop=(kc == 7)),
                             reads=[u_.b, Wg.b], writes=[pA[j].b])
                    P.op("act", lambda e, j=j: e.activation(out=sig.t[:, j * 512:(j + 1) * 512], in_=pA[j].t[:, 0:512], func=AF.Sigmoid),
                         reads=[pA[j].b], writes=[sig.b])

            def MA2(ob):
                g_ = gTb[ob % 2]
                for j in range(2):
                    for kc in range(16):
                        P.op("pe", lambda e, j=j, kc=kc: e.matmul(pA[j].t[:, 0:512], lhsT=g_.t[:, kc, :], rhs=Wro.t[:, kc, j * 512:(j + 1) * 512],
                                                                  start=(kc == 0), stop=(kc == 15)),
                             reads=[g_.b, Wro.b], writes=[pA[j].b])
                    P.op("dve", lambda e, j=j: e.tensor_tensor(out=m1.t[:, j * 512:(j + 1) * 512], in0=pA[j].t[:, 0:512],
                                                               in1=sig.t[:, j * 512:(j + 1) * 512], op=ALU.mult),
                         reads=[pA[j].b, sig.b], writes=[m1.b])

            def MA3(ob):
                d_ = dTb[ob % 2]
                mb_ = mb[ob % 2]
                for j in range(2):
                    for kc in range(8):
                        P.op("pe", lambda e, j=j, kc=kc: e.matmul(pA[2 + j].t[:, 0:512], lhsT=d_.t[:, kc, :], rhs=Wdo.t[:, kc, j * 512:(j + 1) * 512],
                                                                  start=(kc == 0), stop=(kc == 7)),
                             reads=[d_.b, Wdo.b], writes=[pA[2 + j].b])
                    P.op("dve", lambda e, j=j: e.tensor_tensor(out=m2.t[:, j * 512:(j + 1) * 512], in0=pA[2 + j].t[:, 0:512],
                                                               in1=sig.t[:, 1024 + j * 512:1024 + (j + 1) * 512], op=ALU.mult),
                         reads=[pA[2 + j].b, sig.b], writes=[m2.b])
                P.op("pool", lambda e: e.tensor_tensor(out=mb_.t[:], in0=m1.t[:], in1=m2.t[:], op=ALU.add), reads=[m1.b, m2.b], writes=[mb_.b])

            def MB1(ob):
                mb_ = mb[ob % 2]
                for half in range(2):
                    pt = pTm[half]
                    for j in range(4):
                        kc = half * 4 + j
                        P.op("pe", lambda e, kc=kc, j=j, pt=pt: e.transpose(pt.t[:, j * 128:(j + 1) * 128], mb_.t[:, kc * 128:(kc + 1) * 128], ident.t[:]),
                             reads=[mb_.b, ident.b], writes=[pt.b])
                    if half == 0:
                        P.op("dve", lambda e, pt=pt: e.tensor_copy(mT.t[:, 0:4, :].rearrange("p a b -> p (a b)"), pt.t[:, 0:512]),
                             reads=[pt.b], writes=[mT.b])
                    else:
                        P.op("act", lambda e, pt=pt: e.copy(mT.t[:, 4:8, :].rearrange("p a b -> p (a b)"), pt.t[:, 0:512]),
                             reads=[pt.b], writes=[mT.b])

            def MB2(ob):
                h_ = h2[ob % 2]
                x_ = xb[ob % 2]
                for j in range(2):
                    for kc in range(8):
                        P.op("pe", lambda e, j=j, kc=kc: e.matmul(pB[j].t[:, 0:512], lhsT=mT.t[:, kc, :], rhs=Wou.t[:, kc, j * 512:(j + 1) * 512],
                                                                  start=(kc == 0), stop=(kc == 7)),
                             reads=[mT.b, Wou.b], writes=[pB[j].b])
                    P.op("dve", lambda e, j=j: e.tensor_tensor(out=h_.t[:, j * 512:(j + 1) * 512], in0=pB[j].t[:, 0:512],
                                                               in1=x_.t[:, j * 512:(j + 1) * 512], op=ALU.add),
                         reads=[pB[j].b, x_.b], writes=[h_.b])
                P.dma("pool", lambda e: e.dma_start(out=H2[ob * 128:(ob + 1) * 128, :], in_=h_.t[:]),
                      h_.b, reads=[h_.b], writes=[b_H2[ob]])
                norm_part(h_, n2w, sq, ss, rs, ub, use_pow=True)

            def MB3(ob):
                t_ = u2T[ob % 2]
                tr_part(ub, pTm, t_)
                P.dma("pool", lambda e: e.dma_start(out=U2T[ob], in_=t_.t[:].rearrange("p a b -> p (a b)")),
                      t_.b, reads=[t_.b], writes=[b_U2T[ob]])

            MA1(0); MA2(0); MA3(0)
            for ob in range(NO):
                nx = ob + 1
                MB1(ob)
                if nx < NO:
                    MA1(nx)
                MB2(ob)
                if nx < NO:
                    MA2(nx)
                MB3(ob)
                if nx < NO:
                    MA3(nx)
            P.emit()

        GB = 3
        NG = (NO + GB - 1) // GB
        with ExitStack() as st:
            Wup = sbt(st, "Wup", [128, 8, 2 * FFN], BF16)
            Wdn = sbt(st, "Wdn", [128, 22, D], BF16)
            cw = sbt(st, "cw", [128, 3, 44], F32)
            cb = sbt(st, "cb", [128, 44], F32)
            wup_v = w_up.rearrange("(k p) n -> p k n", p=128)
            for kc in range(8):
                for c0 in range(0, 2 * FFN, 1408):
                    P.dma("pool", lambda e, kc=kc, c0=c0: e.dma_start(out=Wup.t[:, kc, c0:c0 + 1408], in_=wup_v[:, kc, c0:c0 + 1408]),
                          Wup.b, writes=[Wup.b])
            wdn_v = w_down.rearrange("(k p) n -> p k n", p=128)
            for k0 in range(0, 22, 2):
                P.dma("pool", lambda e, k0=k0: e.dma_start(out=Wdn.t[:, k0:k0 + 2, :], in_=wdn_v[:, k0:k0 + 2, :]), Wdn.b, writes=[Wdn.b])
            for t0 in range(0, 44, 11):
                for k in range(3):
                    P.dma("sp", lambda e, k=k, t0=t0: e.dma_start(out=cw.t[:, k, t0:t0 + 11],
                                                                  in_=conv_w[k].rearrange("(t p) -> p t", p=128)[:, t0:t0 + 11],
                                                                  allow_slow_non_contiguous=True), cw.b, writes=[cw.b])
                P.dma("sp", lambda e, t0=t0: e.dma_start(out=cb.t[:, t0:t0 + 11], in_=conv_b.rearrange("(t p) -> p t", p=128)[:, t0:t0 + 11],
                                                         allow_slow_non_contiguous=True), cb.b, writes=[cb.b])
            NT = GB * 128
            u2g = [sbt(st, "u2g%d" % i, [128, 8, 2 + NT], BF16) for i in range(2)]
            ya = [sbt(st, "ya%d" % i, [128, NT], F32) for i in range(2)]
            yb = [sbt(st, "yb%d" % i, [128, NT], F32) for i in range(2)]
            sa = [sbt(st, "sa%d" % i, [128, NT], F32) for i in range(2)]
            gTt = sbt(st, "gTt", [128, 22, NT], BF16)
            hb = [sbt(st, "fh%d" % i, [128, D], F32) for i in range(2)]
            ob_ = [sbt(st, "fo%d" % i, [128, D], F32) for i in range(2)]
            pU = [pbank(st, "pU%d" % i) for i in range(4)]
            pD = [pbank(st, "pD%d" % i) for i in range(4)]
            P.op("pool", lambda e: e.memset(u2g[0].t[:], 0.0), writes=[u2g[0].b])
            P.op("pool", lambda e: e.memset(u2g[1].t[:], 0.0), writes=[u2g[1].b])
            ui = 0
            for gi in range(NG):
                blks = list(range(gi * GB, min(NO, (gi + 1) * GB)))
                nt = len(blks) * 128
                ug = u2g[gi % 2]
                up_ = u2g[(gi + 1) % 2]
                for j, ob in enumerate(blks):
                    P.dma("sp", lambda e, ug=ug, j=j, ob=ob: e.dma_start(out=ug.t[:, :, 2 + j * 128:2 + (j + 1) * 128],
                                                                        in_=U2T[ob].rearrange("p (a b) -> p a b", a=8)),
                          ug.b, reads=[b_U2T[ob]], writes=[ug.b])
                if gi > 0:
                    P.op("pool", lambda e, ug=ug, up_=up_: e.tensor_copy(ug.t[:, :, 0:2], up_.t[:, :, NT:NT + 2]),
                         reads=[up_.b], writes=[ug.b])
                for ft in range(22):
                    tiles = []
                    for which, fi in ((0, ft), (1, ft + 22)):
                        pu = pU[ui % 4]
                        ui += 1
                        for kc in range(8):
                            P.op("pe", lambda e, pu=pu, kc=kc, fi=fi, ug=ug, nt=nt: e.matmul(pu.t[:, 0:nt + 2], lhsT=Wup.t[:, kc, fi * 128:(fi + 1) * 128],
                                                                                        rhs=ug.t[:, kc, 0:nt + 2], start=(kc == 0), stop=(kc == 7)),
                                 reads=[Wup.b, ug.b], writes=[pu.b])
                        yt = (ya if which == 0 else yb)[ft % 2]
                        P.op("dve", lambda e, pu=pu, yt=yt, fi=fi, nt=nt: e.tensor_scalar(yt.t[:, 0:nt], pu.t[:, 2:nt + 2], cw.t[:, 2, fi:fi + 1], cb.t[:, fi:fi + 1],
                                                                                       op0=ALU.mult, op1=ALU.add),
                             reads=[pu.b, cw.b, cb.b], writes=[yt.b])
                        P.op("dve", lambda e, pu=pu, yt=yt, fi=fi, nt=nt: e.scalar_tensor_tensor(out=yt.t[:, 0:nt], in0=pu.t[:, 1:nt + 1], scalar=cw.t[:, 1, fi:fi + 1],
                                                                                              in1=yt.t[:, 0:nt], op0=ALU.mult, op1=ALU.add),
                             reads=[pu.b, cw.b, yt.b], writes=[yt.b])
                        P.op("dve", lambda e, pu=pu, yt=yt, fi=fi, nt=nt: e.scalar_tensor_tensor(out=yt.t[:, 0:nt], in0=pu.t[:, 0:nt], scalar=cw.t[:, 0, fi:fi + 1],
                                                                                              in1=yt.t[:, 0:nt], op0=ALU.mult, op1=ALU.add),
                             reads=[pu.b, cw.b, yt.b], writes=[yt.b])
                        tiles.append(yt)
                    s_ = sa[ft % 2]
                    P.op("act", lambda e, s_=s_, yt=tiles[0], nt=nt: e.activation(out=s_.t[:, 0:nt], in_=yt.t[:, 0:nt], func=AF.Silu),
                         reads=[tiles[0].b], writes=[s_.b])
                    P.op("pool", lambda e, s_=s_, yt=tiles[1], ft=ft, nt=nt: e.tensor_tensor(out=gTt.t[:, ft, 0:nt], in0=s_.t[:, 0:nt], in1=yt.t[:, 0:nt], op=ALU.mult),
                         reads=[s_.b, tiles[1].b], writes=[gTt.b])
                for j, ob in enumerate(blks):
                    h_ = hb[ob % 2]
                    o_ = ob_[ob % 2]
                    P.dma("sp", lambda e, h_=h_, ob=ob: e.dma_start(out=h_.t[:], in_=H2[ob * 128:(ob + 1) * 128, :]),
                          h_.b, reads=[b_H2[ob]], writes=[h_.b])
                    for half in range(2):
                        pd = pD[(ob * 2 + half) % 4]
                        for ft in range(22):
                            P.op("pe", lambda e, pd=pd, ft=ft, j=j, half=half: e.matmul(pd.t[:, 0:512], lhsT=gTt.t[:, ft, j * 128:(j + 1) * 128],
                                                                                    rhs=Wdn.t[:, ft, half * 512:(half + 1) * 512],
                                                                                    start=(ft == 0), stop=(ft == 21)),
                                 reads=[gTt.b, Wdn.b], writes=[pd.b])
                        P.op("dve", lambda e, pd=pd, half=half, h_=h_, o_=o_: e.tensor_tensor(out=o_.t[:, half * 512:(half + 1) * 512], in0=pd.t[:, 0:512],
                                                                                          in1=h_.t[:, half * 512:(half + 1) * 512], op=ALU.add),
                             reads=[pd.b, h_.b], writes=[o_.b])
                    P.dma("pool", lambda e, o_=o_, ob=ob: e.dma_start(out=y[ob * 128:(ob + 1) * 128, :], in_=o_.t[:]),
                          o_.b, reads=[o_.b], writes=[b_y])
            P.wait_all("pool", [b_y])
            P.emit()
        print("n_inst", P.n_inst, "n_wait", P.n_wait, "ndsem", P.ndsem)
    return nc


def make_tables(NC, NO, p, S):
    NB = NC + NO
    L = N_META + S
    if p == 0:
        ctx_pos = np.full(NC * 128, -1, np.int64)
        own_pos = np.arange(NO * 128)
    else:
        ctx_pos = np.arange(NC * 128) - PAD
        own_pos = L - NO * 128 + np.arange(NO * 128)
    pos = np.concatenate([ctx_pos, own_pos])
    valid = pos >= 0
    posf = np.where(valid, pos, 0).astype(np.float32)
    inv_r = np.power(np.float32(10000.0), -np.arange(128, dtype=np.float32) / np.float32(128))
    ang = posf[:, None] * inv_r[None, :]
    c, s = np.cos(ang), np.sin(ang)
    rope_r = np.concatenate([c, c, s, s], axis=1).astype(np.float32)
    inv_d = np.power(np.float32(500000.0), -np.arange(8, dtype=np.float32) / np.float32(8))
    ang = posf[:, None] * inv_d[None, :]
    c, s = np.cos(ang), np.sin(ang)
    rope_d = np.concatenate([c, c, s, s], axis=1).astype(np.float32)
    kb = np.where(valid, 1.0, 0.0).astype(np.float32).reshape(NB, 128).T.copy()
    idx = np.arange(128)
    cm = (idx[:, None] <= idx[None, :]).astype(np.float32)
    rdec = np.zeros((128, 8), np.float32)
    for h in range(4):
        rdec[:, h] = GAM[h] ** (idx + 1.0)
        rdec[:, 4 + h] = (256 ** -0.5) * GAM[h] ** (127.0 - idx)
    return rope_r, rope_d, kb, cm, rdec


_NC_CACHE = {}


def run(inputs, NC, NO, debug=False, trace=False):
    x = np.asarray(inputs["x"], np.float32)
    B, S, _ = x.shape
    assert S == 128 * (NC + NO - 1)
    L = N_META + S
    meta = np.asarray(inputs["meta_tokens"], np.float32)
    key = (NC, NO, debug)
    if key not in _NC_CACHE:
        _NC_CACHE[key] = build(NC, NO, debug)
    nc = _NC_CACHE[key]
    f = lambda k: np.ascontiguousarray(np.asarray(inputs[k], np.float32)[0])
    common = {
        "w_in": f("w_in"), "w_ret_o": f("w_ret_o"), "w_diff_o": f("w_diff_o"), "w_out": f("w_out"),
        "w_up": f("w_up"), "w_down": f("w_down"), "norm1_w": f("norm1_w"), "norm2_w": f("norm2_w"),
        "qk_norm_w": np.concatenate([f("q_norm_w"), f("q_norm_w"), f("k_norm_w"), f("k_norm_w")]),
        "lambdas": np.concatenate([f("lambda_q1"), f("lambda_k1"), f("lambda_q2"), f("lambda_k2")]),
        "subln_w": f("diff_subln_w"), "conv_w": f("conv_w"), "conv_b": f("conv_b"),
    }
    tabs = [make_tables(NC, NO, p, S) for p in range(2)]
    in_maps = []
    for b in range(B):
        seq = np.concatenate([meta, x[b]], axis=0)
        for p in range(2):
            if p == 0:
                xc_ = np.zeros((NC * 128, D), np.float32)
                xo_ = seq[0:NO * 128]
            else:
                xc_ = np.concatenate([np.zeros((PAD, D), np.float32), seq[0:NC * 128 - PAD]], axis=0)
                xo_ = seq[L - NO * 128:L]
            rr, rd, kb, cm, rdec = tabs[p]
            m = dict(common)
            m.update({"xc": np.ascontiguousarray(xc_), "xo": np.ascontiguousarray(xo_), "rope_r": rr, "rope_d": rd,
                      "kbias": kb, "cmask": cm, "rdec": rdec})
            in_maps.append(m)
    res = run_bass_kernel_spmd(nc, in_maps, core_ids=list(range(len(in_maps))), trace=trace)
    out = np.empty((B, S, D), np.float32)
    split = (NO * 128 - N_META) - 64
    for b in range(B):
        y0 = res.results[2 * b]["y"]
        y1 = res.results[2 * b + 1]["y"]
        out[b, :split] = y0[N_META:N_META + split]
        off1 = L - NO * 128
        out[b, split:] = y1[N_META + split - off1:]
    return out, res


def kernel(**inputs):
    out, _ = run(inputs, 32, 33)
    return out
```

```python
import math
import numpy as np
from contextlib import ExitStack
import concourse.bass as bass
import concourse.mybir as mybir
from concourse.bass_utils import run_bass_kernel_spmd

F32 = mybir.dt.float32
BF16 = mybir.dt.bfloat16
AF = mybir.ActivationFunctionType
ALU = mybir.AluOpType
AX = mybir.AxisListType

D = 1024
N_META = 16
PAD = 112
FFN = 2816
IN_COLS = 11264
EPS = 1e-6
C_RQ, C_RK, C_RV, C_RG, C_DQ, C_DK, C_DV, C_GT = 0, 1024, 2048, 4096, 6144, 7168, 8192, 9216
NEGB = -30000.0
LAM_INIT = 0.8 - 0.6 * math.exp(-0.3 * 0)
GAM = [1.0 - 2.0 ** (-5.0 - h) for h in range(4)]

SAME_ENGINE_SYNC = True


class Buf:
    __slots__ = ("name", "last_write", "reads", "dsem", "dcount")

    def __init__(self, name=""):
        self.name = name
        self.last_write = None
        self.reads = {}
        self.dsem = None
        self.dcount = 0


class T:
    def __init__(self, t, name):
        self.t = t
        self.b = Buf(name)


class Prog:
    ENGS = ("pe", "act", "dve", "pool", "sp")
    ENGOBJ = {"pe": "tensor", "act": "scalar", "dve": "vector", "pool": "gpsimd", "sp": "sync"}

    def __init__(self, nc, stack):
        self.nc = nc
        self.stack = stack
        self.q = {e: [] for e in self.ENGS}
        self.ecount = {e: 0 for e in self.ENGS}
        self.sems = {}
        for e in self.ENGS:
            self.sems[("e", e)] = stack.enter_context(nc.semaphore("s_" + e))
        self.waited = {e: {} for e in self.ENGS}
        self.ndsem = 0
        self.n_inst = 0
        self.n_wait = 0

    def _dsem(self, buf):
        if buf.dsem is None:
            buf.dsem = ("d", self.ndsem)
            self.sems[buf.dsem] = self.stack.enter_context(self.nc.semaphore("d%d" % self.ndsem))
            self.ndsem += 1
        return buf.dsem

    def _deps(self, eng, reads, writes):
        deps = {}

        def add(t):
            if t is None:
                return
            k, v = t
            if deps.get(k, -1) < v:
                deps[k] = v
        for b in reads:
            add(b.last_write)
        for b in writes:
            add(b.last_write)
            for k, v in b.reads.items():
                add((k, v))
        out = []
        w = self.waited[eng]
        for k, v in deps.items():
            if k == ("e", eng) and (eng == "pe" or not SAME_ENGINE_SYNC):
                continue
            if w.get(k, -1) >= v:
                continue
            w[k] = v
            out.append((k, v))
        return out

    def _commit(self, tok, reads, writes):
        k, v = tok
        for b in writes:
            b.last_write = tok
            b.reads = {}
        for b in reads:
            if b.reads.get(k, -1) < v:
                b.reads[k] = v

    def op(self, eng, fn, reads=(), writes=()):
        waits = self._deps(eng, reads, writes)
        self.ecount[eng] += 1
        tok = (("e", eng), self.ecount[eng])
        self.q[eng].append((waits, fn, tok[0], 1))
        self._commit(tok, reads, writes)
        self.n_inst += 1
        self.n_wait += len(waits)
        return tok

    def dma(self, eng, fn, sb, reads=(), writes=()):
        waits = self._deps(eng, reads, writes)
        k = self._dsem(sb)
        sb.dcount += 16
        tok = (k, sb.dcount)
        self.q[eng].append((waits, fn, k, 16))
        self._commit(tok, reads, writes)
        self.n_inst += 1
        self.n_wait += len(waits)
        return tok

    def wait_all(self, eng, bufs):
        waits = self._deps(eng, bufs, bufs)
        self.q[eng].append((waits, None, None, 0))

    def emit(self):
        nc = self.nc
        sems = self.sems
        with nc.Block() as block:
            for e in self.ENGS:
                lst = self.q[e]

                def body(eo, lst=lst):
                    for waits, fn, sk, inc in lst:
                        for k, v in waits:
                            eo.wait_ge(sems[k], v)
                        if fn is not None:
                            fn(eo).then_inc(sems[sk], inc)
                getattr(block, self.ENGOBJ[e])(body)
        self.q = {e: [] for e in self.ENGS}


def bc_mid(ap, n):
    return bass.AP(ap.tensor, ap.offset, [list(ap.ap[0]), [0, n], list(ap.ap[1])])


def bc_last(ap, k):
    return bass.AP(ap.tensor, ap.offset, [list(ap.ap[0]), list(ap.ap[1]), [0, k]])


def bc_part(dram_ap_1d, n):
    return bass.AP(dram_ap_1d.tensor, dram_ap_1d.offset, [[0, 128], [1, n]])


def build(NC, NO, debug=False):
    NB = NC + NO
    nc = bass.Bass("TRN2", target_bir_lowering=False)

    def din(name, shape, dt=F32):
        return nc.dram_tensor(name, list(shape), dt, kind="ExternalInput").ap()

    okind = "ExternalOutput" if debug else "Internal"

    def dscr(name, shape, dt):
        return nc.dram_tensor(name, list(shape), dt, kind=okind).ap()

    xc = din("xc", [NC * 128, D])
    xo = din("xo", [NO * 128, D])
    w_in = din("w_in", [D, IN_COLS])
    w_ret_o = din("w_ret_o", [2048, D])
    w_diff_o = din("w_diff_o", [D, D])
    w_out = din("w_out", [D, D])
    w_up = din("w_up", [D, 2 * FFN])
    w_down = din("w_down", [FFN, D])
    norm1_w = din("norm1_w", [D])
    norm2_w = din("norm2_w", [D])
    qk_norm_w = din("qk_norm_w", [256])
    lambdas = din("lambdas", [256])
    subln_w = din("subln_w", [128])
    conv_w = din("conv_w", [3, 2 * FFN])
    conv_b = din("conv_b", [2 * FFN])
    rope_r = din("rope_r", [NB * 128, 512])
    rope_d = din("rope_d", [NB * 128, 32])
    kbias_d = din("kbias", [128, NB])
    cmask_d = din("cmask", [128, 128])
    rdec_d = din("rdec", [128, 8])
    y = nc.dram_tensor("y", [NO * 128, D], F32, kind="ExternalOutput").ap()

    UT = dscr("UT", [NB, 128, 1024], BF16)
    GT = dscr("GT", [NO, 128, 2048], BF16)
    DOT = dscr("DOT", [NO, 128, 1024], BF16)
    H2 = dscr("H2", [NO * 128, D], F32)
    U2T = dscr("U2T", [NO, 128, 1024], BF16)
    b_UT = [Buf("UT%d" % i) for i in range(NB)]
    b_GT = [Buf("GT%d" % i) for i in range(NO)]
    b_DOT = [Buf("DOT%d" % i) for i in range(NO)]
    b_H2 = [Buf("H2%d" % i) for i in range(NO)]
    b_U2T = [Buf("U2T%d" % i) for i in range(NO)]
    b_y = Buf("y")

    w_in_v = w_in.rearrange("(k p) n -> p k n", p=128)

    with ExitStack() as gst:
        P = Prog(nc, gst)

        def sbt(st, name, shape, dt):
            return T(st.enter_context(nc.sbuf_tensor("sb_" + name, list(shape), dt)), name)

        def pbank(st, name, dt=F32):
            n = 512 if dt == F32 else 1024
            return T(st.enter_context(nc.psum_tensor("ps_" + name, [128, n], dt)), name)

        ident = sbt(gst, "ident", [128, 128], BF16)
        identf = sbt(gst, "identf", [128, 128], F32)
        cmask = sbt(gst, "cmask", [128, 128], F32)
        cmask2 = sbt(gst, "cmask2", [128, 2, 128], BF16)
        kbias = sbt(gst, "kbias", [128, NB], F32)
        rdec = sbt(gst, "rdec", [128, 8], F32)
        lam = sbt(gst, "lam", [128, 4], F32)
        lamv = sbt(gst, "lamv", [128, 256], F32)
        lamt = sbt(gst, "lamt", [128, 128], F32)
        lams = sbt(gst, "lams", [128, 2], F32)
        sublnw = sbt(gst, "sublnw", [128, 128], F32)
        wqk = sbt(gst, "wqk", [128, 4, 64], F32)

        P.op("pool", lambda e: e.iota(identf.t[:], pattern=[[1, 128]], base=0, channel_multiplier=-1,
                                      allow_small_or_imprecise_dtypes=True), writes=[identf.b])
        P.op("dve", lambda e: e.tensor_scalar(ident.t[:], identf.t[:], 0.0, None, op0=ALU.is_equal),
             reads=[identf.b], writes=[ident.b])
        P.dma("sp", lambda e: e.dma_start(out=cmask.t[:], in_=cmask_d), cmask.b, writes=[cmask.b])
        P.dma("sp", lambda e: e.dma_start(out=kbias.t[:], in_=kbias_d), kbias.b, writes=[kbias.b])
        P.dma("sp", lambda e: e.dma_start(out=rdec.t[:], in_=rdec_d), rdec.b, writes=[rdec.b])
        P.dma("sp", lambda e: e.dma_start(out=lamv.t[:], in_=bc_part(lambdas, 256)), lamv.b, writes=[lamv.b])
        P.dma("sp", lambda e: e.dma_start(out=sublnw.t[:], in_=bc_part(subln_w, 128)), sublnw.b, writes=[sublnw.b])
        P.dma("sp", lambda e: e.dma_start(out=wqk.t[:].rearrange("p a b -> p (a b)"), in_=bc_part(qk_norm_w, 256)),
              wqk.b, writes=[wqk.b])
        P.op("dve", lambda e: e.tensor_copy(cmask2.t[:, 0, :], cmask.t[:]), reads=[cmask.b], writes=[cmask2.b])
        P.op("dve", lambda e: e.tensor_copy(cmask2.t[:, 1, :], cmask.t[:]), reads=[cmask.b], writes=[cmask2.b])
        P.op("dve", lambda e: e.tensor_tensor(out=lamt.t[:, 0:64], in0=lamv.t[:, 0:64], in1=lamv.t[:, 64:128], op=ALU.mult),
             reads=[lamv.b], writes=[lamt.b])
        P.op("dve", lambda e: e.tensor_tensor(out=lamt.t[:, 64:128], in0=lamv.t[:, 128:192], in1=lamv.t[:, 192:256], op=ALU.mult),
             reads=[lamv.b], writes=[lamt.b])
        P.op("dve", lambda e: e.tensor_reduce(out=lams.t[:, 0:2], in_=lamt.t[:].rearrange("p (a b) -> p a b", a=2),
                                              axis=AX.X, op=ALU.add), reads=[lamt.b], writes=[lams.b])
        P.op("act", lambda e: e.activation(out=lams.t[:], in_=lams.t[:], func=AF.Exp), reads=[lams.b], writes=[lams.b])
        P.op("dve", lambda e: e.tensor_tensor(out=lam.t[:, 0:1], in0=lams.t[:, 0:1], in1=lams.t[:, 1:2], op=ALU.subtract),
             reads=[lams.b], writes=[lam.b])
        P.op("dve", lambda e: e.tensor_scalar(lam.t[:, 0:1], lam.t[:, 0:1], LAM_INIT, None, op0=ALU.add),
             reads=[lam.b], writes=[lam.b])
        P.op("dve", lambda e: e.tensor_scalar(sublnw.t[:], sublnw.t[:], 1.0 - LAM_INIT, None, op0=ALU.mult),
             reads=[sublnw.b], writes=[sublnw.b])

        def norm_part(xt, nw, sq, ss, rs, ub, use_pow=False):
            P.op("act", lambda e: e.activation(out=sq.t[:], in_=xt.t[:], func=AF.Square, accum_out=ss.t[:, 0:1]),
                 reads=[xt.b], writes=[sq.b, ss.b])
            if use_pow:
                P.op("dve", lambda e: e.tensor_scalar(rs.t[:, 0:1], ss.t[:, 0:1], 1.0 / D, EPS, op0=ALU.mult, op1=ALU.add),
                     reads=[ss.b], writes=[rs.b])
                P.op("pool", lambda e: e.tensor_tensor(out=rs.t[:, 0:1], in0=rs.t[:, 0:1], in1=mhalf.t[:, 0:1], op=ALU.pow),
                     reads=[rs.b, mhalf.b], writes=[rs.b])
            else:
                P.op("act", lambda e: e.activation(out=rs.t[:, 0:1], in_=ss.t[:, 0:1], func=AF.Sqrt, scale=1.0 / D, bias=eps_t.t[:, 0:1]),
                     reads=[ss.b, eps_t.b], writes=[rs.b])
                P.op("dve", lambda e: e.reciprocal(rs.t[:, 0:1], rs.t[:, 0:1]), reads=[rs.b], writes=[rs.b])
            P.op("dve", lambda e: e.scalar_tensor_tensor(out=ub.t[:], in0=xt.t[:], scalar=rs.t[:, 0:1], in1=nw.t[:],
                                                         op0=ALU.mult, op1=ALU.mult),
                 reads=[xt.b, rs.b, nw.b], writes=[ub.b])

        def tr_part(ub, pT, uT):
            for half in range(2):
                pt = pT[half]
                for j in range(4):
                    kc = half * 4 + j
                    P.op("pe", lambda e, kc=kc, j=j, pt=pt: e.transpose(pt.t[:, j * 128:(j + 1) * 128],
                                                                        ub.t[:, kc * 128:(kc + 1) * 128], ident.t[:]),
                         reads=[ub.b, ident.b], writes=[pt.b])
                if half == 0:
                    P.op("dve", lambda e, pt=pt: e.tensor_copy(uT.t[:, 0:4, :].rearrange("p a b -> p (a b)"), pt.t[:, 0:512]),
                         reads=[pt.b], writes=[uT.b])
                else:
                    P.op("act", lambda e, pt=pt: e.copy(uT.t[:, 4:8, :].rearrange("p a b -> p (a b)"), pt.t[:, 0:512]),
                         reads=[pt.b], writes=[uT.b])

        def norm_transpose(xt, nw, sq, ss, rs, ub, pT, uT):
            norm_part(xt, nw, sq, ss, rs, ub)
            tr_part(ub, pT, uT)

        eps_t = sbt(gst, "eps_t", [128, 1], F32)
        P.op("pool", lambda e: e.memset(eps_t.t[:], EPS), writes=[eps_t.b])
        mhalf = sbt(gst, "mhalf", [128, 8], F32)
        P.op("pool", lambda e: e.memset(mhalf.t[:], -0.5), writes=[mhalf.b])

        with ExitStack() as st:
            n1w = sbt(st, "n1w", [128, D], F32)
            P.dma("sp", lambda e: e.dma_start(out=n1w.t[:], in_=bc_part(norm1_w, D)), n1w.b, writes=[n1w.b])
            xb = [sbt(st, "x%d" % i, [128, D], F32) for i in range(3)]
            sq = [sbt(st, "sq%d" % i, [128, D], F32) for i in range(2)]
            ss = [sbt(st, "ss%d" % i, [128, 1], F32) for i in range(2)]
            rs = [sbt(st, "rs%d" % i, [128, 1], F32) for i in range(2)]
            ub = [sbt(st, "ub%d" % i, [128, D], BF16) for i in range(2)]
            uT = [sbt(st, "uT%d" % i, [128, 8, 128], BF16) for i in range(2)]
            pT = [pbank(st, "pT%d" % i, BF16) for i in range(4)]
            def p0_x(blk):
                src = xc[blk * 128:(blk + 1) * 128, :] if blk < NC else xo[(blk - NC) * 128:(blk - NC + 1) * 128, :]
                x_ = xb[blk % 3]
                P.dma("sp", lambda e: e.dma_start(out=x_.t[:], in_=src), x_.b, writes=[x_.b])
                norm_part(x_, n1w, sq[blk % 2], ss[blk % 2], rs[blk % 2], ub[blk % 2])

            def p0_y(blk):
                u_ = uT[blk % 2]
                tr_part(ub[blk % 2], pT[(blk % 2) * 2:(blk % 2) * 2 + 2], u_)
                P.dma("pool", lambda e: e.dma_start(out=UT[blk], in_=u_.t[:].rearrange("p a b -> p (a b)")),
                      u_.b, reads=[u_.b], writes=[b_UT[blk]])
            p0_x(0)
            for blk in range(NB):
                if blk + 1 < NB:
                    p0_x(blk + 1)
                p0_y(blk)
            P.emit()

        with ExitStack() as st:
            WR = [sbt(st, "WR%d" % i, [128, 8, 1536], BF16) for i in range(2)]
            uTb = [sbt(st, "ruT%d" % i, [128, 8, 128], BF16) for i in range(3)]
            RT = [sbt(st, "RT%d" % i, [128, 512], F32) for i in range(3)]
            Rf = sbt(st, "Rf", [128, 2, 512], F32)
            Rb = sbt(st, "Rb", [128, 2, 512], BF16)
            Aq = sbt(st, "Aq", [128, 256], F32)
            Bq = sbt(st, "Bq", [128, 256], F32)
            Ak = sbt(st, "Ak", [128, 256], F32)
            Bk = sbt(st, "Bk", [128, 256], F32)
            qr = [sbt(st, "qr%d" % i, [128, 256], BF16) for i in range(2)]
            kr = [sbt(st, "kr%d" % i, [128, 256], BF16) for i in range(2)]
            vb = [sbt(st, "vb%d" % i, [128, 512], BF16) for i in range(2)]
            sg = [sbt(st, "sg%d" % i, [128, 512], F32) for i in range(2)]
            qkT = sbt(st, "qkT", [128, 4, 128], BF16)
            Sm = sbt(st, "Sm", [128, 128], BF16)
            bst = sbt(st, "bst", [128, 6], F32)
            mv = sbt(st, "mv", [128, 2], F32)
            grs = sbt(st, "grs", [128, 1], F32)
            on = sbt(st, "on", [128, 512], F32)
            gtd = sbt(st, "gtd", [128, 512], BF16)
            gT = [sbt(st, "gT%d" % i, [128, 4, 128], BF16) for i in range(2)]
            pQK = pbank(st, "pQK")
            pV = pbank(st, "pV")
            pG = pbank(st, "pG")
            pTq = pbank(st, "pTq", BF16)
            pSv = pTq.t[:, 512:768].bitcast(F32)
            pTg = pbank(st, "pTg", BF16)
            pO2 = [pbank(st, "pO%d" % i) for i in range(2)]
            pR1 = pbank(st, "pR1")
            Rb2 = [sbt(st, "Rb%d" % i, [128, 2, 512], BF16) for i in range(2)]
            bst2 = [sbt(st, "bst%d" % i, [128, 6], F32) for i in range(2)]
            mv2 = [sbt(st, "mv%d" % i, [128, 2], F32) for i in range(2)]
            grs2 = [sbt(st, "grs%d" % i, [128, 1], F32) for i in range(2)]
            on2 = [sbt(st, "on%d" % i, [128, 512], F32) for i in range(2)]
            gtd2 = [sbt(st, "gtd%d" % i, [128, 512], BF16) for i in range(2)]

            def load_WR(h):
                w = WR[h % 2]
                for (c0, n, o0) in ((C_RQ + h * 256, 256, 0), (C_RK + h * 256, 256, 256),
                                    (C_RV + h * 512, 512, 512), (C_RG + h * 512, 512, 1024)):
                    P.dma("pool", lambda e, w=w, c0=c0, n=n, o0=o0: e.dma_start(out=w.t[:, :, o0:o0 + n],
                                                                                   in_=w_in_v[:, :, c0:c0 + n]),
                          w.b, writes=[w.b])
            load_WR(0)
            for h in range(4):
                if h + 1 < 4:
                    load_WR(h + 1)
                w = WR[h % 2]
                g = GAM[h]
                P.op("pool", lambda e: e.memset(Rf.t[:], 0.0), writes=[Rf.b])
                P.op("pool", lambda e: e.memset(Rb2[0].t[:], 0.0), writes=[Rb2[0].b])
                P.op("pool", lambda e: e.memset(Rb2[1].t[:], 0.0), writes=[Rb2[1].b])

                def A1(blk):
                    own = blk >= NC
                    u_ = uTb[blk % 3]
                    rt = RT[blk % 3]
                    k_ = kr[blk % 2]
                    q_ = qr[blk % 2]
                    P.dma("sp", lambda e: e.dma_start(out=u_.t[:].rearrange("p a b -> p (a b)"), in_=UT[blk]),
                          u_.b, reads=[b_UT[blk]], writes=[u_.b])
                    P.dma("sp", lambda e: e.dma_start(out=rt.t[:], in_=rope_r[blk * 128:(blk + 1) * 128, :]),
                          rt.b, writes=[rt.b])
                    c0 = 0 if own else 256
                    for kc in range(8):
                        P.op("pe", lambda e, kc=kc: e.matmul(pQK.t[:, c0:512], lhsT=u_.t[:, kc, :], rhs=w.t[:, kc, c0:512],
                                                             start=(kc == 0), stop=(kc == 7)),
                             reads=[u_.b, w.b], writes=[pQK.b])
                    P.op("dve", lambda e: e.scalar_tensor_tensor(out=Ak.t[:], in0=pQK.t[:, 256:512], scalar=rdec.t[:, 4 + h:5 + h],
                                                                 in1=rt.t[:, 0:256], op0=ALU.mult, op1=ALU.mult),
                         reads=[pQK.b, rdec.b, rt.b], writes=[Ak.b])
                    P.op("dve", lambda e: e.scalar_tensor_tensor(out=Bk.t[:], in0=pQK.t[:, 256:512], scalar=rdec.t[:, 4 + h:5 + h],
                                                                 in1=rt.t[:, 256:512], op0=ALU.mult, op1=ALU.mult),
                         reads=[pQK.b, rdec.b, rt.b], writes=[Bk.b])
                    if own:
                        P.op("dve", lambda e: e.scalar_tensor_tensor(out=Aq.t[:], in0=pQK.t[:, 0:256], scalar=rdec.t[:, h:h + 1],
                                                                     in1=rt.t[:, 0:256], op0=ALU.mult, op1=ALU.mult),
                             reads=[pQK.b, rdec.b, rt.b], writes=[Aq.b])
                        P.op("dve", lambda e: e.scalar_tensor_tensor(out=Bq.t[:], in0=pQK.t[:, 0:256], scalar=rdec.t[:, h:h + 1],
                                                                     in1=rt.t[:, 256:512], op0=ALU.mult, op1=ALU.mult),
                             reads=[pQK.b, rdec.b, rt.b], writes=[Bq.b])
                    P.op("dve", lambda e: e.tensor_tensor(out=k_.t[:, 0:128], in0=Ak.t[:, 0:128], in1=Bk.t[:, 128:256], op=ALU.subtract),
                         reads=[Ak.b, Bk.b], writes=[k_.b])
                    P.op("dve", lambda e: e.tensor_tensor(out=k_.t[:, 128:256], in0=Ak.t[:, 128:256], in1=Bk.t[:, 0:128], op=ALU.add),
                         reads=[Ak.b, Bk.b], writes=[k_.b])
                    if own:
                        P.op("dve", lambda e: e.tensor_tensor(out=q_.t[:, 0:128], in0=Aq.t[:, 0:128], in1=Bq.t[:, 128:256], op=ALU.subtract),
                             reads=[Aq.b, Bq.b], writes=[q_.b])
                        P.op("dve", lambda e: e.tensor_tensor(out=q_.t[:, 128:256], in0=Aq.t[:, 128:256], in1=Bq.t[:, 0:128], op=ALU.add),
                             reads=[Aq.b, Bq.b], writes=[q_.b])

                def A2(blk):
                    u_ = uTb[blk % 3]
                    v_ = vb[blk % 2]
                    for kc in range(8):
                        P.op("pe", lambda e, kc=kc: e.matmul(pV.t[:, 0:512], lhsT=u_.t[:, kc, :], rhs=w.t[:, kc, 512:1024],
                                                             start=(kc == 0), stop=(kc == 7)),
                             reads=[u_.b, w.b], writes=[pV.b])
                    P.op("act", lambda e: e.copy(v_.t[:], pV.t[:, 0:512]), reads=[pV.b], writes=[v_.b])

                def A3(blk):
                    if blk < NC:
                        return
                    u_ = uTb[blk % 3]
                    s_ = sg[blk % 2]
                    for kc in range(8):
                        P.op("pe", lambda e, kc=kc: e.matmul(pG.t[:, 0:512], lhsT=u_.t[:, kc, :], rhs=w.t[:, kc, 1024:1536],
                                                             start=(kc == 0), stop=(kc == 7)),
                             reads=[u_.b, w.b], writes=[pG.b])
                    P.op("act", lambda e: e.activation(out=s_.t[:], in_=pG.t[:, 0:512], func=AF.Silu), reads=[pG.b], writes=[s_.b])

                def B1(blk):
                    if blk < NC:
                        return
                    k_ = kr[blk % 2]
                    q_ = qr[blk % 2]
                    for j in range(4):
                        srcT = q_ if j < 2 else k_
                        c = j % 2
                        P.op("pe", lambda e, j=j, c=c, srcT=srcT: e.transpose(pTq.t[:, j * 128:(j + 1) * 128],
                                                                              srcT.t[:, c * 128:(c + 1) * 128], ident.t[:]),
                             reads=[srcT.b, ident.b], writes=[pTq.b])
                    P.op("act", lambda e: e.copy(qkT.t[:].rearrange("p a b -> p (a b)"), pTq.t[:, 0:512]),
                         reads=[pTq.b], writes=[qkT.b])

                def B2(blk):
                    if blk < NC:
                        return
                    for c in range(2):
                        P.op("pe", lambda e, c=c: e.matmul(pSv, lhsT=qkT.t[:, 2 + c, :], rhs=qkT.t[:, c, :],
                                                           start=(c == 0), stop=(c == 1)),
                             reads=[qkT.b], writes=[pTq.b])
                    P.op("dve", lambda e: e.scalar_tensor_tensor(out=Sm.t[:], in0=pSv, scalar=float(g ** -128.0),
                                                                 in1=cmask.t[:], op0=ALU.mult, op1=ALU.mult),
                         reads=[pTq.b, cmask.b], writes=[Sm.b])

                def B3a(blk):
                    if blk < NC:
                        return
                    v_ = vb[blk % 2]
                    po = pO2[blk % 2]
                    rb = Rb2[(blk - 1) % 2]
                    P.op("pe", lambda e: e.matmul(po.t[:, 0:512], lhsT=Sm.t[:], rhs=v_.t[:], start=True, stop=False),
                         reads=[Sm.b, v_.b], writes=[po.b])
                    for c in range(2):
                        P.op("pe", lambda e, c=c: e.matmul(po.t[:, 0:512], lhsT=qkT.t[:, c, :], rhs=rb.t[:, c, :],
                                                           start=False, stop=(c == 1)),
                             reads=[qkT.b, rb.b], writes=[po.b])

                def CH(blk):
                    if blk < NC:
                        return
                    par = blk % 2
                    po = pO2[par]
                    s_ = sg[par]
                    bst_, mv_, grs_, on_, gtd_ = bst2[par], mv2[par], grs2[par], on2[par], gtd2[par]
                    P.op("dve", lambda e: e.bn_stats(bst_.t[:], po.t[:, 0:512]), reads=[po.b], writes=[bst_.b])
                    P.op("dve", lambda e: e.bn_aggr(mv_.t[:], bst_.t[:]), reads=[bst_.b], writes=[mv_.b])
                    P.op("dve", lambda e: e.tensor_scalar(grs_.t[:], mv_.t[:, 1:2], EPS, None, op0=ALU.add),
                         reads=[mv_.b], writes=[grs_.b])
                    P.op("pool", lambda e: e.tensor_tensor(out=grs_.t[:], in0=grs_.t[:], in1=mhalf.t[:, 0:1], op=ALU.pow),
                         reads=[grs_.b, mhalf.b], writes=[grs_.b])
                    P.op("dve", lambda e: e.tensor_scalar(on_.t[:], po.t[:, 0:512], mv_.t[:, 0:1], grs_.t[:, 0:1],
                                                          op0=ALU.subtract, op1=ALU.mult),
                         reads=[po.b, mv_.b, grs_.b], writes=[on_.b])
                    P.op("dve", lambda e: e.tensor_tensor(out=gtd_.t[:], in0=on_.t[:], in1=s_.t[:], op=ALU.mult),
                         reads=[on_.b, s_.b], writes=[gtd_.b])

                def ST(blk, c):
                    if blk >= NB - 1:
                        return
                    k_ = kr[blk % 2]
                    v_ = vb[blk % 2]
                    P.op("pe", lambda e: e.matmul(pR1.t[:, 0:512], lhsT=k_.t[:, c * 128:(c + 1) * 128], rhs=v_.t[:],
                                                  start=True, stop=True),
                         reads=[k_.b, v_.b], writes=[pR1.b])
                    P.op("dve", lambda e: e.scalar_tensor_tensor(out=Rf.t[:, c, :], in0=Rf.t[:, c, :], scalar=float(g ** 128.0),
                                                                 in1=pR1.t[:, 0:512], op0=ALU.mult, op1=ALU.add),
                         reads=[Rf.b, pR1.b], writes=[Rf.b])
                    if c == 1:
                        rb = Rb2[blk % 2]
                        P.op("act", lambda e: e.copy(rb.t[:].rearrange("p a b -> p (a b)"), Rf.t[:].rearrange("p a b -> p (a b)")),
                             reads=[Rf.b], writes=[rb.b])

                def G(blk):
                    if blk < NC or blk >= NB:
                        return
                    ob = blk - NC
                    g_ = gT[ob % 2]
                    gtd_ = gtd2[blk % 2]
                    for j in range(4):
                        P.op("pe", lambda e, j=j: e.transpose(pTg.t[:, j * 128:(j + 1) * 128],
                                                              gtd_.t[:, j * 128:(j + 1) * 128], ident.t[:]),
                             reads=[gtd_.b, ident.b], writes=[pTg.b])
                    P.op("act", lambda e: e.copy(g_.t[:].rearrange("p a b -> p (a b)"), pTg.t[:, 0:512]),
                         reads=[pTg.b], writes=[g_.b])
                    P.dma("pool", lambda e: e.dma_start(out=GT[ob][:, h * 512:(h + 1) * 512],
                                                        in_=g_.t[:].rearrange("p a b -> p (a b)")),
                          g_.b, reads=[g_.b], writes=[b_GT[ob]])

                A1(0); A2(0); A3(0)
                for blk in range(NB):
                    nx = blk + 1
                    B1(blk)
                    if nx < NB:
                        A1(nx)
                    B2(blk)
                    if nx < NB:
                        A2(nx)
                    B3a(blk)
                    G(blk - 1)
                    ST(blk, 0)
                    if nx < NB:
                        A3(nx)
                    ST(blk, 1)
                    CH(blk)
                G(NB - 1)
                P.emit()

        KTs = dscr("KTs", [8, 128, NB * 128], BF16)
        QTs = dscr("QTs", [8, 128, NO * 128], BF16)
        VVs = dscr("VVs", [8, 128, NB, 128], BF16)
        b_KTs = Buf("KTs"); b_QTs = Buf("QTs"); b_VVs = Buf("VVs")
        KTs_w = KTs.rearrange("h p (b t) -> p h b t", t=128)
        QTs_w = QTs.rearrange("h p (b t) -> p h b t", t=128)
        VVs_w = VVs.rearrange("h p b e -> p h b e")
        with ExitStack() as st:
            WDa = sbt(st, "WDa", [128, 8, 3072], BF16)
            for k0 in range(0, 8, 2):
                P.dma("pool", lambda e, k0=k0: e.dma_start(out=WDa.t[:, k0:k0 + 2, :], in_=w_in_v[:, k0:k0 + 2, C_DQ:C_DQ + 3072]),
                      WDa.b, writes=[WDa.b])
            ropd = sbt(st, "ropd", [128, NB, 32], F32)
            ropd_v = rope_d.rearrange("(b p) c -> p b c", p=128)
            for b0 in range(0, NB, 16):
                b1 = min(NB, b0 + 16)
                P.dma("sp", lambda e, b0=b0, b1=b1: e.dma_start(out=ropd.t[:, b0:b1, :], in_=ropd_v[:, b0:b1, :]),
                      ropd.b, writes=[ropd.b])
            wq8 = sbt(st, "wq8", [128, 8, 64], F32)
            wk8 = sbt(st, "wk8", [128, 8, 64], F32)
            for g8 in range(8):
                P.op("pool", lambda e, g8=g8: e.tensor_copy(wq8.t[:, g8, :], wqk.t[:, 0, :]), reads=[wqk.b], writes=[wq8.b])
                P.op("pool", lambda e, g8=g8: e.tensor_copy(wk8.t[:, g8, :], wqk.t[:, 2, :]), reads=[wqk.b], writes=[wk8.b])
            uTb = [sbt(st, "duT%d" % i, [128, 8, 128], BF16) for i in range(3)]
            NCH = 4
            sqd = [sbt(st, "sqd%d" % i, [128, 8, 64], F32) for i in range(NCH)]
            ssd = [sbt(st, "ssd%d" % i, [128, 8], F32) for i in range(NCH)]
            rsd = [sbt(st, "rsd%d" % i, [128, 8], F32) for i in range(NCH)]
            xn = [[sbt(st, "xn%d_%d" % (pp, i), [128, 8, 64], F32) for i in range(NCH)] for pp in range(2)]
            xbq = [[sbt(st, "xbq%d_%d" % (pp, i), [128, 8, 64], BF16) for i in range(NCH)] for pp in range(2)]
            rc = [[sbt(st, "rc%d_%d" % (pp, i), [128, 8, 16], F32) for i in range(NCH)] for pp in range(2)]
            rsn = [[sbt(st, "rsn%d_%d" % (pp, i), [128, 8, 16], F32) for i in range(NCH)] for pp in range(2)]
            kst = [sbt(st, "kst%d" % i, [128, 8, 128], BF16) for i in range(2)]
            qst = [sbt(st, "qst%d" % i, [128, 8, 128], BF16) for i in range(2)]
            vst = [sbt(st, "vst%d" % i, [128, 8, 128], BF16) for i in range(2)]
            pq = [pbank(st, "pq%d" % i) for i in range(2)]
            pk = [pbank(st, "pk%d" % i) for i in range(2)]
            pvv = [pbank(st, "pvv%d" % i) for i in range(2)]
            pTk = pbank(st, "pTk", BF16)
            pTq = pbank(st, "pTq1", BF16)
            def mk_chains(blk):
                chains = []
                for half in range(2):
                    chains.append((pk[half], wk8, pTk, half, 1024 + half * 512))
                if blk >= NC:
                    for half in range(2):
                        chains.append((pq[half], wq8, pTq, half, half * 512))
                return chains

            def d1_early(blk):
                u_ = uTb[blk % 3]
                par = blk % 2
                P.dma("sp", lambda e: e.dma_start(out=u_.t[:].rearrange("p a b -> p (a b)"), in_=UT[blk]),
                      u_.b, reads=[b_UT[blk]], writes=[u_.b])
                chains = mk_chains(blk)
                for (pb, wt, ptT, half, c0) in chains:
                    for kc in range(8):
                        P.op("pe", lambda e, kc=kc, pb=pb, c0=c0: e.matmul(pb.t[:, 0:512], lhsT=u_.t[:, kc, :], rhs=WDa.t[:, kc, c0:c0 + 512],
                                                                           start=(kc == 0), stop=(kc == 7)),
                             reads=[u_.b, WDa.b], writes=[pb.b])
                for half in range(2):
                    for kc in range(8):
                        P.op("pe", lambda e, kc=kc, half=half: e.matmul(pvv[half].t[:, 0:512], lhsT=u_.t[:, kc, :],
                                                                       rhs=WDa.t[:, kc, 2048 + half * 512:2048 + (half + 1) * 512],
                                                                       start=(kc == 0), stop=(kc == 7)),
                             reads=[u_.b, WDa.b], writes=[pvv[half].b])
                nch = len(chains)
                pvw = [ch[0].t[:, 0:512].rearrange("p (a b) -> p a b", b=64) for ch in chains]
                for ci in range(nch):
                    P.op("act", lambda e, ci=ci: e.activation(out=sqd[ci].t[:], in_=pvw[ci], func=AF.Square),
                         reads=[chains[ci][0].b], writes=[sqd[ci].b])
                for ci in range(nch):
                    P.op("dve", lambda e, ci=ci: e.tensor_reduce(out=ssd[ci].t[:], in_=sqd[ci].t[:], axis=AX.X, op=ALU.add),
                         reads=[sqd[ci].b], writes=[ssd[ci].b])
                for ci in range(nch):
                    P.op("act", lambda e, ci=ci: e.activation(out=rsd[ci].t[:], in_=ssd[ci].t[:], func=AF.Sqrt, scale=1.0 / 64, bias=eps_t.t[:, 0:1]),
                         reads=[ssd[ci].b, eps_t.b], writes=[rsd[ci].b])
                v_ = vst[par]
                for half in range(2):
                    P.op("act", lambda e, half=half: e.copy(v_.t[:, half * 4:half * 4 + 4, :].rearrange("p a b -> p (a b)"), pvv[half].t[:, 0:512]),
                         reads=[pvv[half].b], writes=[v_.b])
                P.dma("sp", lambda e: e.dma_start(out=VVs_w[:, :, blk, :], in_=v_.t[:]), v_.b, reads=[v_.b], writes=[b_VVs])
                for ci in range(nch):
                    P.op("dve", lambda e, ci=ci: e.reciprocal(rsd[ci].t[:], rsd[ci].t[:]), reads=[rsd[ci].b], writes=[rsd[ci].b])
                for ci in range(nch):
                    P.op("dve", lambda e, ci=ci: e.tensor_tensor(out=xn[par][ci].t[:], in0=pvw[ci], in1=bc_last(rsd[ci].t[:], 64), op=ALU.mult),
                         reads=[chains[ci][0].b, rsd[ci].b], writes=[xn[par][ci].b])

            def d1_late(blk):
                par = blk % 2
                own = blk >= NC
                ob = blk - NC
                chains = mk_chains(blk)
                nch = len(chains)
                xn_, xb_, rc_, rsn_ = xn[par], xbq[par], rc[par], rsn[par]
                for ci in range(nch):
                    eng = "pool"
                    P.op(eng, lambda e, ci=ci: e.tensor_tensor(out=xn_[ci].t[:], in0=xn_[ci].t[:], in1=chains[ci][1].t[:], op=ALU.mult),
                         reads=[xn_[ci].b, chains[ci][1].b], writes=[xn_[ci].b])
                for ci in range(nch):
                    eng = "pool"
                    P.op(eng, lambda e, ci=ci: e.tensor_copy(xb_[ci].t[:], xn_[ci].t[:]), reads=[xn_[ci].b], writes=[xb_[ci].b])
                    P.op(eng, lambda e, ci=ci: e.tensor_tensor(out=rc_[ci].t[:], in0=xn_[ci].t[:, :, 0:16],
                                                              in1=bc_mid(ropd.t[:, blk, 0:16], 8), op=ALU.mult),
                         reads=[xn_[ci].b, ropd.b], writes=[rc_[ci].b])
                    P.op(eng, lambda e, ci=ci: e.tensor_tensor(out=rsn_[ci].t[:], in0=xn_[ci].t[:, :, 0:16],
                                                              in1=bc_mid(ropd.t[:, blk, 16:32], 8), op=ALU.mult),
                         reads=[xn_[ci].b, ropd.b], writes=[rsn_[ci].b])
                    P.op(eng, lambda e, ci=ci: e.tensor_tensor(out=xb_[ci].t[:, :, 0:8], in0=rc_[ci].t[:, :, 0:8], in1=rsn_[ci].t[:, :, 8:16], op=ALU.subtract),
                         reads=[rc_[ci].b, rsn_[ci].b], writes=[xb_[ci].b])
                    P.op(eng, lambda e, ci=ci: e.tensor_tensor(out=xb_[ci].t[:, :, 8:16], in0=rc_[ci].t[:, :, 8:16], in1=rsn_[ci].t[:, :, 0:8], op=ALU.add),
                         reads=[rc_[ci].b, rsn_[ci].b], writes=[xb_[ci].b])
                for ci in range(nch):
                    ptT, half = chains[ci][2], chains[ci][3]
                    for hh in range(4):
                        P.op("pe", lambda e, ci=ci, hh=hh, ptT=ptT, half=half: e.transpose(ptT.t[:, (half * 4 + hh) * 128:(half * 4 + hh + 1) * 128],
                                                                                          xb_[ci].t[:, 2 * hh:2 * hh + 2, :].rearrange("p a b -> p (a b)"),
                                                                                          ident.t[:]),
                             reads=[xb_[ci].b, ident.b], writes=[ptT.b])
                k_ = kst[par]
                P.op("dve", lambda e: e.tensor_copy(k_.t[:].rearrange("p a b -> p (a b)"), pTk.t[:, 0:1024]), reads=[pTk.b], writes=[k_.b])
                P.dma("sp", lambda e: e.dma_start(out=KTs_w[:, :, blk, :], in_=k_.t[:]), k_.b, reads=[k_.b], writes=[b_KTs])
                if own:
                    q_ = qst[par]
                    P.op("act", lambda e: e.copy(q_.t[:].rearrange("p a b -> p (a b)"), pTq.t[:, 0:1024]), reads=[pTq.b], writes=[q_.b])
                    P.dma("sp", lambda e: e.dma_start(out=QTs_w[:, :, ob, :], in_=q_.t[:]), q_.b, reads=[q_.b], writes=[b_QTs])

            d1_early(0)
            for blk in range(NB):
                if blk + 1 < NB:
                    d1_early(blk + 1)
                d1_late(blk)
            P.emit()

        with ExitStack() as st:
            KTb = [sbt(st, "KT%d" % i, [128, NB * 128], BF16) for i in range(2)]
            VVb = [sbt(st, "VV%d" % i, [128, NB, 130], BF16) for i in range(2)]
            QT2b = [sbt(st, "QT2%d" % i, [128, NO, 256], BF16) for i in range(2)]
            NPT = 6
            PT = [sbt(st, "PT%d" % i, [128, 4, 128], BF16) for i in range(NPT)]
            zz = sbt(st, "zz", [128, 2], F32)
            a1 = sbt(st, "a1", [128, 128], F32)
            aa = sbt(st, "aa", [128, 128], F32)
            asq = sbt(st, "asq", [128, 128], F32)
            ass = sbt(st, "ass", [128, 1], F32)
            ars = sbt(st, "ars", [128, 1], F32)
            dob = sbt(st, "dob", [128, 128], BF16)
            doT = [sbt(st, "doT%d" % i, [128, 128], BF16) for i in range(2)]
            pP = pbank(st, "pP")
            pTd = pbank(st, "pTd", BF16)
            pSd = [pbank(st, "pSd%d" % i) for i in range(2)]
            pO0 = [pbank(st, "pO0%d" % i) for i in range(2)]
            pO1 = [pbank(st, "pO1%d" % i) for i in range(2)]
            assert NC % 2 == 0
            for i2 in range(2):
                P.op("pool", lambda e, i2=i2: e.memset(VVb[i2].t[:], 0.0), writes=[VVb[i2].b])
                P.op("dve", lambda e, i2=i2: e.tensor_copy(VVb[i2].t[:, :, 128:129], kbias.t[:].rearrange("p (a b) -> p a b", b=1)),
                     reads=[kbias.b], writes=[VVb[i2].b])
                P.op("pool", lambda e, i2=i2: e.memset(QT2b[i2].t[:], 0.0), writes=[QT2b[i2].b])

            def load_head(h):
                kt, vv, q2 = KTb[h % 2], VVb[h % 2], QT2b[h % 2]
                P.dma("sp", lambda e: e.dma_start(out=kt.t[:], in_=KTs[h]), kt.b, reads=[b_KTs], writes=[kt.b])
                P.dma("sp", lambda e: e.dma_start(out=vv.t[:, :, 0:128], in_=VVs[h]), vv.b, reads=[b_VVs], writes=[vv.b])
                P.dma("sp", lambda e: e.dma_start(out=q2.t[0:64, :, 0:128], in_=QTs[h][0:64, :].rearrange("p (i t) -> p i t", t=128)),
                      q2.b, reads=[b_QTs], writes=[q2.b])
                P.dma("sp", lambda e: e.dma_start(out=q2.t[64:128, :, 128:256], in_=QTs[h][64:128, :].rearrange("p (i t) -> p i t", t=128)),
                      q2.b, reads=[b_QTs], writes=[q2.b])
            load_head(0)
            for h in range(8):
                if h + 1 < 8:
                    load_head(h + 1)
                KT, VV, QT2 = KTb[h % 2], VVb[h % 2], QT2b[h % 2]
                b_K = [KT.b] * NB
                b_Kv = VV.b
                b_Q = [QT2.b] * NO
                items = []
                for i in range(NO):
                    nk = NC + i + 1
                    for kb0 in range(0, nk, 2):
                        items.append((i, kb0, min(2, nk - kb0)))
                SKEW = 2
                pS3 = [pSd[0], pSd[1], pP]

                def qk_exp(n):
                    i, kb0, nb = items[n]
                    nk = NC + i + 1
                    ps = pS3[n % 3]
                    pt = PT[n % NPT]
                    for j in range(nb):
                        kb = kb0 + j
                        P.op("pe", lambda e, j=j, kb=kb: e.matmul(ps.t[:, j * 256:(j + 1) * 256], lhsT=KT.t[:, kb * 128:(kb + 1) * 128],
                                                                  rhs=QT2.t[:, i, :], start=True, stop=True),
                             reads=[b_K[kb], b_Q[i]], writes=[ps.b])
                    P.op("act", lambda e: e.activation(out=pt.t[:, 0:2 * nb, :].rearrange("p a b -> p (a b)"),
                                                       in_=ps.t[:, 0:256 * nb], func=AF.Exp, scale=0.125),
                         reads=[ps.b], writes=[pt.b])
                    if kb0 + nb == nk:
                        jl = nb - 1
                        P.op("pool", lambda e: e.tensor_tensor(out=pt.t[:, 2 * jl:2 * jl + 2, :], in0=pt.t[:, 2 * jl:2 * jl + 2, :],
                                                               in1=cmask2.t[:], op=ALU.mult),
                             reads=[pt.b, cmask2.b], writes=[pt.b])

                def pv(n):
                    i, kb0, nb = items[n]
                    nk = NC + i + 1
                    o0 = pO0[i % 2]
                    o1 = pO1[i % 2]
                    pt = PT[n % NPT]
                    for j in range(nb):
                        kb = kb0 + j
                        P.op("pe", lambda e, j=j, kb=kb: e.matmul(o0.t[:, 0:129], lhsT=pt.t[:, 2 * j, :], rhs=VV.t[:, kb, 0:129],
                                                                  start=(kb == 0), stop=(kb == nk - 1)),
                             reads=[pt.b, b_Kv], writes=[o0.b])
                        P.op("pe", lambda e, j=j, kb=kb: e.matmul(o1.t[:, 0:129], lhsT=pt.t[:, 2 * j + 1, :], rhs=VV.t[:, kb, 0:129],
                                                                  start=(kb == 0), stop=(kb == nk - 1)),
                             reads=[pt.b, b_Kv], writes=[o1.b])
                    if kb0 + nb == nk:
                        finalize(i, o0, o1)

                def finalize(i, o0, o1):
                    while pend_fin:
                        pend_fin.pop(0)[1]()
                    P.op("dve", lambda e, o0=o0: e.reciprocal(zz.t[:, 0:1], o0.t[:, 128:129]), reads=[o0.b], writes=[zz.b])
                    P.op("dve", lambda e, o1=o1: e.reciprocal(zz.t[:, 1:2], o1.t[:, 128:129]), reads=[o1.b], writes=[zz.b])
                    P.op("dve", lambda e: e.tensor_tensor(out=zz.t[:, 1:2], in0=zz.t[:, 1:2], in1=lam.t[:, 0:1], op=ALU.mult),
                         reads=[zz.b, lam.b], writes=[zz.b])
                    P.op("dve", lambda e, o1=o1: e.tensor_scalar(a1.t[:], o1.t[:, 0:128], zz.t[:, 1:2], None, op0=ALU.mult),
                         reads=[o1.b, zz.b], writes=[a1.b])
                    P.op("dve", lambda e, o0=o0: e.scalar_tensor_tensor(out=aa.t[:], in0=o0.t[:, 0:128], scalar=zz.t[:, 0:1], in1=a1.t[:],
                                                                        op0=ALU.mult, op1=ALU.subtract),
                         reads=[o0.b, zz.b, a1.b], writes=[aa.b])
                    pend_fin.append((n_now[0] + 3, lambda: finalize_b(i)))

                def finalize_b(i):
                    P.op("dve", lambda e: e.tensor_tensor(out=asq.t[:], in0=aa.t[:], in1=aa.t[:], op=ALU.mult), reads=[aa.b], writes=[asq.b])
                    P.op("dve", lambda e: e.tensor_reduce(out=ass.t[:, 0:1], in_=asq.t[:], axis=AX.X, op=ALU.add), reads=[asq.b], writes=[ass.b])
                    P.op("dve", lambda e: e.tensor_scalar(ars.t[:], ass.t[:], 1.0 / 128, EPS, op0=ALU.mult, op1=ALU.add),
                         reads=[ass.b], writes=[ars.b])
                    P.op("pool", lambda e: e.tensor_tensor(out=ars.t[:], in0=ars.t[:], in1=mhalf.t[:, 0:1], op=ALU.pow),
                         reads=[ars.b, mhalf.b], writes=[ars.b])
                    P.op("dve", lambda e: e.scalar_tensor_tensor(out=dob.t[:], in0=aa.t[:], scalar=ars.t[:, 0:1], in1=sublnw.t[:],
                                                                 op0=ALU.mult, op1=ALU.mult),
                         reads=[aa.b, ars.b, sublnw.b], writes=[dob.b])
                    P.op("pe", lambda e: e.transpose(pTd.t[:, 256:384], dob.t[:], ident.t[:]), reads=[dob.b, ident.b], writes=[pTd.b])
                    d_ = doT[i % 2]
                    P.op("dve", lambda e, d_=d_: e.tensor_copy(d_.t[:], pTd.t[:, 256:384]), reads=[pTd.b], writes=[d_.b])
                    P.dma("pool", lambda e, d_=d_, i=i: e.dma_start(out=DOT[i][:, h * 128:(h + 1) * 128], in_=d_.t[:]),
                          d_.b, reads=[d_.b], writes=[b_DOT[i]])
                pend_fin = []
                n_now = [0]
                for n in range(len(items) + SKEW + 4):
                    n_now[0] = n
                    if n < len(items):
                        qk_exp(n)
                    if 0 <= n - SKEW < len(items):
                        pv(n - SKEW)
                    while pend_fin and pend_fin[0][0] <= n:
                        pend_fin.pop(0)[1]()
                assert not pend_fin
                P.emit()

        with ExitStack() as st:
            Wro = sbt(st, "Wro", [128, 16, 1024], BF16)
            Wdo = sbt(st, "Wdo", [128, 8, 1024], BF16)
            Wou = sbt(st, "Wou", [128, 8, 1024], BF16)
            Wg = sbt(st, "Wg", [128, 8, 2048], BF16)
            n2w = sbt(st, "n2w", [128, D], F32)
            P.dma("sp", lambda e: e.dma_start(out=n2w.t[:], in_=bc_part(norm2_w, D)), n2w.b, writes=[n2w.b])
            wro_v = w_ret_o.rearrange("(k p) n -> p k n", p=128)
            for k0 in range(0, 16, 4):
                P.dma("pool", lambda e, k0=k0: e.dma_start(out=Wro.t[:, k0:k0 + 4, :], in_=wro_v[:, k0:k0 + 4, :]), Wro.b, writes=[Wro.b])
            wdo_v = w_diff_o.rearrange("(k p) n -> p k n", p=128)
            wou_v = w_out.rearrange("(k p) n -> p k n", p=128)
            for k0 in range(0, 8, 4):
                P.dma("pool", lambda e, k0=k0: e.dma_start(out=Wdo.t[:, k0:k0 + 4, :], in_=wdo_v[:, k0:k0 + 4, :]), Wdo.b, writes=[Wdo.b])
                P.dma("pool", lambda e, k0=k0: e.dma_start(out=Wou.t[:, k0:k0 + 4, :], in_=wou_v[:, k0:k0 + 4, :]), Wou.b, writes=[Wou.b])
            for k0 in range(0, 8, 2):
                P.dma("pool", lambda e, k0=k0: e.dma_start(out=Wg.t[:, k0:k0 + 2, :], in_=w_in_v[:, k0:k0 + 2, C_GT:C_GT + 2048]), Wg.b, writes=[Wg.b])
            gTb = [sbt(st, "mgT%d" % i, [128, 16, 128], BF16) for i in range(2)]
            dTb = [sbt(st, "mdT%d" % i, [128, 8, 128], BF16) for i in range(2)]
            uTb = [sbt(st, "muT%d" % i, [128, 8, 128], BF16) for i in range(2)]
            xb = [sbt(st, "mx%d" % i, [128, D], F32) for i in range(2)]
            sig = sbt(st, "sig", [128, 2048], F32)
            m1 = sbt(st, "m1", [128, D], F32)
            m2 = sbt(st, "m2", [128, D], F32)
            mb = [sbt(st, "mb%d" % i, [128, D], BF16) for i in range(2)]
            mT = sbt(st, "mT", [128, 8, 128], BF16)
            h2 = [sbt(st, "h2%d" % i, [128, D], F32) for i in range(2)]
            sq = sbt(st, "msq", [128, D], F32)
            ss = sbt(st, "mss", [128, 1], F32)
            rs = sbt(st, "mrs", [128, 1], F32)
            ub = sbt(st, "mub", [128, D], BF16)
            u2T = [sbt(st, "mu2T%d" % i, [128, 8, 128], BF16) for i in range(2)]
            pA = [pbank(st, "pA%d" % i) for i in range(4)]
            pB = [pbank(st, "pB%d" % i) for i in range(2)]
            pTm = [pbank(st, "pTm%d" % i, BF16) for i in range(2)]
            def MA1(ob):
                blk = NC + ob
                g_ = gTb[ob % 2]; d_ = dTb[ob % 2]; u_ = uTb[ob % 2]; x_ = xb[ob % 2]
                P.dma("sp", lambda e: e.dma_start(out=g_.t[:].rearrange("p a b -> p (a b)"), in_=GT[ob]),
                      g_.b, reads=[b_GT[ob]], writes=[g_.b])
                P.dma("sp", lambda e: e.dma_start(out=d_.t[:].rearrange("p a b -> p (a b)"), in_=DOT[ob]),
                      d_.b, reads=[b_DOT[ob]], writes=[d_.b])
                P.dma("sp", lambda e: e.dma_start(out=u_.t[:].rearrange("p a b -> p (a b)"), in_=UT[blk]),
                      u_.b, reads=[b_UT[blk]], writes=[u_.b])
                P.dma("sp", lambda e: e.dma_start(out=x_.t[:], in_=xo[ob * 128:(ob + 1) * 128, :]), x_.b, writes=[x_.b])
                for j in range(4):
                    for kc in range(8):
                        P.op("pe", lambda e, j=j, kc=kc: e.matmul(pA[j].t[:, 0:512], lhsT=u_.t[:, kc, :], rhs=Wg.t[:, kc, j * 512:(j + 1) * 512],
                                                                  start=(kc == 0), stop=(kc == 7)),
                             reads=[u_.b, Wg.b], writes=[pA[j].b])
                    P.op("act", lambda e, j=j: e.activation(out=sig.t[:, j * 512:(j + 1) * 512], in_=pA[j].t[:, 0:512], func=AF.Sigmoid),
                         reads=[pA[j].b], writes=[sig.b])

            def MA2(ob):
                g_ = gTb[ob % 2]
                for j in range(2):
                    for kc in range(16):
                        P.op("pe", lambda e, j=j, kc=kc: e.matmul(pA[j].t[:, 0:512], lhsT=g_.t[:, kc, :], rhs=Wro.t[:, kc, j * 512:(j + 1) * 512],
                                                                  start=(kc == 0), stop=(kc == 15)),
                             reads=[g_.b, Wro.b], writes=[pA[j].b])
                    P.op("dve", lambda e, j=j: e.tensor_tensor(out=m1.t[:, j * 512:(j + 1) * 512], in0=pA[j].t[:, 0:512],
                                                               in1=sig.t[:, j * 512:(j + 1) * 512], op=ALU.mult),
                         reads=[pA[j].b, sig.b], writes=[m1.b])

            def MA3(ob):
                d_ = dTb[ob % 2]
                mb_ = mb[ob % 2]
                for j in range(2):
                    for kc in range(8):
                        P.op("pe", lambda e, j=j, kc=kc: e.matmul(pA[2 + j].t[:, 0:512], lhsT=d_.t[:, kc, :], rhs=Wdo.t[:, kc, j * 512:(j + 1) * 512],
                                                                  start=(kc == 0), stop=(kc == 7)),
                             reads=[d_.b, Wdo.b], writes=[pA[2 + j].b])
                    P.op("dve", lambda e, j=j: e.tensor_tensor(out=m2.t[:, j * 512:(j + 1) * 512], in0=pA[2 + j].t[:, 0:512],
                                                               in1=sig.t[:, 1024 + j * 512:1024 + (j + 1) * 512], op=ALU.mult),
                         reads=[pA[2 + j].b, sig.b], writes=[m2.b])
                P.op("pool", lambda e: e.tensor_tensor(out=mb_.t[:], in0=m1.t[:], in1=m2.t[:], op=ALU.add), reads=[m1.b, m2.b], writes=[mb_.b])

            def MB1(ob):
                mb_ = mb[ob % 2]
                for half in range(2):
                    pt = pTm[half]
                    for j in range(4):
                        kc = half * 4 + j
                        P.op("pe", lambda e, kc=kc, j=j, pt=pt: e.transpose(pt.t[:, j * 128:(j + 1) * 128], mb_.t[:, kc * 128:(kc + 1) * 128], ident.t[:]),
                             reads=[mb_.b, ident.b], writes=[pt.b])
                    if half == 0:
                        P.op("dve", lambda e, pt=pt: e.tensor_copy(mT.t[:, 0:4, :].rearrange("p a b -> p (a b)"), pt.t[:, 0:512]),
                             reads=[pt.b], writes=[mT.b])
                    else:
                        P.op("act", lambda e, pt=pt: e.copy(mT.t[:, 4:8, :].rearrange("p a b -> p (a b)"), pt.t[:, 0:512]),
                             reads=[pt.b], writes=[mT.b])

            def MB2(ob):
                h_ = h2[ob % 2]
                x_ = xb[ob % 2]
                for j in range(2):
                    for kc in range(8):
                        P.op("pe", lambda e, j=j, kc=kc: e.matmul(pB[j].t[:, 0:512], lhsT=mT.t[:, kc, :], rhs=Wou.t[:, kc, j * 512:(j + 1) * 512],
                                                                  start=(kc == 0), stop=(kc == 7)),
                             reads=[mT.b, Wou.b], writes=[pB[j].b])
                    P.op("dve", lambda e, j=j: e.tensor_tensor(out=h_.t[:, j * 512:(j + 1) * 512], in0=pB[j].t[:, 0:512],
                                                               in1=x_.t[:, j * 512:(j + 1) * 512], op=ALU.add),
                         reads=[pB[j].b, x_.b], writes=[h_.b])
                P.dma("pool", lambda e: e.dma_start(out=H2[ob * 128:(ob + 1) * 128, :], in_=h_.t[:]),
                      h_.b, reads=[h_.b], writes=[b_H2[ob]])
                norm_part(h_, n2w, sq, ss, rs, ub, use_pow=True)

            def MB3(ob):
                t_ = u2T[ob % 2]
                tr_part(ub, pTm, t_)
                P.dma("pool", lambda e: e.dma_start(out=U2T[ob], in_=t_.t[:].rearrange("p a b -> p (a b)")),
                      t_.b, reads=[t_.b], writes=[b_U2T[ob]])

            MA1(0); MA2(0); MA3(0)
            for ob in range(NO):
                nx = ob + 1
                MB1(ob)
                if nx < NO:
                    MA1(nx)
                MB2(ob)
                if nx < NO:
                    MA2(nx)
                MB3(ob)
                if nx < NO:
                    MA3(nx)
            P.emit()

        GB = 3
        NG = (NO + GB - 1) // GB
        with ExitStack() as st:
            Wup = sbt(st, "Wup", [128, 8, 2 * FFN], BF16)
            Wdn = sbt(st, "Wdn", [128, 22, D], BF16)
            cw = sbt(st, "cw", [128, 3, 44], F32)
            cb = sbt(st, "cb", [128, 44], F32)
            wup_v = w_up.rearrange("(k p) n -> p k n", p=128)
            for kc in range(8):
                for c0 in range(0, 2 * FFN, 1408):
                    P.dma("pool", lambda e, kc=kc, c0=c0: e.dma_start(out=Wup.t[:, kc, c0:c0 + 1408], in_=wup_v[:, kc, c0:c0 + 1408]),
                          Wup.b, writes=[Wup.b])
            wdn_v = w_down.rearrange("(k p) n -> p k n", p=128)
            for k0 in range(0, 22, 2):
                P.dma("pool", lambda e, k0=k0: e.dma_start(out=Wdn.t[:, k0:k0 + 2, :], in_=wdn_v[:, k0:k0 + 2, :]), Wdn.b, writes=[Wdn.b])
            for t0 in range(0, 44, 11):
                for k in range(3):
                    P.dma("sp", lambda e, k=k, t0=t0: e.dma_start(out=cw.t[:, k, t0:t0 + 11],
                                                                  in_=conv_w[k].rearrange("(t p) -> p t", p=128)[:, t0:t0 + 11],
                                                                  allow_slow_non_contiguous=True), cw.b, writes=[cw.b])
                P.dma("sp", lambda e, t0=t0: e.dma_start(out=cb.t[:, t0:t0 + 11], in_=conv_b.rearrange("(t p) -> p t", p=128)[:, t0:t0 + 11],
                                                         allow_slow_non_contiguous=True), cb.b, writes=[cb.b])
            NT = GB * 128
            u2g = [sbt(st, "u2g%d" % i, [128, 8, 2 + NT], BF16) for i in range(2)]
            ya = [sbt(st, "ya%d" % i, [128, NT], F32) for i in range(2)]
            yb = [sbt(st, "yb%d" % i, [128, NT], F32) for i in range(2)]
            sa = [sbt(st, "sa%d" % i, [128, NT], F32) for i in range(2)]
            gTt = sbt(st, "gTt", [128, 22, NT], BF16)
            hb = [sbt(st, "fh%d" % i, [128, D], F32) for i in range(2)]
            ob_ = [sbt(st, "fo%d" % i, [128, D], F32) for i in range(2)]
            pU = [pbank(st, "pU%d" % i) for i in range(4)]
            pD = [pbank(st, "pD%d" % i) for i in range(4)]
            P.op("pool", lambda e: e.memset(u2g[0].t[:], 0.0), writes=[u2g[0].b])
            P.op("pool", lambda e: e.memset(u2g[1].t[:], 0.0), writes=[u2g[1].b])
            ui = 0
            for gi in range(NG):
                blks = list(range(gi * GB, min(NO, (gi + 1) * GB)))
                nt = len(blks) * 128
                ug = u2g[gi % 2]
                up_ = u2g[(gi + 1) % 2]
                for j, ob in enumerate(blks):
                    P.dma("sp", lambda e, ug=ug, j=j, ob=ob: e.dma_start(out=ug.t[:, :, 2 + j * 128:2 + (j + 1) * 128],
                                                                        in_=U2T[ob].rearrange("p (a b) -> p a b", a=8)),
                          ug.b, reads=[b_U2T[ob]], writes=[ug.b])
                if gi > 0:
                    P.op("pool", lambda e, ug=ug, up_=up_: e.tensor_copy(ug.t[:, :, 0:2], up_.t[:, :, NT:NT + 2]),
                         reads=[up_.b], writes=[ug.b])
                for ft in range(22):
                    tiles = []
                    for which, fi in ((0, ft), (1, ft + 22)):
                        pu = pU[ui % 4]
                        ui += 1
                        for kc in range(8):
                            P.op("pe", lambda e, pu=pu, kc=kc, fi=fi, ug=ug, nt=nt: e.matmul(pu.t[:, 0:nt + 2], lhsT=Wup.t[:, kc, fi * 128:(fi + 1) * 128],
                                                                                        rhs=ug.t[:, kc, 0:nt + 2], start=(kc == 0), stop=(kc == 7)),
                                 reads=[Wup.b, ug.b], writes=[pu.b])
                        yt = (ya if which == 0 else yb)[ft % 2]
                        P.op("dve", lambda e, pu=pu, yt=yt, fi=fi, nt=nt: e.tensor_scalar(yt.t[:, 0:nt], pu.t[:, 2:nt + 2], cw.t[:, 2, fi:fi + 1], cb.t[:, fi:fi + 1],
                                                                                       op0=ALU.mult, op1=ALU.add),
                             reads=[pu.b, cw.b, cb.b], writes=[yt.b])
                        P.op("dve", lambda e, pu=pu, yt=yt, fi=fi, nt=nt: e.scalar_tensor_tensor(out=yt.t[:, 0:nt], in0=pu.t[:, 1:nt + 1], scalar=cw.t[:, 1, fi:fi + 1],
                                                                                              in1=yt.t[:, 0:nt], op0=ALU.mult, op1=ALU.add),
                             reads=[pu.b, cw.b, yt.b], writes=[yt.b])
                        P.op("dve", lambda e, pu=pu, yt=yt, fi=fi, nt=nt: e.scalar_tensor_tensor(out=yt.t[:, 0:nt], in0=pu.t[:, 0:nt], scalar=cw.t[:, 0, fi:fi + 1],
                                                                                              in1=yt.t[:, 0:nt], op0=ALU.mult, op1=ALU.add),
                             reads=[pu.b, cw.b, yt.b], writes=[yt.b])
                        tiles.append(yt)
                    s_ = sa[ft % 2]
                    P.op("act", lambda e, s_=s_, yt=tiles[0], nt=nt: e.activation(out=s_.t[:, 0:nt], in_=yt.t[:, 0:nt], func=AF.Silu),
                         reads=[tiles[0].b], writes=[s_.b])
                    P.op("pool", lambda e, s_=s_, yt=tiles[1], ft=ft, nt=nt: e.tensor_tensor(out=gTt.t[:, ft, 0:nt], in0=s_.t[:, 0:nt], in1=yt.t[:, 0:nt], op=ALU.mult),
                         reads=[s_.b, tiles[1].b], writes=[gTt.b])
                for j, ob in enumerate(blks):
                    h_ = hb[ob % 2]
                    o_ = ob_[ob % 2]
                    P.dma("sp", lambda e, h_=h_, ob=ob: e.dma_start(out=h_.t[:], in_=H2[ob * 128:(ob + 1) * 128, :]),
                          h_.b, reads=[b_H2[ob]], writes=[h_.b])
                    for half in range(2):
                        pd = pD[(ob * 2 + half) % 4]
                        for ft in range(22):
                            P.op("pe", lambda e, pd=pd, ft=ft, j=j, half=half: e.matmul(pd.t[:, 0:512], lhsT=gTt.t[:, ft, j * 128:(j + 1) * 128],
                                                                                    rhs=Wdn.t[:, ft, half * 512:(half + 1) * 512],
                                                                                    start=(ft == 0), stop=(ft == 21)),
                                 reads=[gTt.b, Wdn.b], writes=[pd.b])
                        P.op("dve", lambda e, pd=pd, half=half, h_=h_, o_=o_: e.tensor_tensor(out=o_.t[:, half * 512:(half + 1) * 512], in0=pd.t[:, 0:512],
                                                                                          in1=h_.t[:, half * 512:(half + 1) * 512], op=ALU.add),
                             reads=[pd.b, h_.b], writes=[o_.b])
                    P.dma("pool", lambda e, o_=o_, ob=ob: e.dma_start(out=y[ob * 128:(ob + 1) * 128, :], in_=o_.t[:]),
                          o_.b, reads=[o_.b], writes=[b_y])
            P.wait_all("pool", [b_y])
            P.emit()
        print("n_inst", P.n_inst, "n_wait", P.n_wait, "ndsem", P.ndsem)
    return nc


def make_tables(NC, NO, p, S):
    NB = NC + NO
    L = N_META + S
    if p == 0:
        ctx_pos = np.full(NC * 128, -1, np.int64)
        own_pos = np.arange(NO * 128)
    else:
        ctx_pos = np.arange(NC * 128) - PAD
        own_pos = L - NO * 128 + np.arange(NO * 128)
    pos = np.concatenate([ctx_pos, own_pos])
    valid = pos >= 0
    posf = np.where(valid, pos, 0).astype(np.float32)
    inv_r = np.power(np.float32(10000.0), -np.arange(128, dtype=np.float32) / np.float32(128))
    ang = posf[:, None] * inv_r[None, :]
    c, s = np.cos(ang), np.sin(ang)
    rope_r = np.concatenate([c, c, s, s], axis=1).astype(np.float32)
    inv_d = np.power(np.float32(500000.0), -np.arange(8, dtype=np.float32) / np.float32(8))
    ang = posf[:, None] * inv_d[None, :]
    c, s = np.cos(ang), np.sin(ang)
    rope_d = np.concatenate([c, c, s, s], axis=1).astype(np.float32)
    kb = np.where(valid, 1.0, 0.0).astype(np.float32).reshape(NB, 128).T.copy()
    idx = np.arange(128)
    cm = (idx[:, None] <= idx[None, :]).astype(np.float32)
    rdec = np.zeros((128, 8), np.float32)
    for h in range(4):
        rdec[:, h] = GAM[h] ** (idx + 1.0)
        rdec[:, 4 + h] = (256 ** -0.5) * GAM[h] ** (127.0 - idx)
    return rope_r, rope_d, kb, cm, rdec


_NC_CACHE = {}


def run(inputs, NC, NO, debug=False, trace=False):
    x = np.asarray(inputs["x"], np.float32)
    B, S, _ = x.shape
    assert S == 128 * (NC + NO - 1)
    L = N_META + S
    meta = np.asarray(inputs["meta_tokens"], np.float32)
    key = (NC, NO, debug)
    if key not in _NC_CACHE:
        _NC_CACHE[key] = build(NC, NO, debug)
    nc = _NC_CACHE[key]
    f = lambda k: np.ascontiguousarray(np.asarray(inputs[k], np.float32)[0])
    common = {
        "w_in": f("w_in"), "w_ret_o": f("w_ret_o"), "w_diff_o": f("w_diff_o"), "w_out": f("w_out"),
        "w_up": f("w_up"), "w_down": f("w_down"), "norm1_w": f("norm1_w"), "norm2_w": f("norm2_w"),
        "qk_norm_w": np.concatenate([f("q_norm_w"), f("q_norm_w"), f("k_norm_w"), f("k_norm_w")]),
        "lambdas": np.concatenate([f("lambda_q1"), f("lambda_k1"), f("lambda_q2"), f("lambda_k2")]),
        "subln_w": f("diff_subln_w"), "conv_w": f("conv_w"), "conv_b": f("conv_b"),
    }
    tabs = [make_tables(NC, NO, p, S) for p in range(2)]
    in_maps = []
    for b in range(B):
        seq = np.concatenate([meta, x[b]], axis=0)
        for p in range(2):
            if p == 0:
                xc_ = np.zeros((NC * 128, D), np.float32)
                xo_ = seq[0:NO * 128]
            else:
                xc_ = np.concatenate([np.zeros((PAD, D), np.float32), seq[0:NC * 128 - PAD]], axis=0)
                xo_ = seq[L - NO * 128:L]
            rr, rd, kb, cm, rdec = tabs[p]
            m = dict(common)
            m.update({"xc": np.ascontiguousarray(xc_), "xo": np.ascontiguousarray(xo_), "rope_r": rr, "rope_d": rd,
                      "kbias": kb, "cmask": cm, "rdec": rdec})
            in_maps.append(m)
    res = run_bass_kernel_spmd(nc, in_maps, core_ids=list(range(len(in_maps))), trace=trace)
    out = np.empty((B, S, D), np.float32)
    split = (NO * 128 - N_META) - 64
    for b in range(B):
        y0 = res.results[2 * b]["y"]
        y1 = res.results[2 * b + 1]["y"]
        out[b, :split] = y0[N_META:N_META + split]
        off1 = L - NO * 128
        out[b, split:] = y1[N_META + split - off1:]
    return out, res


def kernel(**inputs):
    out, _ = run(inputs, 32, 33)
    return out
```

```python
import math
import numpy as np
from contextlib import ExitStack
import concourse.bass as bass
import concourse.mybir as mybir
from concourse.bass_utils import run_bass_kernel_spmd

F32 = mybir.dt.float32
BF16 = mybir.dt.bfloat16
AF = mybir.ActivationFunctionType
ALU = mybir.AluOpType
AX = mybir.AxisListType

D = 1024
N_META = 16
PAD = 112
FFN = 2816
IN_COLS = 11264
EPS = 1e-6
C_RQ, C_RK, C_RV, C_RG, C_DQ, C_DK, C_DV, C_GT = 0, 1024, 2048, 4096, 6144, 7168, 8192, 9216
NEGB = -30000.0
LAM_INIT = 0.8 - 0.6 * math.exp(-0.3 * 0)
GAM = [1.0 - 2.0 ** (-5.0 - h) for h in range(4)]

SAME_ENGINE_SYNC = True


class Buf:
    __slots__ = ("name", "last_write", "reads", "dsem", "dcount")

    def __init__(self, name=""):
        self.name = name
        self.last_write = None
        self.reads = {}
        self.dsem = None
        self.dcount = 0


class T:
    def __init__(self, t, name):
        self.t = t
        self.b = Buf(name)


class Prog:
    ENGS = ("pe", "act", "dve", "pool", "sp")
    ENGOBJ = {"pe": "tensor", "act": "scalar", "dve": "vector", "pool": "gpsimd", "sp": "sync"}

    def __init__(self, nc, stack):
        self.nc = nc
        self.stack = stack
        self.q = {e: [] for e in self.ENGS}
        self.ecount = {e: 0 for e in self.ENGS}
        self.sems = {}
        for e in self.ENGS:
            self.sems[("e", e)] = stack.enter_context(nc.semaphore("s_" + e))
        self.waited = {e: {} for e in self.ENGS}
        self.ndsem = 0
        self.n_inst = 0
        self.n_wait = 0

    def _dsem(self, buf):
        if buf.dsem is None:
            buf.dsem = ("d", self.ndsem)
            self.sems[buf.dsem] = self.stack.enter_context(self.nc.semaphore("d%d" % self.ndsem))
            self.ndsem += 1
        return buf.dsem

    def _deps(self, eng, reads, writes):
        deps = {}

        def add(t):
            if t is None:
                return
            k, v = t
            if deps.get(k, -1) < v:
                deps[k] = v
        for b in reads:
            add(b.last_write)
        for b in writes:
            add(b.last_write)
            for k, v in b.reads.items():
                add((k, v))
        out = []
        w = self.waited[eng]
        for k, v in deps.items():
            if k == ("e", eng) and (eng == "pe" or not SAME_ENGINE_SYNC):
                continue
            if w.get(k, -1) >= v:
                continue
            w[k] = v
            out.append((k, v))
        return out

    def _commit(self, tok, reads, writes):
        k, v = tok
        for b in writes:
            b.last_write = tok
            b.reads = {}
        for b in reads:
            if b.reads.get(k, -1) < v:
                b.reads[k] = v

    def op(self, eng, fn, reads=(), writes=()):
        waits = self._deps(eng, reads, writes)
        self.ecount[eng] += 1
        tok = (("e", eng), self.ecount[eng])
        self.q[eng].append((waits, fn, tok[0], 1))
        self._commit(tok, reads, writes)
        self.n_inst += 1
        self.n_wait += len(waits)
        return tok

    def dma(self, eng, fn, sb, reads=(), writes=()):
        waits = self._deps(eng, reads, writes)
        k = self._dsem(sb)
        sb.dcount += 16
        tok = (k, sb.dcount)
        self.q[eng].append((waits, fn, k, 16))
        self._commit(tok, reads, writes)
        self.n_inst += 1
        self.n_wait += len(waits)
        return tok

    def wait_all(self, eng, bufs):
        waits = self._deps(eng, bufs, bufs)
        self.q[eng].append((waits, None, None, 0))

    def emit(self):
        nc = self.nc
        sems = self.sems
        with nc.Block() as block:
            for e in self.ENGS:
                lst = self.q[e]

                def body(eo, lst=lst):
                    for waits, fn, sk, inc in lst:
                        for k, v in waits:
                            eo.wait_ge(sems[k], v)
                        if fn is not None:
                            fn(eo).then_inc(sems[sk], inc)
                getattr(block, self.ENGOBJ[e])(body)
        self.q = {e: [] for e in self.ENGS}


def bc_mid(ap, n):
    return bass.AP(ap.tensor, ap.offset, [list(ap.ap[0]), [0, n], list(ap.ap[1])])


def bc_last(ap, k):
    return bass.AP(ap.tensor, ap.offset, [list(ap.ap[0]), list(ap.ap[1]), [0, k]])


def bc_part(dram_ap_1d, n):
    return bass.AP(dram_ap_1d.tensor, dram_ap_1d.offset, [[0, 128], [1, n]])


def build(NC, NO, debug=False):
    NB = NC + NO
    nc = bass.Bass("TRN2", target_bir_lowering=False)

    def din(name, shape, dt=F32):
        return nc.dram_tensor(name, list(shape), dt, kind="ExternalInput").ap()

    okind = "ExternalOutput" if debug else "Internal"

    def dscr(name, shape, dt):
        return nc.dram_tensor(name, list(shape), dt, kind=okind).ap()

    xc = din("xc", [NC * 128, D])
    xo = din("xo", [NO * 128, D])
    w_in = din("w_in", [D, IN_COLS])
    w_ret_o = din("w_ret_o", [2048, D])
    w_diff_o = din("w_diff_o", [D, D])
    w_out = din("w_out", [D, D])
    w_up = din("w_up", [D, 2 * FFN])
    w_down = din("w_down", [FFN, D])
    norm1_w = din("norm1_w", [D])
    norm2_w = din("norm2_w", [D])
    qk_norm_w = din("qk_norm_w", [256])
    lambdas = din("lambdas", [256])
    subln_w = din("subln_w", [128])
    conv_w = din("conv_w", [3, 2 * FFN])
    conv_b = din("conv_b", [2 * FFN])
    rope_r = din("rope_r", [NB * 128, 512])
    rope_d = din("rope_d", [NB * 128, 32])
    kbias_d = din("kbias", [128, NB])
    cmask_d = din("cmask", [128, 128])
    rdec_d = din("rdec", [128, 8])
    y = nc.dram_tensor("y", [NO * 128, D], F32, kind="ExternalOutput").ap()

    UT = dscr("UT", [NB, 128, 1024], BF16)
    GT = dscr("GT", [NO, 128, 2048], BF16)
    DOT = dscr("DOT", [NO, 128, 1024], BF16)
    H2 = dscr("H2", [NO * 128, D], F32)
    U2T = dscr("U2T", [NO, 128, 1024], BF16)
    b_UT = [Buf("UT%d" % i) for i in range(NB)]
    b_GT = [Buf("GT%d" % i) for i in range(NO)]
    b_DOT = [Buf("DOT%d" % i) for i in range(NO)]
    b_H2 = [Buf("H2%d" % i) for i in range(NO)]
    b_U2T = [Buf("U2T%d" % i) for i in range(NO)]
    b_y = Buf("y")

    w_in_v = w_in.rearrange("(k p) n -> p k n", p=128)

    with ExitStack() as gst:
        P = Prog(nc, gst)

        def sbt(st, name, shape, dt):
            return T(st.enter_context(nc.sbuf_tensor("sb_" + name, list(shape), dt)), name)

        def pbank(st, name, dt=F32):
            n = 512 if dt == F32 else 1024
            return T(st.enter_context(nc.psum_tensor("ps_" + name, [128, n], dt)), name)

        ident = sbt(gst, "ident", [128, 128], BF16)
        identf = sbt(gst, "identf", [128, 128], F32)
        cmask = sbt(gst, "cmask", [128, 128], F32)
        cmask2 = sbt(gst, "cmask2", [128, 2, 128], BF16)
        kbias = sbt(gst, "kbias", [128, NB], F32)
        rdec = sbt(gst, "rdec", [128, 8], F32)
        lam = sbt(gst, "lam", [128, 4], F32)
        lamv = sbt(gst, "lamv", [128, 256], F32)
        lamt = sbt(gst, "lamt", [128, 128], F32)
        lams = sbt(gst, "lams", [128, 2], F32)
        sublnw = sbt(gst, "sublnw", [128, 128], F32)
        wqk = sbt(gst, "wqk", [128, 4, 64], F32)

        P.op("pool", lambda e: e.iota(identf.t[:], pattern=[[1, 128]], base=0, channel_multiplier=-1,
                                      allow_small_or_imprecise_dtypes=True), writes=[identf.b])
        P.op("dve", lambda e: e.tensor_scalar(ident.t[:], identf.t[:], 0.0, None, op0=ALU.is_equal),
             reads=[identf.b], writes=[ident.b])
        P.dma("sp", lambda e: e.dma_start(out=cmask.t[:], in_=cmask_d), cmask.b, writes=[cmask.b])
        P.dma("sp", lambda e: e.dma_start(out=kbias.t[:], in_=kbias_d), kbias.b, writes=[kbias.b])
        P.dma("sp", lambda e: e.dma_start(out=rdec.t[:], in_=rdec_d), rdec.b, writes=[rdec.b])
        P.dma("sp", lambda e: e.dma_start(out=lamv.t[:], in_=bc_part(lambdas, 256)), lamv.b, writes=[lamv.b])
        P.dma("sp", lambda e: e.dma_start(out=sublnw.t[:], in_=bc_part(subln_w, 128)), sublnw.b, writes=[sublnw.b])
        P.dma("sp", lambda e: e.dma_start(out=wqk.t[:].rearrange("p a b -> p (a b)"), in_=bc_part(qk_norm_w, 256)),
              wqk.b, writes=[wqk.b])
        P.op("dve", lambda e: e.tensor_copy(cmask2.t[:, 0, :], cmask.t[:]), reads=[cmask.b], writes=[cmask2.b])
        P.op("dve", lambda e: e.tensor_copy(cmask2.t[:, 1, :], cmask.t[:]), reads=[cmask.b], writes=[cmask2.b])
        P.op("dve", lambda e: e.tensor_tensor(out=lamt.t[:, 0:64], in0=lamv.t[:, 0:64], in1=lamv.t[:, 64:128], op=ALU.mult),
             reads=[lamv.b], writes=[lamt.b])
        P.op("dve", lambda e: e.tensor_tensor(out=lamt.t[:, 64:128], in0=lamv.t[:, 128:192], in1=lamv.t[:, 192:256], op=ALU.mult),
             reads=[lamv.b], writes=[lamt.b])
        P.op("dve", lambda e: e.tensor_reduce(out=lams.t[:, 0:2], in_=lamt.t[:].rearrange("p (a b) -> p a b", a=2),
                                              axis=AX.X, op=ALU.add), reads=[lamt.b], writes=[lams.b])
        P.op("act", lambda e: e.activation(out=lams.t[:], in_=lams.t[:], func=AF.Exp), reads=[lams.b], writes=[lams.b])
        P.op("dve", lambda e: e.tensor_tensor(out=lam.t[:, 0:1], in0=lams.t[:, 0:1], in1=lams.t[:, 1:2], op=ALU.subtract),
             reads=[lams.b], writes=[lam.b])
        P.op("dve", lambda e: e.tensor_scalar(lam.t[:, 0:1], lam.t[:, 0:1], LAM_INIT, None, op0=ALU.add),
             reads=[lam.b], writes=[lam.b])
        P.op("dve", lambda e: e.tensor_scalar(sublnw.t[:], sublnw.t[:], 1.0 - LAM_INIT, None, op0=ALU.mult),
             reads=[sublnw.b], writes=[sublnw.b])

        def norm_part(xt, nw, sq, ss, rs, ub, use_pow=False):
            P.op("act", lambda e: e.activation(out=sq.t[:], in_=xt.t[:], func=AF.Square, accum_out=ss.t[:, 0:1]),
                 reads=[xt.b], writes=[sq.b, ss.b])
            if use_pow:
                P.op("dve", lambda e: e.tensor_scalar(rs.t[:, 0:1], ss.t[:, 0:1], 1.0 / D, EPS, op0=ALU.mult, op1=ALU.add),
                     reads=[ss.b], writes=[rs.b])
                P.op("pool", lambda e: e.tensor_tensor(out=rs.t[:, 0:1], in0=rs.t[:, 0:1], in1=mhalf.t[:, 0:1], op=ALU.pow),
                     reads=[rs.b, mhalf.b], writes=[rs.b])
            else:
                P.op("act", lambda e: e.activation(out=rs.t[:, 0:1], in_=ss.t[:, 0:1], func=AF.Sqrt, scale=1.0 / D, bias=eps_t.t[:, 0:1]),
                     reads=[ss.b, eps_t.b], writes=[rs.b])
                P.op("dve", lambda e: e.reciprocal(rs.t[:, 0:1], rs.t[:, 0:1]), reads=[rs.b], writes=[rs.b])
            P.op("dve", lambda e: e.scalar_tensor_tensor(out=ub.t[:], in0=xt.t[:], scalar=rs.t[:, 0:1], in1=nw.t[:],
                                                         op0=ALU.mult, op1=ALU.mult),
                 reads=[xt.b, rs.b, nw.b], writes=[ub.b])

        def tr_part(ub, pT, uT):
            for half in range(2):
                pt = pT[half]
                for j in range(4):
                    kc = half * 4 + j
                    P.op("pe", lambda e, kc=kc, j=j, pt=pt: e.transpose(pt.t[:, j * 128:(j + 1) * 128],
                                                                        ub.t[:, kc * 128:(kc + 1) * 128], ident.t[:]),
                         reads=[ub.b, ident.b], writes=[pt.b])
                if half == 0:
                    P.op("dve", lambda e, pt=pt: e.tensor_copy(uT.t[:, 0:4, :].rearrange("p a b -> p (a b)"), pt.t[:, 0:512]),
                         reads=[pt.b], writes=[uT.b])
                else:
                    P.op("act", lambda e, pt=pt: e.copy(uT.t[:, 4:8, :].rearrange("p a b -> p (a b)"), pt.t[:, 0:512]),
                         reads=[pt.b], writes=[uT.b])

        def norm_transpose(xt, nw, sq, ss, rs, ub, pT, uT):
            norm_part(xt, nw, sq, ss, rs, ub)
            tr_part(ub, pT, uT)

        eps_t = sbt(gst, "eps_t", [128, 1], F32)
        P.op("pool", lambda e: e.memset(eps_t.t[:], EPS), writes=[eps_t.b])
        mhalf = sbt(gst, "mhalf", [128, 8], F32)
        P.op("pool", lambda e: e.memset(mhalf.t[:], -0.5), writes=[mhalf.b])

        with ExitStack() as st:
            n1w = sbt(st, "n1w", [128, D], F32)
            P.dma("sp", lambda e: e.dma_start(out=n1w.t[:], in_=bc_part(norm1_w, D)), n1w.b, writes=[n1w.b])
            xb = [sbt(st, "x%d" % i, [128, D], F32) for i in range(3)]
            sq = [sbt(st, "sq%d" % i, [128, D], F32) for i in range(2)]
            ss = [sbt(st, "ss%d" % i, [128, 1], F32) for i in range(2)]
            rs = [sbt(st, "rs%d" % i, [128, 1], F32) for i in range(2)]
            ub = [sbt(st, "ub%d" % i, [128, D], BF16) for i in range(2)]
            uT = [sbt(st, "uT%d" % i, [128, 8, 128], BF16) for i in range(2)]
            pT = [pbank(st, "pT%d" % i, BF16) for i in range(4)]
            def p0_x(blk):
                src = xc[blk * 128:(blk + 1) * 128, :] if blk < NC else xo[(blk - NC) * 128:(blk - NC + 1) * 128, :]
                x_ = xb[blk % 3]
                P.dma("sp", lambda e: e.dma_start(out=x_.t[:], in_=src), x_.b, writes=[x_.b])
                norm_part(x_, n1w, sq[blk % 2], ss[blk % 2], rs[blk % 2], ub[blk % 2])

            def p0_y(blk):
                u_ = uT[blk % 2]
                tr_part(ub[blk % 2], pT[(blk % 2) * 2:(blk % 2) * 2 + 2], u_)
                P.dma("pool", lambda e: e.dma_start(out=UT[blk], in_=u_.t[:].rearrange("p a b -> p (a b)")),
                      u_.b, reads=[u_.b], writes=[b_UT[blk]])
            p0_x(0)
            for blk in range(NB):
                if blk + 1 < NB:
                    p0_x(blk + 1)
                p0_y(blk)
            P.emit()

        with ExitStack() as st:
            WR = [sbt(st, "WR%d" % i, [128, 8, 1536], BF16) for i in range(2)]
            uTb = [sbt(st, "ruT%d" % i, [128, 8, 128], BF16) for i in range(3)]
            RT = [sbt(st, "RT%d" % i, [128, 512], F32) for i in range(3)]
            Rf = sbt(st, "Rf", [128, 2, 512], F32)
            Rb = sbt(st, "Rb", [128, 2, 512], BF16)
            Aq = sbt(st, "Aq", [128, 256], F32)
            Bq = sbt(st, "Bq", [128, 256], F32)
            Ak = sbt(st, "Ak", [128, 256], F32)
            Bk = sbt(st, "Bk", [128, 256], F32)
            qr = [sbt(st, "qr%d" % i, [128, 256], BF16) for i in range(2)]
            kr = [sbt(st, "kr%d" % i, [128, 256], BF16) for i in range(2)]
            vb = [sbt(st, "vb%d" % i, [128, 512], BF16) for i in range(2)]
            sg = [sbt(st, "sg%d" % i, [128, 512], F32) for i in range(2)]
            qkT = sbt(st, "qkT", [128, 4, 128], BF16)
            Sm = sbt(st, "Sm", [128, 128], BF16)
            bst = sbt(st, "bst", [128, 6], F32)
            mv = sbt(st, "mv", [128, 2], F32)
            grs = sbt(st, "grs", [128, 1], F32)
            on = sbt(st, "on", [128, 512], F32)
            gtd = sbt(st, "gtd", [128, 512], BF16)
            gT = [sbt(st, "gT%d" % i, [128, 4, 128], BF16) for i in range(2)]
            pQK = pbank(st, "pQK")
            pV = pbank(st, "pV")
            pG = pbank(st, "pG")
            pTq = pbank(st, "pTq", BF16)
            pSv = pTq.t[:, 512:768].bitcast(F32)
            pTg = pbank(st, "pTg", BF16)
            pO2 = [pbank(st, "pO%d" % i) for i in range(2)]
            pR1 = pbank(st, "pR1")
            Rb2 = [sbt(st, "Rb%d" % i, [128, 2, 512], BF16) for i in range(2)]
            bst2 = [sbt(st, "bst%d" % i, [128, 6], F32) for i in range(2)]
            mv2 = [sbt(st, "mv%d" % i, [128, 2], F32) for i in range(2)]
            grs2 = [sbt(st, "grs%d" % i, [128, 1], F32) for i in range(2)]
            on2 = [sbt(st, "on%d" % i, [128, 512], F32) for i in range(2)]
            gtd2 = [sbt(st, "gtd%d" % i, [128, 512], BF16) for i in range(2)]

            def load_WR(h):
                w = WR[h % 2]
                for (c0, n, o0) in ((C_RQ + h * 256, 256, 0), (C_RK + h * 256, 256, 256),
                                    (C_RV + h * 512, 512, 512), (C_RG + h * 512, 512, 1024)):
                    P.dma("pool", lambda e, w=w, c0=c0, n=n, o0=o0: e.dma_start(out=w.t[:, :, o0:o0 + n],
                                                                                   in_=w_in_v[:, :, c0:c0 + n]),
                          w.b, writes=[w.b])
            load_WR(0)
            for h in range(4):
                if h + 1 < 4:
                    load_WR(h + 1)
                w = WR[h % 2]
                g = GAM[h]
                P.op("pool", lambda e: e.memset(Rf.t[:], 0.0), writes=[Rf.b])
                P.op("pool", lambda e: e.memset(Rb2[0].t[:], 0.0), writes=[Rb2[0].b])
                P.op("pool", lambda e: e.memset(Rb2[1].t[:], 0.0), writes=[Rb2[1].b])

                def A1(blk):
                    own = blk >= NC
                    u_ = uTb[blk % 3]
                    rt = RT[blk % 3]
                    k_ = kr[blk % 2]
                    q_ = qr[blk % 2]
                    P.dma("sp", lambda e: e.dma_start(out=u_.t[:].rearrange("p a b -> p (a b)"), in_=UT[blk]),
                          u_.b, reads=[b_UT[blk]], writes=[u_.b])
                    P.dma("sp", lambda e: e.dma_start(out=rt.t[:], in_=rope_r[blk * 128:(blk + 1) * 128, :]),
                          rt.b, writes=[rt.b])
                    c0 = 0 if own else 256
                    for kc in range(8):
                        P.op("pe", lambda e, kc=kc: e.matmul(pQK.t[:, c0:512], lhsT=u_.t[:, kc, :], rhs=w.t[:, kc, c0:512],
                                                             start=(kc == 0), stop=(kc == 7)),
                             reads=[u_.b, w.b], writes=[pQK.b])
                    P.op("dve", lambda e: e.scalar_tensor_tensor(out=Ak.t[:], in0=pQK.t[:, 256:512], scalar=rdec.t[:, 4 + h:5 + h],
                                                                 in1=rt.t[:, 0:256], op0=ALU.mult, op1=ALU.mult),
                         reads=[pQK.b, rdec.b, rt.b], writes=[Ak.b])
                    P.op("dve", lambda e: e.scalar_tensor_tensor(out=Bk.t[:], in0=pQK.t[:, 256:512], scalar=rdec.t[:, 4 + h:5 + h],
                                                                 in1=rt.t[:, 256:512], op0=ALU.mult, op1=ALU.mult),
                         reads=[pQK.b, rdec.b, rt.b], writes=[Bk.b])
                    if own:
                        P.op("dve", lambda e: e.scalar_tensor_tensor(out=Aq.t[:], in0=pQK.t[:, 0:256], scalar=rdec.t[:, h:h + 1],
                                                                     in1=rt.t[:, 0:256], op0=ALU.mult, op1=ALU.mult),
                             reads=[pQK.b, rdec.b, rt.b], writes=[Aq.b])
                        P.op("dve", lambda e: e.scalar_tensor_tensor(out=Bq.t[:], in0=pQK.t[:, 0:256], scalar=rdec.t[:, h:h + 1],
                                                                     in1=rt.t[:, 256:512], op0=ALU.mult, op1=ALU.mult),
                             reads=[pQK.b, rdec.b, rt.b], writes=[Bq.b])
                    P.op("dve", lambda e: e.tensor_tensor(out=k_.t[:, 0:128], in0=Ak.t[:, 0:128], in1=Bk.t[:, 128:256], op=ALU.subtract),
                         reads=[Ak.b, Bk.b], writes=[k_.b])
                    P.op("dve", lambda e: e.tensor_tensor(out=k_.t[:, 128:256], in0=Ak.t[:, 128:256], in1=Bk.t[:, 0:128], op=ALU.add),
                         reads=[Ak.b, Bk.b], writes=[k_.b])
                    if own:
                        P.op("dve", lambda e: e.tensor_tensor(out=q_.t[:, 0:128], in0=Aq.t[:, 0:128], in1=Bq.t[:, 128:256], op=ALU.subtract),
                             reads=[Aq.b, Bq.b], writes=[q_.b])
                        P.op("dve", lambda e: e.tensor_tensor(out=q_.t[:, 128:256], in0=Aq.t[:, 128:256], in1=Bq.t[:, 0:128], op=ALU.add),
                             reads=[Aq.b, Bq.b], writes=[q_.b])

                def A2(blk):
                    u_ = uTb[blk % 3]
                    v_ = vb[blk % 2]
                    for kc in range(8):
                        P.op("pe", lambda e, kc=kc: e.matmul(pV.t[:, 0:512], lhsT=u_.t[:, kc, :], rhs=w.t[:, kc, 512:1024],
                                                             start=(kc == 0), stop=(kc == 7)),
                             reads=[u_.b, w.b], writes=[pV.b])
                    P.op("act", lambda e: e.copy(v_.t[:], pV.t[:, 0:512]), reads=[pV.b], writes=[v_.b])

                def A3(blk):
                    if blk < NC:
                        return
                    u_ = uTb[blk % 3]
                    s_ = sg[blk % 2]
                    for kc in range(8):
                        P.op("pe", lambda e, kc=kc: e.matmul(pG.t[:, 0:512], lhsT=u_.t[:, kc, :], rhs=w.t[:, kc, 1024:1536],
                                                             start=(kc == 0), stop=(kc == 7)),
                             reads=[u_.b, w.b], writes=[pG.b])
                    P.op("act", lambda e: e.activation(out=s_.t[:], in_=pG.t[:, 0:512], func=AF.Silu), reads=[pG.b], writes=[s_.b])

                def B1(blk):
                    if blk < NC:
                        return
                    k_ = kr[blk % 2]
                    q_ = qr[blk % 2]
                    for j in range(4):
                        srcT = q_ if j < 2 else k_
                        c = j % 2
                        P.op("pe", lambda e, j=j, c=c, srcT=srcT: e.transpose(pTq.t[:, j * 128:(j + 1) * 128],
                                                                              srcT.t[:, c * 128:(c + 1) * 128], ident.t[:]),
                             reads=[srcT.b, ident.b], writes=[pTq.b])
                    P.op("act", lambda e: e.copy(qkT.t[:].rearrange("p a b -> p (a b)"), pTq.t[:, 0:512]),
                         reads=[pTq.b], writes=[qkT.b])

                def B2(blk):
                    if blk < NC:
                        return
                    for c in range(2):
                        P.op("pe", lambda e, c=c: e.matmul(pSv, lhsT=qkT.t[:, 2 + c, :], rhs=qkT.t[:, c, :],
                                                           start=(c == 0), stop=(c == 1)),
                             reads=[qkT.b], writes=[pTq.b])
                    P.op("dve", lambda e: e.scalar_tensor_tensor(out=Sm.t[:], in0=pSv, scalar=float(g ** -128.0),
                                                                 in1=cmask.t[:], op0=ALU.mult, op1=ALU.mult),
                         reads=[pTq.b, cmask.b], writes=[Sm.b])

                def B3a(blk):
                    if blk < NC:
                        return
                    v_ = vb[blk % 2]
                    po = pO2[blk % 2]
                    rb = Rb2[(blk - 1) % 2]
                    P.op("pe", lambda e: e.matmul(po.t[:, 0:512], lhsT=Sm.t[:], rhs=v_.t[:], start=True, stop=False),
                         reads=[Sm.b, v_.b], writes=[po.b])
                    for c in range(2):
                        P.op("pe", lambda e, c=c: e.matmul(po.t[:, 0:512], lhsT=qkT.t[:, c, :], rhs=rb.t[:, c, :],
                                                           start=False, stop=(c == 1)),
                             reads=[qkT.b, rb.b], writes=[po.b])

                def CH(blk):
                    if blk < NC:
                        return
                    par = blk % 2
                    po = pO2[par]
                    s_ = sg[par]
                    bst_, mv_, grs_, on_, gtd_ = bst2[par], mv2[par], grs2[par], on2[par], gtd2[par]
                    P.op("dve", lambda e: e.bn_stats(bst_.t[:], po.t[:, 0:512]), reads=[po.b], writes=[bst_.b])
                    P.op("dve", lambda e: e.bn_aggr(mv_.t[:], bst_.t[:]), reads=[bst_.b], writes=[mv_.b])
                    P.op("dve", lambda e: e.tensor_scalar(grs_.t[:], mv_.t[:, 1:2], EPS, None, op0=ALU.add),
                         reads=[mv_.b], writes=[grs_.b])
                    P.op("pool", lambda e: e.tensor_tensor(out=grs_.t[:], in0=grs_.t[:], in1=mhalf.t[:, 0:1], op=ALU.pow),
                         reads=[grs_.b, mhalf.b], writes=[grs_.b])
                    P.op("dve", lambda e: e.tensor_scalar(on_.t[:], po.t[:, 0:512], mv_.t[:, 0:1], grs_.t[:, 0:1],
                                                          op0=ALU.subtract, op1=ALU.mult),
                         reads=[po.b, mv_.b, grs_.b], writes=[on_.b])
                    P.op("dve", lambda e: e.tensor_tensor(out=gtd_.t[:], in0=on_.t[:], in1=s_.t[:], op=ALU.mult),
                         reads=[on_.b, s_.b], writes=[gtd_.b])

                def ST(blk, c):
                    if blk >= NB - 1:
                        return
                    k_ = kr[blk % 2]
                    v_ = vb[blk % 2]
                    P.op("pe", lambda e: e.matmul(pR1.t[:, 0:512], lhsT=k_.t[:, c * 128:(c + 1) * 128], rhs=v_.t[:],
                                                  start=True, stop=True),
                         reads=[k_.b, v_.b], writes=[pR1.b])
                    P.op("dve", lambda e: e.scalar_tensor_tensor(out=Rf.t[:, c, :], in0=Rf.t[:, c, :], scalar=float(g ** 128.0),
                                                                 in1=pR1.t[:, 0:512], op0=ALU.mult, op1=ALU.add),
                         reads=[Rf.b, pR1.b], writes=[Rf.b])
                    if c == 1:
                        rb = Rb2[blk % 2]
                        P.op("act", lambda e: e.copy(rb.t[:].rearrange("p a b -> p (a b)"), Rf.t[:].rearrange("p a b -> p (a b)")),
                             reads=[Rf.b], writes=[rb.b])

                def G(blk):
                    if blk < NC or blk >= NB:
                        return
                    ob = blk - NC
                    g_ = gT[ob % 2]
                    gtd_ = gtd2[blk % 2]
                    for j in range(4):
                        P.op("pe", lambda e, j=j: e.transpose(pTg.t[:, j * 128:(j + 1) * 128],
                                                              gtd_.t[:, j * 128:(j + 1) * 128], ident.t[:]),
                             reads=[gtd_.b, ident.b], writes=[pTg.b])
                    P.op("act", lambda e: e.copy(g_.t[:].rearrange("p a b -> p (a b)"), pTg.t[:, 0:512]),
                         reads=[pTg.b], writes=[g_.b])
                    P.dma("pool", lambda e: e.dma_start(out=GT[ob][:, h * 512:(h + 1) * 512],
                                                        in_=g_.t[:].rearrange("p a b -> p (a b)")),
                          g_.b, reads=[g_.b], writes=[b_GT[ob]])

                A1(0); A2(0); A3(0)
                for blk in range(NB):
                    nx = blk + 1
                    B1(blk)
                    if nx < NB:
                        A1(nx)
                    B2(blk)
                    if nx < NB:
                        A2(nx)
                    B3a(blk)
                    G(blk - 1)
                    ST(blk, 0)
                    if nx < NB:
                        A3(nx)
                    ST(blk, 1)
                    CH(blk)
                G(NB - 1)
                P.emit()

        KTs = dscr("KTs", [8, 128, NB * 128], BF16)
        QTs = dscr("QTs", [8, 128, NO * 128], BF16)
        VVs = dscr("VVs", [8, 128, NB, 128], BF16)
        b_KTs = Buf("KTs"); b_QTs = Buf("QTs"); b_VVs = Buf("VVs")
        KTs_w = KTs.rearrange("h p (b t) -> p h b t", t=128)
        QTs_w = QTs.rearrange("h p (b t) -> p h b t", t=128)
        VVs_w = VVs.rearrange("h p b e -> p h b e")
        with ExitStack() as st:
            WDa = sbt(st, "WDa", [128, 8, 3072], BF16)
            for k0 in range(0, 8, 2):
                P.dma("pool", lambda e, k0=k0: e.dma_start(out=WDa.t[:, k0:k0 + 2, :], in_=w_in_v[:, k0:k0 + 2, C_DQ:C_DQ + 3072]),
                      WDa.b, writes=[WDa.b])
            ropd = sbt(st, "ropd", [128, NB, 32], F32)
            ropd_v = rope_d.rearrange("(b p) c -> p b c", p=128)
            for b0 in range(0, NB, 16):
                b1 = min(NB, b0 + 16)
                P.dma("sp", lambda e, b0=b0, b1=b1: e.dma_start(out=ropd.t[:, b0:b1, :], in_=ropd_v[:, b0:b1, :]),
                      ropd.b, writes=[ropd.b])
            wq8 = sbt(st, "wq8", [128, 8, 64], F32)
            wk8 = sbt(st, "wk8", [128, 8, 64], F32)
            for g8 in range(8):
                P.op("pool", lambda e, g8=g8: e.tensor_copy(wq8.t[:, g8, :], wqk.t[:, 0, :]), reads=[wqk.b], writes=[wq8.b])
                P.op("pool", lambda e, g8=g8: e.tensor_copy(wk8.t[:, g8, :], wqk.t[:, 2, :]), reads=[wqk.b], writes=[wk8.b])
            uTb = [sbt(st, "duT%d" % i, [128, 8, 128], BF16) for i in range(3)]
            NCH = 4
            sqd = [sbt(st, "sqd%d" % i, [128, 8, 64], F32) for i in range(NCH)]
            ssd = [sbt(st, "ssd%d" % i, [128, 8], F32) for i in range(NCH)]
            rsd = [sbt(st, "rsd%d" % i, [128, 8], F32) for i in range(NCH)]
            xn = [[sbt(st, "xn%d_%d" % (pp, i), [128, 8, 64], F32) for i in range(NCH)] for pp in range(2)]
            xbq = [[sbt(st, "xbq%d_%d" % (pp, i), [128, 8, 64], BF16) for i in range(NCH)] for pp in range(2)]
            rc = [[sbt(st, "rc%d_%d" % (pp, i), [128, 8, 16], F32) for i in range(NCH)] for pp in range(2)]
            xw = [[sbt(st, "xw%d_%d" % (pp, i), [128, 8, 16], F32) for i in range(NCH)] for pp in range(2)]
            rsn = [[sbt(st, "rsn%d_%d" % (pp, i), [128, 8, 16], F32) for i in range(NCH)] for pp in range(2)]
            kst = [sbt(st, "kst%d" % i, [128, 8, 128], BF16) for i in range(2)]
            qst = [sbt(st, "qst%d" % i, [128, 8, 128], BF16) for i in range(2)]
            vst = [sbt(st, "vst%d" % i, [128, 8, 128], BF16) for i in range(2)]
            pq = [pbank(st, "pq%d" % i) for i in range(2)]
            pk = [pbank(st, "pk%d" % i) for i in range(2)]
            pvv = [pbank(st, "pvv%d" % i) for i in range(2)]
            pTk = pbank(st, "pTk", BF16)
            pTq = pbank(st, "pTq1", BF16)
            def mk_chains(blk):
                chains = []
                for half in range(2):
                    chains.append((pk[half], wk8, pTk, half, 1024 + half * 512))
                if blk >= NC:
                    for half in range(2):
                        chains.append((pq[half], wq8, pTq, half, half * 512))
                return chains

            def d1_early(blk):
                u_ = uTb[blk % 3]
                par = blk % 2
                P.dma("sp", lambda e: e.dma_start(out=u_.t[:].rearrange("p a b -> p (a b)"), in_=UT[blk]),
                      u_.b, reads=[b_UT[blk]], writes=[u_.b])
                chains = mk_chains(blk)
                for (pb, wt, ptT, half, c0) in chains:
                    for kc in range(8):
                        P.op("pe", lambda e, kc=kc, pb=pb, c0=c0: e.matmul(pb.t[:, 0:512], lhsT=u_.t[:, kc, :], rhs=WDa.t[:, kc, c0:c0 + 512],
                                                                           start=(kc == 0), stop=(kc == 7)),
                             reads=[u_.b, WDa.b], writes=[pb.b])
                for half in range(2):
                    for kc in range(8):
                        P.op("pe", lambda e, kc=kc, half=half: e.matmul(pvv[half].t[:, 0:512], lhsT=u_.t[:, kc, :],
                                                                       rhs=WDa.t[:, kc, 2048 + half * 512:2048 + (half + 1) * 512],
                                                                       start=(kc == 0), stop=(kc == 7)),
                             reads=[u_.b, WDa.b], writes=[pvv[half].b])
                nch = len(chains)
                pvw = [ch[0].t[:, 0:512].rearrange("p (a b) -> p a b", b=64) for ch in chains]
                for ci in range(nch):
                    P.op("act", lambda e, ci=ci: e.activation(out=sqd[ci].t[:], in_=pvw[ci], func=AF.Square),
                         reads=[chains[ci][0].b], writes=[sqd[ci].b])
                for ci in range(nch):
                    P.op("dve", lambda e, ci=ci: e.tensor_reduce(out=ssd[ci].t[:], in_=sqd[ci].t[:], axis=AX.X, op=ALU.add),
                         reads=[sqd[ci].b], writes=[ssd[ci].b])
                for ci in range(nch):
                    P.op("act", lambda e, ci=ci: e.activation(out=rsd[ci].t[:], in_=ssd[ci].t[:], func=AF.Sqrt, scale=1.0 / 64, bias=eps_t.t[:, 0:1]),
                         reads=[ssd[ci].b, eps_t.b], writes=[rsd[ci].b])
                v_ = vst[par]
                for half in range(2):
                    P.op("act", lambda e, half=half: e.copy(v_.t[:, half * 4:half * 4 + 4, :].rearrange("p a b -> p (a b)"), pvv[half].t[:, 0:512]),
                         reads=[pvv[half].b], writes=[v_.b])
                P.dma("sp", lambda e: e.dma_start(out=VVs_w[:, :, blk, :], in_=v_.t[:]), v_.b, reads=[v_.b], writes=[b_VVs])
                for ci in range(nch):
                    P.op("dve", lambda e, ci=ci: e.reciprocal(rsd[ci].t[:], rsd[ci].t[:]), reads=[rsd[ci].b], writes=[rsd[ci].b])
                for ci in range(nch):
                    P.op("dve", lambda e, ci=ci: e.tensor_tensor(out=xn[par][ci].t[:], in0=pvw[ci], in1=bc_last(rsd[ci].t[:], 64), op=ALU.mult),
                         reads=[chains[ci][0].b, rsd[ci].b], writes=[xn[par][ci].b])

            def d1_late(blk):
                par = blk % 2
                own = blk >= NC
                ob = blk - NC
                chains = mk_chains(blk)
                nch = len(chains)
                xn_, xb_, rc_, rsn_, xw_ = xn[par], xbq[par], rc[par], rsn[par], xw[par]
                for ci in range(nch):
                    eng = "pool"
                    wt = chains[ci][1]
                    P.op(eng, lambda e, ci=ci, wt=wt: e.tensor_tensor(out=xb_[ci].t[:], in0=xn_[ci].t[:], in1=wt.t[:], op=ALU.mult),
                         reads=[xn_[ci].b, wt.b], writes=[xb_[ci].b])
                    P.op(eng, lambda e, ci=ci, wt=wt: e.tensor_tensor(out=xw_[ci].t[:], in0=xn_[ci].t[:, :, 0:16], in1=wt.t[:, :, 0:16], op=ALU.mult),
                         reads=[xn_[ci].b, wt.b], writes=[xw_[ci].b])
                    P.op(eng, lambda e, ci=ci: e.tensor_tensor(out=rc_[ci].t[:], in0=xw_[ci].t[:],
                                                              in1=bc_mid(ropd.t[:, blk, 0:16], 8), op=ALU.mult),
                         reads=[xw_[ci].b, ropd.b], writes=[rc_[ci].b])
                    P.op(eng, lambda e, ci=ci: e.tensor_tensor(out=rsn_[ci].t[:], in0=xw_[ci].t[:],
                                                              in1=bc_mid(ropd.t[:, blk, 16:32], 8), op=ALU.mult),
                         reads=[xw_[ci].b, ropd.b], writes=[rsn_[ci].b])
                    P.op(eng, lambda e, ci=ci: e.tensor_tensor(out=xb_[ci].t[:, :, 0:8], in0=rc_[ci].t[:, :, 0:8], in1=rsn_[ci].t[:, :, 8:16], op=ALU.subtract),
                         reads=[rc_[ci].b, rsn_[ci].b], writes=[xb_[ci].b])
                    P.op(eng, lambda e, ci=ci: e.tensor_tensor(out=xb_[ci].t[:, :, 8:16], in0=rc_[ci].t[:, :, 8:16], in1=rsn_[ci].t[:, :, 0:8], op=ALU.add),
                         reads=[rc_[ci].b, rsn_[ci].b], writes=[xb_[ci].b])
                for ci in range(nch):
                    ptT, half = chains[ci][2], chains[ci][3]
                    for hh in range(4):
                        P.op("pe", lambda e, ci=ci, hh=hh, ptT=ptT, half=half: e.transpose(ptT.t[:, (half * 4 + hh) * 128:(half * 4 + hh + 1) * 128],
                                                                                          xb_[ci].t[:, 2 * hh:2 * hh + 2, :].rearrange("p a b -> p (a b)"),
                                                                                          ident.t[:]),
                             reads=[xb_[ci].b, ident.b], writes=[ptT.b])
                k_ = kst[par]
                P.op("dve", lambda e: e.tensor_copy(k_.t[:].rearrange("p a b -> p (a b)"), pTk.t[:, 0:1024]), reads=[pTk.b], writes=[k_.b])
                P.dma("sp", lambda e: e.dma_start(out=KTs_w[:, :, blk, :], in_=k_.t[:]), k_.b, reads=[k_.b], writes=[b_KTs])
                if own:
                    q_ = qst[par]
                    P.op("act", lambda e: e.copy(q_.t[:].rearrange("p a b -> p (a b)"), pTq.t[:, 0:1024]), reads=[pTq.b], writes=[q_.b])
                    P.dma("sp", lambda e: e.dma_start(out=QTs_w[:, :, ob, :], in_=q_.t[:]), q_.b, reads=[q_.b], writes=[b_QTs])

            d1_early(0)
            for blk in range(NB):
                if blk + 1 < NB:
                    d1_early(blk + 1)
                d1_late(blk)
            P.emit()

        with ExitStack() as st:
            KTb = [sbt(st, "KT%d" % i, [128, NB * 128], BF16) for i in range(2)]
            VVb = [sbt(st, "VV%d" % i, [128, NB, 130], BF16) for i in range(2)]
            QT2b = [sbt(st, "QT2%d" % i, [128, NO, 256], BF16) for i in range(2)]
            NPT = 6
            PT = [sbt(st, "PT%d" % i, [128, 4, 128], BF16) for i in range(NPT)]
            zz = sbt(st, "zz", [128, 2], F32)
            a1 = sbt(st, "a1", [128, 128], F32)
            aa = sbt(st, "aa", [128, 128], F32)
            asq = sbt(st, "asq", [128, 128], F32)
            ass = sbt(st, "ass", [128, 1], F32)
            ars = sbt(st, "ars", [128, 1], F32)
            dob = sbt(st, "dob", [128, 128], BF16)
            doT = [sbt(st, "doT%d" % i, [128, 128], BF16) for i in range(2)]
            pP = pbank(st, "pP")
            pTd = pbank(st, "pTd", BF16)
            pSd = [pbank(st, "pSd%d" % i) for i in range(2)]
            pO0 = [pbank(st, "pO0%d" % i) for i in range(2)]
            pO1 = [pbank(st, "pO1%d" % i) for i in range(2)]
            assert NC % 2 == 0
            for i2 in range(2):
                P.op("pool", lambda e, i2=i2: e.memset(VVb[i2].t[:], 0.0), writes=[VVb[i2].b])
                P.op("dve", lambda e, i2=i2: e.tensor_copy(VVb[i2].t[:, :, 128:129], kbias.t[:].rearrange("p (a b) -> p a b", b=1)),
                     reads=[kbias.b], writes=[VVb[i2].b])
                P.op("pool", lambda e, i2=i2: e.memset(QT2b[i2].t[:], 0.0), writes=[QT2b[i2].b])

            def load_head(h):
                kt, vv, q2 = KTb[h % 2], VVb[h % 2], QT2b[h % 2]
                P.dma("sp", lambda e: e.dma_start(out=kt.t[:], in_=KTs[h]), kt.b, reads=[b_KTs], writes=[kt.b])
                P.dma("sp", lambda e: e.dma_start(out=vv.t[:, :, 0:128], in_=VVs[h]), vv.b, reads=[b_VVs], writes=[vv.b])
                P.dma("sp", lambda e: e.dma_start(out=q2.t[0:64, :, 0:128], in_=QTs[h][0:64, :].rearrange("p (i t) -> p i t", t=128)),
                      q2.b, reads=[b_QTs], writes=[q2.b])
                P.dma("sp", lambda e: e.dma_start(out=q2.t[64:128, :, 128:256], in_=QTs[h][64:128, :].rearrange("p (i t) -> p i t", t=128)),
                      q2.b, reads=[b_QTs], writes=[q2.b])
            load_head(0)
            for h in range(8):
                if h + 1 < 8:
                    load_head(h + 1)
                KT, VV, QT2 = KTb[h % 2], VVb[h % 2], QT2b[h % 2]
                b_K = [KT.b] * NB
                b_Kv = VV.b
                b_Q = [QT2.b] * NO
                items = []
                for i in range(NO):
                    nk = NC + i + 1
                    for kb0 in range(0, nk, 2):
                        items.append((i, kb0, min(2, nk - kb0)))
                SKEW = 2
                pS3 = [pSd[0], pSd[1], pP]

                def qk_exp(n):
                    i, kb0, nb = items[n]
                    nk = NC + i + 1
                    ps = pS3[n % 3]
                    pt = PT[n % NPT]
                    for j in range(nb):
                        kb = kb0 + j
                        P.op("pe", lambda e, j=j, kb=kb: e.matmul(ps.t[:, j * 256:(j + 1) * 256], lhsT=KT.t[:, kb * 128:(kb + 1) * 128],
                                                                  rhs=QT2.t[:, i, :], start=True, stop=True),
                             reads=[b_K[kb], b_Q[i]], writes=[ps.b])
                    P.op("act", lambda e: e.activation(out=pt.t[:, 0:2 * nb, :].rearrange("p a b -> p (a b)"),
                                                       in_=ps.t[:, 0:256 * nb], func=AF.Exp, scale=0.125),
                         reads=[ps.b], writes=[pt.b])
                    if kb0 + nb == nk:
                        jl = nb - 1
                        P.op("pool", lambda e: e.tensor_tensor(out=pt.t[:, 2 * jl:2 * jl + 2, :], in0=pt.t[:, 2 * jl:2 * jl + 2, :],
                                                               in1=cmask2.t[:], op=ALU.mult),
                             reads=[pt.b, cmask2.b], writes=[pt.b])

                def pv(n):
                    i, kb0, nb = items[n]
                    nk = NC + i + 1
                    o0 = pO0[i % 2]
                    o1 = pO1[i % 2]
                    pt = PT[n % NPT]
                    for j in range(nb):
                        kb = kb0 + j
                        P.op("pe", lambda e, j=j, kb=kb: e.matmul(o0.t[:, 0:129], lhsT=pt.t[:, 2 * j, :], rhs=VV.t[:, kb, 0:129],
                                                                  start=(kb == 0), stop=(kb == nk - 1)),
                             reads=[pt.b, b_Kv], writes=[o0.b])
                        P.op("pe", lambda e, j=j, kb=kb: e.matmul(o1.t[:, 0:129], lhsT=pt.t[:, 2 * j + 1, :], rhs=VV.t[:, kb, 0:129],
                                                                  start=(kb == 0), stop=(kb == nk - 1)),
                             reads=[pt.b, b_Kv], writes=[o1.b])
                    if kb0 + nb == nk:
                        finalize(i, o0, o1)

                def finalize(i, o0, o1):
                    while pend_fin:
                        pend_fin.pop(0)[1]()
                    P.op("dve", lambda e, o0=o0: e.reciprocal(zz.t[:, 0:1], o0.t[:, 128:129]), reads=[o0.b], writes=[zz.b])
                    P.op("dve", lambda e, o1=o1: e.reciprocal(zz.t[:, 1:2], o1.t[:, 128:129]), reads=[o1.b], writes=[zz.b])
                    P.op("dve", lambda e: e.tensor_tensor(out=zz.t[:, 1:2], in0=zz.t[:, 1:2], in1=lam.t[:, 0:1], op=ALU.mult),
                         reads=[zz.b, lam.b], writes=[zz.b])
                    P.op("dve", lambda e, o1=o1: e.tensor_scalar(a1.t[:], o1.t[:, 0:128], zz.t[:, 1:2], None, op0=ALU.mult),
                         reads=[o1.b, zz.b], writes=[a1.b])
                    P.op("dve", lambda e, o0=o0: e.scalar_tensor_tensor(out=aa.t[:], in0=o0.t[:, 0:128], scalar=zz.t[:, 0:1], in1=a1.t[:],
                                                                        op0=ALU.mult, op1=ALU.subtract),
                         reads=[o0.b, zz.b, a1.b], writes=[aa.b])
                    finalize_b(i)
                    pend_fin.append((n_now[0] + 8, lambda: finalize_c(i)))

                def finalize_b(i):
                    P.op("dve", lambda e: e.tensor_tensor(out=asq.t[:], in0=aa.t[:], in1=aa.t[:], op=ALU.mult), reads=[aa.b], writes=[asq.b])
                    P.op("dve", lambda e: e.tensor_reduce(out=ass.t[:, 0:1], in_=asq.t[:], axis=AX.X, op=ALU.add), reads=[asq.b], writes=[ass.b])
                    P.op("dve", lambda e: e.tensor_scalar(ars.t[:], ass.t[:], 1.0 / 128, EPS, op0=ALU.mult, op1=ALU.add),
                         reads=[ass.b], writes=[ars.b])
                    P.op("pool", lambda e: e.tensor_tensor(out=ars.t[:], in0=ars.t[:], in1=mhalf.t[:, 0:1], op=ALU.pow),
                         reads=[ars.b, mhalf.b], writes=[ars.b])
                    P.op("dve", lambda e: e.scalar_tensor_tensor(out=dob.t[:], in0=aa.t[:], scalar=ars.t[:, 0:1], in1=sublnw.t[:],
                                                                 op0=ALU.mult, op1=ALU.mult),
                         reads=[aa.b, ars.b, sublnw.b], writes=[dob.b])

                def finalize_c(i):
                    P.op("pe", lambda e: e.transpose(pTd.t[:, 256:384], dob.t[:], ident.t[:]), reads=[dob.b, ident.b], writes=[pTd.b])
                    d_ = doT[i % 2]
                    P.op("dve", lambda e, d_=d_: e.tensor_copy(d_.t[:], pTd.t[:, 256:384]), reads=[pTd.b], writes=[d_.b])
                    P.dma("pool", lambda e, d_=d_, i=i: e.dma_start(out=DOT[i][:, h * 128:(h + 1) * 128], in_=d_.t[:]),
                          d_.b, reads=[d_.b], writes=[b_DOT[i]])
                pend_fin = []
                n_now = [0]
                for n in range(len(items) + SKEW + 10):
                    n_now[0] = n
                    if n < len(items):
                        qk_exp(n)
                    if 0 <= n - SKEW < len(items):
                        pv(n - SKEW)
                    while pend_fin and pend_fin[0][0] <= n:
                        pend_fin.pop(0)[1]()
                assert not pend_fin
                P.emit()

        with ExitStack() as st:
            Wro = sbt(st, "Wro", [128, 16, 1024], BF16)
            Wdo = sbt(st, "Wdo", [128, 8, 1024], BF16)
            Wou = sbt(st, "Wou", [128, 8, 1024], BF16)
            Wg = sbt(st, "Wg", [128, 8, 2048], BF16)
            n2w = sbt(st, "n2w", [128, D], F32)
            P.dma("sp", lambda e: e.dma_start(out=n2w.t[:], in_=bc_part(norm2_w, D)), n2w.b, writes=[n2w.b])
            wro_v = w_ret_o.rearrange("(k p) n -> p k n", p=128)
            for k0 in range(0, 16, 4):
                P.dma("pool", lambda e, k0=k0: e.dma_start(out=Wro.t[:, k0:k0 + 4, :], in_=wro_v[:, k0:k0 + 4, :]), Wro.b, writes=[Wro.b])
            wdo_v = w_diff_o.rearrange("(k p) n -> p k n", p=128)
            wou_v = w_out.rearrange("(k p) n -> p k n", p=128)
            for k0 in range(0, 8, 4):
                P.dma("pool", lambda e, k0=k0: e.dma_start(out=Wdo.t[:, k0:k0 + 4, :], in_=wdo_v[:, k0:k0 + 4, :]), Wdo.b, writes=[Wdo.b])
                P.dma("pool", lambda e, k0=k0: e.dma_start(out=Wou.t[:, k0:k0 + 4, :], in_=wou_v[:, k0:k0 + 4, :]), Wou.b, writes=[Wou.b])
            for k0 in range(0, 8, 2):
                P.dma("pool", lambda e, k0=k0: e.dma_start(out=Wg.t[:, k0:k0 + 2, :], in_=w_in_v[:, k0:k0 + 2, C_GT:C_GT + 2048]), Wg.b, writes=[Wg.b])
            gTb = [sbt(st, "mgT%d" % i, [128, 16, 128], BF16) for i in range(2)]
            dTb = [sbt(st, "mdT%d" % i, [128, 8, 128], BF16) for i in range(2)]
            uTb = [sbt(st, "muT%d" % i, [128, 8, 128], BF16) for i in range(2)]
            xb = [sbt(st, "mx%d" % i, [128, D], F32) for i in range(2)]
            sig = sbt(st, "sig", [128, 2048], F32)
            m1 = sbt(st, "m1", [128, D], F32)
            m2 = sbt(st, "m2", [128, D], F32)
            mb = [sbt(st, "mb%d" % i, [128, D], BF16) for i in range(2)]
            mT = sbt(st, "mT", [128, 8, 128], BF16)
            h2 = [sbt(st, "h2%d" % i, [128, D], F32) for i in range(2)]
            sq = sbt(st, "msq", [128, D], F32)
            ss = sbt(st, "mss", [128, 1], F32)
            rs = sbt(st, "mrs", [128, 1], F32)
            ub = sbt(st, "mub", [128, D], BF16)
            u2T = [sbt(st, "mu2T%d" % i, [128, 8, 128], BF16) for i in range(2)]
            pA = [pbank(st, "pA%d" % i) for i in range(4)]
            pB = [pbank(st, "pB%d" % i) for i in range(2)]
            pTm = [pbank(st, "pTm%d" % i, BF16) for i in range(2)]
            def MA1(ob):
                blk = NC + ob
                g_ = gTb[ob % 2]; d_ = dTb[ob % 2]; u_ = uTb[ob % 2]; x_ = xb[ob % 2]
                P.dma("sp", lambda e: e.dma_start(out=g_.t[:].rearrange("p a b -> p (a b)"), in_=GT[ob]),
                      g_.b, reads=[b_GT[ob]], writes=[g_.b])
                P.dma("sp", lambda e: e.dma_start(out=d_.t[:].rearrange("p a b -> p (a b)"), in_=DOT[ob]),
                      d_.b, reads=[b_DOT[ob]], writes=[d_.b])
                P.dma("sp", lambda e: e.dma_start(out=u_.t[:].rearrange("p a b -> p (a b)"), in_=UT[blk]),
                      u_.b, reads=[b_UT[blk]], writes=[u_.b])
                P.dma("sp", lambda e: e.dma_start(out=x_.t[:], in_=xo[ob * 128:(ob + 1) * 128, :]), x_.b, writes=[x_.b])
                for j in range(4):
                    for kc in range(8):
                        P.op("pe", lambda e, j=j, kc=kc: e.matmul(pA[j].t[:, 0:512], lhsT=u_.t[:, kc, :], rhs=Wg.t[:, kc, j * 512:(j + 1) * 512],
                                                                  start=(kc == 0), stop=(kc == 7)),
                             reads=[u_.b, Wg.b], writes=[pA[j].b])
                    P.op("act", lambda e, j=j: e.activation(out=sig.t[:, j * 512:(j + 1) * 512], in_=pA[j].t[:, 0:512], func=AF.Sigmoid),
                         reads=[pA[j].b], writes=[sig.b])

            def MA2(ob):
                g_ = gTb[ob % 2]
                for j in range(2):
                    for kc in range(16):
                        P.op("pe", lambda e, j=j, kc=kc: e.matmul(pA[j].t[:, 0:512], lhsT=g_.t[:, kc, :], rhs=Wro.t[:, kc, j * 512:(j + 1) * 512],
                                                                  start=(kc == 0), stop=(kc == 15)),
                             reads=[g_.b, Wro.b], writes=[pA[j].b])
                    P.op("dve", lambda e, j=j: e.tensor_tensor(out=m1.t[:, j * 512:(j + 1) * 512], in0=pA[j].t[:, 0:512],
                                                               in1=sig.t[:, j * 512:(j + 1) * 512], op=ALU.mult),
                         reads=[pA[j].b, sig.b], writes=[m1.b])

            def MA3(ob):
                d_ = dTb[ob % 2]
                mb_ = mb[ob % 2]
                for j in range(2):
                    for kc in range(8):
                        P.op("pe", lambda e, j=j, kc=kc: e.matmul(pA[2 + j].t[:, 0:512], lhsT=d_.t[:, kc, :], rhs=Wdo.t[:, kc, j * 512:(j + 1) * 512],
                                                                  start=(kc == 0), stop=(kc == 7)),
                             reads=[d_.b, Wdo.b], writes=[pA[2 + j].b])
                    P.op("dve", lambda e, j=j: e.tensor_tensor(out=m2.t[:, j * 512:(j + 1) * 512], in0=pA[2 + j].t[:, 0:512],
                                                               in1=sig.t[:, 1024 + j * 512:1024 + (j + 1) * 512], op=ALU.mult),
                         reads=[pA[2 + j].b, sig.b], writes=[m2.b])
                P.op("pool", lambda e: e.tensor_tensor(out=mb_.t[:], in0=m1.t[:], in1=m2.t[:], op=ALU.add), reads=[m1.b, m2.b], writes=[mb_.b])

            def MB1(ob):
                mb_ = mb[ob % 2]
                for half in range(2):
                    pt = pTm[half]
                    for j in range(4):
                        kc = half * 4 + j
                        P.op("pe", lambda e, kc=kc, j=j, pt=pt: e.transpose(pt.t[:, j * 128:(j + 1) * 128], mb_.t[:, kc * 128:(kc + 1) * 128], ident.t[:]),
                             reads=[mb_.b, ident.b], writes=[pt.b])
                    if half == 0:
                        P.op("dve", lambda e, pt=pt: e.tensor_copy(mT.t[:, 0:4, :].rearrange("p a b -> p (a b)"), pt.t[:, 0:512]),
                             reads=[pt.b], writes=[mT.b])
                    else:
                        P.op("act", lambda e, pt=pt: e.copy(mT.t[:, 4:8, :].rearrange("p a b -> p (a b)"), pt.t[:, 0:512]),
                             reads=[pt.b], writes=[mT.b])

            def MB2(ob):
                h_ = h2[ob % 2]
                x_ = xb[ob % 2]
                for j in range(2):
                    for kc in range(8):
                        P.op("pe", lambda e, j=j, kc=kc: e.matmul(pB[j].t[:, 0:512], lhsT=mT.t[:, kc, :], rhs=Wou.t[:, kc, j * 512:(j + 1) * 512],
                                                                  start=(kc == 0), stop=(kc == 7)),
                             reads=[mT.b, Wou.b], writes=[pB[j].b])
                    P.op("dve", lambda e, j=j: e.tensor_tensor(out=h_.t[:, j * 512:(j + 1) * 512], in0=pB[j].t[:, 0:512],
                                                               in1=x_.t[:, j * 512:(j + 1) * 512], op=ALU.add),
                         reads=[pB[j].b, x_.b], writes=[h_.b])
                P.dma("pool", lambda e: e.dma_start(out=H2[ob * 128:(ob + 1) * 128, :], in_=h_.t[:]),
                      h_.b, reads=[h_.b], writes=[b_H2[ob]])
                norm_part(h_, n2w, sq, ss, rs, ub, use_pow=True)

            def MB3(ob):
                t_ = u2T[ob % 2]
                tr_part(ub, pTm, t_)
                P.dma("pool", lambda e: e.dma_start(out=U2T[ob], in_=t_.t[:].rearrange("p a b -> p (a b)")),
                      t_.b, reads=[t_.b], writes=[b_U2T[ob]])

            MA1(0); MA2(0); MA3(0)
            for ob in range(NO):
                nx = ob + 1
                MB1(ob)
                if nx < NO:
                    MA1(nx)
                MB2(ob)
                if nx < NO:
                    MA2(nx)
                MB3(ob)
                if nx < NO:
                    MA3(nx)
            P.emit()

        GB = 3
        NG = (NO + GB - 1) // GB
        with ExitStack() as st:
            Wup = sbt(st, "Wup", [128, 8, 2 * FFN], BF16)
            Wdn = sbt(st, "Wdn", [128, 22, D], BF16)
            cw = sbt(st, "cw", [128, 3, 44], F32)
            cb = sbt(st, "cb", [128, 44], F32)
            wup_v = w_up.rearrange("(k p) n -> p k n", p=128)
            for kc in range(8):
                for c0 in range(0, 2 * FFN, 1408):
                    P.dma("pool", lambda e, kc=kc, c0=c0: e.dma_start(out=Wup.t[:, kc, c0:c0 + 1408], in_=wup_v[:, kc, c0:c0 + 1408]),
                          Wup.b, writes=[Wup.b])
            wdn_v = w_down.rearrange("(k p) n -> p k n", p=128)
            for k0 in range(0, 22, 2):
                P.dma("pool", lambda e, k0=k0: e.dma_start(out=Wdn.t[:, k0:k0 + 2, :], in_=wdn_v[:, k0:k0 + 2, :]), Wdn.b, writes=[Wdn.b])
            for t0 in range(0, 44, 11):
                for k in range(3):
                    P.dma("sp", lambda e, k=k, t0=t0: e.dma_start(out=cw.t[:, k, t0:t0 + 11],
                                                                  in_=conv_w[k].rearrange("(t p) -> p t", p=128)[:, t0:t0 + 11],
                                                                  allow_slow_non_contiguous=True), cw.b, writes=[cw.b])
                P.dma("sp", lambda e, t0=t0: e.dma_start(out=cb.t[:, t0:t0 + 11], in_=conv_b.rearrange("(t p) -> p t", p=128)[:, t0:t0 + 11],
                                                         allow_slow_non_contiguous=True), cb.b, writes=[cb.b])
            NT = GB * 128
            u2g = [sbt(st, "u2g%d" % i, [128, 8, 2 + NT], BF16) for i in range(2)]
            ya = [sbt(st, "ya%d" % i, [128, NT], F32) for i in range(2)]
            yb = [sbt(st, "yb%d" % i, [128, NT], F32) for i in range(2)]
            sa = [sbt(st, "sa%d" % i, [128, NT], F32) for i in range(2)]
            gTt = sbt(st, "gTt", [128, 22, NT], BF16)
            hb = [sbt(st, "fh%d" % i, [128, D], F32) for i in range(2)]
            ob_ = [sbt(st, "fo%d" % i, [128, D], F32) for i in range(2)]
            pU = [pbank(st, "pU%d" % i) for i in range(4)]
            pD = [pbank(st, "pD%d" % i) for i in range(4)]
            P.op("pool", lambda e: e.memset(u2g[0].t[:], 0.0), writes=[u2g[0].b])
            P.op("pool", lambda e: e.memset(u2g[1].t[:], 0.0), writes=[u2g[1].b])
            ui = 0
            for gi in range(NG):
                blks = list(range(gi * GB, min(NO, (gi + 1) * GB)))
                nt = len(blks) * 128
                ug = u2g[gi % 2]
                up_ = u2g[(gi + 1) % 2]
                for j, ob in enumerate(blks):
                    P.dma("sp", lambda e, ug=ug, j=j, ob=ob: e.dma_start(out=ug.t[:, :, 2 + j * 128:2 + (j + 1) * 128],
                                                                        in_=U2T[ob].rearrange("p (a b) -> p a b", a=8)),
                          ug.b, reads=[b_U2T[ob]], writes=[ug.b])
                if gi > 0:
                    P.op("pool", lambda e, ug=ug, up_=up_: e.tensor_copy(ug.t[:, :, 0:2], up_.t[:, :, NT:NT + 2]),
                         reads=[up_.b], writes=[ug.b])
                for ft in range(22):
                    tiles = []
                    for which, fi in ((0, ft), (1, ft + 22)):
                        pu = pU[ui % 4]
                        ui += 1
                        for kc in range(8):
                            P.op("pe", lambda e, pu=pu, kc=kc, fi=fi, ug=ug, nt=nt: e.matmul(pu.t[:, 0:nt + 2], lhsT=Wup.t[:, kc, fi * 128:(fi + 1) * 128],
                                                                                        rhs=ug.t[:, kc, 0:nt + 2], start=(kc == 0), stop=(kc == 7)),
                                 reads=[Wup.b, ug.b], writes=[pu.b])
                        yt = (ya if which == 0 else yb)[ft % 2]
                        P.op("dve", lambda e, pu=pu, yt=yt, fi=fi, nt=nt: e.tensor_scalar(yt.t[:, 0:nt], pu.t[:, 2:nt + 2], cw.t[:, 2, fi:fi + 1], cb.t[:, fi:fi + 1],
                                                                                       op0=ALU.mult, op1=ALU.add),
                             reads=[pu.b, cw.b, cb.b], writes=[yt.b])
                        P.op("dve", lambda e, pu=pu, yt=yt, fi=fi, nt=nt: e.scalar_tensor_tensor(out=yt.t[:, 0:nt], in0=pu.t[:, 1:nt + 1], scalar=cw.t[:, 1, fi:fi + 1],
                                                                                              in1=yt.t[:, 0:nt], op0=ALU.mult, op1=ALU.add),
                             reads=[pu.b, cw.b, yt.b], writes=[yt.b])
                        P.op("dve", lambda e, pu=pu, yt=yt, fi=fi, nt=nt: e.scalar_tensor_tensor(out=yt.t[:, 0:nt], in0=pu.t[:, 0:nt], scalar=cw.t[:, 0, fi:fi + 1],
                                                                                              in1=yt.t[:, 0:nt], op0=ALU.mult, op1=ALU.add),
                             reads=[pu.b, cw.b, yt.b], writes=[yt.b])
                        tiles.append(yt)
                    s_ = sa[ft % 2]
                    P.op("act", lambda e, s_=s_, yt=tiles[0], nt=nt: e.activation(out=s_.t[:, 0:nt], in_=yt.t[:, 0:nt], func=AF.Silu),
                         reads=[tiles[0].b], writes=[s_.b])
                    P.op("pool", lambda e, s_=s_, yt=tiles[1], ft=ft, nt=nt: e.tensor_tensor(out=gTt.t[:, ft, 0:nt], in0=s_.t[:, 0:nt], in1=yt.t[:, 0:nt], op=ALU.mult),
                         reads=[s_.b, tiles[1].b], writes=[gTt.b])
                for j, ob in enumerate(blks):
                    h_ = hb[ob % 2]
                    o_ = ob_[ob % 2]
                    P.dma("sp", lambda e, h_=h_, ob=ob: e.dma_start(out=h_.t[:], in_=H2[ob * 128:(ob + 1) * 128, :]),
                          h_.b, reads=[b_H2[ob]], writes=[h_.b])
                    for half in range(2):
                        pd = pD[(ob * 2 + half) % 4]
                        for ft in range(22):
                            P.op("pe", lambda e, pd=pd, ft=ft, j=j, half=half: e.matmul(pd.t[:, 0:512], lhsT=gTt.t[:, ft, j * 128:(j + 1) * 128],
                                                                                    rhs=Wdn.t[:, ft, half * 512:(half + 1) * 512],
                                                                                    start=(ft == 0), stop=(ft == 21)),
                                 reads=[gTt.b, Wdn.b], writes=[pd.b])
                        P.op("dve", lambda e, pd=pd, half=half, h_=h_, o_=o_: e.tensor_tensor(out=o_.t[:, half * 512:(half + 1) * 512], in0=pd.t[:, 0:512],
                                                                                          in1=h_.t[:, half * 512:(half + 1) * 512], op=ALU.add),
                             reads=[pd.b, h_.b], writes=[o_.b])
                    P.dma("pool", lambda e, o_=o_, ob=ob: e.dma_start(out=y[ob * 128:(ob + 1) * 128, :], in_=o_.t[:]),
                          o_.b, reads=[o_.b], writes=[b_y])
            P.wait_all("pool", [b_y])
            P.emit()
        print("n_inst", P.n_inst, "n_wait", P.n_wait, "ndsem", P.ndsem)
    return nc


def make_tables(NC, NO, p, S):
    NB = NC + NO
    L = N_META + S
    if p == 0:
        ctx_pos = np.full(NC * 128, -1, np.int64)
        own_pos = np.arange(NO * 128)
    else:
        ctx_pos = np.arange(NC * 128) - PAD
        own_pos = L - NO * 128 + np.arange(NO * 128)
    pos = np.concatenate([ctx_pos, own_pos])
    valid = pos >= 0
    posf = np.where(valid, pos, 0).astype(np.float32)
    inv_r = np.power(np.float32(10000.0), -np.arange(128, dtype=np.float32) / np.float32(128))
    ang = posf[:, None] * inv_r[None, :]
    c, s = np.cos(ang), np.sin(ang)
    rope_r = np.concatenate([c, c, s, s], axis=1).astype(np.float32)
    inv_d = np.power(np.float32(500000.0), -np.arange(8, dtype=np.float32) / np.float32(8))
    ang = posf[:, None] * inv_d[None, :]
    c, s = np.cos(ang), np.sin(ang)
    rope_d = np.concatenate([c, c, s, s], axis=1).astype(np.float32)
    kb = np.where(valid, 1.0, 0.0).astype(np.float32).reshape(NB, 128).T.copy()
    idx = np.arange(128)
    cm = (idx[:, None] <= idx[None, :]).astype(np.float32)
    rdec = np.zeros((128, 8), np.float32)
    for h in range(4):
        rdec[:, h] = GAM[h] ** (idx + 1.0)
        rdec[:, 4 + h] = (256 ** -0.5) * GAM[h] ** (127.0 - idx)
    return rope_r, rope_d, kb, cm, rdec


_NC_CACHE = {}


def run(inputs, NC, NO, debug=False, trace=False):
    x = np.asarray(inputs["x"], np.float32)
    B, S, _ = x.shape
    assert S == 128 * (NC + NO - 1)
    L = N_META + S
    meta = np.asarray(inputs["meta_tokens"], np.float32)
    key = (NC, NO, debug)
    if key not in _NC_CACHE:
        _NC_CACHE[key] = build(NC, NO, debug)
    nc = _NC_CACHE[key]
    f = lambda k: np.ascontiguousarray(np.asarray(inputs[k], np.float32)[0])
    common = {
        "w_in": f("w_in"), "w_ret_o": f("w_ret_o"), "w_diff_o": f("w_diff_o"), "w_out": f("w_out"),
        "w_up": f("w_up"), "w_down": f("w_down"), "norm1_w": f("norm1_w"), "norm2_w": f("norm2_w"),
        "qk_norm_w": np.concatenate([f("q_norm_w"), f("q_norm_w"), f("k_norm_w"), f("k_norm_w")]),
        "lambdas": np.concatenate([f("lambda_q1"), f("lambda_k1"), f("lambda_q2"), f("lambda_k2")]),
        "subln_w": f("diff_subln_w"), "conv_w": f("conv_w"), "conv_b": f("conv_b"),
    }
    tabs = [make_tables(NC, NO, p, S) for p in range(2)]
    in_maps = []
    for b in range(B):
        seq = np.concatenate([meta, x[b]], axis=0)
        for p in range(2):
            if p == 0:
                xc_ = np.zeros((NC * 128, D), np.float32)
                xo_ = seq[0:NO * 128]
            else:
                xc_ = np.concatenate([np.zeros((PAD, D), np.float32), seq[0:NC * 128 - PAD]], axis=0)
                xo_ = seq[L - NO * 128:L]
            rr, rd, kb, cm, rdec = tabs[p]
            m = dict(common)
            m.update({"xc": np.ascontiguousarray(xc_), "xo": np.ascontiguousarray(xo_), "rope_r": rr, "rope_d": rd,
                      "kbias": kb, "cmask": cm, "rdec": rdec})
            in_maps.append(m)
    res = run_bass_kernel_spmd(nc, in_maps, core_ids=list(range(len(in_maps))), trace=trace)
    out = np.empty((B, S, D), np.float32)
    split = (NO * 128 - N_META) - 64
    for b in range(B):
        y0 = res.results[2 * b]["y"]
        y1 = res.results[2 * b + 1]["y"]
        out[b, :split] = y0[N_META:N_META + split]
        off1 = L - NO * 128
        out[b, split:] = y1[N_META + split - off1:]
    return out, res


def kernel(**inputs):
    out, _ = run(inputs, 32, 33)
    return out
```

```python
import math
import numpy as np
from contextlib import ExitStack
import concourse.bass as bass
import concourse.mybir as mybir
from concourse.bass_utils import run_bass_kernel_spmd

F32 = mybir.dt.float32
BF16 = mybir.dt.bfloat16
AF = mybir.ActivationFunctionType
ALU = mybir.AluOpType
AX = mybir.AxisListType

D = 1024
N_META = 16
PAD = 112
FFN = 2816
IN_COLS = 11264
EPS = 1e-6
C_RQ, C_RK, C_RV, C_RG, C_DQ, C_DK, C_DV, C_GT = 0, 1024, 2048, 4096, 6144, 7168, 8192, 9216
NEGB = -30000.0
LAM_INIT = 0.8 - 0.6 * math.exp(-0.3 * 0)
GAM = [1.0 - 2.0 ** (-5.0 - h) for h in range(4)]

SAME_ENGINE_SYNC = True


class Buf:
    __slots__ = ("name", "last_write", "reads", "dsem", "dcount")

    def __init__(self, name=""):
        self.name = name
        self.last_write = None
        self.reads = {}
        self.dsem = None
        self.dcount = 0


class T:
    def __init__(self, t, name):
        self.t = t
        self.b = Buf(name)


class Prog:
    ENGS = ("pe", "act", "dve", "pool", "sp")
    ENGOBJ = {"pe": "tensor", "act": "scalar", "dve": "vector", "pool": "gpsimd", "sp": "sync"}

    def __init__(self, nc, stack):
        self.nc = nc
        self.stack = stack
        self.q = {e: [] for e in self.ENGS}
        self.ecount = {e: 0 for e in self.ENGS}
        self.sems = {}
        for e in self.ENGS:
            self.sems[("e", e)] = stack.enter_context(nc.semaphore("s_" + e))
        self.waited = {e: {} for e in self.ENGS}
        self.ndsem = 0
        self.n_inst = 0
        self.n_wait = 0

    def _dsem(self, buf):
        if buf.dsem is None:
            buf.dsem = ("d", self.ndsem)
            self.sems[buf.dsem] = self.stack.enter_context(self.nc.semaphore("d%d" % self.ndsem))
            self.ndsem += 1
        return buf.dsem

    def _deps(self, eng, reads, writes):
        deps = {}

        def add(t):
            if t is None:
                return
            k, v = t
            if deps.get(k, -1) < v:
                deps[k] = v
        for b in reads:
            add(b.last_write)
        for b in writes:
            add(b.last_write)
            for k, v in b.reads.items():
                add((k, v))
        out = []
        w = self.waited[eng]
        for k, v in deps.items():
            if k == ("e", eng) and (eng == "pe" or not SAME_ENGINE_SYNC):
                continue
            if w.get(k, -1) >= v:
                continue
            w[k] = v
            out.append((k, v))
        return out

    def _commit(self, tok, reads, writes):
        k, v = tok
        for b in writes:
            b.last_write = tok
            b.reads = {}
        for b in reads:
            if b.reads.get(k, -1) < v:
                b.reads[k] = v

    def op(self, eng, fn, reads=(), writes=()):
        waits = self._deps(eng, reads, writes)
        self.ecount[eng] += 1
        tok = (("e", eng), self.ecount[eng])
        self.q[eng].append((waits, fn, tok[0], 1))
        self._commit(tok, reads, writes)
        self.n_inst += 1
        self.n_wait += len(waits)
        return tok

    def dma(self, eng, fn, sb, reads=(), writes=()):
        waits = self._deps(eng, reads, writes)
        k = self._dsem(sb)
        sb.dcount += 16
        tok = (k, sb.dcount)
        self.q[eng].append((waits, fn, k, 16))
        self._commit(tok, reads, writes)
        self.n_inst += 1
        self.n_wait += len(waits)
        return tok

    def wait_all(self, eng, bufs):
        waits = self._deps(eng, bufs, bufs)
        self.q[eng].append((waits, None, None, 0))

    def emit(self):
        nc = self.nc
        sems = self.sems
        with nc.Block() as block:
            for e in self.ENGS:
                lst = self.q[e]

                def body(eo, lst=lst):
                    for waits, fn, sk, inc in lst:
                        for k, v in waits:
                            eo.wait_ge(sems[k], v)
                        if fn is not None:
                            fn(eo).then_inc(sems[sk], inc)
                getattr(block, self.ENGOBJ[e])(body)
        self.q = {e: [] for e in self.ENGS}


def bc_mid(ap, n):
    return bass.AP(ap.tensor, ap.offset, [list(ap.ap[0]), [0, n], list(ap.ap[1])])


def bc_last(ap, k):
    return bass.AP(ap.tensor, ap.offset, [list(ap.ap[0]), list(ap.ap[1]), [0, k]])


def bc_part(dram_ap_1d, n):
    return bass.AP(dram_ap_1d.tensor, dram_ap_1d.offset, [[0, 128], [1, n]])


def build(NC, NO, debug=False):
    NB = NC + NO
    nc = bass.Bass("TRN2", target_bir_lowering=False)

    def din(name, shape, dt=F32):
        return nc.dram_tensor(name, list(shape), dt, kind="ExternalInput").ap()

    okind = "ExternalOutput" if debug else "Internal"

    def dscr(name, shape, dt):
        return nc.dram_tensor(name, list(shape), dt, kind=okind).ap()

    xc = din("xc", [NC * 128, D])
    xo = din("xo", [NO * 128, D])
    w_in = din("w_in", [D, IN_COLS])
    w_ret_o = din("w_ret_o", [2048, D])
    w_diff_o = din("w_diff_o", [D, D])
    w_out = din("w_out", [D, D])
    w_up = din("w_up", [D, 2 * FFN])
    w_down = din("w_down", [FFN, D])
    norm1_w = din("norm1_w", [D])
    norm2_w = din("norm2_w", [D])
    qk_norm_w = din("qk_norm_w", [256])
    lambdas = din("lambdas", [256])
    subln_w = din("subln_w", [128])
    conv_w = din("conv_w", [3, 2 * FFN])
    conv_b = din("conv_b", [2 * FFN])
    rope_r = din("rope_r", [NB * 128, 512])
    rope_d = din("rope_d", [NB * 128, 32])
    kbias_d = din("kbias", [128, NB])
    cmask_d = din("cmask", [128, 128])
    rdec_d = din("rdec", [128, 8])
    y = nc.dram_tensor("y", [NO * 128, D], F32, kind="ExternalOutput").ap()

    UT = dscr("UT", [NB, 128, 1024], BF16)
    GT = dscr("GT", [NO, 128, 2048], BF16)
    DOT = dscr("DOT", [NO, 128, 1024], BF16)
    H2 = dscr("H2", [NO * 128, D], F32)
    U2T = dscr("U2T", [NO, 128, 1024], BF16)
    b_UT = [Buf("UT%d" % i) for i in range(NB)]
    b_GT = [Buf("GT%d" % i) for i in range(NO)]
    b_DOT = [Buf("DOT%d" % i) for i in range(NO)]
    b_H2 = [Buf("H2%d" % i) for i in range(NO)]
    b_U2T = [Buf("U2T%d" % i) for i in range(NO)]
    b_y = Buf("y")

    w_in_v = w_in.rearrange("(k p) n -> p k n", p=128)

    with ExitStack() as gst:
        P = Prog(nc, gst)

        def sbt(st, name, shape, dt):
            return T(st.enter_context(nc.sbuf_tensor("sb_" + name, list(shape), dt)), name)

        def pbank(st, name, dt=F32):
            n = 512 if dt == F32 else 1024
            return T(st.enter_context(nc.psum_tensor("ps_" + name, [128, n], dt)), name)

        ident = sbt(gst, "ident", [128, 128], BF16)
        identf = sbt(gst, "identf", [128, 128], F32)
        cmask = sbt(gst, "cmask", [128, 128], F32)
        cmask2 = sbt(gst, "cmask2", [128, 2, 128], BF16)
        kbias = sbt(gst, "kbias", [128, NB], F32)
        rdec = sbt(gst, "rdec", [128, 8], F32)
        lam = sbt(gst, "lam", [128, 4], F32)
        lamv = sbt(gst, "lamv", [128, 256], F32)
        lamt = sbt(gst, "lamt", [128, 128], F32)
        lams = sbt(gst, "lams", [128, 2], F32)
        sublnw = sbt(gst, "sublnw", [128, 128], F32)
        wqk = sbt(gst, "wqk", [128, 4, 64], F32)

        P.op("pool", lambda e: e.iota(identf.t[:], pattern=[[1, 128]], base=0, channel_multiplier=-1,
                                      allow_small_or_imprecise_dtypes=True), writes=[identf.b])
        P.op("dve", lambda e: e.tensor_scalar(ident.t[:], identf.t[:], 0.0, None, op0=ALU.is_equal),
             reads=[identf.b], writes=[ident.b])
        P.dma("sp", lambda e: e.dma_start(out=cmask.t[:], in_=cmask_d), cmask.b, writes=[cmask.b])
        P.dma("sp", lambda e: e.dma_start(out=kbias.t[:], in_=kbias_d), kbias.b, writes=[kbias.b])
        P.dma("sp", lambda e: e.dma_start(out=rdec.t[:], in_=rdec_d), rdec.b, writes=[rdec.b])
        P.dma("sp", lambda e: e.dma_start(out=lamv.t[:], in_=bc_part(lambdas, 256)), lamv.b, writes=[lamv.b])
        P.dma("sp", lambda e: e.dma_start(out=sublnw.t[:], in_=bc_part(subln_w, 128)), sublnw.b, writes=[sublnw.b])
        P.dma("sp", lambda e: e.dma_start(out=wqk.t[:].rearrange("p a b -> p (a b)"), in_=bc_part(qk_norm_w, 256)),
              wqk.b, writes=[wqk.b])
        P.op("dve", lambda e: e.tensor_copy(cmask2.t[:, 0, :], cmask.t[:]), reads=[cmask.b], writes=[cmask2.b])
        P.op("dve", lambda e: e.tensor_copy(cmask2.t[:, 1, :], cmask.t[:]), reads=[cmask.b], writes=[cmask2.b])
        P.op("dve", lambda e: e.tensor_tensor(out=lamt.t[:, 0:64], in0=lamv.t[:, 0:64], in1=lamv.t[:, 64:128], op=ALU.mult),
             reads=[lamv.b], writes=[lamt.b])
        P.op("dve", lambda e: e.tensor_tensor(out=lamt.t[:, 64:128], in0=lamv.t[:, 128:192], in1=lamv.t[:, 192:256], op=ALU.mult),
             reads=[lamv.b], writes=[lamt.b])
        P.op("dve", lambda e: e.tensor_reduce(out=lams.t[:, 0:2], in_=lamt.t[:].rearrange("p (a b) -> p a b", a=2),
                                              axis=AX.X, op=ALU.add), reads=[lamt.b], writes=[lams.b])
        P.op("act", lambda e: e.activation(out=lams.t[:], in_=lams.t[:], func=AF.Exp), reads=[lams.b], writes=[lams.b])
        P.op("dve", lambda e: e.tensor_tensor(out=lam.t[:, 0:1], in0=lams.t[:, 0:1], in1=lams.t[:, 1:2], op=ALU.subtract),
             reads=[lams.b], writes=[lam.b])
        P.op("dve", lambda e: e.tensor_scalar(lam.t[:, 0:1], lam.t[:, 0:1], LAM_INIT, None, op0=ALU.add),
             reads=[lam.b], writes=[lam.b])
        P.op("dve", lambda e: e.tensor_scalar(sublnw.t[:], sublnw.t[:], 1.0 - LAM_INIT, None, op0=ALU.mult),
             reads=[sublnw.b], writes=[sublnw.b])

        def norm_part(xt, nw, sq, ss, rs, ub, use_pow=False):
            P.op("act", lambda e: e.activation(out=sq.t[:], in_=xt.t[:], func=AF.Square, accum_out=ss.t[:, 0:1]),
                 reads=[xt.b], writes=[sq.b, ss.b])
            if use_pow:
                P.op("dve", lambda e: e.tensor_scalar(rs.t[:, 0:1], ss.t[:, 0:1], 1.0 / D, EPS, op0=ALU.mult, op1=ALU.add),
                     reads=[ss.b], writes=[rs.b])
                P.op("pool", lambda e: e.tensor_tensor(out=rs.t[:, 0:1], in0=rs.t[:, 0:1], in1=mhalf.t[:, 0:1], op=ALU.pow),
                     reads=[rs.b, mhalf.b], writes=[rs.b])
            else:
                P.op("act", lambda e: e.activation(out=rs.t[:, 0:1], in_=ss.t[:, 0:1], func=AF.Sqrt, scale=1.0 / D, bias=eps_t.t[:, 0:1]),
                     reads=[ss.b, eps_t.b], writes=[rs.b])
                P.op("dve", lambda e: e.reciprocal(rs.t[:, 0:1], rs.t[:, 0:1]), reads=[rs.b], writes=[rs.b])
            P.op("dve", lambda e: e.scalar_tensor_tensor(out=ub.t[:], in0=xt.t[:], scalar=rs.t[:, 0:1], in1=nw.t[:],
                                                         op0=ALU.mult, op1=ALU.mult),
                 reads=[xt.b, rs.b, nw.b], writes=[ub.b])

        def tr_part(ub, pT, uT):
            for half in range(2):
                pt = pT[half]
                for j in range(4):
                    kc = half * 4 + j
                    P.op("pe", lambda e, kc=kc, j=j, pt=pt: e.transpose(pt.t[:, j * 128:(j + 1) * 128],
                                                                        ub.t[:, kc * 128:(kc + 1) * 128], ident.t[:]),
                         reads=[ub.b, ident.b], writes=[pt.b])
                if half == 0:
                    P.op("dve", lambda e, pt=pt: e.tensor_copy(uT.t[:, 0:4, :].rearrange("p a b -> p (a b)"), pt.t[:, 0:512]),
                         reads=[pt.b], writes=[uT.b])
                else:
                    P.op("act", lambda e, pt=pt: e.copy(uT.t[:, 4:8, :].rearrange("p a b -> p (a b)"), pt.t[:, 0:512]),
                         reads=[pt.b], writes=[uT.b])

        def norm_transpose(xt, nw, sq, ss, rs, ub, pT, uT):
            norm_part(xt, nw, sq, ss, rs, ub)
            tr_part(ub, pT, uT)

        eps_t = sbt(gst, "eps_t", [128, 1], F32)
        P.op("pool", lambda e: e.memset(eps_t.t[:], EPS), writes=[eps_t.b])
        mhalf = sbt(gst, "mhalf", [128, 8], F32)
        P.op("pool", lambda e: e.memset(mhalf.t[:], -0.5), writes=[mhalf.b])

        with ExitStack() as st:
            n1w = sbt(st, "n1w", [128, D], F32)
            P.dma("sp", lambda e: e.dma_start(out=n1w.t[:], in_=bc_part(norm1_w, D)), n1w.b, writes=[n1w.b])
            xb = [sbt(st, "x%d" % i, [128, D], F32) for i in range(3)]
            sq = [sbt(st, "sq%d" % i, [128, D], F32) for i in range(2)]
            ss = [sbt(st, "ss%d" % i, [128, 1], F32) for i in range(2)]
            rs = [sbt(st, "rs%d" % i, [128, 1], F32) for i in range(2)]
            ub = [sbt(st, "ub%d" % i, [128, D], BF16) for i in range(2)]
            uT = [sbt(st, "uT%d" % i, [128, 8, 128], BF16) for i in range(2)]
            pT = [pbank(st, "pT%d" % i, BF16) for i in range(4)]
            def p0_x(blk):
                src = xc[blk * 128:(blk + 1) * 128, :] if blk < NC else xo[(blk - NC) * 128:(blk - NC + 1) * 128, :]
                x_ = xb[blk % 3]
                P.dma("sp", lambda e: e.dma_start(out=x_.t[:], in_=src), x_.b, writes=[x_.b])
                norm_part(x_, n1w, sq[blk % 2], ss[blk % 2], rs[blk % 2], ub[blk % 2])

            def p0_y(blk):
                u_ = uT[blk % 2]
                tr_part(ub[blk % 2], pT[(blk % 2) * 2:(blk % 2) * 2 + 2], u_)
                P.dma("pool", lambda e: e.dma_start(out=UT[blk], in_=u_.t[:].rearrange("p a b -> p (a b)")),
                      u_.b, reads=[u_.b], writes=[b_UT[blk]])
            p0_x(0)
            for blk in range(NB):
                if blk + 1 < NB:
                    p0_x(blk + 1)
                p0_y(blk)
            P.emit()

        with ExitStack() as st:
            WR = [sbt(st, "WR%d" % i, [128, 8, 1536], BF16) for i in range(2)]
            uTb = [sbt(st, "ruT%d" % i, [128, 8, 128], BF16) for i in range(3)]
            RT = [sbt(st, "RT%d" % i, [128, 512], F32) for i in range(3)]
            Rf = sbt(st, "Rf", [128, 2, 512], F32)
            Rb = sbt(st, "Rb", [128, 2, 512], BF16)
            Aq = sbt(st, "Aq", [128, 256], F32)
            Bq = sbt(st, "Bq", [128, 256], F32)
            Ak = sbt(st, "Ak", [128, 256], F32)
            Bk = sbt(st, "Bk", [128, 256], F32)
            qr = [sbt(st, "qr%d" % i, [128, 256], BF16) for i in range(2)]
            kr = [sbt(st, "kr%d" % i, [128, 256], BF16) for i in range(2)]
            vb = [sbt(st, "vb%d" % i, [128, 512], BF16) for i in range(2)]
            sg = [sbt(st, "sg%d" % i, [128, 512], F32) for i in range(2)]
            qkT = sbt(st, "qkT", [128, 4, 128], BF16)
            Sm = sbt(st, "Sm", [128, 128], BF16)
            bst = sbt(st, "bst", [128, 6], F32)
            mv = sbt(st, "mv", [128, 2], F32)
            grs = sbt(st, "grs", [128, 1], F32)
            on = sbt(st, "on", [128, 512], F32)
            gtd = sbt(st, "gtd", [128, 512], BF16)
            gT = [sbt(st, "gT%d" % i, [128, 4, 128], BF16) for i in range(2)]
            pQK = pbank(st, "pQK")
            pV = pbank(st, "pV")
            pG = pbank(st, "pG")
            pTq = pbank(st, "pTq", BF16)
            pSv = pTq.t[:, 512:768].bitcast(F32)
            pTg = pbank(st, "pTg", BF16)
            pO2 = [pbank(st, "pO%d" % i) for i in range(2)]
            pR1 = pbank(st, "pR1")
            Rb2 = [sbt(st, "Rb%d" % i, [128, 2, 512], BF16) for i in range(2)]
            bst2 = [sbt(st, "bst%d" % i, [128, 6], F32) for i in range(2)]
            mv2 = [sbt(st, "mv%d" % i, [128, 2], F32) for i in range(2)]
            grs2 = [sbt(st, "grs%d" % i, [128, 1], F32) for i in range(2)]
            on2 = [sbt(st, "on%d" % i, [128, 512], F32) for i in range(2)]
            gtd2 = [sbt(st, "gtd%d" % i, [128, 512], BF16) for i in range(2)]

            def load_WR(h):
                w = WR[h % 2]
                for (c0, n, o0) in ((C_RQ + h * 256, 256, 0), (C_RK + h * 256, 256, 256),
                                    (C_RV + h * 512, 512, 512), (C_RG + h * 512, 512, 1024)):
                    P.dma("pool", lambda e, w=w, c0=c0, n=n, o0=o0: e.dma_start(out=w.t[:, :, o0:o0 + n],
                                                                                   in_=w_in_v[:, :, c0:c0 + n]),
                          w.b, writes=[w.b])
            load_WR(0)
            for h in range(4):
                if h + 1 < 4:
                    load_WR(h + 1)
                w = WR[h % 2]
                g = GAM[h]
                P.op("pool", lambda e: e.memset(Rf.t[:], 0.0), writes=[Rf.b])
                P.op("pool", lambda e: e.memset(Rb2[0].t[:], 0.0), writes=[Rb2[0].b])
                P.op("pool", lambda e: e.memset(Rb2[1].t[:], 0.0), writes=[Rb2[1].b])

                def A1(blk):
                    own = blk >= NC
                    u_ = uTb[blk % 3]
                    rt = RT[blk % 3]
                    k_ = kr[blk % 2]
                    q_ = qr[blk % 2]
                    P.dma("sp", lambda e: e.dma_start(out=u_.t[:].rearrange("p a b -> p (a b)"), in_=UT[blk]),
                          u_.b, reads=[b_UT[blk]], writes=[u_.b])
                    P.dma("sp", lambda e: e.dma_start(out=rt.t[:], in_=rope_r[blk * 128:(blk + 1) * 128, :]),
                          rt.b, writes=[rt.b])
                    c0 = 0 if own else 256
                    for kc in range(8):
                        P.op("pe", lambda e, kc=kc: e.matmul(pQK.t[:, c0:512], lhsT=u_.t[:, kc, :], rhs=w.t[:, kc, c0:512],
                                                             start=(kc == 0), stop=(kc == 7)),
                             reads=[u_.b, w.b], writes=[pQK.b])
                    P.op("dve", lambda e: e.scalar_tensor_tensor(out=Ak.t[:], in0=pQK.t[:, 256:512], scalar=rdec.t[:, 4 + h:5 + h],
                                                                 in1=rt.t[:, 0:256], op0=ALU.mult, op1=ALU.mult),
                         reads=[pQK.b, rdec.b, rt.b], writes=[Ak.b])
                    P.op("dve", lambda e: e.scalar_tensor_tensor(out=Bk.t[:], in0=pQK.t[:, 256:512], scalar=rdec.t[:, 4 + h:5 + h],
                                                                 in1=rt.t[:, 256:512], op0=ALU.mult, op1=ALU.mult),
                         reads=[pQK.b, rdec.b, rt.b], writes=[Bk.b])
                    if own:
                        P.op("dve", lambda e: e.scalar_tensor_tensor(out=Aq.t[:], in0=pQK.t[:, 0:256], scalar=rdec.t[:, h:h + 1],
                                                                     in1=rt.t[:, 0:256], op0=ALU.mult, op1=ALU.mult),
                             reads=[pQK.b, rdec.b, rt.b], writes=[Aq.b])
                        P.op("dve", lambda e: e.scalar_tensor_tensor(out=Bq.t[:], in0=pQK.t[:, 0:256], scalar=rdec.t[:, h:h + 1],
                                                                     in1=rt.t[:, 256:512], op0=ALU.mult, op1=ALU.mult),
                             reads=[pQK.b, rdec.b, rt.b], writes=[Bq.b])
                    P.op("dve", lambda e: e.tensor_tensor(out=k_.t[:, 0:128], in0=Ak.t[:, 0:128], in1=Bk.t[:, 128:256], op=ALU.subtract),
                         reads=[Ak.b, Bk.b], writes=[k_.b])
                    P.op("dve", lambda e: e.tensor_tensor(out=k_.t[:, 128:256], in0=Ak.t[:, 128:256], in1=Bk.t[:, 0:128], op=ALU.add),
                         reads=[Ak.b, Bk.b], writes=[k_.b])
                    if own:
                        P.op("dve", lambda e: e.tensor_tensor(out=q_.t[:, 0:128], in0=Aq.t[:, 0:128], in1=Bq.t[:, 128:256], op=ALU.subtract),
                             reads=[Aq.b, Bq.b], writes=[q_.b])
                        P.op("dve", lambda e: e.tensor_tensor(out=q_.t[:, 128:256], in0=Aq.t[:, 128:256], in1=Bq.t[:, 0:128], op=ALU.add),
                             reads=[Aq.b, Bq.b], writes=[q_.b])

                def A2(blk):
                    u_ = uTb[blk % 3]
                    v_ = vb[blk % 2]
                    for kc in range(8):
                        P.op("pe", lambda e, kc=kc: e.matmul(pV.t[:, 0:512], lhsT=u_.t[:, kc, :], rhs=w.t[:, kc, 512:1024],
                                                             start=(kc == 0), stop=(kc == 7)),
                             reads=[u_.b, w.b], writes=[pV.b])
                    P.op("act", lambda e: e.copy(v_.t[:], pV.t[:, 0:512]), reads=[pV.b], writes=[v_.b])

                def A3(blk):
                    if blk < NC:
                        return
                    u_ = uTb[blk % 3]
                    s_ = sg[blk % 2]
                    for kc in range(8):
                        P.op("pe", lambda e, kc=kc: e.matmul(pG.t[:, 0:512], lhsT=u_.t[:, kc, :], rhs=w.t[:, kc, 1024:1536],
                                                             start=(kc == 0), stop=(kc == 7)),
                             reads=[u_.b, w.b], writes=[pG.b])
                    P.op("act", lambda e: e.activation(out=s_.t[:], in_=pG.t[:, 0:512], func=AF.Silu), reads=[pG.b], writes=[s_.b])

                def B1(blk):
                    if blk < NC:
                        return
                    k_ = kr[blk % 2]
                    q_ = qr[blk % 2]
                    for j in range(4):
                        srcT = q_ if j < 2 else k_
                        c = j % 2
                        P.op("pe", lambda e, j=j, c=c, srcT=srcT: e.transpose(pTq.t[:, j * 128:(j + 1) * 128],
                                                                              srcT.t[:, c * 128:(c + 1) * 128], ident.t[:]),
                             reads=[srcT.b, ident.b], writes=[pTq.b])
                    P.op("act", lambda e: e.copy(qkT.t[:].rearrange("p a b -> p (a b)"), pTq.t[:, 0:512]),
                         reads=[pTq.b], writes=[qkT.b])

                def B2(blk):
                    if blk < NC:
                        return
                    for c in range(2):
                        P.op("pe", lambda e, c=c: e.matmul(pSv, lhsT=qkT.t[:, 2 + c, :], rhs=qkT.t[:, c, :],
                                                           start=(c == 0), stop=(c == 1)),
                             reads=[qkT.b], writes=[pTq.b])
                    P.op("dve", lambda e: e.scalar_tensor_tensor(out=Sm.t[:], in0=pSv, scalar=float(g ** -128.0),
                                                                 in1=cmask.t[:], op0=ALU.mult, op1=ALU.mult),
                         reads=[pTq.b, cmask.b], writes=[Sm.b])

                def B3a(blk):
                    if blk < NC:
                        return
                    v_ = vb[blk % 2]
                    po = pO2[blk % 2]
                    rb = Rb2[(blk - 1) % 2]
                    P.op("pe", lambda e: e.matmul(po.t[:, 0:512], lhsT=Sm.t[:], rhs=v_.t[:], start=True, stop=False),
                         reads=[Sm.b, v_.b], writes=[po.b])
                    for c in range(2):
                        P.op("pe", lambda e, c=c: e.matmul(po.t[:, 0:512], lhsT=qkT.t[:, c, :], rhs=rb.t[:, c, :],
                                                           start=False, stop=(c == 1)),
                             reads=[qkT.b, rb.b], writes=[po.b])

                def CH(blk):
                    if blk < NC:
                        return
                    par = blk % 2
                    po = pO2[par]
                    s_ = sg[par]
                    bst_, mv_, grs_, on_, gtd_ = bst2[par], mv2[par], grs2[par], on2[par], gtd2[par]
                    P.op("dve", lambda e: e.bn_stats(bst_.t[:], po.t[:, 0:512]), reads=[po.b], writes=[bst_.b])
                    P.op("dve", lambda e: e.bn_aggr(mv_.t[:], bst_.t[:]), reads=[bst_.b], writes=[mv_.b])
                    P.op("dve", lambda e: e.tensor_scalar(grs_.t[:], mv_.t[:, 1:2], EPS, None, op0=ALU.add),
                         reads=[mv_.b], writes=[grs_.b])
                    P.op("pool", lambda e: e.tensor_tensor(out=grs_.t[:], in0=grs_.t[:], in1=mhalf.t[:, 0:1], op=ALU.pow),
                         reads=[grs_.b, mhalf.b], writes=[grs_.b])
                    P.op("dve", lambda e: e.tensor_scalar(on_.t[:], po.t[:, 0:512], mv_.t[:, 0:1], grs_.t[:, 0:1],
                                                          op0=ALU.subtract, op1=ALU.mult),
                         reads=[po.b, mv_.b, grs_.b], writes=[on_.b])
                    P.op("dve", lambda e: e.tensor_tensor(out=gtd_.t[:], in0=on_.t[:], in1=s_.t[:], op=ALU.mult),
                         reads=[on_.b, s_.b], writes=[gtd_.b])

                def ST(blk, c):
                    if blk >= NB - 1:
                        return
                    k_ = kr[blk % 2]
                    v_ = vb[blk % 2]
                    P.op("pe", lambda e: e.matmul(pR1.t[:, 0:512], lhsT=k_.t[:, c * 128:(c + 1) * 128], rhs=v_.t[:],
                                                  start=True, stop=True),
                         reads=[k_.b, v_.b], writes=[pR1.b])
                    P.op("dve", lambda e: e.scalar_tensor_tensor(out=Rf.t[:, c, :], in0=Rf.t[:, c, :], scalar=float(g ** 128.0),
                                                                 in1=pR1.t[:, 0:512], op0=ALU.mult, op1=ALU.add),
                         reads=[Rf.b, pR1.b], writes=[Rf.b])
                    if c == 1:
                        rb = Rb2[blk % 2]
                        P.op("act", lambda e: e.copy(rb.t[:].rearrange("p a b -> p (a b)"), Rf.t[:].rearrange("p a b -> p (a b)")),
                             reads=[Rf.b], writes=[rb.b])

                def G(blk):
                    if blk < NC or blk >= NB:
                        return
                    ob = blk - NC
                    g_ = gT[ob % 2]
                    gtd_ = gtd2[blk % 2]
                    for j in range(4):
                        P.op("pe", lambda e, j=j: e.transpose(pTg.t[:, j * 128:(j + 1) * 128],
                                                              gtd_.t[:, j * 128:(j + 1) * 128], ident.t[:]),
                             reads=[gtd_.b, ident.b], writes=[pTg.b])
                    P.op("act", lambda e: e.copy(g_.t[:].rearrange("p a b -> p (a b)"), pTg.t[:, 0:512]),
                         reads=[pTg.b], writes=[g_.b])
                    P.dma("pool", lambda e: e.dma_start(out=GT[ob][:, h * 512:(h + 1) * 512],
                                                        in_=g_.t[:].rearrange("p a b -> p (a b)")),
                          g_.b, reads=[g_.b], writes=[b_GT[ob]])

                A1(0); A2(0); A3(0)
                for blk in range(NB):
                    nx = blk + 1
                    B1(blk)
                    if nx < NB:
                        A1(nx)
                    B2(blk)
                    if nx < NB:
                        A2(nx)
                    B3a(blk)
                    G(blk - 1)
                    ST(blk, 0)
                    if nx < NB:
                        A3(nx)
                    ST(blk, 1)
                    CH(blk)
                G(NB - 1)
                P.emit()

        KTs = dscr("KTs", [8, 128, NB * 128], BF16)
        QTs = dscr("QTs", [8, 128, NO * 128], BF16)
        VVs = dscr("VVs", [8, 128, NB, 128], BF16)
        b_KTs = Buf("KTs"); b_QTs = Buf("QTs"); b_VVs = Buf("VVs")
        KTs_w = KTs.rearrange("h p (b t) -> p h b t", t=128)
        QTs_w = QTs.rearrange("h p (b t) -> p h b t", t=128)
        VVs_w = VVs.rearrange("h p b e -> p h b e")
        with ExitStack() as st:
            WDa = sbt(st, "WDa", [128, 8, 3072], BF16)
            for k0 in range(0, 8, 2):
                P.dma("pool", lambda e, k0=k0: e.dma_start(out=WDa.t[:, k0:k0 + 2, :], in_=w_in_v[:, k0:k0 + 2, C_DQ:C_DQ + 3072]),
                      WDa.b, writes=[WDa.b])
            ropd = sbt(st, "ropd", [128, NB, 32], F32)
            ropd_v = rope_d.rearrange("(b p) c -> p b c", p=128)
            for b0 in range(0, NB, 16):
                b1 = min(NB, b0 + 16)
                P.dma("sp", lambda e, b0=b0, b1=b1: e.dma_start(out=ropd.t[:, b0:b1, :], in_=ropd_v[:, b0:b1, :]),
                      ropd.b, writes=[ropd.b])
            wq8 = sbt(st, "wq8", [128, 8, 64], F32)
            wk8 = sbt(st, "wk8", [128, 8, 64], F32)
            for g8 in range(8):
                P.op("pool", lambda e, g8=g8: e.tensor_copy(wq8.t[:, g8, :], wqk.t[:, 0, :]), reads=[wqk.b], writes=[wq8.b])
                P.op("pool", lambda e, g8=g8: e.tensor_copy(wk8.t[:, g8, :], wqk.t[:, 2, :]), reads=[wqk.b], writes=[wk8.b])
            uTb = [sbt(st, "duT%d" % i, [128, 8, 128], BF16) for i in range(3)]
            NCH = 4
            sqd = [sbt(st, "sqd%d" % i, [128, 8, 64], F32) for i in range(NCH)]
            ssd = [sbt(st, "ssd%d" % i, [128, 8], F32) for i in range(NCH)]
            rsd = [sbt(st, "rsd%d" % i, [128, 8], F32) for i in range(NCH)]
            xn = [[sbt(st, "xn%d_%d" % (pp, i), [128, 8, 64], F32) for i in range(NCH)] for pp in range(2)]
            xbq = [[sbt(st, "xbq%d_%d" % (pp, i), [128, 8, 64], BF16) for i in range(NCH)] for pp in range(2)]
            rc = [[sbt(st, "rc%d_%d" % (pp, i), [128, 8, 16], F32) for i in range(NCH)] for pp in range(2)]
            xw = [[sbt(st, "xw%d_%d" % (pp, i), [128, 8, 16], F32) for i in range(NCH)] for pp in range(2)]
            rsn = [[sbt(st, "rsn%d_%d" % (pp, i), [128, 8, 16], F32) for i in range(NCH)] for pp in range(2)]
            kst = [sbt(st, "kst%d" % i, [128, 8, 128], BF16) for i in range(2)]
            qst = [sbt(st, "qst%d" % i, [128, 8, 128], BF16) for i in range(2)]
            vst = [sbt(st, "vst%d" % i, [128, 8, 128], BF16) for i in range(2)]
            pq = [pbank(st, "pq%d" % i) for i in range(2)]
            pk = [pbank(st, "pk%d" % i) for i in range(2)]
            pvv = [pbank(st, "pvv%d" % i) for i in range(2)]
            pTk = pbank(st, "pTk", BF16)
            pTq = pbank(st, "pTq1", BF16)
            def mk_chains(blk):
                chains = []
                for half in range(2):
                    chains.append((pk[half], wk8, pTk, half, 1024 + half * 512))
                if blk >= NC:
                    for half in range(2):
                        chains.append((pq[half], wq8, pTq, half, half * 512))
                return chains

            def d1_early(blk):
                u_ = uTb[blk % 3]
                par = blk % 2
                P.dma("sp", lambda e: e.dma_start(out=u_.t[:].rearrange("p a b -> p (a b)"), in_=UT[blk]),
                      u_.b, reads=[b_UT[blk]], writes=[u_.b])
                chains = mk_chains(blk)
                for (pb, wt, ptT, half, c0) in chains:
                    for kc in range(8):
                        P.op("pe", lambda e, kc=kc, pb=pb, c0=c0: e.matmul(pb.t[:, 0:512], lhsT=u_.t[:, kc, :], rhs=WDa.t[:, kc, c0:c0 + 512],
                                                                           start=(kc == 0), stop=(kc == 7)),
                             reads=[u_.b, WDa.b], writes=[pb.b])
                for half in range(2):
                    for kc in range(8):
                        P.op("pe", lambda e, kc=kc, half=half: e.matmul(pvv[half].t[:, 0:512], lhsT=u_.t[:, kc, :],
                                                                       rhs=WDa.t[:, kc, 2048 + half * 512:2048 + (half + 1) * 512],
                                                                       start=(kc == 0), stop=(kc == 7)),
                             reads=[u_.b, WDa.b], writes=[pvv[half].b])
                nch = len(chains)
                pvw = [ch[0].t[:, 0:512].rearrange("p (a b) -> p a b", b=64) for ch in chains]
                for ci in range(nch):
                    P.op("act", lambda e, ci=ci: e.activation(out=sqd[ci].t[:], in_=pvw[ci], func=AF.Square),
                         reads=[chains[ci][0].b], writes=[sqd[ci].b])
                for ci in range(nch):
                    P.op("dve", lambda e, ci=ci: e.tensor_reduce(out=ssd[ci].t[:], in_=sqd[ci].t[:], axis=AX.X, op=ALU.add),
                         reads=[sqd[ci].b], writes=[ssd[ci].b])
                for ci in range(nch):
                    P.op("act", lambda e, ci=ci: e.activation(out=rsd[ci].t[:], in_=ssd[ci].t[:], func=AF.Sqrt, scale=1.0 / 64, bias=eps_t.t[:, 0:1]),
                         reads=[ssd[ci].b, eps_t.b], writes=[rsd[ci].b])
                v_ = vst[par]
                for half in range(2):
                    P.op("act", lambda e, half=half: e.copy(v_.t[:, half * 4:half * 4 + 4, :].rearrange("p a b -> p (a b)"), pvv[half].t[:, 0:512]),
                         reads=[pvv[half].b], writes=[v_.b])
                P.dma("sp", lambda e: e.dma_start(out=VVs_w[:, :, blk, :], in_=v_.t[:]), v_.b, reads=[v_.b], writes=[b_VVs])
                for ci in range(nch):
                    P.op("dve", lambda e, ci=ci: e.reciprocal(rsd[ci].t[:], rsd[ci].t[:]), reads=[rsd[ci].b], writes=[rsd[ci].b])
                for ci in range(nch):
                    P.op("dve", lambda e, ci=ci: e.tensor_tensor(out=xn[par][ci].t[:], in0=pvw[ci], in1=bc_last(rsd[ci].t[:], 64), op=ALU.mult),
                         reads=[chains[ci][0].b, rsd[ci].b], writes=[xn[par][ci].b])

            def d1_late(blk):
                par = blk % 2
                own = blk >= NC
                ob = blk - NC
                chains = mk_chains(blk)
                nch = len(chains)
                xn_, xb_, rc_, rsn_, xw_ = xn[par], xbq[par], rc[par], rsn[par], xw[par]
                for ci in range(nch):
                    eng = "pool"
                    wt = chains[ci][1]
                    P.op(eng, lambda e, ci=ci, wt=wt: e.tensor_tensor(out=xb_[ci].t[:], in0=xn_[ci].t[:], in1=wt.t[:], op=ALU.mult),
                         reads=[xn_[ci].b, wt.b], writes=[xb_[ci].b])
                    P.op(eng, lambda e, ci=ci, wt=wt: e.tensor_tensor(out=xw_[ci].t[:], in0=xn_[ci].t[:, :, 0:16], in1=wt.t[:, :, 0:16], op=ALU.mult),
                         reads=[xn_[ci].b, wt.b], writes=[xw_[ci].b])
                    P.op(eng, lambda e, ci=ci: e.tensor_tensor(out=rc_[ci].t[:], in0=xw_[ci].t[:],
                                                              in1=bc_mid(ropd.t[:, blk, 0:16], 8), op=ALU.mult),
                         reads=[xw_[ci].b, ropd.b], writes=[rc_[ci].b])
                    P.op(eng, lambda e, ci=ci: e.tensor_tensor(out=rsn_[ci].t[:], in0=xw_[ci].t[:],
                                                              in1=bc_mid(ropd.t[:, blk, 16:32], 8), op=ALU.mult),
                         reads=[xw_[ci].b, ropd.b], writes=[rsn_[ci].b])
                    P.op(eng, lambda e, ci=ci: e.tensor_tensor(out=xb_[ci].t[:, :, 0:8], in0=rc_[ci].t[:, :, 0:8], in1=rsn_[ci].t[:, :, 8:16], op=ALU.subtract),
                         reads=[rc_[ci].b, rsn_[ci].b], writes=[xb_[ci].b])
                    P.op(eng, lambda e, ci=ci: e.tensor_tensor(out=xb_[ci].t[:, :, 8:16], in0=rc_[ci].t[:, :, 8:16], in1=rsn_[ci].t[:, :, 0:8], op=ALU.add),
                         reads=[rc_[ci].b, rsn_[ci].b], writes=[xb_[ci].b])
                for ci in range(nch):
                    ptT, half = chains[ci][2], chains[ci][3]
                    for hh in range(4):
                        P.op("pe", lambda e, ci=ci, hh=hh, ptT=ptT, half=half: e.transpose(ptT.t[:, (half * 4 + hh) * 128:(half * 4 + hh + 1) * 128],
                                                                                          xb_[ci].t[:, 2 * hh:2 * hh + 2, :].rearrange("p a b -> p (a b)"),
                                                                                          ident.t[:]),
                             reads=[xb_[ci].b, ident.b], writes=[ptT.b])
                k_ = kst[par]
                P.op("dve", lambda e: e.tensor_copy(k_.t[:].rearrange("p a b -> p (a b)"), pTk.t[:, 0:1024]), reads=[pTk.b], writes=[k_.b])
                P.dma("sp", lambda e: e.dma_start(out=KTs_w[:, :, blk, :], in_=k_.t[:]), k_.b, reads=[k_.b], writes=[b_KTs])
                if own:
                    q_ = qst[par]
                    P.op("act", lambda e: e.copy(q_.t[:].rearrange("p a b -> p (a b)"), pTq.t[:, 0:1024]), reads=[pTq.b], writes=[q_.b])
                    P.dma("sp", lambda e: e.dma_start(out=QTs_w[:, :, ob, :], in_=q_.t[:]), q_.b, reads=[q_.b], writes=[b_QTs])

            d1_early(0)
            for blk in range(NB):
                if blk + 1 < NB:
                    d1_early(blk + 1)
                d1_late(blk)
            P.emit()

        with ExitStack() as st:
            KTb = [sbt(st, "KT%d" % i, [128, NB * 128], BF16) for i in range(2)]
            VVb = [sbt(st, "VV%d" % i, [128, NB, 130], BF16) for i in range(2)]
            QT2b = [sbt(st, "QT2%d" % i, [128, NO, 256], BF16) for i in range(2)]
            NPT = 6
            PT = [sbt(st, "PT%d" % i, [128, 4, 128], BF16) for i in range(NPT)]
            zz = sbt(st, "zz", [128, 2], F32)
            a1 = sbt(st, "a1", [128, 128], F32)
            aa = sbt(st, "aa", [128, 128], F32)
            asq = sbt(st, "asq", [128, 128], F32)
            ass = sbt(st, "ass", [128, 1], F32)
            ars = sbt(st, "ars", [128, 1], F32)
            dob = sbt(st, "dob", [128, 128], BF16)
            doT = [sbt(st, "doT%d" % i, [128, 128], BF16) for i in range(2)]
            pP = pbank(st, "pP")
            pTd = pbank(st, "pTd", BF16)
            pSd = [pbank(st, "pSd%d" % i) for i in range(2)]
            pO0 = [pbank(st, "pO0%d" % i) for i in range(2)]
            pO1 = [pbank(st, "pO1%d" % i) for i in range(2)]
            assert NC % 2 == 0
            for i2 in range(2):
                P.op("pool", lambda e, i2=i2: e.memset(VVb[i2].t[:], 0.0), writes=[VVb[i2].b])
                P.op("dve", lambda e, i2=i2: e.tensor_copy(VVb[i2].t[:, :, 128:129], kbias.t[:].rearrange("p (a b) -> p a b", b=1)),
                     reads=[kbias.b], writes=[VVb[i2].b])
                P.op("pool", lambda e, i2=i2: e.memset(QT2b[i2].t[:], 0.0), writes=[QT2b[i2].b])

            def load_head(h):
                kt, vv, q2 = KTb[h % 2], VVb[h % 2], QT2b[h % 2]
                P.dma("sp", lambda e: e.dma_start(out=kt.t[:], in_=KTs[h]), kt.b, reads=[b_KTs], writes=[kt.b])
                P.dma("sp", lambda e: e.dma_start(out=vv.t[:, :, 0:128], in_=VVs[h]), vv.b, reads=[b_VVs], writes=[vv.b])
                P.dma("sp", lambda e: e.dma_start(out=q2.t[0:64, :, 0:128], in_=QTs[h][0:64, :].rearrange("p (i t) -> p i t", t=128)),
                      q2.b, reads=[b_QTs], writes=[q2.b])
                P.dma("sp", lambda e: e.dma_start(out=q2.t[64:128, :, 128:256], in_=QTs[h][64:128, :].rearrange("p (i t) -> p i t", t=128)),
                      q2.b, reads=[b_QTs], writes=[q2.b])
            load_head(0)
            for h in range(8):
                if h + 1 < 8:
                    load_head(h + 1)
                KT, VV, QT2 = KTb[h % 2], VVb[h % 2], QT2b[h % 2]
                b_K = [KT.b] * NB
                b_Kv = VV.b
                b_Q = [QT2.b] * NO
                items = []
                for i in range(NO):
                    nk = NC + i + 1
                    for kb0 in range(0, nk, 2):
                        items.append((i, kb0, min(2, nk - kb0)))
                SKEW = 2
                pS3 = [pSd[0], pSd[1], pP]

                def qk_exp(n):
                    i, kb0, nb = items[n]
                    nk = NC + i + 1
                    ps = pS3[n % 3]
                    pt = PT[n % NPT]
                    for j in range(nb):
                        kb = kb0 + j
                        P.op("pe", lambda e, j=j, kb=kb: e.matmul(ps.t[:, j * 256:(j + 1) * 256], lhsT=KT.t[:, kb * 128:(kb + 1) * 128],
                                                                  rhs=QT2.t[:, i, :], start=True, stop=True),
                             reads=[b_K[kb], b_Q[i]], writes=[ps.b])
                    P.op("act", lambda e: e.activation(out=pt.t[:, 0:2 * nb, :].rearrange("p a b -> p (a b)"),
                                                       in_=ps.t[:, 0:256 * nb], func=AF.Exp, scale=0.125),
                         reads=[ps.b], writes=[pt.b])
                    if kb0 + nb == nk:
                        jl = nb - 1
                        P.op("pool", lambda e: e.tensor_tensor(out=pt.t[:, 2 * jl:2 * jl + 2, :], in0=pt.t[:, 2 * jl:2 * jl + 2, :],
                                                               in1=cmask2.t[:], op=ALU.mult),
                             reads=[pt.b, cmask2.b], writes=[pt.b])

                def pv(n):
                    i, kb0, nb = items[n]
                    nk = NC + i + 1
                    o0 = pO0[i % 2]
                    o1 = pO1[i % 2]
                    pt = PT[n % NPT]
                    for j in range(nb):
                        kb = kb0 + j
                        P.op("pe", lambda e, j=j, kb=kb: e.matmul(o0.t[:, 0:129], lhsT=pt.t[:, 2 * j, :], rhs=VV.t[:, kb, 0:129],
                                                                  start=(kb == 0), stop=(kb == nk - 1)),
                             reads=[pt.b, b_Kv], writes=[o0.b])
                        P.op("pe", lambda e, j=j, kb=kb: e.matmul(o1.t[:, 0:129], lhsT=pt.t[:, 2 * j + 1, :], rhs=VV.t[:, kb, 0:129],
                                                                  start=(kb == 0), stop=(kb == nk - 1)),
                             reads=[pt.b, b_Kv], writes=[o1.b])
                    if kb0 + nb == nk:
                        finalize(i, o0, o1)

                def finalize(i, o0, o1):
                    while pend_fin:
                        pend_fin.pop(0)[1]()
                    P.op("dve", lambda e, o0=o0: e.reciprocal(zz.t[:, 0:1], o0.t[:, 128:129]), reads=[o0.b], writes=[zz.b])
                    P.op("dve", lambda e, o1=o1: e.reciprocal(zz.t[:, 1:2], o1.t[:, 128:129]), reads=[o1.b], writes=[zz.b])
                    P.op("dve", lambda e: e.tensor_tensor(out=zz.t[:, 1:2], in0=zz.t[:, 1:2], in1=lam.t[:, 0:1], op=ALU.mult),
                         reads=[zz.b, lam.b], writes=[zz.b])
                    P.op("dve", lambda e, o1=o1: e.tensor_scalar(a1.t[:], o1.t[:, 0:128], zz.t[:, 1:2], None, op0=ALU.mult),
                         reads=[o1.b, zz.b], writes=[a1.b])
                    P.op("dve", lambda e, o0=o0: e.scalar_tensor_tensor(out=aa.t[:], in0=o0.t[:, 0:128], scalar=zz.t[:, 0:1], in1=a1.t[:],
                                                                        op0=ALU.mult, op1=ALU.subtract),
                         reads=[o0.b, zz.b, a1.b], writes=[aa.b])
                    finalize_b(i)
                    pend_fin.append((n_now[0] + 8, lambda: finalize_c(i)))

                def finalize_b(i):
                    P.op("dve", lambda e: e.tensor_tensor(out=asq.t[:], in0=aa.t[:], in1=aa.t[:], op=ALU.mult), reads=[aa.b], writes=[asq.b])
                    P.op("dve", lambda e: e.tensor_reduce(out=ass.t[:, 0:1], in_=asq.t[:], axis=AX.X, op=ALU.add), reads=[asq.b], writes=[ass.b])
                    P.op("dve", lambda e: e.tensor_scalar(ars.t[:], ass.t[:], 1.0 / 128, EPS, op0=ALU.mult, op1=ALU.add),
                         reads=[ass.b], writes=[ars.b])
                    P.op("pool", lambda e: e.tensor_tensor(out=ars.t[:], in0=ars.t[:], in1=mhalf.t[:, 0:1], op=ALU.pow),
                         reads=[ars.b, mhalf.b], writes=[ars.b])
                    P.op("dve", lambda e: e.scalar_tensor_tensor(out=dob.t[:], in0=aa.t[:], scalar=ars.t[:, 0:1], in1=sublnw.t[:],
                                                                 op0=ALU.mult, op1=ALU.mult),
                         reads=[aa.b, ars.b, sublnw.b], writes=[dob.b])

                def finalize_c(i):
                    P.op("pe", lambda e: e.transpose(pTd.t[:, 256:384], dob.t[:], ident.t[:]), reads=[dob.b, ident.b], writes=[pTd.b])
                    d_ = doT[i % 2]
                    P.op("dve", lambda e, d_=d_: e.tensor_copy(d_.t[:], pTd.t[:, 256:384]), reads=[pTd.b], writes=[d_.b])
                    P.dma("pool", lambda e, d_=d_, i=i: e.dma_start(out=DOT[i][:, h * 128:(h + 1) * 128], in_=d_.t[:]),
                          d_.b, reads=[d_.b], writes=[b_DOT[i]])
                pend_fin = []
                n_now = [0]
                for n in range(len(items) + SKEW + 10):
                    n_now[0] = n
                    if n < len(items):
                        qk_exp(n)
                    if 0 <= n - SKEW < len(items):
                        pv(n - SKEW)
                    while pend_fin and pend_fin[0][0] <= n:
                        pend_fin.pop(0)[1]()
                assert not pend_fin
                P.emit()

        with ExitStack() as st:
            Wro = sbt(st, "Wro", [128, 16, 1024], BF16)
            Wdo = sbt(st, "Wdo", [128, 8, 1024], BF16)
            Wou = sbt(st, "Wou", [128, 8, 1024], BF16)
            Wg = sbt(st, "Wg", [128, 8, 2048], BF16)
            n2w = sbt(st, "n2w", [128, D], F32)
            P.dma("sp", lambda e: e.dma_start(out=n2w.t[:], in_=bc_part(norm2_w, D)), n2w.b, writes=[n2w.b])
            wro_v = w_ret_o.rearrange("(k p) n -> p k n", p=128)
            for k0 in range(0, 16, 4):
                P.dma("pool", lambda e, k0=k0: e.dma_start(out=Wro.t[:, k0:k0 + 4, :], in_=wro_v[:, k0:k0 + 4, :]), Wro.b, writes=[Wro.b])
            wdo_v = w_diff_o.rearrange("(k p) n -> p k n", p=128)
            wou_v = w_out.rearrange("(k p) n -> p k n", p=128)
            for k0 in range(0, 8, 4):
                P.dma("pool", lambda e, k0=k0: e.dma_start(out=Wdo.t[:, k0:k0 + 4, :], in_=wdo_v[:, k0:k0 + 4, :]), Wdo.b, writes=[Wdo.b])
                P.dma("pool", lambda e, k0=k0: e.dma_start(out=Wou.t[:, k0:k0 + 4, :], in_=wou_v[:, k0:k0 + 4, :]), Wou.b, writes=[Wou.b])
            for k0 in range(0, 8, 2):
                P.dma("pool", lambda e, k0=k0: e.dma_start(out=Wg.t[:, k0:k0 + 2, :], in_=w_in_v[:, k0:k0 + 2, C_GT:C_GT + 2048]), Wg.b, writes=[Wg.b])
            gTb = [sbt(st, "mgT%d" % i, [128, 16, 128], BF16) for i in range(2)]
            dTb = [sbt(st, "mdT%d" % i, [128, 8, 128], BF16) for i in range(2)]
            uTb = [sbt(st, "muT%d" % i, [128, 8, 128], BF16) for i in range(2)]
            xb = [sbt(st, "mx%d" % i, [128, D], F32) for i in range(2)]
            sig = sbt(st, "sig", [128, 2048], F32)
            m1 = sbt(st, "m1", [128, D], F32)
            m2 = sbt(st, "m2", [128, D], F32)
            mb = [sbt(st, "mb%d" % i, [128, D], BF16) for i in range(2)]
            mT = sbt(st, "mT", [128, 8, 128], BF16)
            h2 = [sbt(st, "h2%d" % i, [128, D], F32) for i in range(2)]
            sq = sbt(st, "msq", [128, D], F32)
            ss = sbt(st, "mss", [128, 1], F32)
            rs = sbt(st, "mrs", [128, 1], F32)
            ub = sbt(st, "mub", [128, D], BF16)
            u2T = [sbt(st, "mu2T%d" % i, [128, 8, 128], BF16) for i in range(2)]
            pA = [pbank(st, "pA%d" % i) for i in range(4)]
            pB = [pbank(st, "pB%d" % i) for i in range(2)]
            pTm = [pbank(st, "pTm%d" % i, BF16) for i in range(2)]
            def MA1(ob):
                blk = NC + ob
                g_ = gTb[ob % 2]; d_ = dTb[ob % 2]; u_ = uTb[ob % 2]; x_ = xb[ob % 2]
                P.dma("sp", lambda e: e.dma_start(out=g_.t[:].rearrange("p a b -> p (a b)"), in_=GT[ob]),
                      g_.b, reads=[b_GT[ob]], writes=[g_.b])
                P.dma("sp", lambda e: e.dma_start(out=d_.t[:].rearrange("p a b -> p (a b)"), in_=DOT[ob]),
                      d_.b, reads=[b_DOT[ob]], writes=[d_.b])
                P.dma("sp", lambda e: e.dma_start(out=u_.t[:].rearrange("p a b -> p (a b)"), in_=UT[blk]),
                      u_.b, reads=[b_UT[blk]], writes=[u_.b])
                P.dma("sp", lambda e: e.dma_start(out=x_.t[:], in_=xo[ob * 128:(ob + 1) * 128, :]), x_.b, writes=[x_.b])
                for j in range(4):
                    for kc in range(8):
                        P.op("pe", lambda e, j=j, kc=kc: e.matmul(pA[j].t[:, 0:512], lhsT=u_.t[:, kc, :], rhs=Wg.t[:, kc, j * 512:(j + 1) * 512],
                                                                  start=(kc == 0), stop=(kc == 7)),
                             reads=[u_.b, Wg.b], writes=[pA[j].b])
                    P.op("act", lambda e, j=j: e.activation(out=sig.t[:, j * 512:(j + 1) * 512], in_=pA[j].t[:, 0:512], func=AF.Sigmoid),
                         reads=[pA[j].b], writes=[sig.b])

            def MA2(ob):
                g_ = gTb[ob % 2]
                for j in range(2):
                    for kc in range(16):
                        P.op("pe", lambda e, j=j, kc=kc: e.matmul(pA[j].t[:, 0:512], lhsT=g_.t[:, kc, :], rhs=Wro.t[:, kc, j * 512:(j + 1) * 512],
                                                                  start=(kc == 0), stop=(kc == 15)),
                             reads=[g_.b, Wro.b], writes=[pA[j].b])
                    P.op("dve", lambda e, j=j: e.tensor_tensor(out=m1.t[:, j * 512:(j + 1) * 512], in0=pA[j].t[:, 0:512],
                                                               in1=sig.t[:, j * 512:(j + 1) * 512], op=ALU.mult),
                         reads=[pA[j].b, sig.b], writes=[m1.b])

            def MA3(ob):
                d_ = dTb[ob % 2]
                mb_ = mb[ob % 2]
                for j in range(2):
                    for kc in range(8):
                        P.op("pe", lambda e, j=j, kc=kc: e.matmul(pA[2 + j].t[:, 0:512], lhsT=d_.t[:, kc, :], rhs=Wdo.t[:, kc, j * 512:(j + 1) * 512],
                                                                  start=(kc == 0), stop=(kc == 7)),
                             reads=[d_.b, Wdo.b], writes=[pA[2 + j].b])
                    P.op("dve", lambda e, j=j: e.tensor_tensor(out=m2.t[:, j * 512:(j + 1) * 512], in0=pA[2 + j].t[:, 0:512],
                                                               in1=sig.t[:, 1024 + j * 512:1024 + (j + 1) * 512], op=ALU.mult),
                         reads=[pA[2 + j].b, sig.b], writes=[m2.b])
                P.op("pool", lambda e: e.tensor_tensor(out=mb_.t[:], in0=m1.t[:], in1=m2.t[:], op=ALU.add), reads=[m1.b, m2.b], writes=[mb_.b])

            def MB1(ob):
                mb_ = mb[ob % 2]
                for half in range(2):
                    pt = pTm[half]
                    for j in range(4):
                        kc = half * 4 + j
                        P.op("pe", lambda e, kc=kc, j=j, pt=pt: e.transpose(pt.t[:, j * 128:(j + 1) * 128], mb_.t[:, kc * 128:(kc + 1) * 128], ident.t[:]),
                             reads=[mb_.b, ident.b], writes=[pt.b])
                    if half == 0:
                        P.op("dve", lambda e, pt=pt: e.tensor_copy(mT.t[:, 0:4, :].rearrange("p a b -> p (a b)"), pt.t[:, 0:512]),
                             reads=[pt.b], writes=[mT.b])
                    else:
                        P.op("act", lambda e, pt=pt: e.copy(mT.t[:, 4:8, :].rearrange("p a b -> p (a b)"), pt.t[:, 0:512]),
                             reads=[pt.b], writes=[mT.b])

            def MB2(ob):
                h_ = h2[ob % 2]
                x_ = xb[ob % 2]
                for j in range(2):
                    for kc in range(8):
                        P.op("pe", lambda e, j=j, kc=kc: e.matmul(pB[j].t[:, 0:512], lhsT=mT.t[:, kc, :], rhs=Wou.t[:, kc, j * 512:(j + 1) * 512],
                                                                  start=(kc == 0), stop=(kc == 7)),
                             reads=[mT.b, Wou.b], writes=[pB[j].b])
                    P.op("dve", lambda e, j=j: e.tensor_tensor(out=h_.t[:, j * 512:(j + 1) * 512], in0=pB[j].t[:, 0:512],
                                                               in1=x_.t[:, j * 512:(j + 1) * 512], op=ALU.add),
                         reads=[pB[j].b, x_.b], writes=[h_.b])
                P.dma("pool", lambda e: e.dma_start(out=H2[ob * 128:(ob + 1) * 128, :], in_=h_.t[:]),
                      h_.b, reads=[h_.b], writes=[b_H2[ob]])
                norm_part(h_, n2w, sq, ss, rs, ub, use_pow=True)

            def MB3(ob):
                t_ = u2T[ob % 2]
                tr_part(ub, pTm, t_)
                P.dma("pool", lambda e: e.dma_start(out=U2T[ob], in_=t_.t[:].rearrange("p a b -> p (a b)")),
                      t_.b, reads=[t_.b], writes=[b_U2T[ob]])

            MA1(0); MA2(0); MA3(0)
            for ob in range(NO):
                nx = ob + 1
                MB1(ob)
                if nx < NO:
                    MA1(nx)
                MB2(ob)
                if nx < NO:
                    MA2(nx)
                MB3(ob)
                if nx < NO:
                    MA3(nx)
            P.emit()

        GB = 3
        NG = (NO + GB - 1) // GB
        with ExitStack() as st:
            Wup = sbt(st, "Wup", [128, 8, 2 * FFN], BF16)
            Wdn = sbt(st, "Wdn", [128, 22, D], BF16)
            cw = sbt(st, "cw", [128, 3, 44], F32)
            cb = sbt(st, "cb", [128, 44], F32)
            wup_v = w_up.rearrange("(k p) n -> p k n", p=128)
            for kc in range(8):
                for c0 in range(0, 2 * FFN, 1408):
                    P.dma("pool", lambda e, kc=kc, c0=c0: e.dma_start(out=Wup.t[:, kc, c0:c0 + 1408], in_=wup_v[:, kc, c0:c0 + 1408]),
                          Wup.b, writes=[Wup.b])
            wdn_v = w_down.rearrange("(k p) n -> p k n", p=128)
            for k0 in range(0, 22, 2):
                P.dma("pool", lambda e, k0=k0: e.dma_start(out=Wdn.t[:, k0:k0 + 2, :], in_=wdn_v[:, k0:k0 + 2, :]), Wdn.b, writes=[Wdn.b])
            for t0 in range(0, 44, 11):
                for k in range(3):
                    P.dma("sp", lambda e, k=k, t0=t0: e.dma_start(out=cw.t[:, k, t0:t0 + 11],
                                                                  in_=conv_w[k].rearrange("(t p) -> p t", p=128)[:, t0:t0 + 11],
                                                                  allow_slow_non_contiguous=True), cw.b, writes=[cw.b])
                P.dma("sp", lambda e, t0=t0: e.dma_start(out=cb.t[:, t0:t0 + 11], in_=conv_b.rearrange("(t p) -> p t", p=128)[:, t0:t0 + 11],
                                                         allow_slow_non_contiguous=True), cb.b, writes=[cb.b])
            NT = GB * 128
            u2g = [sbt(st, "u2g%d" % i, [128, 8, 2 + NT], BF16) for i in range(2)]
            ya = [sbt(st, "ya%d" % i, [128, NT], F32) for i in range(2)]
            yb = [sbt(st, "yb%d" % i, [128, NT], F32) for i in range(2)]
            sa = [sbt(st, "sa%d" % i, [128, NT], F32) for i in range(2)]
            gTt2 = [sbt(st, "gTt%d" % i, [128, 22, NT], BF16) for i in range(2)]
            hb = [sbt(st, "fh%d" % i, [128, D], F32) for i in range(1)] * 2
            ob_ = [sbt(st, "fo%d" % i, [128, D], F32) for i in range(2)]
            pU = [pbank(st, "pU%d" % i) for i in range(4)]
            pD = [pbank(st, "pD%d" % i) for i in range(4)]
            P.op("pool", lambda e: e.memset(u2g[0].t[:], 0.0), writes=[u2g[0].b])
            P.op("pool", lambda e: e.memset(u2g[1].t[:], 0.0), writes=[u2g[1].b])
            ui = [0]

            def up_pair(gi, ft, ug, nt):
                gt = gTt2[gi % 2]
                tiles = []
                for which, fi in ((0, ft), (1, ft + 22)):
                    pu = pU[ui[0] % 4]
                    ui[0] += 1
                    for kc in range(8):
                        P.op("pe", lambda e, pu=pu, kc=kc, fi=fi: e.matmul(pu.t[:, 0:nt + 2], lhsT=Wup.t[:, kc, fi * 128:(fi + 1) * 128],
                                                                           rhs=ug.t[:, kc, 0:nt + 2], start=(kc == 0), stop=(kc == 7)),
                             reads=[Wup.b, ug.b], writes=[pu.b])
                    yt = (ya if which == 0 else yb)[ft % 2]
                    P.op("dve", lambda e, pu=pu, yt=yt, fi=fi: e.tensor_scalar(yt.t[:, 0:nt], pu.t[:, 2:nt + 2], cw.t[:, 2, fi:fi + 1], cb.t[:, fi:fi + 1],
                                                                               op0=ALU.mult, op1=ALU.add),
                         reads=[pu.b, cw.b, cb.b], writes=[yt.b])
                    P.op("dve", lambda e, pu=pu, yt=yt, fi=fi: e.scalar_tensor_tensor(out=yt.t[:, 0:nt], in0=pu.t[:, 1:nt + 1], scalar=cw.t[:, 1, fi:fi + 1],
                                                                                      in1=yt.t[:, 0:nt], op0=ALU.mult, op1=ALU.add),
                         reads=[pu.b, cw.b, yt.b], writes=[yt.b])
                    P.op("dve", lambda e, pu=pu, yt=yt, fi=fi: e.scalar_tensor_tensor(out=yt.t[:, 0:nt], in0=pu.t[:, 0:nt], scalar=cw.t[:, 0, fi:fi + 1],
                                                                                      in1=yt.t[:, 0:nt], op0=ALU.mult, op1=ALU.add),
                         reads=[pu.b, cw.b, yt.b], writes=[yt.b])
                    tiles.append(yt)
                s_ = sa[ft % 2]
                P.op("act", lambda e: e.activation(out=s_.t[:, 0:nt], in_=tiles[0].t[:, 0:nt], func=AF.Silu),
                     reads=[tiles[0].b], writes=[s_.b])
                P.op("pool", lambda e: e.tensor_tensor(out=gt.t[:, ft, 0:nt], in0=s_.t[:, 0:nt], in1=tiles[1].t[:, 0:nt], op=ALU.mult),
                     reads=[s_.b, tiles[1].b], writes=[gt.b])

            def down_unit(gi, j, ob, half):
                gt = gTt2[gi % 2]
                h_ = hb[ob % 2]
                o_ = ob_[ob % 2]
                if half == 0:
                    P.dma("sp", lambda e: e.dma_start(out=h_.t[:], in_=H2[ob * 128:(ob + 1) * 128, :]),
                          h_.b, reads=[b_H2[ob]], writes=[h_.b])
                pd = pD[(ob * 2 + half) % 4]
                for ft in range(22):
                    P.op("pe", lambda e, ft=ft: e.matmul(pd.t[:, 0:512], lhsT=gt.t[:, ft, j * 128:(j + 1) * 128],
                                                         rhs=Wdn.t[:, ft, half * 512:(half + 1) * 512],
                                                         start=(ft == 0), stop=(ft == 21)),
                         reads=[gt.b, Wdn.b], writes=[pd.b])
                P.op("dve", lambda e: e.tensor_tensor(out=o_.t[:, half * 512:(half + 1) * 512], in0=pd.t[:, 0:512],
                                                      in1=h_.t[:, half * 512:(half + 1) * 512], op=ALU.add),
                     reads=[pd.b, h_.b], writes=[o_.b])
                if half == 1:
                    P.dma("pool", lambda e: e.dma_start(out=y[ob * 128:(ob + 1) * 128, :], in_=o_.t[:]),
                          o_.b, reads=[o_.b], writes=[b_y])

            pending_down = []
            for gi in range(NG):
                blks = list(range(gi * GB, min(NO, (gi + 1) * GB)))
                nt = len(blks) * 128
                ug = u2g[gi % 2]
                up_ = u2g[(gi + 1) % 2]
                for j, ob in enumerate(blks):
                    P.dma("sp", lambda e, ug=ug, j=j, ob=ob: e.dma_start(out=ug.t[:, :, 2 + j * 128:2 + (j + 1) * 128],
                                                                        in_=U2T[ob].rearrange("p (a b) -> p a b", a=8)),
                          ug.b, reads=[b_U2T[ob]], writes=[ug.b])
                if gi > 0:
                    P.op("pool", lambda e, ug=ug, up_=up_: e.tensor_copy(ug.t[:, :, 0:2], up_.t[:, :, NT:NT + 2]),
                         reads=[up_.b], writes=[ug.b])
                nd = len(pending_down)
                slots = {int(round((k + 1) * 22.0 / (nd + 1))): k for k in range(nd)} if nd else {}
                for ft in range(22):
                    up_pair(gi, ft, ug, nt)
                    if (ft + 1) in slots:
                        down_unit(*pending_down[slots[ft + 1]])
                pending_down = [(gi, j, ob, half) for j, ob in enumerate(blks) for half in range(2)]
            for u in pending_down:
                down_unit(*u)
            P.wait_all("pool", [b_y])
            P.emit()
        print("n_inst", P.n_inst, "n_wait", P.n_wait, "ndsem", P.ndsem)
    return nc


def make_tables(NC, NO, p, S):
    NB = NC + NO
    L = N_META + S
    if p == 0:
        ctx_pos = np.full(NC * 128, -1, np.int64)
        own_pos = np.arange(NO * 128)
    else:
        ctx_pos = np.arange(NC * 128) - PAD
        own_pos = L - NO * 128 + np.arange(NO * 128)
    pos = np.concatenate([ctx_pos, own_pos])
    valid = pos >= 0
    posf = np.where(valid, pos, 0).astype(np.float32)
    inv_r = np.power(np.float32(10000.0), -np.arange(128, dtype=np.float32) / np.float32(128))
    ang = posf[:, None] * inv_r[None, :]
    c, s = np.cos(ang), np.sin(ang)
    rope_r = np.concatenate([c, c, s, s], axis=1).astype(np.float32)
    inv_d = np.power(np.float32(500000.0), -np.arange(8, dtype=np.float32) / np.float32(8))
    ang = posf[:, None] * inv_d[None, :]
    c, s = np.cos(ang), np.sin(ang)
    rope_d = np.concatenate([c, c, s, s], axis=1).astype(np.float32)
    kb = np.where(valid, 1.0, 0.0).astype(np.float32).reshape(NB, 128).T.copy()
    idx = np.arange(128)
    cm = (idx[:, None] <= idx[None, :]).astype(np.float32)
    rdec = np.zeros((128, 8), np.float32)
    for h in range(4):
        rdec[:, h] = GAM[h] ** (idx + 1.0)
        rdec[:, 4 + h] = (256 ** -0.5) * GAM[h] ** (127.0 - idx)
    return rope_r, rope_d, kb, cm, rdec


_NC_CACHE = {}


def run(inputs, NC, NO, debug=False, trace=False):
    x = np.asarray(inputs["x"], np.float32)
    B, S, _ = x.shape
    assert S == 128 * (NC + NO - 1)
    L = N_META + S
    meta = np.asarray(inputs["meta_tokens"], np.float32)
    key = (NC, NO, debug)
    if key not in _NC_CACHE:
        _NC_CACHE[key] = build(NC, NO, debug)
    nc = _NC_CACHE[key]
    f = lambda k: np.ascontiguousarray(np.asarray(inputs[k], np.float32)[0])
    common = {
        "w_in": f("w_in"), "w_ret_o": f("w_ret_o"), "w_diff_o": f("w_diff_o"), "w_out": f("w_out"),
        "w_up": f("w_up"), "w_down": f("w_down"), "norm1_w": f("norm1_w"), "norm2_w": f("norm2_w"),
        "qk_norm_w": np.concatenate([f("q_norm_w"), f("q_norm_w"), f("k_norm_w"), f("k_norm_w")]),
        "lambdas": np.concatenate([f("lambda_q1"), f("lambda_k1"), f("lambda_q2"), f("lambda_k2")]),
        "subln_w": f("diff_subln_w"), "conv_w": f("conv_w"), "conv_b": f("conv_b"),
    }
    tabs = [make_tables(NC, NO, p, S) for p in range(2)]
    in_maps = []
    for b in range(B):
        seq = np.concatenate([meta, x[b]], axis=0)
        for p in range(2):
            if p == 0:
                xc_ = np.zeros((NC * 128, D), np.float32)
                xo_ = seq[0:NO * 128]
            else:
                xc_ = np.concatenate([np.zeros((PAD, D), np.float32), seq[0:NC * 128 - PAD]], axis=0)
                xo_ = seq[L - NO * 128:L]
            rr, rd, kb, cm, rdec = tabs[p]
            m = dict(common)
            m.update({"xc": np.ascontiguousarray(xc_), "xo": np.ascontiguousarray(xo_), "rope_r": rr, "rope_d": rd,
                      "kbias": kb, "cmask": cm, "rdec": rdec})
            in_maps.append(m)
    res = run_bass_kernel_spmd(nc, in_maps, core_ids=list(range(len(in_maps))), trace=trace)
    out = np.empty((B, S, D), np.float32)
    split = (NO * 128 - N_META) - 64
    for b in range(B):
        y0 = res.results[2 * b]["y"]
        y1 = res.results[2 * b + 1]["y"]
        out[b, :split] = y0[N_META:N_META + split]
        off1 = L - NO * 128
        out[b, split:] = y1[N_META + split - off1:]
    return out, res


def kernel(**inputs):
    out, _ = run(inputs, 32, 33)
    return out
```

```python
import math
import numpy as np
from contextlib import ExitStack
import concourse.bass as bass
import concourse.mybir as mybir
from concourse.bass_utils import run_bass_kernel_spmd

F32 = mybir.dt.float32
BF16 = mybir.dt.bfloat16
AF = mybir.ActivationFunctionType
ALU = mybir.AluOpType
AX = mybir.AxisListType

D = 1024
N_META = 16
PAD = 112
FFN = 2816
IN_COLS = 11264
EPS = 1e-6
C_RQ, C_RK, C_RV, C_RG, C_DQ, C_DK, C_DV, C_GT = 0, 1024, 2048, 4096, 6144, 7168, 8192, 9216
NEGB = -30000.0
LAM_INIT = 0.8 - 0.6 * math.exp(-0.3 * 0)
GAM = [1.0 - 2.0 ** (-5.0 - h) for h in range(4)]

SAME_ENGINE_SYNC = True


class Buf:
    __slots__ = ("name", "last_write", "reads", "dsem", "dcount")

    def __init__(self, name=""):
        self.name = name
        self.last_write = None
        self.reads = {}
        self.dsem = None
        self.dcount = 0


class T:
    def __init__(self, t, name):
        self.t = t
        self.b = Buf(name)


class Prog:
    ENGS = ("pe", "act", "dve", "pool", "sp")
    ENGOBJ = {"pe": "tensor", "act": "scalar", "dve": "vector", "pool": "gpsimd", "sp": "sync"}

    def __init__(self, nc, stack):
        self.nc = nc
        self.stack = stack
        self.q = {e: [] for e in self.ENGS}
        self.ecount = {e: 0 for e in self.ENGS}
        self.sems = {}
        for e in self.ENGS:
            self.sems[("e", e)] = stack.enter_context(nc.semaphore("s_" + e))
        self.waited = {e: {} for e in self.ENGS}
        self.ndsem = 0
        self.n_inst = 0
        self.n_wait = 0

    def _dsem(self, buf):
        if buf.dsem is None:
            buf.dsem = ("d", self.ndsem)
            self.sems[buf.dsem] = self.stack.enter_context(self.nc.semaphore("d%d" % self.ndsem))
            self.ndsem += 1
        return buf.dsem

    def _deps(self, eng, reads, writes):
        deps = {}

        def add(t):
            if t is None:
                return
            k, v = t
            if deps.get(k, -1) < v:
                deps[k] = v
        for b in reads:
            add(b.last_write)
        for b in writes:
            add(b.last_write)
            for k, v in b.reads.items():
                add((k, v))
        out = []
        w = self.waited[eng]
        for k, v in deps.items():
            if k == ("e", eng) and (eng == "pe" or not SAME_ENGINE_SYNC):
                continue
            if w.get(k, -1) >= v:
                continue
            w[k] = v
            out.append((k, v))
        return out

    def _commit(self, tok, reads, writes):
        k, v = tok
        for b in writes:
            b.last_write = tok
            b.reads = {}
        for b in reads:
            if b.reads.get(k, -1) < v:
                b.reads[k] = v

    def op(self, eng, fn, reads=(), writes=()):
        waits = self._deps(eng, reads, writes)
        self.ecount[eng] += 1
        tok = (("e", eng), self.ecount[eng])
        self.q[eng].append((waits, fn, tok[0], 1))
        self._commit(tok, reads, writes)
        self.n_inst += 1
        self.n_wait += len(waits)
        return tok

    def dma(self, eng, fn, sb, reads=(), writes=()):
        waits = self._deps(eng, reads, writes)
        k = self._dsem(sb)
        sb.dcount += 16
        tok = (k, sb.dcount)
        self.q[eng].append((waits, fn, k, 16))
        self._commit(tok, reads, writes)
        self.n_inst += 1
        self.n_wait += len(waits)
        return tok

    def wait_all(self, eng, bufs):
        waits = self._deps(eng, bufs, bufs)
        self.q[eng].append((waits, None, None, 0))

    def emit(self):
        nc = self.nc
        sems = self.sems
        with nc.Block() as block:
            for e in self.ENGS:
                lst = self.q[e]

                def body(eo, lst=lst):
                    for waits, fn, sk, inc in lst:
                        for k, v in waits:
                            eo.wait_ge(sems[k], v)
                        if fn is not None:
                            fn(eo).then_inc(sems[sk], inc)
                getattr(block, self.ENGOBJ[e])(body)
        self.q = {e: [] for e in self.ENGS}


def bc_mid(ap, n):
    return bass.AP(ap.tensor, ap.offset, [list(ap.ap[0]), [0, n], list(ap.ap[1])])


def bc_last(ap, k):
    return bass.AP(ap.tensor, ap.offset, [list(ap.ap[0]), list(ap.ap[1]), [0, k]])


def bc_part(dram_ap_1d, n):
    return bass.AP(dram_ap_1d.tensor, dram_ap_1d.offset, [[0, 128], [1, n]])


def build(NC, NO, debug=False):
    NB = NC + NO
    nc = bass.Bass("TRN2", target_bir_lowering=False)

    def din(name, shape, dt=F32):
        return nc.dram_tensor(name, list(shape), dt, kind="ExternalInput").ap()

    okind = "ExternalOutput" if debug else "Internal"

    def dscr(name, shape, dt):
        return nc.dram_tensor(name, list(shape), dt, kind=okind).ap()

    xc = din("xc", [NC * 128, D])
    xo = din("xo", [NO * 128, D])
    w_in = din("w_in", [D, IN_COLS])
    w_ret_o = din("w_ret_o", [2048, D])
    w_diff_o = din("w_diff_o", [D, D])
    w_out = din("w_out", [D, D])
    w_up = din("w_up", [D, 2 * FFN])
    w_down = din("w_down", [FFN, D])
    norm1_w = din("norm1_w", [D])
    norm2_w = din("norm2_w", [D])
    qk_norm_w = din("qk_norm_w", [256])
    lambdas = din("lambdas", [256])
    subln_w = din("subln_w", [128])
    conv_w = din("conv_w", [3, 2 * FFN])
    conv_b = din("conv_b", [2 * FFN])
    rope_r = din("rope_r", [NB * 128, 512])
    rope_d = din("rope_d", [NB * 128, 32])
    kbias_d = din("kbias", [128, NB])
    cmask_d = din("cmask", [128, 128])
    rdec_d = din("rdec", [128, 8])
    y = nc.dram_tensor("y", [NO * 128, D], F32, kind="ExternalOutput").ap()

    UT = dscr("UT", [NB, 128, 1024], BF16)
    GT = dscr("GT", [NO, 128, 2048], BF16)
    DOT = dscr("DOT", [NO, 128, 1024], BF16)
    H2 = dscr("H2", [NO * 128, D], F32)
    U2T = dscr("U2T", [NO, 128, 1024], BF16)
    b_UT = [Buf("UT%d" % i) for i in range(NB)]
    b_GT = [Buf("GT%d" % i) for i in range(NO)]
    b_DOT = [Buf("DOT%d" % i) for i in range(NO)]
    b_H2 = [Buf("H2%d" % i) for i in range(NO)]
    b_U2T = [Buf("U2T%d" % i) for i in range(NO)]
    b_y = Buf("y")

    w_in_v = w_in.rearrange("(k p) n -> p k n", p=128)

    with ExitStack() as gst:
        P = Prog(nc, gst)

        def sbt(st, name, shape, dt):
            return T(st.enter_context(nc.sbuf_tensor("sb_" + name, list(shape), dt)), name)

        def pbank(st, name, dt=F32):
            n = 512 if dt == F32 else 1024
            return T(st.enter_context(nc.psum_tensor("ps_" + name, [128, n], dt)), name)

        ident = sbt(gst, "ident", [128, 128], BF16)
        identf = sbt(gst, "identf", [128, 128], F32)
        cmask = sbt(gst, "cmask", [128, 128], F32)
        cmask2 = sbt(gst, "cmask2", [128, 2, 128], BF16)
        kbias = sbt(gst, "kbias", [128, NB], F32)
        rdec = sbt(gst, "rdec", [128, 8], F32)
        lam = sbt(gst, "lam", [128, 4], F32)
        lamv = sbt(gst, "lamv", [128, 256], F32)
        lamt = sbt(gst, "lamt", [128, 128], F32)
        lams = sbt(gst, "lams", [128, 2], F32)
        sublnw = sbt(gst, "sublnw", [128, 128], F32)
        wqk = sbt(gst, "wqk", [128, 4, 64], F32)

        P.op("pool", lambda e: e.iota(identf.t[:], pattern=[[1, 128]], base=0, channel_multiplier=-1,
                                      allow_small_or_imprecise_dtypes=True), writes=[identf.b])
        P.op("dve", lambda e: e.tensor_scalar(ident.t[:], identf.t[:], 0.0, None, op0=ALU.is_equal),
             reads=[identf.b], writes=[ident.b])
        P.dma("sp", lambda e: e.dma_start(out=cmask.t[:], in_=cmask_d), cmask.b, writes=[cmask.b])
        P.dma("sp", lambda e: e.dma_start(out=kbias.t[:], in_=kbias_d), kbias.b, writes=[kbias.b])
        P.dma("sp", lambda e: e.dma_start(out=rdec.t[:], in_=rdec_d), rdec.b, writes=[rdec.b])
        P.dma("sp", lambda e: e.dma_start(out=lamv.t[:], in_=bc_part(lambdas, 256)), lamv.b, writes=[lamv.b])
        P.dma("sp", lambda e: e.dma_start(out=sublnw.t[:], in_=bc_part(subln_w, 128)), sublnw.b, writes=[sublnw.b])
        P.dma("sp", lambda e: e.dma_start(out=wqk.t[:].rearrange("p a b -> p (a b)"), in_=bc_part(qk_norm_w, 256)),
              wqk.b, writes=[wqk.b])
        P.op("dve", lambda e: e.tensor_copy(cmask2.t[:, 0, :], cmask.t[:]), reads=[cmask.b], writes=[cmask2.b])
        P.op("dve", lambda e: e.tensor_copy(cmask2.t[:, 1, :], cmask.t[:]), reads=[cmask.b], writes=[cmask2.b])
        P.op("dve", lambda e: e.tensor_tensor(out=lamt.t[:, 0:64], in0=lamv.t[:, 0:64], in1=lamv.t[:, 64:128], op=ALU.mult),
             reads=[lamv.b], writes=[lamt.b])
        P.op("dve", lambda e: e.tensor_tensor(out=lamt.t[:, 64:128], in0=lamv.t[:, 128:192], in1=lamv.t[:, 192:256], op=ALU.mult),
             reads=[lamv.b], writes=[lamt.b])
        P.op("dve", lambda e: e.tensor_reduce(out=lams.t[:, 0:2], in_=lamt.t[:].rearrange("p (a b) -> p a b", a=2),
                                              axis=AX.X, op=ALU.add), reads=[lamt.b], writes=[lams.b])
        P.op("act", lambda e: e.activation(out=lams.t[:], in_=lams.t[:], func=AF.Exp), reads=[lams.b], writes=[lams.b])
        P.op("dve", lambda e: e.tensor_tensor(out=lam.t[:, 0:1], in0=lams.t[:, 0:1], in1=lams.t[:, 1:2], op=ALU.subtract),
             reads=[lams.b], writes=[lam.b])
        P.op("dve", lambda e: e.tensor_scalar(lam.t[:, 0:1], lam.t[:, 0:1], LAM_INIT, None, op0=ALU.add),
             reads=[lam.b], writes=[lam.b])
        P.op("dve", lambda e: e.tensor_scalar(sublnw.t[:], sublnw.t[:], 1.0 - LAM_INIT, None, op0=ALU.mult),
             reads=[sublnw.b], writes=[sublnw.b])

        def norm_part(xt, nw, sq, ss, rs, ub, use_pow=False):
            P.op("act", lambda e: e.activation(out=sq.t[:], in_=xt.t[:], func=AF.Square, accum_out=ss.t[:, 0:1]),
                 reads=[xt.b], writes=[sq.b, ss.b])
            if use_pow:
                P.op("dve", lambda e: e.tensor_scalar(rs.t[:, 0:1], ss.t[:, 0:1], 1.0 / D, EPS, op0=ALU.mult, op1=ALU.add),
                     reads=[ss.b], writes=[rs.b])
                P.op("pool", lambda e: e.tensor_tensor(out=rs.t[:, 0:1], in0=rs.t[:, 0:1], in1=mhalf.t[:, 0:1], op=ALU.pow),
                     reads=[rs.b, mhalf.b], writes=[rs.b])
            else:
                P.op("act", lambda e: e.activation(out=rs.t[:, 0:1], in_=ss.t[:, 0:1], func=AF.Sqrt, scale=1.0 / D, bias=eps_t.t[:, 0:1]),
                     reads=[ss.b, eps_t.b], writes=[rs.b])
                P.op("dve", lambda e: e.reciprocal(rs.t[:, 0:1], rs.t[:, 0:1]), reads=[rs.b], writes=[rs.b])
            P.op("dve", lambda e: e.scalar_tensor_tensor(out=ub.t[:], in0=xt.t[:], scalar=rs.t[:, 0:1], in1=nw.t[:],
                                                         op0=ALU.mult, op1=ALU.mult),
                 reads=[xt.b, rs.b, nw.b], writes=[ub.b])

        def tr_part(ub, pT, uT):
            for half in range(2):
                pt = pT[half]
                for j in range(4):
                    kc = half * 4 + j
                    P.op("pe", lambda e, kc=kc, j=j, pt=pt: e.transpose(pt.t[:, j * 128:(j + 1) * 128],
                                                                        ub.t[:, kc * 128:(kc + 1) * 128], ident.t[:]),
                         reads=[ub.b, ident.b], writes=[pt.b])
                if half == 0:
                    P.op("dve", lambda e, pt=pt: e.tensor_copy(uT.t[:, 0:4, :].rearrange("p a b -> p (a b)"), pt.t[:, 0:512]),
                         reads=[pt.b], writes=[uT.b])
                else:
                    P.op("act", lambda e, pt=pt: e.copy(uT.t[:, 4:8, :].rearrange("p a b -> p (a b)"), pt.t[:, 0:512]),
                         reads=[pt.b], writes=[uT.b])

        def norm_transpose(xt, nw, sq, ss, rs, ub, pT, uT):
            norm_part(xt, nw, sq, ss, rs, ub)
            tr_part(ub, pT, uT)

        eps_t = sbt(gst, "eps_t", [128, 1], F32)
        P.op("pool", lambda e: e.memset(eps_t.t[:], EPS), writes=[eps_t.b])
        mhalf = sbt(gst, "mhalf", [128, 8], F32)
        P.op("pool", lambda e: e.memset(mhalf.t[:], -0.5), writes=[mhalf.b])

        with ExitStack() as st:
            n1w = sbt(st, "n1w", [128, D], F32)
            P.dma("sp", lambda e: e.dma_start(out=n1w.t[:], in_=bc_part(norm1_w, D)), n1w.b, writes=[n1w.b])
            xb = [sbt(st, "x%d" % i, [128, D], F32) for i in range(3)]
            sq = [sbt(st, "sq%d" % i, [128, D], F32) for i in range(2)]
            ss = [sbt(st, "ss%d" % i, [128, 1], F32) for i in range(2)]
            rs = [sbt(st, "rs%d" % i, [128, 1], F32) for i in range(2)]
            ub = [sbt(st, "ub%d" % i, [128, D], BF16) for i in range(2)]
            uT = [sbt(st, "uT%d" % i, [128, 8, 128], BF16) for i in range(2)]
            pT = [pbank(st, "pT%d" % i, BF16) for i in range(4)]
            def p0_x(blk):
                src = xc[blk * 128:(blk + 1) * 128, :] if blk < NC else xo[(blk - NC) * 128:(blk - NC + 1) * 128, :]
                x_ = xb[blk % 3]
                P.dma("sp", lambda e: e.dma_start(out=x_.t[:], in_=src), x_.b, writes=[x_.b])
                norm_part(x_, n1w, sq[blk % 2], ss[blk % 2], rs[blk % 2], ub[blk % 2])

            def p0_y(blk):
                u_ = uT[blk % 2]
                tr_part(ub[blk % 2], pT[(blk % 2) * 2:(blk % 2) * 2 + 2], u_)
                P.dma("pool", lambda e: e.dma_start(out=UT[blk], in_=u_.t[:].rearrange("p a b -> p (a b)")),
                      u_.b, reads=[u_.b], writes=[b_UT[blk]])
            p0_x(0)
            for blk in range(NB):
                if blk + 1 < NB:
                    p0_x(blk + 1)
                p0_y(blk)
            P.emit()

        with ExitStack() as st:
            WR = [sbt(st, "WR%d" % i, [128, 8, 1536], BF16) for i in range(2)]
            uTb = [sbt(st, "ruT%d" % i, [128, 8, 128], BF16) for i in range(3)]
            RT = [sbt(st, "RT%d" % i, [128, 512], F32) for i in range(3)]
            Rf = sbt(st, "Rf", [128, 2, 512], F32)
            Rb = sbt(st, "Rb", [128, 2, 512], BF16)
            Aq = sbt(st, "Aq", [128, 256], F32)
            Bq = sbt(st, "Bq", [128, 256], F32)
            Ak = sbt(st, "Ak", [128, 256], F32)
            Bk = sbt(st, "Bk", [128, 256], F32)
            qr = [sbt(st, "qr%d" % i, [128, 256], BF16) for i in range(2)]
            kr = [sbt(st, "kr%d" % i, [128, 256], BF16) for i in range(2)]
            vb = [sbt(st, "vb%d" % i, [128, 512], BF16) for i in range(2)]
            sg = [sbt(st, "sg%d" % i, [128, 512], F32) for i in range(2)]
            qkT = sbt(st, "qkT", [128, 4, 128], BF16)
            Sm = sbt(st, "Sm", [128, 128], BF16)
            bst = sbt(st, "bst", [128, 6], F32)
            mv = sbt(st, "mv", [128, 2], F32)
            grs = sbt(st, "grs", [128, 1], F32)
            on = sbt(st, "on", [128, 512], F32)
            gtd = sbt(st, "gtd", [128, 512], BF16)
            gT = [sbt(st, "gT%d" % i, [128, 4, 128], BF16) for i in range(2)]
            pQK = pbank(st, "pQK")
            pV = pbank(st, "pV")
            pG = pbank(st, "pG")
            pTq = pbank(st, "pTq", BF16)
            pSv = pTq.t[:, 512:768].bitcast(F32)
            pTg = pbank(st, "pTg", BF16)
            pO2 = [pbank(st, "pO%d" % i) for i in range(2)]
            pR1 = pbank(st, "pR1")
            Rb2 = [sbt(st, "Rb%d" % i, [128, 2, 512], BF16) for i in range(2)]
            bst2 = [sbt(st, "bst%d" % i, [128, 6], F32) for i in range(2)]
            mv2 = [sbt(st, "mv%d" % i, [128, 2], F32) for i in range(2)]
            grs2 = [sbt(st, "grs%d" % i, [128, 1], F32) for i in range(2)]
            on2 = [sbt(st, "on%d" % i, [128, 512], F32) for i in range(2)]
            gtd2 = [sbt(st, "gtd%d" % i, [128, 512], BF16) for i in range(2)]

            def load_WR(h):
                w = WR[h % 2]
                for (c0, n, o0) in ((C_RQ + h * 256, 256, 0), (C_RK + h * 256, 256, 256),
                                    (C_RV + h * 512, 512, 512), (C_RG + h * 512, 512, 1024)):
                    P.dma("pool", lambda e, w=w, c0=c0, n=n, o0=o0: e.dma_start(out=w.t[:, :, o0:o0 + n],
                                                                                   in_=w_in_v[:, :, c0:c0 + n]),
                          w.b, writes=[w.b])
            load_WR(0)
            for h in range(4):
                if h + 1 < 4:
                    load_WR(h + 1)
                w = WR[h % 2]
                g = GAM[h]
                P.op("pool", lambda e: e.memset(Rf.t[:], 0.0), writes=[Rf.b])
                P.op("pool", lambda e: e.memset(Rb2[0].t[:], 0.0), writes=[Rb2[0].b])
                P.op("pool", lambda e: e.memset(Rb2[1].t[:], 0.0), writes=[Rb2[1].b])

                def A1(blk):
                    own = blk >= NC
                    u_ = uTb[blk % 3]
                    rt = RT[blk % 3]
                    k_ = kr[blk % 2]
                    q_ = qr[blk % 2]
                    P.dma("sp", lambda e: e.dma_start(out=u_.t[:].rearrange("p a b -> p (a b)"), in_=UT[blk]),
                          u_.b, reads=[b_UT[blk]], writes=[u_.b])
                    P.dma("sp", lambda e: e.dma_start(out=rt.t[:], in_=rope_r[blk * 128:(blk + 1) * 128, :]),
                          rt.b, writes=[rt.b])
                    c0 = 0 if own else 256
                    for kc in range(8):
                        P.op("pe", lambda e, kc=kc: e.matmul(pQK.t[:, c0:512], lhsT=u_.t[:, kc, :], rhs=w.t[:, kc, c0:512],
                                                             start=(kc == 0), stop=(kc == 7)),
                             reads=[u_.b, w.b], writes=[pQK.b])
                    P.op("dve", lambda e: e.scalar_tensor_tensor(out=Ak.t[:], in0=pQK.t[:, 256:512], scalar=rdec.t[:, 4 + h:5 + h],
                                                                 in1=rt.t[:, 0:256], op0=ALU.mult, op1=ALU.mult),
                         reads=[pQK.b, rdec.b, rt.b], writes=[Ak.b])
                    P.op("dve", lambda e: e.scalar_tensor_tensor(out=Bk.t[:], in0=pQK.t[:, 256:512], scalar=rdec.t[:, 4 + h:5 + h],
                                                                 in1=rt.t[:, 256:512], op0=ALU.mult, op1=ALU.mult),
                         reads=[pQK.b, rdec.b, rt.b], writes=[Bk.b])
                    if own:
                        P.op("dve", lambda e: e.scalar_tensor_tensor(out=Aq.t[:], in0=pQK.t[:, 0:256], scalar=rdec.t[:, h:h + 1],
                                                                     in1=rt.t[:, 0:256], op0=ALU.mult, op1=ALU.mult),
                             reads=[pQK.b, rdec.b, rt.b], writes=[Aq.b])
                        P.op("dve", lambda e: e.scalar_tensor_tensor(out=Bq.t[:], in0=pQK.t[:, 0:256], scalar=rdec.t[:, h:h + 1],
                                                                     in1=rt.t[:, 256:512], op0=ALU.mult, op1=ALU.mult),
                             reads=[pQK.b, rdec.b, rt.b], writes=[Bq.b])
                    P.op("dve", lambda e: e.tensor_tensor(out=k_.t[:, 0:128], in0=Ak.t[:, 0:128], in1=Bk.t[:, 128:256], op=ALU.subtract),
                         reads=[Ak.b, Bk.b], writes=[k_.b])
                    P.op("dve", lambda e: e.tensor_tensor(out=k_.t[:, 128:256], in0=Ak.t[:, 128:256], in1=Bk.t[:, 0:128], op=ALU.add),
                         reads=[Ak.b, Bk.b], writes=[k_.b])
                    if own:
                        P.op("dve", lambda e: e.tensor_tensor(out=q_.t[:, 0:128], in0=Aq.t[:, 0:128], in1=Bq.t[:, 128:256], op=ALU.subtract),
                             reads=[Aq.b, Bq.b], writes=[q_.b])
                        P.op("dve", lambda e: e.tensor_tensor(out=q_.t[:, 128:256], in0=Aq.t[:, 128:256], in1=Bq.t[:, 0:128], op=ALU.add),
                             reads=[Aq.b, Bq.b], writes=[q_.b])

                def A2(blk):
                    u_ = uTb[blk % 3]
                    v_ = vb[blk % 2]
                    for kc in range(8):
                        P.op("pe", lambda e, kc=kc: e.matmul(pV.t[:, 0:512], lhsT=u_.t[:, kc, :], rhs=w.t[:, kc, 512:1024],
                                                             start=(kc == 0), stop=(kc == 7)),
                             reads=[u_.b, w.b], writes=[pV.b])
                    P.op("act", lambda e: e.copy(v_.t[:], pV.t[:, 0:512]), reads=[pV.b], writes=[v_.b])

                def A3(blk):
                    if blk < NC:
                        return
                    u_ = uTb[blk % 3]
                    s_ = sg[blk % 2]
                    for kc in range(8):
                        P.op("pe", lambda e, kc=kc: e.matmul(pG.t[:, 0:512], lhsT=u_.t[:, kc, :], rhs=w.t[:, kc, 1024:1536],
                                                             start=(kc == 0), stop=(kc == 7)),
                             reads=[u_.b, w.b], writes=[pG.b])
                    P.op("act", lambda e: e.activation(out=s_.t[:], in_=pG.t[:, 0:512], func=AF.Silu), reads=[pG.b], writes=[s_.b])

                def B1(blk):
                    if blk < NC:
                        return
                    k_ = kr[blk % 2]
                    q_ = qr[blk % 2]
                    for j in range(4):
                        srcT = q_ if j < 2 else k_
                        c = j % 2
                        P.op("pe", lambda e, j=j, c=c, srcT=srcT: e.transpose(pTq.t[:, j * 128:(j + 1) * 128],
                                                                              srcT.t[:, c * 128:(c + 1) * 128], ident.t[:]),
                             reads=[srcT.b, ident.b], writes=[pTq.b])
                    P.op("act", lambda e: e.copy(qkT.t[:].rearrange("p a b -> p (a b)"), pTq.t[:, 0:512]),
                         reads=[pTq.b], writes=[qkT.b])

                def B2(blk):
                    if blk < NC:
                        return
                    for c in range(2):
                        P.op("pe", lambda e, c=c: e.matmul(pSv, lhsT=qkT.t[:, 2 + c, :], rhs=qkT.t[:, c, :],
                                                           start=(c == 0), stop=(c == 1)),
                             reads=[qkT.b], writes=[pTq.b])
                    P.op("dve", lambda e: e.scalar_tensor_tensor(out=Sm.t[:], in0=pSv, scalar=float(g ** -128.0),
                                                                 in1=cmask.t[:], op0=ALU.mult, op1=ALU.mult),
                         reads=[pTq.b, cmask.b], writes=[Sm.b])

                def B3a(blk):
                    if blk < NC:
                        return
                    v_ = vb[blk % 2]
                    po = pO2[blk % 2]
                    rb = Rb2[(blk - 1) % 2]
                    P.op("pe", lambda e: e.matmul(po.t[:, 0:512], lhsT=Sm.t[:], rhs=v_.t[:], start=True, stop=False),
                         reads=[Sm.b, v_.b], writes=[po.b])
                    for c in range(2):
                        P.op("pe", lambda e, c=c: e.matmul(po.t[:, 0:512], lhsT=qkT.t[:, c, :], rhs=rb.t[:, c, :],
                                                           start=False, stop=(c == 1)),
                             reads=[qkT.b, rb.b], writes=[po.b])

                def CH(blk):
                    if blk < NC:
                        return
                    par = blk % 2
                    po = pO2[par]
                    s_ = sg[par]
                    bst_, mv_, grs_, on_, gtd_ = bst2[par], mv2[par], grs2[par], on2[par], gtd2[par]
                    P.op("dve", lambda e: e.bn_stats(bst_.t[:], po.t[:, 0:512]), reads=[po.b], writes=[bst_.b])
                    P.op("dve", lambda e: e.bn_aggr(mv_.t[:], bst_.t[:]), reads=[bst_.b], writes=[mv_.b])
                    P.op("dve", lambda e: e.tensor_scalar(grs_.t[:], mv_.t[:, 1:2], EPS, None, op0=ALU.add),
                         reads=[mv_.b], writes=[grs_.b])
                    P.op("pool", lambda e: e.tensor_tensor(out=grs_.t[:], in0=grs_.t[:], in1=mhalf.t[:, 0:1], op=ALU.pow),
                         reads=[grs_.b, mhalf.b], writes=[grs_.b])
                    P.op("dve", lambda e: e.tensor_scalar(on_.t[:], po.t[:, 0:512], mv_.t[:, 0:1], grs_.t[:, 0:1],
                                                          op0=ALU.subtract, op1=ALU.mult),
                         reads=[po.b, mv_.b, grs_.b], writes=[on_.b])
                    P.op("dve", lambda e: e.tensor_tensor(out=gtd_.t[:], in0=on_.t[:], in1=s_.t[:], op=ALU.mult),
                         reads=[on_.b, s_.b], writes=[gtd_.b])

                def ST(blk, c):
                    if blk >= NB - 1:
                        return
                    k_ = kr[blk % 2]
                    v_ = vb[blk % 2]
                    P.op("pe", lambda e: e.matmul(pR1.t[:, 0:512], lhsT=k_.t[:, c * 128:(c + 1) * 128], rhs=v_.t[:],
                                                  start=True, stop=True),
                         reads=[k_.b, v_.b], writes=[pR1.b])
                    P.op("dve", lambda e: e.scalar_tensor_tensor(out=Rf.t[:, c, :], in0=Rf.t[:, c, :], scalar=float(g ** 128.0),
                                                                 in1=pR1.t[:, 0:512], op0=ALU.mult, op1=ALU.add),
                         reads=[Rf.b, pR1.b], writes=[Rf.b])
                    if c == 1:
                        rb = Rb2[blk % 2]
                        P.op("act", lambda e: e.copy(rb.t[:].rearrange("p a b -> p (a b)"), Rf.t[:].rearrange("p a b -> p (a b)")),
                             reads=[Rf.b], writes=[rb.b])

                def G(blk):
                    if blk < NC or blk >= NB:
                        return
                    ob = blk - NC
                    g_ = gT[ob % 2]
                    gtd_ = gtd2[blk % 2]
                    for j in range(4):
                        P.op("pe", lambda e, j=j: e.transpose(pTg.t[:, j * 128:(j + 1) * 128],
                                                              gtd_.t[:, j * 128:(j + 1) * 128], ident.t[:]),
                             reads=[gtd_.b, ident.b], writes=[pTg.b])
                    P.op("act", lambda e: e.copy(g_.t[:].rearrange("p a b -> p (a b)"), pTg.t[:, 0:512]),
                         reads=[pTg.b], writes=[g_.b])
                    P.dma("pool", lambda e: e.dma_start(out=GT[ob][:, h * 512:(h + 1) * 512],
                                                        in_=g_.t[:].rearrange("p a b -> p (a b)")),
                          g_.b, reads=[g_.b], writes=[b_GT[ob]])

                A1(0); A2(0); A3(0)
                for blk in range(NB):
                    nx = blk + 1
                    B1(blk)
                    if nx < NB:
                        A1(nx)
                    B2(blk)
                    if nx < NB:
                        A2(nx)
                    B3a(blk)
                    G(blk - 1)
                    ST(blk, 0)
                    if nx < NB:
                        A3(nx)
                    ST(blk, 1)
                    CH(blk)
                G(NB - 1)
                P.emit()

        KTs = dscr("KTs", [8, 128, NB * 128], BF16)
        QTs = dscr("QTs", [8, 128, NO * 128], BF16)
        VVs = dscr("VVs", [8, 128, NB, 128], BF16)
        b_KTs = Buf("KTs"); b_QTs = Buf("QTs"); b_VVs = Buf("VVs")
        KTs_w = KTs.rearrange("h p (b t) -> p h b t", t=128)
        QTs_w = QTs.rearrange("h p (b t) -> p h b t", t=128)
        VVs_w = VVs.rearrange("h p b e -> p h b e")
        with ExitStack() as st:
            WDa = sbt(st, "WDa", [128, 8, 3072], BF16)
            for k0 in range(0, 8, 2):
                P.dma("pool", lambda e, k0=k0: e.dma_start(out=WDa.t[:, k0:k0 + 2, :], in_=w_in_v[:, k0:k0 + 2, C_DQ:C_DQ + 3072]),
                      WDa.b, writes=[WDa.b])
            ropd = sbt(st, "ropd", [128, NB, 32], F32)
            ropd_v = rope_d.rearrange("(b p) c -> p b c", p=128)
            for b0 in range(0, NB, 16):
                b1 = min(NB, b0 + 16)
                P.dma("sp", lambda e, b0=b0, b1=b1: e.dma_start(out=ropd.t[:, b0:b1, :], in_=ropd_v[:, b0:b1, :]),
                      ropd.b, writes=[ropd.b])
            wq8 = sbt(st, "wq8", [128, 8, 64], F32)
            wk8 = sbt(st, "wk8", [128, 8, 64], F32)
            for g8 in range(8):
                P.op("pool", lambda e, g8=g8: e.tensor_copy(wq8.t[:, g8, :], wqk.t[:, 0, :]), reads=[wqk.b], writes=[wq8.b])
                P.op("pool", lambda e, g8=g8: e.tensor_copy(wk8.t[:, g8, :], wqk.t[:, 2, :]), reads=[wqk.b], writes=[wk8.b])
            uTb = [sbt(st, "duT%d" % i, [128, 8, 128], BF16) for i in range(3)]
            NCH = 4
            sqd = [sbt(st, "sqd%d" % i, [128, 8, 64], F32) for i in range(NCH)]
            ssd = [sbt(st, "ssd%d" % i, [128, 8], F32) for i in range(NCH)]
            rsd = [sbt(st, "rsd%d" % i, [128, 8], F32) for i in range(NCH)]
            xn = [[sbt(st, "xn%d_%d" % (pp, i), [128, 8, 64], F32) for i in range(NCH)] for pp in range(2)]
            xbq = [[sbt(st, "xbq%d_%d" % (pp, i), [128, 8, 64], BF16) for i in range(NCH)] for pp in range(2)]
            rc = [[sbt(st, "rc%d_%d" % (pp, i), [128, 8, 16], F32) for i in range(NCH)] for pp in range(2)]
            xw = [[sbt(st, "xw%d_%d" % (pp, i), [128, 8, 16], F32) for i in range(NCH)] for pp in range(2)]
            rsn = [[sbt(st, "rsn%d_%d" % (pp, i), [128, 8, 16], F32) for i in range(NCH)] for pp in range(2)]
            kst = [sbt(st, "kst%d" % i, [128, 8, 128], BF16) for i in range(2)]
            qst = [sbt(st, "qst%d" % i, [128, 8, 128], BF16) for i in range(2)]
            vst = [sbt(st, "vst%d" % i, [128, 8, 128], BF16) for i in range(2)]
            pq = [pbank(st, "pq%d" % i) for i in range(2)]
            pk = [pbank(st, "pk%d" % i) for i in range(2)]
            pvv = [pbank(st, "pvv%d" % i) for i in range(2)]
            pTk = pbank(st, "pTk", BF16)
            pTq = pbank(st, "pTq1", BF16)
            def mk_chains(blk):
                chains = []
                for half in range(2):
                    chains.append((pk[half], wk8, pTk, half, 1024 + half * 512))
                if blk >= NC:
                    for half in range(2):
                        chains.append((pq[half], wq8, pTq, half, half * 512))
                return chains

            def d1_early(blk):
                u_ = uTb[blk % 3]
                par = blk % 2
                P.dma("sp", lambda e: e.dma_start(out=u_.t[:].rearrange("p a b -> p (a b)"), in_=UT[blk]),
                      u_.b, reads=[b_UT[blk]], writes=[u_.b])
                chains = mk_chains(blk)
                for (pb, wt, ptT, half, c0) in chains:
                    for kc in range(8):
                        P.op("pe", lambda e, kc=kc, pb=pb, c0=c0: e.matmul(pb.t[:, 0:512], lhsT=u_.t[:, kc, :], rhs=WDa.t[:, kc, c0:c0 + 512],
                                                                           start=(kc == 0), stop=(kc == 7)),
                             reads=[u_.b, WDa.b], writes=[pb.b])
                for half in range(2):
                    for kc in range(8):
                        P.op("pe", lambda e, kc=kc, half=half: e.matmul(pvv[half].t[:, 0:512], lhsT=u_.t[:, kc, :],
                                                                       rhs=WDa.t[:, kc, 2048 + half * 512:2048 + (half + 1) * 512],
                                                                       start=(kc == 0), stop=(kc == 7)),
                             reads=[u_.b, WDa.b], writes=[pvv[half].b])
                nch = len(chains)
                pvw = [ch[0].t[:, 0:512].rearrange("p (a b) -> p a b", b=64) for ch in chains]
                for ci in range(nch):
                    P.op("act", lambda e, ci=ci: e.activation(out=sqd[ci].t[:], in_=pvw[ci], func=AF.Square),
                         reads=[chains[ci][0].b], writes=[sqd[ci].b])
                for ci in range(nch):
                    P.op("dve", lambda e, ci=ci: e.tensor_reduce(out=ssd[ci].t[:], in_=sqd[ci].t[:], axis=AX.X, op=ALU.add),
                         reads=[sqd[ci].b], writes=[ssd[ci].b])
                for ci in range(nch):
                    P.op("act", lambda e, ci=ci: e.activation(out=rsd[ci].t[:], in_=ssd[ci].t[:], func=AF.Sqrt, scale=1.0 / 64, bias=eps_t.t[:, 0:1]),
                         reads=[ssd[ci].b, eps_t.b], writes=[rsd[ci].b])
                v_ = vst[par]
                for half in range(2):
                    P.op("act", lambda e, half=half: e.copy(v_.t[:, half * 4:half * 4 + 4, :].rearrange("p a b -> p (a b)"), pvv[half].t[:, 0:512]),
                         reads=[pvv[half].b], writes=[v_.b])
                P.dma("act", lambda e: e.dma_start(out=VVs_w[:, :, blk, :], in_=v_.t[:]), v_.b, reads=[v_.b], writes=[b_VVs])
                for ci in range(nch):
                    P.op("dve", lambda e, ci=ci: e.reciprocal(rsd[ci].t[:], rsd[ci].t[:]), reads=[rsd[ci].b], writes=[rsd[ci].b])
                for ci in range(nch):
                    P.op("dve", lambda e, ci=ci: e.tensor_tensor(out=xn[par][ci].t[:], in0=pvw[ci], in1=bc_last(rsd[ci].t[:], 64), op=ALU.mult),
                         reads=[chains[ci][0].b, rsd[ci].b], writes=[xn[par][ci].b])

            def d1_late(blk):
                par = blk % 2
                own = blk >= NC
                ob = blk - NC
                chains = mk_chains(blk)
                nch = len(chains)
                xn_, xb_, rc_, rsn_, xw_ = xn[par], xbq[par], rc[par], rsn[par], xw[par]
                for ci in range(nch):
                    eng = "pool"
                    wt = chains[ci][1]
                    P.op(eng, lambda e, ci=ci, wt=wt: e.tensor_tensor(out=xb_[ci].t[:], in0=xn_[ci].t[:], in1=wt.t[:], op=ALU.mult),
                         reads=[xn_[ci].b, wt.b], writes=[xb_[ci].b])
                    P.op(eng, lambda e, ci=ci, wt=wt: e.tensor_tensor(out=xw_[ci].t[:], in0=xn_[ci].t[:, :, 0:16], in1=wt.t[:, :, 0:16], op=ALU.mult),
                         reads=[xn_[ci].b, wt.b], writes=[xw_[ci].b])
                    P.op(eng, lambda e, ci=ci: e.tensor_tensor(out=rc_[ci].t[:], in0=xw_[ci].t[:],
                                                              in1=bc_mid(ropd.t[:, blk, 0:16], 8), op=ALU.mult),
                         reads=[xw_[ci].b, ropd.b], writes=[rc_[ci].b])
                    P.op(eng, lambda e, ci=ci: e.tensor_tensor(out=rsn_[ci].t[:], in0=xw_[ci].t[:],
                                                              in1=bc_mid(ropd.t[:, blk, 16:32], 8), op=ALU.mult),
                         reads=[xw_[ci].b, ropd.b], writes=[rsn_[ci].b])
                    P.op(eng, lambda e, ci=ci: e.tensor_tensor(out=xb_[ci].t[:, :, 0:8], in0=rc_[ci].t[:, :, 0:8], in1=rsn_[ci].t[:, :, 8:16], op=ALU.subtract),
                         reads=[rc_[ci].b, rsn_[ci].b], writes=[xb_[ci].b])
                    P.op(eng, lambda e, ci=ci: e.tensor_tensor(out=xb_[ci].t[:, :, 8:16], in0=rc_[ci].t[:, :, 8:16], in1=rsn_[ci].t[:, :, 0:8], op=ALU.add),
                         reads=[rc_[ci].b, rsn_[ci].b], writes=[xb_[ci].b])
                for ci in range(nch):
                    ptT, half = chains[ci][2], chains[ci][3]
                    for hh in range(4):
                        P.op("pe", lambda e, ci=ci, hh=hh, ptT=ptT, half=half: e.transpose(ptT.t[:, (half * 4 + hh) * 128:(half * 4 + hh + 1) * 128],
                                                                                          xb_[ci].t[:, 2 * hh:2 * hh + 2, :].rearrange("p a b -> p (a b)"),
                                                                                          ident.t[:]),
                             reads=[xb_[ci].b, ident.b], writes=[ptT.b])
                k_ = kst[par]
                P.op("dve", lambda e: e.tensor_copy(k_.t[:].rearrange("p a b -> p (a b)"), pTk.t[:, 0:1024]), reads=[pTk.b], writes=[k_.b])
                P.dma("act", lambda e: e.dma_start(out=KTs_w[:, :, blk, :], in_=k_.t[:]), k_.b, reads=[k_.b], writes=[b_KTs])
                if own:
                    q_ = qst[par]
                    P.op("act", lambda e: e.copy(q_.t[:].rearrange("p a b -> p (a b)"), pTq.t[:, 0:1024]), reads=[pTq.b], writes=[q_.b])
                    P.dma("act", lambda e: e.dma_start(out=QTs_w[:, :, ob, :], in_=q_.t[:]), q_.b, reads=[q_.b], writes=[b_QTs])

            d1_early(0)
            for blk in range(NB):
                if blk + 1 < NB:
                    d1_early(blk + 1)
                d1_late(blk)
            P.emit()

        with ExitStack() as st:
            KTb = [sbt(st, "KT%d" % i, [128, NB * 128], BF16) for i in range(2)]
            VVb = [sbt(st, "VV%d" % i, [128, NB, 130], BF16) for i in range(2)]
            QT2b = [sbt(st, "QT2%d" % i, [128, NO, 256], BF16) for i in range(2)]
            NPT = 6
            PT = [sbt(st, "PT%d" % i, [128, 4, 128], BF16) for i in range(NPT)]
            zz = sbt(st, "zz", [128, 2], F32)
            a1 = sbt(st, "a1", [128, 128], F32)
            aa = sbt(st, "aa", [128, 128], F32)
            asq = sbt(st, "asq", [128, 128], F32)
            ass = sbt(st, "ass", [128, 1], F32)
            ars = sbt(st, "ars", [128, 1], F32)
            dob = sbt(st, "dob", [128, 128], BF16)
            doT = [sbt(st, "doT%d" % i, [128, 128], BF16) for i in range(2)]
            pP = pbank(st, "pP")
            pTd = pbank(st, "pTd", BF16)
            pSd = [pbank(st, "pSd%d" % i) for i in range(2)]
            pO0 = [pbank(st, "pO0%d" % i) for i in range(2)]
            pO1 = [pbank(st, "pO1%d" % i) for i in range(2)]
            assert NC % 2 == 0
            for i2 in range(2):
                P.op("pool", lambda e, i2=i2: e.memset(VVb[i2].t[:], 0.0), writes=[VVb[i2].b])
                P.op("dve", lambda e, i2=i2: e.tensor_copy(VVb[i2].t[:, :, 128:129], kbias.t[:].rearrange("p (a b) -> p a b", b=1)),
                     reads=[kbias.b], writes=[VVb[i2].b])
                P.op("pool", lambda e, i2=i2: e.memset(QT2b[i2].t[:], 0.0), writes=[QT2b[i2].b])

            def load_head(h):
                kt, vv, q2 = KTb[h % 2], VVb[h % 2], QT2b[h % 2]
                P.dma("sp", lambda e: e.dma_start(out=kt.t[:], in_=KTs[h]), kt.b, reads=[b_KTs], writes=[kt.b])
                P.dma("sp", lambda e: e.dma_start(out=vv.t[:, :, 0:128], in_=VVs[h]), vv.b, reads=[b_VVs], writes=[vv.b])
                P.dma("sp", lambda e: e.dma_start(out=q2.t[0:64, :, 0:128], in_=QTs[h][0:64, :].rearrange("p (i t) -> p i t", t=128)),
                      q2.b, reads=[b_QTs], writes=[q2.b])
                P.dma("sp", lambda e: e.dma_start(out=q2.t[64:128, :, 128:256], in_=QTs[h][64:128, :].rearrange("p (i t) -> p i t", t=128)),
                      q2.b, reads=[b_QTs], writes=[q2.b])
            load_head(0)
            for h in range(8):
                if h + 1 < 8:
                    load_head(h + 1)
                KT, VV, QT2 = KTb[h % 2], VVb[h % 2], QT2b[h % 2]
                b_K = [KT.b] * NB
                b_Kv = VV.b
                b_Q = [QT2.b] * NO
                items = []
                for i in range(NO):
                    nk = NC + i + 1
                    for kb0 in range(0, nk, 2):
                        items.append((i, kb0, min(2, nk - kb0)))
                SKEW = 2
                pS3 = [pSd[0], pSd[1], pP]

                def qk_exp(n):
                    i, kb0, nb = items[n]
                    nk = NC + i + 1
                    ps = pS3[n % 3]
                    pt = PT[n % NPT]
                    for j in range(nb):
                        kb = kb0 + j
                        P.op("pe", lambda e, j=j, kb=kb: e.matmul(ps.t[:, j * 256:(j + 1) * 256], lhsT=KT.t[:, kb * 128:(kb + 1) * 128],
                                                                  rhs=QT2.t[:, i, :], start=True, stop=True),
                             reads=[b_K[kb], b_Q[i]], writes=[ps.b])
                    P.op("act", lambda e: e.activation(out=pt.t[:, 0:2 * nb, :].rearrange("p a b -> p (a b)"),
                                                       in_=ps.t[:, 0:256 * nb], func=AF.Exp, scale=0.125),
                         reads=[ps.b], writes=[pt.b])
                    if kb0 + nb == nk:
                        jl = nb - 1
                        P.op("pool", lambda e: e.tensor_tensor(out=pt.t[:, 2 * jl:2 * jl + 2, :], in0=pt.t[:, 2 * jl:2 * jl + 2, :],
                                                               in1=cmask2.t[:], op=ALU.mult),
                             reads=[pt.b, cmask2.b], writes=[pt.b])

                def pv(n):
                    i, kb0, nb = items[n]
                    nk = NC + i + 1
                    o0 = pO0[i % 2]
                    o1 = pO1[i % 2]
                    pt = PT[n % NPT]
                    for j in range(nb):
                        kb = kb0 + j
                        P.op("pe", lambda e, j=j, kb=kb: e.matmul(o0.t[:, 0:129], lhsT=pt.t[:, 2 * j, :], rhs=VV.t[:, kb, 0:129],
                                                                  start=(kb == 0), stop=(kb == nk - 1)),
                             reads=[pt.b, b_Kv], writes=[o0.b])
                        P.op("pe", lambda e, j=j, kb=kb: e.matmul(o1.t[:, 0:129], lhsT=pt.t[:, 2 * j + 1, :], rhs=VV.t[:, kb, 0:129],
                                                                  start=(kb == 0), stop=(kb == nk - 1)),
                             reads=[pt.b, b_Kv], writes=[o1.b])
                    if kb0 + nb == nk:
                        finalize(i, o0, o1)

                def finalize(i, o0, o1):
                    while pend_fin:
                        pend_fin.pop(0)[1]()
                    P.op("dve", lambda e, o0=o0: e.reciprocal(zz.t[:, 0:1], o0.t[:, 128:129]), reads=[o0.b], writes=[zz.b])
                    P.op("dve", lambda e, o1=o1: e.reciprocal(zz.t[:, 1:2], o1.t[:, 128:129]), reads=[o1.b], writes=[zz.b])
                    P.op("dve", lambda e: e.tensor_tensor(out=zz.t[:, 1:2], in0=zz.t[:, 1:2], in1=lam.t[:, 0:1], op=ALU.mult),
                         reads=[zz.b, lam.b], writes=[zz.b])
                    P.op("dve", lambda e, o1=o1: e.tensor_scalar(a1.t[:], o1.t[:, 0:128], zz.t[:, 1:2], None, op0=ALU.mult),
                         reads=[o1.b, zz.b], writes=[a1.b])
                    P.op("dve", lambda e, o0=o0: e.scalar_tensor_tensor(out=aa.t[:], in0=o0.t[:, 0:128], scalar=zz.t[:, 0:1], in1=a1.t[:],
                                                                        op0=ALU.mult, op1=ALU.subtract),
                         reads=[o0.b, zz.b, a1.b], writes=[aa.b])
                    finalize_b(i)
                    pend_fin.append((n_now[0] + 8, lambda: finalize_c(i)))

                def finalize_b(i):
                    P.op("dve", lambda e: e.tensor_tensor(out=asq.t[:], in0=aa.t[:], in1=aa.t[:], op=ALU.mult), reads=[aa.b], writes=[asq.b])
                    P.op("dve", lambda e: e.tensor_reduce(out=ass.t[:, 0:1], in_=asq.t[:], axis=AX.X, op=ALU.add), reads=[asq.b], writes=[ass.b])
                    P.op("dve", lambda e: e.tensor_scalar(ars.t[:], ass.t[:], 1.0 / 128, EPS, op0=ALU.mult, op1=ALU.add),
                         reads=[ass.b], writes=[ars.b])
                    P.op("pool", lambda e: e.tensor_tensor(out=ars.t[:], in0=ars.t[:], in1=mhalf.t[:, 0:1], op=ALU.pow),
                         reads=[ars.b, mhalf.b], writes=[ars.b])
                    P.op("dve", lambda e: e.scalar_tensor_tensor(out=dob.t[:], in0=aa.t[:], scalar=ars.t[:, 0:1], in1=sublnw.t[:],
                                                                 op0=ALU.mult, op1=ALU.mult),
                         reads=[aa.b, ars.b, sublnw.b], writes=[dob.b])

                def finalize_c(i):
                    P.op("pe", lambda e: e.transpose(pTd.t[:, 256:384], dob.t[:], ident.t[:]), reads=[dob.b, ident.b], writes=[pTd.b])
                    d_ = doT[i % 2]
                    P.op("dve", lambda e, d_=d_: e.tensor_copy(d_.t[:], pTd.t[:, 256:384]), reads=[pTd.b], writes=[d_.b])
                    P.dma("pool", lambda e, d_=d_, i=i: e.dma_start(out=DOT[i][:, h * 128:(h + 1) * 128], in_=d_.t[:]),
                          d_.b, reads=[d_.b], writes=[b_DOT[i]])
                pend_fin = []
                n_now = [0]
                for n in range(len(items) + SKEW + 10):
                    n_now[0] = n
                    if n < len(items):
                        qk_exp(n)
                    if 0 <= n - SKEW < len(items):
                        pv(n - SKEW)
                    while pend_fin and pend_fin[0][0] <= n:
                        pend_fin.pop(0)[1]()
                assert not pend_fin
                P.emit()

        with ExitStack() as st:
            Wro = sbt(st, "Wro", [128, 16, 1024], BF16)
            Wdo = sbt(st, "Wdo", [128, 8, 1024], BF16)
            Wou = sbt(st, "Wou", [128, 8, 1024], BF16)
            Wg = sbt(st, "Wg", [128, 8, 2048], BF16)
            n2w = sbt(st, "n2w", [128, D], F32)
            P.dma("sp", lambda e: e.dma_start(out=n2w.t[:], in_=bc_part(norm2_w, D)), n2w.b, writes=[n2w.b])
            wro_v = w_ret_o.rearrange("(k p) n -> p k n", p=128)
            for k0 in range(0, 16, 4):
                P.dma("pool", lambda e, k0=k0: e.dma_start(out=Wro.t[:, k0:k0 + 4, :], in_=wro_v[:, k0:k0 + 4, :]), Wro.b, writes=[Wro.b])
            wdo_v = w_diff_o.rearrange("(k p) n -> p k n", p=128)
            wou_v = w_out.rearrange("(k p) n -> p k n", p=128)
            for k0 in range(0, 8, 4):
                P.dma("pool", lambda e, k0=k0: e.dma_start(out=Wdo.t[:, k0:k0 + 4, :], in_=wdo_v[:, k0:k0 + 4, :]), Wdo.b, writes=[Wdo.b])
                P.dma("pool", lambda e, k0=k0: e.dma_start(out=Wou.t[:, k0:k0 + 4, :], in_=wou_v[:, k0:k0 + 4, :]), Wou.b, writes=[Wou.b])
            for k0 in range(0, 8, 2):
                P.dma("pool", lambda e, k0=k0: e.dma_start(out=Wg.t[:, k0:k0 + 2, :], in_=w_in_v[:, k0:k0 + 2, C_GT:C_GT + 2048]), Wg.b, writes=[Wg.b])
            gTb = [sbt(st, "mgT%d" % i, [128, 16, 128], BF16) for i in range(2)]
            dTb = [sbt(st, "mdT%d" % i, [128, 8, 128], BF16) for i in range(2)]
            uTb = [sbt(st, "muT%d" % i, [128, 8, 128], BF16) for i in range(2)]
            xb = [sbt(st, "mx%d" % i, [128, D], F32) for i in range(2)]
            sig = sbt(st, "sig", [128, 2048], F32)
            m1 = sbt(st, "m1", [128, D], F32)
            m2 = sbt(st, "m2", [128, D], F32)
            mb = [sbt(st, "mb%d" % i, [128, D], BF16) for i in range(2)]
            mT = sbt(st, "mT", [128, 8, 128], BF16)
            h2 = [sbt(st, "h2%d" % i, [128, D], F32) for i in range(2)]
            sq = sbt(st, "msq", [128, D], F32)
            ss = sbt(st, "mss", [128, 1], F32)
            rs = sbt(st, "mrs", [128, 1], F32)
            ub = sbt(st, "mub", [128, D], BF16)
            u2T = [sbt(st, "mu2T%d" % i, [128, 8, 128], BF16) for i in range(2)]
            pA = [pbank(st, "pA%d" % i) for i in range(4)]
            pB = [pbank(st, "pB%d" % i) for i in range(2)]
            pTm = [pbank(st, "pTm%d" % i, BF16) for i in range(2)]
            def MA1(ob):
                blk = NC + ob
                g_ = gTb[ob % 2]; d_ = dTb[ob % 2]; u_ = uTb[ob % 2]; x_ = xb[ob % 2]
                P.dma("sp", lambda e: e.dma_start(out=g_.t[:].rearrange("p a b -> p (a b)"), in_=GT[ob]),
                      g_.b, reads=[b_GT[ob]], writes=[g_.b])
                P.dma("sp", lambda e: e.dma_start(out=d_.t[:].rearrange("p a b -> p (a b)"), in_=DOT[ob]),
                      d_.b, reads=[b_DOT[ob]], writes=[d_.b])
                P.dma("sp", lambda e: e.dma_start(out=u_.t[:].rearrange("p a b -> p (a b)"), in_=UT[blk]),
                      u_.b, reads=[b_UT[blk]], writes=[u_.b])
                P.dma("sp", lambda e: e.dma_start(out=x_.t[:], in_=xo[ob * 128:(ob + 1) * 128, :]), x_.b, writes=[x_.b])
                for j in range(4):
                    for kc in range(8):
                        P.op("pe", lambda e, j=j, kc=kc: e.matmul(pA[j].t[:, 0:512], lhsT=u_.t[:, kc, :], rhs=Wg.t[:, kc, j * 512:(j + 1) * 512],
                                                                  start=(kc == 0), stop=(kc == 7)),
                             reads=[u_.b, Wg.b], writes=[pA[j].b])
                    P.op("act", lambda e, j=j: e.activation(out=sig.t[:, j * 512:(j + 1) * 512], in_=pA[j].t[:, 0:512], func=AF.Sigmoid),
                         reads=[pA[j].b], writes=[sig.b])

            def MA2(ob):
                g_ = gTb[ob % 2]
                for j in range(2):
                    for kc in range(16):
                        P.op("pe", lambda e, j=j, kc=kc: e.matmul(pA[j].t[:, 0:512], lhsT=g_.t[:, kc, :], rhs=Wro.t[:, kc, j * 512:(j + 1) * 512],
                                                                  start=(kc == 0), stop=(kc == 15)),
                             reads=[g_.b, Wro.b], writes=[pA[j].b])
                    P.op("dve", lambda e, j=j: e.tensor_tensor(out=m1.t[:, j * 512:(j + 1) * 512], in0=pA[j].t[:, 0:512],
                                                               in1=sig.t[:, j * 512:(j + 1) * 512], op=ALU.mult),
                         reads=[pA[j].b, sig.b], writes=[m1.b])

            def MA3(ob):
                d_ = dTb[ob % 2]
                mb_ = mb[ob % 2]
                for j in range(2):
                    for kc in range(8):
                        P.op("pe", lambda e, j=j, kc=kc: e.matmul(pA[2 + j].t[:, 0:512], lhsT=d_.t[:, kc, :], rhs=Wdo.t[:, kc, j * 512:(j + 1) * 512],
                                                                  start=(kc == 0), stop=(kc == 7)),
                             reads=[d_.b, Wdo.b], writes=[pA[2 + j].b])
                    P.op("dve", lambda e, j=j: e.tensor_tensor(out=m2.t[:, j * 512:(j + 1) * 512], in0=pA[2 + j].t[:, 0:512],
                                                               in1=sig.t[:, 1024 + j * 512:1024 + (j + 1) * 512], op=ALU.mult),
                         reads=[pA[2 + j].b, sig.b], writes=[m2.b])
                P.op("pool", lambda e: e.tensor_tensor(out=mb_.t[:], in0=m1.t[:], in1=m2.t[:], op=ALU.add), reads=[m1.b, m2.b], writes=[mb_.b])

            def MB1(ob):
                mb_ = mb[ob % 2]
                for half in range(2):
                    pt = pTm[half]
                    for j in range(4):
                        kc = half * 4 + j
                        P.op("pe", lambda e, kc=kc, j=j, pt=pt: e.transpose(pt.t[:, j * 128:(j + 1) * 128], mb_.t[:, kc * 128:(kc + 1) * 128], ident.t[:]),
                             reads=[mb_.b, ident.b], writes=[pt.b])
                    if half == 0:
                        P.op("dve", lambda e, pt=pt: e.tensor_copy(mT.t[:, 0:4, :].rearrange("p a b -> p (a b)"), pt.t[:, 0:512]),
                             reads=[pt.b], writes=[mT.b])
                    else:
                        P.op("act", lambda e, pt=pt: e.copy(mT.t[:, 4:8, :].rearrange("p a b -> p (a b)"), pt.t[:, 0:512]),
                             reads=[pt.b], writes=[mT.b])

            def MB2(ob):
                h_ = h2[ob % 2]
                x_ = xb[ob % 2]
                for j in range(2):
                    for kc in range(8):
                        P.op("pe", lambda e, j=j, kc=kc: e.matmul(pB[j].t[:, 0:512], lhsT=mT.t[:, kc, :], rhs=Wou.t[:, kc, j * 512:(j + 1) * 512],
                                                                  start=(kc == 0), stop=(kc == 7)),
                             reads=[mT.b, Wou.b], writes=[pB[j].b])
                    P.op("dve", lambda e, j=j: e.tensor_tensor(out=h_.t[:, j * 512:(j + 1) * 512], in0=pB[j].t[:, 0:512],
                                                               in1=x_.t[:, j * 512:(j + 1) * 512], op=ALU.add),
                         reads=[pB[j].b, x_.b], writes=[h_.b])
                P.dma("pool", lambda e: e.dma_start(out=H2[ob * 128:(ob + 1) * 128, :], in_=h_.t[:]),
                      h_.b, reads=[h_.b], writes=[b_H2[ob]])
                norm_part(h_, n2w, sq, ss, rs, ub, use_pow=True)

            def MB3(ob):
                t_ = u2T[ob % 2]
                tr_part(ub, pTm, t_)
                P.dma("pool", lambda e: e.dma_start(out=U2T[ob], in_=t_.t[:].rearrange("p a b -> p (a b)")),
                      t_.b, reads=[t_.b], writes=[b_U2T[ob]])

            MA1(0); MA2(0); MA3(0)
            for ob in range(NO):
                nx = ob + 1
                MB1(ob)
                if nx < NO:
                    MA1(nx)
                MB2(ob)
                if nx < NO:
                    MA2(nx)
                MB3(ob)
                if nx < NO:
                    MA3(nx)
            P.emit()

        GB = 3
        NG = (NO + GB - 1) // GB
        with ExitStack() as st:
            Wup = sbt(st, "Wup", [128, 8, 2 * FFN], BF16)
            Wdn = sbt(st, "Wdn", [128, 22, D], BF16)
            cw = sbt(st, "cw", [128, 3, 44], F32)
            cb = sbt(st, "cb", [128, 44], F32)
            wup_v = w_up.rearrange("(k p) n -> p k n", p=128)
            for kc in range(8):
                for c0 in range(0, 2 * FFN, 1408):
                    P.dma("pool", lambda e, kc=kc, c0=c0: e.dma_start(out=Wup.t[:, kc, c0:c0 + 1408], in_=wup_v[:, kc, c0:c0 + 1408]),
                          Wup.b, writes=[Wup.b])
            wdn_v = w_down.rearrange("(k p) n -> p k n", p=128)
            for k0 in range(0, 22, 2):
                P.dma("pool", lambda e, k0=k0: e.dma_start(out=Wdn.t[:, k0:k0 + 2, :], in_=wdn_v[:, k0:k0 + 2, :]), Wdn.b, writes=[Wdn.b])
            for t0 in range(0, 44, 11):
                for k in range(3):
                    P.dma("sp", lambda e, k=k, t0=t0: e.dma_start(out=cw.t[:, k, t0:t0 + 11],
                                                                  in_=conv_w[k].rearrange("(t p) -> p t", p=128)[:, t0:t0 + 11],
                                                                  allow_slow_non_contiguous=True), cw.b, writes=[cw.b])
                P.dma("sp", lambda e, t0=t0: e.dma_start(out=cb.t[:, t0:t0 + 11], in_=conv_b.rearrange("(t p) -> p t", p=128)[:, t0:t0 + 11],
                                                         allow_slow_non_contiguous=True), cb.b, writes=[cb.b])
            NT = GB * 128
            u2g = [sbt(st, "u2g%d" % i, [128, 8, 2 + NT], BF16) for i in range(2)]
            ya = [sbt(st, "ya%d" % i, [128, NT], F32) for i in range(2)]
            yb = [sbt(st, "yb%d" % i, [128, NT], F32) for i in range(2)]
            sa = [sbt(st, "sa%d" % i, [128, NT], F32) for i in range(2)]
            gTt2 = [sbt(st, "gTt%d" % i, [128, 22, NT], BF16) for i in range(2)]
            hb = [sbt(st, "fh%d" % i, [128, D], F32) for i in range(1)] * 2
            ob_ = [sbt(st, "fo%d" % i, [128, D], F32) for i in range(2)]
            pU = [pbank(st, "pU%d" % i) for i in range(4)]
            pD = [pbank(st, "pD%d" % i) for i in range(4)]
            P.op("pool", lambda e: e.memset(u2g[0].t[:], 0.0), writes=[u2g[0].b])
            P.op("pool", lambda e: e.memset(u2g[1].t[:], 0.0), writes=[u2g[1].b])
            ui = [0]

            def up_pair(gi, ft, ug, nt):
                gt = gTt2[gi % 2]
                tiles = []
                for which, fi in ((0, ft), (1, ft + 22)):
                    pu = pU[ui[0] % 4]
                    ui[0] += 1
                    for kc in range(8):
                        P.op("pe", lambda e, pu=pu, kc=kc, fi=fi: e.matmul(pu.t[:, 0:nt + 2], lhsT=Wup.t[:, kc, fi * 128:(fi + 1) * 128],
                                                                           rhs=ug.t[:, kc, 0:nt + 2], start=(kc == 0), stop=(kc == 7)),
                             reads=[Wup.b, ug.b], writes=[pu.b])
                    yt = (ya if which == 0 else yb)[ft % 2]
                    P.op("dve", lambda e, pu=pu, yt=yt, fi=fi: e.tensor_scalar(yt.t[:, 0:nt], pu.t[:, 2:nt + 2], cw.t[:, 2, fi:fi + 1], cb.t[:, fi:fi + 1],
                                                                               op0=ALU.mult, op1=ALU.add),
                         reads=[pu.b, cw.b, cb.b], writes=[yt.b])
                    P.op("dve", lambda e, pu=pu, yt=yt, fi=fi: e.scalar_tensor_tensor(out=yt.t[:, 0:nt], in0=pu.t[:, 1:nt + 1], scalar=cw.t[:, 1, fi:fi + 1],
                                                                                      in1=yt.t[:, 0:nt], op0=ALU.mult, op1=ALU.add),
                         reads=[pu.b, cw.b, yt.b], writes=[yt.b])
                    P.op("dve", lambda e, pu=pu, yt=yt, fi=fi: e.scalar_tensor_tensor(out=yt.t[:, 0:nt], in0=pu.t[:, 0:nt], scalar=cw.t[:, 0, fi:fi + 1],
                                                                                      in1=yt.t[:, 0:nt], op0=ALU.mult, op1=ALU.add),
                         reads=[pu.b, cw.b, yt.b], writes=[yt.b])
                    tiles.append(yt)
                s_ = sa[ft % 2]
                P.op("act", lambda e: e.activation(out=s_.t[:, 0:nt], in_=tiles[0].t[:, 0:nt], func=AF.Silu),
                     reads=[tiles[0].b], writes=[s_.b])
                P.op("pool", lambda e: e.tensor_tensor(out=gt.t[:, ft, 0:nt], in0=s_.t[:, 0:nt], in1=tiles[1].t[:, 0:nt], op=ALU.mult),
                     reads=[s_.b, tiles[1].b], writes=[gt.b])

            def down_unit(gi, j, ob, half):
                gt = gTt2[gi % 2]
                h_ = hb[ob % 2]
                o_ = ob_[ob % 2]
                if half == 0:
                    P.dma("sp", lambda e: e.dma_start(out=h_.t[:], in_=H2[ob * 128:(ob + 1) * 128, :]),
                          h_.b, reads=[b_H2[ob]], writes=[h_.b])
                pd = pD[(ob * 2 + half) % 4]
                for ft in range(22):
                    P.op("pe", lambda e, ft=ft: e.matmul(pd.t[:, 0:512], lhsT=gt.t[:, ft, j * 128:(j + 1) * 128],
                                                         rhs=Wdn.t[:, ft, half * 512:(half + 1) * 512],
                                                         start=(ft == 0), stop=(ft == 21)),
                         reads=[gt.b, Wdn.b], writes=[pd.b])
                P.op("dve", lambda e: e.tensor_tensor(out=o_.t[:, half * 512:(half + 1) * 512], in0=pd.t[:, 0:512],
                                                      in1=h_.t[:, half * 512:(half + 1) * 512], op=ALU.add),
                     reads=[pd.b, h_.b], writes=[o_.b])
                if half == 1:
                    P.dma("pool", lambda e: e.dma_start(out=y[ob * 128:(ob + 1) * 128, :], in_=o_.t[:]),
                          o_.b, reads=[o_.b], writes=[b_y])

            pending_down = []
            for gi in range(NG):
                blks = list(range(gi * GB, min(NO, (gi + 1) * GB)))
                nt = len(blks) * 128
                ug = u2g[gi % 2]
                up_ = u2g[(gi + 1) % 2]
                for j, ob in enumerate(blks):
                    P.dma("sp", lambda e, ug=ug, j=j, ob=ob: e.dma_start(out=ug.t[:, :, 2 + j * 128:2 + (j + 1) * 128],
                                                                        in_=U2T[ob].rearrange("p (a b) -> p a b", a=8)),
                          ug.b, reads=[b_U2T[ob]], writes=[ug.b])
                if gi > 0:
                    P.op("pool", lambda e, ug=ug, up_=up_: e.tensor_copy(ug.t[:, :, 0:2], up_.t[:, :, NT:NT + 2]),
                         reads=[up_.b], writes=[ug.b])
                nd = len(pending_down)
                slots = {int(round((k + 1) * 22.0 / (nd + 1))): k for k in range(nd)} if nd else {}
                for ft in range(22):
                    up_pair(gi, ft, ug, nt)
                    if (ft + 1) in slots:
                        down_unit(*pending_down[slots[ft + 1]])
                pending_down = [(gi, j, ob, half) for j, ob in enumerate(blks) for half in range(2)]
            for u in pending_down:
                down_unit(*u)
            P.wait_all("pool", [b_y])
            P.emit()
        print("n_inst", P.n_inst, "n_wait", P.n_wait, "ndsem", P.ndsem)
    return nc


def make_tables(NC, NO, p, S):
    NB = NC + NO
    L = N_META + S
    if p == 0:
        ctx_pos = np.full(NC * 128, -1, np.int64)
        own_pos = np.arange(NO * 128)
    else:
        ctx_pos = np.arange(NC * 128) - PAD
        own_pos = L - NO * 128 + np.arange(NO * 128)
    pos = np.concatenate([ctx_pos, own_pos])
    valid = pos >= 0
    posf = np.where(valid, pos, 0).astype(np.float32)
    inv_r = np.power(np.float32(10000.0), -np.arange(128, dtype=np.float32) / np.float32(128))
    ang = posf[:, None] * inv_r[None, :]
    c, s = np.cos(ang), np.sin(ang)
    rope_r = np.concatenate([c, c, s, s], axis=1).astype(np.float32)
    inv_d = np.power(np.float32(500000.0), -np.arange(8, dtype=np.float32) / np.float32(8))
    ang = posf[:, None] * inv_d[None, :]
    c, s = np.cos(ang), np.sin(ang)
    rope_d = np.concatenate([c, c, s, s], axis=1).astype(np.float32)
    kb = np.where(valid, 1.0, 0.0).astype(np.float32).reshape(NB, 128).T.copy()
    idx = np.arange(128)
    cm = (idx[:, None] <= idx[None, :]).astype(np.float32)
    rdec = np.zeros((128, 8), np.float32)
    for h in range(4):
        rdec[:, h] = GAM[h] ** (idx + 1.0)
        rdec[:, 4 + h] = (256 ** -0.5) * GAM[h] ** (127.0 - idx)
    return rope_r, rope_d, kb, cm, rdec


_NC_CACHE = {}


def run(inputs, NC, NO, debug=False, trace=False):
    x = np.asarray(inputs["x"], np.float32)
    B, S, _ = x.shape
    assert S == 128 * (NC + NO - 1)
    L = N_META + S
    meta = np.asarray(inputs["meta_tokens"], np.float32)
    key = (NC, NO, debug)
    if key not in _NC_CACHE:
        _NC_CACHE[key] = build(NC, NO, debug)
    nc = _NC_CACHE[key]
    f = lambda k: np.ascontiguousarray(np.asarray(inputs[k], np.float32)[0])
    common = {
        "w_in": f("w_in"), "w_ret_o": f("w_ret_o"), "w_diff_o": f("w_diff_o"), "w_out": f("w_out"),
        "w_up": f("w_up"), "w_down": f("w_down"), "norm1_w": f("norm1_w"), "norm2_w": f("norm2_w"),
        "qk_norm_w": np.concatenate([f("q_norm_w"), f("q_norm_w"), f("k_norm_w"), f("k_norm_w")]),
        "lambdas": np.concatenate([f("lambda_q1"), f("lambda_k1"), f("lambda_q2"), f("lambda_k2")]),
        "subln_w": f("diff_subln_w"), "conv_w": f("conv_w"), "conv_b": f("conv_b"),
    }
    tabs = [make_tables(NC, NO, p, S) for p in range(2)]
    in_maps = []
    for b in range(B):
        seq = np.concatenate([meta, x[b]], axis=0)
        for p in range(2):
            if p == 0:
                xc_ = np.zeros((NC * 128, D), np.float32)
                xo_ = seq[0:NO * 128]
            else:
                xc_ = np.concatenate([np.zeros((PAD, D), np.float32), seq[0:NC * 128 - PAD]], axis=0)
                xo_ = seq[L - NO * 128:L]
            rr, rd, kb, cm, rdec = tabs[p]
            m = dict(common)
            m.update({"xc": np.ascontiguousarray(xc_), "xo": np.ascontiguousarray(xo_), "rope_r": rr, "rope_d": rd,
                      "kbias": kb, "cmask": cm, "rdec": rdec})
            in_maps.append(m)
    res = run_bass_kernel_spmd(nc, in_maps, core_ids=list(range(len(in_maps))), trace=trace)
    out = np.empty((B, S, D), np.float32)
    split = (NO * 128 - N_META) - 64
    for b in range(B):
        y0 = res.results[2 * b]["y"]
        y1 = res.results[2 * b + 1]["y"]
        out[b, :split] = y0[N_META:N_META + split]
        off1 = L - NO * 128
        out[b, split:] = y1[N_META + split - off1:]
    return out, res


def kernel(**inputs):
    out, _ = run(inputs, 32, 33)
    return out
```

```python
import math
import numpy as np
from contextlib import ExitStack
import concourse.bass as bass
import concourse.mybir as mybir
from concourse.bass_utils import run_bass_kernel_spmd

F32 = mybir.dt.float32
BF16 = mybir.dt.bfloat16
AF = mybir.ActivationFunctionType
ALU = mybir.AluOpType
AX = mybir.AxisListType

D = 1024
N_META = 16
PAD = 112
FFN = 2816
IN_COLS = 11264
EPS = 1e-6
C_RQ, C_RK, C_RV, C_RG, C_DQ, C_DK, C_DV, C_GT = 0, 1024, 2048, 4096, 6144, 7168, 8192, 9216
NEGB = -30000.0
LAM_INIT = 0.8 - 0.6 * math.exp(-0.3 * 0)
GAM = [1.0 - 2.0 ** (-5.0 - h) for h in range(4)]

SAME_ENGINE_SYNC = True


class Buf:
    __slots__ = ("name", "last_write", "reads", "dsem", "dcount")

    def __init__(self, name=""):
        self.name = name
        self.last_write = None
        self.reads = {}
        self.dsem = None
        self.dcount = 0


class T:
    def __init__(self, t, name):
        self.t = t
        self.b = Buf(name)


class Prog:
    ENGS = ("pe", "act", "dve", "pool", "sp")
    ENGOBJ = {"pe": "tensor", "act": "scalar", "dve": "vector", "pool": "gpsimd", "sp": "sync"}

    def __init__(self, nc, stack):
        self.nc = nc
        self.stack = stack
        self.q = {e: [] for e in self.ENGS}
        self.ecount = {e: 0 for e in self.ENGS}
        self.sems = {}
        for e in self.ENGS:
            self.sems[("e", e)] = stack.enter_context(nc.semaphore("s_" + e))
        self.waited = {e: {} for e in self.ENGS}
        self.ndsem = 0
        self.n_inst = 0
        self.n_wait = 0

    def _dsem(self, buf):
        if buf.dsem is None:
            buf.dsem = ("d", self.ndsem)
            self.sems[buf.dsem] = self.stack.enter_context(self.nc.semaphore("d%d" % self.ndsem))
            self.ndsem += 1
        return buf.dsem

    def _deps(self, eng, reads, writes):
        deps = {}

        def add(t):
            if t is None:
                return
            k, v = t
            if deps.get(k, -1) < v:
                deps[k] = v
        for b in reads:
            add(b.last_write)
        for b in writes:
            add(b.last_write)
            for k, v in b.reads.items():
                add((k, v))
        out = []
        w = self.waited[eng]
        for k, v in deps.items():
            if k == ("e", eng) and (eng == "pe" or not SAME_ENGINE_SYNC):
                continue
            if w.get(k, -1) >= v:
                continue
            w[k] = v
            out.append((k, v))
        return out

    def _commit(self, tok, reads, writes):
        k, v = tok
        for b in writes:
            b.last_write = tok
            b.reads = {}
        for b in reads:
            if b.reads.get(k, -1) < v:
                b.reads[k] = v

    def op(self, eng, fn, reads=(), writes=()):
        waits = self._deps(eng, reads, writes)
        self.ecount[eng] += 1
        tok = (("e", eng), self.ecount[eng])
        self.q[eng].append((waits, fn, tok[0], 1))
        self._commit(tok, reads, writes)
        self.n_inst += 1
        self.n_wait += len(waits)
        return tok

    def dma(self, eng, fn, sb, reads=(), writes=()):
        waits = self._deps(eng, reads, writes)
        k = self._dsem(sb)
        sb.dcount += 16
        tok = (k, sb.dcount)
        self.q[eng].append((waits, fn, k, 16))
        self._commit(tok, reads, writes)
        self.n_inst += 1
        self.n_wait += len(waits)
        return tok

    def wait_all(self, eng, bufs):
        waits = self._deps(eng, bufs, bufs)
        self.q[eng].append((waits, None, None, 0))

    def emit(self):
        nc = self.nc
        sems = self.sems
        with nc.Block() as block:
            for e in self.ENGS:
                lst = self.q[e]

                def body(eo, lst=lst):
                    for waits, fn, sk, inc in lst:
                        for k, v in waits:
                            eo.wait_ge(sems[k], v)
                        if fn is not None:
                            fn(eo).then_inc(sems[sk], inc)
                getattr(block, self.ENGOBJ[e])(body)
        self.q = {e: [] for e in self.ENGS}


def bc_mid(ap, n):
    return bass.AP(ap.tensor, ap.offset, [list(ap.ap[0]), [0, n], list(ap.ap[1])])


def bc_last(ap, k):
    return bass.AP(ap.tensor, ap.offset, [list(ap.ap[0]), list(ap.ap[1]), [0, k]])


def bc_part(dram_ap_1d, n):
    return bass.AP(dram_ap_1d.tensor, dram_ap_1d.offset, [[0, 128], [1, n]])


def build(NC, NO, debug=False):
    NB = NC + NO
    nc = bass.Bass("TRN2", target_bir_lowering=False)

    def din(name, shape, dt=F32):
        return nc.dram_tensor(name, list(shape), dt, kind="ExternalInput").ap()

    okind = "ExternalOutput" if debug else "Internal"

    def dscr(name, shape, dt):
        return nc.dram_tensor(name, list(shape), dt, kind=okind).ap()

    xc = din("xc", [NC * 128, D])
    xo = din("xo", [NO * 128, D])
    w_in = din("w_in", [D, IN_COLS])
    w_ret_o = din("w_ret_o", [2048, D])
    w_diff_o = din("w_diff_o", [D, D])
    w_out = din("w_out", [D, D])
    w_up = din("w_up", [D, 2 * FFN])
    w_down = din("w_down", [FFN, D])
    norm1_w = din("norm1_w", [D])
    norm2_w = din("norm2_w", [D])
    qk_norm_w = din("qk_norm_w", [256])
    lambdas = din("lambdas", [256])
    subln_w = din("subln_w", [128])
    conv_w = din("conv_w", [3, 2 * FFN])
    conv_b = din("conv_b", [2 * FFN])
    rope_r = din("rope_r", [NB * 128, 512])
    rope_d = din("rope_d", [NB * 128, 32])
    kbias_d = din("kbias", [128, NB])
    cmask_d = din("cmask", [128, 128])
    rdec_d = din("rdec", [128, 8])
    y = nc.dram_tensor("y", [NO * 128, D], F32, kind="ExternalOutput").ap()

    UT = dscr("UT", [NB, 128, 1024], BF16)
    GT = dscr("GT", [NO, 128, 2048], BF16)
    DOT = dscr("DOT", [NO, 128, 1024], BF16)
    H2 = dscr("H2", [NO * 128, D], F32)
    U2T = dscr("U2T", [NO, 128, 1024], BF16)
    b_UT = [Buf("UT%d" % i) for i in range(NB)]
    b_GT = [Buf("GT%d" % i) for i in range(NO)]
    b_DOT = [Buf("DOT%d" % i) for i in range(NO)]
    b_H2 = [Buf("H2%d" % i) for i in range(NO)]
    b_U2T = [Buf("U2T%d" % i) for i in range(NO)]
    b_y = Buf("y")

    w_in_v = w_in.rearrange("(k p) n -> p k n", p=128)

    with ExitStack() as gst:
        P = Prog(nc, gst)

        def sbt(st, name, shape, dt):
            return T(st.enter_context(nc.sbuf_tensor("sb_" + name, list(shape), dt)), name)

        def pbank(st, name, dt=F32):
            n = 512 if dt == F32 else 1024
            return T(st.enter_context(nc.psum_tensor("ps_" + name, [128, n], dt)), name)

        ident = sbt(gst, "ident", [128, 128], BF16)
        identf = sbt(gst, "identf", [128, 128], F32)
        cmask = sbt(gst, "cmask", [128, 128], F32)
        cmask2 = sbt(gst, "cmask2", [128, 2, 128], BF16)
        kbias = sbt(gst, "kbias", [128, NB], F32)
        rdec = sbt(gst, "rdec", [128, 8], F32)
        lam = sbt(gst, "lam", [128, 4], F32)
        lamv = sbt(gst, "lamv", [128, 256], F32)
        lamt = sbt(gst, "lamt", [128, 128], F32)
        lams = sbt(gst, "lams", [128, 2], F32)
        sublnw = sbt(gst, "sublnw", [128, 128], F32)
        wqk = sbt(gst, "wqk", [128, 4, 64], F32)

        P.op("pool", lambda e: e.iota(identf.t[:], pattern=[[1, 128]], base=0, channel_multiplier=-1,
                                      allow_small_or_imprecise_dtypes=True), writes=[identf.b])
        P.op("dve", lambda e: e.tensor_scalar(ident.t[:], identf.t[:], 0.0, None, op0=ALU.is_equal),
             reads=[identf.b], writes=[ident.b])
        P.dma("sp", lambda e: e.dma_start(out=cmask.t[:], in_=cmask_d), cmask.b, writes=[cmask.b])
        P.dma("sp", lambda e: e.dma_start(out=kbias.t[:], in_=kbias_d), kbias.b, writes=[kbias.b])
        P.dma("sp", lambda e: e.dma_start(out=rdec.t[:], in_=rdec_d), rdec.b, writes=[rdec.b])
        P.dma("sp", lambda e: e.dma_start(out=lamv.t[:], in_=bc_part(lambdas, 256)), lamv.b, writes=[lamv.b])
        P.dma("sp", lambda e: e.dma_start(out=sublnw.t[:], in_=bc_part(subln_w, 128)), sublnw.b, writes=[sublnw.b])
        P.dma("sp", lambda e: e.dma_start(out=wqk.t[:].rearrange("p a b -> p (a b)"), in_=bc_part(qk_norm_w, 256)),
              wqk.b, writes=[wqk.b])
        P.op("dve", lambda e: e.tensor_copy(cmask2.t[:, 0, :], cmask.t[:]), reads=[cmask.b], writes=[cmask2.b])
        P.op("dve", lambda e: e.tensor_copy(cmask2.t[:, 1, :], cmask.t[:]), reads=[cmask.b], writes=[cmask2.b])
        P.op("dve", lambda e: e.tensor_tensor(out=lamt.t[:, 0:64], in0=lamv.t[:, 0:64], in1=lamv.t[:, 64:128], op=ALU.mult),
             reads=[lamv.b], writes=[lamt.b])
        P.op("dve", lambda e: e.tensor_tensor(out=lamt.t[:, 64:128], in0=lamv.t[:, 128:192], in1=lamv.t[:, 192:256], op=ALU.mult),
             reads=[lamv.b], writes=[lamt.b])
        P.op("dve", lambda e: e.tensor_reduce(out=lams.t[:, 0:2], in_=lamt.t[:].rearrange("p (a b) -> p a b", a=2),
                                              axis=AX.X, op=ALU.add), reads=[lamt.b], writes=[lams.b])
        P.op("act", lambda e: e.activation(out=lams.t[:], in_=lams.t[:], func=AF.Exp), reads=[lams.b], writes=[lams.b])
        P.op("dve", lambda e: e.tensor_tensor(out=lam.t[:, 0:1], in0=lams.t[:, 0:1], in1=lams.t[:, 1:2], op=ALU.subtract),
             reads=[lams.b], writes=[lam.b])
        P.op("dve", lambda e: e.tensor_scalar(lam.t[:, 0:1], lam.t[:, 0:1], LAM_INIT, None, op0=ALU.add),
             reads=[lam.b], writes=[lam.b])
        P.op("dve", lambda e: e.tensor_scalar(sublnw.t[:], sublnw.t[:], 1.0 - LAM_INIT, None, op0=ALU.mult),
             reads=[sublnw.b], writes=[sublnw.b])

        def norm_part(xt, nw, sq, ss, rs, ub, use_pow=False):
            P.op("act", lambda e: e.activation(out=sq.t[:], in_=xt.t[:], func=AF.Square, accum_out=ss.t[:, 0:1]),
                 reads=[xt.b], writes=[sq.b, ss.b])
            if use_pow:
                P.op("dve", lambda e: e.tensor_scalar(rs.t[:, 0:1], ss.t[:, 0:1], 1.0 / D, EPS, op0=ALU.mult, op1=ALU.add),
                     reads=[ss.b], writes=[rs.b])
                P.op("pool", lambda e: e.tensor_tensor(out=rs.t[:, 0:1], in0=rs.t[:, 0:1], in1=mhalf.t[:, 0:1], op=ALU.pow),
                     reads=[rs.b, mhalf.b], writes=[rs.b])
            else:
                P.op("act", lambda e: e.activation(out=rs.t[:, 0:1], in_=ss.t[:, 0:1], func=AF.Sqrt, scale=1.0 / D, bias=eps_t.t[:, 0:1]),
                     reads=[ss.b, eps_t.b], writes=[rs.b])
                P.op("dve", lambda e: e.reciprocal(rs.t[:, 0:1], rs.t[:, 0:1]), reads=[rs.b], writes=[rs.b])
            P.op("dve", lambda e: e.scalar_tensor_tensor(out=ub.t[:], in0=xt.t[:], scalar=rs.t[:, 0:1], in1=nw.t[:],
                                                         op0=ALU.mult, op1=ALU.mult),
                 reads=[xt.b, rs.b, nw.b], writes=[ub.b])

        def tr_part(ub, pT, uT):
            for half in range(2):
                pt = pT[half]
                for j in range(4):
                    kc = half * 4 + j
                    P.op("pe", lambda e, kc=kc, j=j, pt=pt: e.transpose(pt.t[:, j * 128:(j + 1) * 128],
                                                                        ub.t[:, kc * 128:(kc + 1) * 128], ident.t[:]),
                         reads=[ub.b, ident.b], writes=[pt.b])
                if half == 0:
                    P.op("dve", lambda e, pt=pt: e.tensor_copy(uT.t[:, 0:4, :].rearrange("p a b -> p (a b)"), pt.t[:, 0:512]),
                         reads=[pt.b], writes=[uT.b])
                else:
                    P.op("act", lambda e, pt=pt: e.copy(uT.t[:, 4:8, :].rearrange("p a b -> p (a b)"), pt.t[:, 0:512]),
                         reads=[pt.b], writes=[uT.b])

        def norm_transpose(xt, nw, sq, ss, rs, ub, pT, uT):
            norm_part(xt, nw, sq, ss, rs, ub)
            tr_part(ub, pT, uT)

        eps_t = sbt(gst, "eps_t", [128, 1], F32)
        P.op("pool", lambda e: e.memset(eps_t.t[:], EPS), writes=[eps_t.b])
        mhalf = sbt(gst, "mhalf", [128, 8], F32)
        P.op("pool", lambda e: e.memset(mhalf.t[:], -0.5), writes=[mhalf.b])

        with ExitStack() as st:
            n1w = sbt(st, "n1w", [128, D], F32)
            P.dma("sp", lambda e: e.dma_start(out=n1w.t[:], in_=bc_part(norm1_w, D)), n1w.b, writes=[n1w.b])
            xb = [sbt(st, "x%d" % i, [128, D], F32) for i in range(3)]
            sq = [sbt(st, "sq%d" % i, [128, D], F32) for i in range(2)]
            ss = [sbt(st, "ss%d" % i, [128, 1], F32) for i in range(2)]
            rs = [sbt(st, "rs%d" % i, [128, 1], F32) for i in range(2)]
            ub = [sbt(st, "ub%d" % i, [128, D], BF16) for i in range(2)]
            uT = [sbt(st, "uT%d" % i, [128, 8, 128], BF16) for i in range(2)]
            pT = [pbank(st, "pT%d" % i, BF16) for i in range(4)]
            def p0_x(blk):
                src = xc[blk * 128:(blk + 1) * 128, :] if blk < NC else xo[(blk - NC) * 128:(blk - NC + 1) * 128, :]
                x_ = xb[blk % 3]
                P.dma("sp", lambda e: e.dma_start(out=x_.t[:], in_=src), x_.b, writes=[x_.b])
                norm_part(x_, n1w, sq[blk % 2], ss[blk % 2], rs[blk % 2], ub[blk % 2])

            def p0_y(blk):
                u_ = uT[blk % 2]
                tr_part(ub[blk % 2], pT[(blk % 2) * 2:(blk % 2) * 2 + 2], u_)
                P.dma("pool", lambda e: e.dma_start(out=UT[blk], in_=u_.t[:].rearrange("p a b -> p (a b)")),
                      u_.b, reads=[u_.b], writes=[b_UT[blk]])
            p0_x(0)
            for blk in range(NB):
                if blk + 1 < NB:
                    p0_x(blk + 1)
                p0_y(blk)
            P.emit()

        with ExitStack() as st:
            WR = [sbt(st, "WR%d" % i, [128, 8, 1536], BF16) for i in range(2)]
            uTb = [sbt(st, "ruT%d" % i, [128, 8, 128], BF16) for i in range(3)]
            RT = [sbt(st, "RT%d" % i, [128, 512], F32) for i in range(3)]
            Rf = sbt(st, "Rf", [128, 2, 512], F32)
            Rb = sbt(st, "Rb", [128, 2, 512], BF16)
            Aq = sbt(st, "Aq", [128, 256], F32)
            Bq = sbt(st, "Bq", [128, 256], F32)
            Ak = sbt(st, "Ak", [128, 256], F32)
            Bk = sbt(st, "Bk", [128, 256], F32)
            qr = [sbt(st, "qr%d" % i, [128, 256], BF16) for i in range(2)]
            kr = [sbt(st, "kr%d" % i, [128, 256], BF16) for i in range(2)]
            vb = [sbt(st, "vb%d" % i, [128, 512], BF16) for i in range(2)]
            sg = [sbt(st, "sg%d" % i, [128, 512], F32) for i in range(2)]
            qkT = sbt(st, "qkT", [128, 4, 128], BF16)
            Sm = sbt(st, "Sm", [128, 128], BF16)
            bst = sbt(st, "bst", [128, 6], F32)
            mv = sbt(st, "mv", [128, 2], F32)
            grs = sbt(st, "grs", [128, 1], F32)
            on = sbt(st, "on", [128, 512], F32)
            gtd = sbt(st, "gtd", [128, 512], BF16)
            gT = [sbt(st, "gT%d" % i, [128, 4, 128], BF16) for i in range(2)]
            pQK = pbank(st, "pQK")
            pV = pbank(st, "pV")
            pG = pbank(st, "pG")
            pTq = pbank(st, "pTq", BF16)
            pSv = pTq.t[:, 512:768].bitcast(F32)
            pTg = pbank(st, "pTg", BF16)
            pO2 = [pbank(st, "pO%d" % i) for i in range(2)]
            pR1 = pbank(st, "pR1")
            Rb2 = [sbt(st, "Rb%d" % i, [128, 2, 512], BF16) for i in range(2)]
            bst2 = [sbt(st, "bst%d" % i, [128, 6], F32) for i in range(2)]
            mv2 = [sbt(st, "mv%d" % i, [128, 2], F32) for i in range(2)]
            grs2 = [sbt(st, "grs%d" % i, [128, 1], F32) for i in range(2)]
            on2 = [sbt(st, "on%d" % i, [128, 512], F32) for i in range(2)]
            gtd2 = [sbt(st, "gtd%d" % i, [128, 512], BF16) for i in range(2)]

            def load_WR(h):
                w = WR[h % 2]
                for (c0, n, o0) in ((C_RQ + h * 256, 256, 0), (C_RK + h * 256, 256, 256),
                                    (C_RV + h * 512, 512, 512), (C_RG + h * 512, 512, 1024)):
                    P.dma("pool", lambda e, w=w, c0=c0, n=n, o0=o0: e.dma_start(out=w.t[:, :, o0:o0 + n],
                                                                                   in_=w_in_v[:, :, c0:c0 + n]),
                          w.b, writes=[w.b])
            load_WR(0)
            for h in range(4):
                if h + 1 < 4:
                    load_WR(h + 1)
                w = WR[h % 2]
                g = GAM[h]
                P.op("pool", lambda e: e.memset(Rf.t[:], 0.0), writes=[Rf.b])
                P.op("pool", lambda e: e.memset(Rb2[0].t[:], 0.0), writes=[Rb2[0].b])
                P.op("pool", lambda e: e.memset(Rb2[1].t[:], 0.0), writes=[Rb2[1].b])

                def A1(blk):
                    own = blk >= NC
                    u_ = uTb[blk % 3]
                    rt = RT[blk % 3]
                    k_ = kr[blk % 2]
                    q_ = qr[blk % 2]
                    P.dma("sp", lambda e: e.dma_start(out=u_.t[:].rearrange("p a b -> p (a b)"), in_=UT[blk]),
                          u_.b, reads=[b_UT[blk]], writes=[u_.b])
                    P.dma("sp", lambda e: e.dma_start(out=rt.t[:], in_=rope_r[blk * 128:(blk + 1) * 128, :]),
                          rt.b, writes=[rt.b])
                    c0 = 0 if own else 256
                    for kc in range(8):
                        P.op("pe", lambda e, kc=kc: e.matmul(pQK.t[:, c0:512], lhsT=u_.t[:, kc, :], rhs=w.t[:, kc, c0:512],
                                                             start=(kc == 0), stop=(kc == 7)),
                             reads=[u_.b, w.b], writes=[pQK.b])
                    P.op("dve", lambda e: e.scalar_tensor_tensor(out=Ak.t[:], in0=pQK.t[:, 256:512], scalar=rdec.t[:, 4 + h:5 + h],
                                                                 in1=rt.t[:, 0:256], op0=ALU.mult, op1=ALU.mult),
                         reads=[pQK.b, rdec.b, rt.b], writes=[Ak.b])
                    P.op("dve", lambda e: e.scalar_tensor_tensor(out=Bk.t[:], in0=pQK.t[:, 256:512], scalar=rdec.t[:, 4 + h:5 + h],
                                                                 in1=rt.t[:, 256:512], op0=ALU.mult, op1=ALU.mult),
                         reads=[pQK.b, rdec.b, rt.b], writes=[Bk.b])
                    if own:
                        P.op("dve", lambda e: e.scalar_tensor_tensor(out=Aq.t[:], in0=pQK.t[:, 0:256], scalar=rdec.t[:, h:h + 1],
                                                                     in1=rt.t[:, 0:256], op0=ALU.mult, op1=ALU.mult),
                             reads=[pQK.b, rdec.b, rt.b], writes=[Aq.b])
                        P.op("dve", lambda e: e.scalar_tensor_tensor(out=Bq.t[:], in0=pQK.t[:, 0:256], scalar=rdec.t[:, h:h + 1],
                                                                     in1=rt.t[:, 256:512], op0=ALU.mult, op1=ALU.mult),
                             reads=[pQK.b, rdec.b, rt.b], writes=[Bq.b])
                    P.op("pool", lambda e: e.tensor_tensor(out=k_.t[:, 0:128], in0=Ak.t[:, 0:128], in1=Bk.t[:, 128:256], op=ALU.subtract),
                         reads=[Ak.b, Bk.b], writes=[k_.b])
                    P.op("pool", lambda e: e.tensor_tensor(out=k_.t[:, 128:256], in0=Ak.t[:, 128:256], in1=Bk.t[:, 0:128], op=ALU.add),
                         reads=[Ak.b, Bk.b], writes=[k_.b])
                    if own:
                        P.op("pool", lambda e: e.tensor_tensor(out=q_.t[:, 0:128], in0=Aq.t[:, 0:128], in1=Bq.t[:, 128:256], op=ALU.subtract),
                             reads=[Aq.b, Bq.b], writes=[q_.b])
                        P.op("pool", lambda e: e.tensor_tensor(out=q_.t[:, 128:256], in0=Aq.t[:, 128:256], in1=Bq.t[:, 0:128], op=ALU.add),
                             reads=[Aq.b, Bq.b], writes=[q_.b])

                def A2(blk):
                    u_ = uTb[blk % 3]
                    v_ = vb[blk % 2]
                    for kc in range(8):
                        P.op("pe", lambda e, kc=kc: e.matmul(pV.t[:, 0:512], lhsT=u_.t[:, kc, :], rhs=w.t[:, kc, 512:1024],
                                                             start=(kc == 0), stop=(kc == 7)),
                             reads=[u_.b, w.b], writes=[pV.b])
                    P.op("act", lambda e: e.copy(v_.t[:], pV.t[:, 0:512]), reads=[pV.b], writes=[v_.b])

                def A3(blk):
                    if blk < NC:
                        return
                    u_ = uTb[blk % 3]
                    s_ = sg[blk % 2]
                    for kc in range(8):
                        P.op("pe", lambda e, kc=kc: e.matmul(pG.t[:, 0:512], lhsT=u_.t[:, kc, :], rhs=w.t[:, kc, 1024:1536],
                                                             start=(kc == 0), stop=(kc == 7)),
                             reads=[u_.b, w.b], writes=[pG.b])
                    P.op("act", lambda e: e.activation(out=s_.t[:], in_=pG.t[:, 0:512], func=AF.Silu), reads=[pG.b], writes=[s_.b])

                def B1(blk):
                    if blk < NC:
                        return
                    k_ = kr[blk % 2]
                    q_ = qr[blk % 2]
                    for j in range(4):
                        srcT = q_ if j < 2 else k_
                        c = j % 2
                        P.op("pe", lambda e, j=j, c=c, srcT=srcT: e.transpose(pTq.t[:, j * 128:(j + 1) * 128],
                                                                              srcT.t[:, c * 128:(c + 1) * 128], ident.t[:]),
                             reads=[srcT.b, ident.b], writes=[pTq.b])
                    P.op("act", lambda e: e.copy(qkT.t[:].rearrange("p a b -> p (a b)"), pTq.t[:, 0:512]),
                         reads=[pTq.b], writes=[qkT.b])

                def B2(blk):
                    if blk < NC:
                        return
                    for c in range(2):
                        P.op("pe", lambda e, c=c: e.matmul(pSv, lhsT=qkT.t[:, 2 + c, :], rhs=qkT.t[:, c, :],
                                                           start=(c == 0), stop=(c == 1)),
                             reads=[qkT.b], writes=[pTq.b])
                    P.op("dve", lambda e: e.scalar_tensor_tensor(out=Sm.t[:], in0=pSv, scalar=float(g ** -128.0),
                                                                 in1=cmask.t[:], op0=ALU.mult, op1=ALU.mult),
                         reads=[pTq.b, cmask.b], writes=[Sm.b])

                def B3a(blk):
                    if blk < NC:
                        return
                    v_ = vb[blk % 2]
                    po = pO2[blk % 2]
                    rb = Rb2[(blk - 1) % 2]
                    P.op("pe", lambda e: e.matmul(po.t[:, 0:512], lhsT=Sm.t[:], rhs=v_.t[:], start=True, stop=False),
                         reads=[Sm.b, v_.b], writes=[po.b])
                    for c in range(2):
                        P.op("pe", lambda e, c=c: e.matmul(po.t[:, 0:512], lhsT=qkT.t[:, c, :], rhs=rb.t[:, c, :],
                                                           start=False, stop=(c == 1)),
                             reads=[qkT.b, rb.b], writes=[po.b])

                def CH(blk):
                    if blk < NC:
                        return
                    par = blk % 2
                    po = pO2[par]
                    s_ = sg[par]
                    bst_, mv_, grs_, on_, gtd_ = bst2[par], mv2[par], grs2[par], on2[par], gtd2[par]
                    P.op("dve", lambda e: e.bn_stats(bst_.t[:], po.t[:, 0:512]), reads=[po.b], writes=[bst_.b])
                    P.op("dve", lambda e: e.bn_aggr(mv_.t[:], bst_.t[:]), reads=[bst_.b], writes=[mv_.b])
                    P.op("dve", lambda e: e.tensor_scalar(grs_.t[:], mv_.t[:, 1:2], EPS, None, op0=ALU.add),
                         reads=[mv_.b], writes=[grs_.b])
                    P.op("pool", lambda e: e.tensor_tensor(out=grs_.t[:], in0=grs_.t[:], in1=mhalf.t[:, 0:1], op=ALU.pow),
                         reads=[grs_.b, mhalf.b], writes=[grs_.b])
                    P.op("dve", lambda e: e.tensor_scalar(on_.t[:], po.t[:, 0:512], mv_.t[:, 0:1], grs_.t[:, 0:1],
                                                          op0=ALU.subtract, op1=ALU.mult),
                         reads=[po.b, mv_.b, grs_.b], writes=[on_.b])
                    P.op("pool", lambda e: e.tensor_tensor(out=gtd_.t[:], in0=on_.t[:], in1=s_.t[:], op=ALU.mult),
                         reads=[on_.b, s_.b], writes=[gtd_.b])

                def ST(blk, c):
                    if blk >= NB - 1:
                        return
                    k_ = kr[blk % 2]
                    v_ = vb[blk % 2]
                    P.op("pe", lambda e: e.matmul(pR1.t[:, 0:512], lhsT=k_.t[:, c * 128:(c + 1) * 128], rhs=v_.t[:],
                                                  start=True, stop=True),
                         reads=[k_.b, v_.b], writes=[pR1.b])
                    P.op("dve", lambda e: e.scalar_tensor_tensor(out=Rf.t[:, c, :], in0=Rf.t[:, c, :], scalar=float(g ** 128.0),
                                                                 in1=pR1.t[:, 0:512], op0=ALU.mult, op1=ALU.add),
                         reads=[Rf.b, pR1.b], writes=[Rf.b])
                    if c == 1:
                        rb = Rb2[blk % 2]
                        P.op("act", lambda e: e.copy(rb.t[:].rearrange("p a b -> p (a b)"), Rf.t[:].rearrange("p a b -> p (a b)")),
                             reads=[Rf.b], writes=[rb.b])

                def G(blk):
                    if blk < NC or blk >= NB:
                        return
                    ob = blk - NC
                    g_ = gT[ob % 2]
                    gtd_ = gtd2[blk % 2]
                    for j in range(4):
                        P.op("pe", lambda e, j=j: e.transpose(pTg.t[:, j * 128:(j + 1) * 128],
                                                              gtd_.t[:, j * 128:(j + 1) * 128], ident.t[:]),
                             reads=[gtd_.b, ident.b], writes=[pTg.b])
                    P.op("act", lambda e: e.copy(g_.t[:].rearrange("p a b -> p (a b)"), pTg.t[:, 0:512]),
                         reads=[pTg.b], writes=[g_.b])
                    P.dma("pool", lambda e: e.dma_start(out=GT[ob][:, h * 512:(h + 1) * 512],
                                                        in_=g_.t[:].rearrange("p a b -> p (a b)")),
                          g_.b, reads=[g_.b], writes=[b_GT[ob]])

                A1(0); A2(0); A3(0)
                for blk in range(NB):
                    nx = blk + 1
                    B1(blk)
                    if nx < NB:
                        A1(nx)
                    B2(blk)
                    if blk < NC:
                        ST(blk, 0)
                        if nx < NB:
                            A2(nx)
                        ST(blk, 1)
                        if nx < NB:
                            A3(nx)
                        continue
                    if nx < NB:
                        A2(nx)
                    B3a(blk)
                    G(blk - 1)
                    ST(blk, 0)
                    if nx < NB:
                        A3(nx)
                    ST(blk, 1)
                    CH(blk)
                G(NB - 1)
                P.emit()

        KTs = dscr("KTs", [8, 128, NB * 128], BF16)
        QTs = dscr("QTs", [8, 128, NO * 128], BF16)
        VVs = dscr("VVs", [8, 128, NB, 128], BF16)
        b_KTs = Buf("KTs"); b_QTs = Buf("QTs"); b_VVs = Buf("VVs")
        KTs_w = KTs.rearrange("h p (b t) -> p h b t", t=128)
        QTs_w = QTs.rearrange("h p (b t) -> p h b t", t=128)
        VVs_w = VVs.rearrange("h p b e -> p h b e")
        with ExitStack() as st:
            WDa = sbt(st, "WDa", [128, 8, 3072], BF16)
            for k0 in range(0, 8, 2):
                P.dma("pool", lambda e, k0=k0: e.dma_start(out=WDa.t[:, k0:k0 + 2, :], in_=w_in_v[:, k0:k0 + 2, C_DQ:C_DQ + 3072]),
                      WDa.b, writes=[WDa.b])
            ropd = sbt(st, "ropd", [128, NB, 32], F32)
            ropd_v = rope_d.rearrange("(b p) c -> p b c", p=128)
            for b0 in range(0, NB, 16):
                b1 = min(NB, b0 + 16)
                P.dma("sp", lambda e, b0=b0, b1=b1: e.dma_start(out=ropd.t[:, b0:b1, :], in_=ropd_v[:, b0:b1, :]),
                      ropd.b, writes=[ropd.b])
            wq8 = sbt(st, "wq8", [128, 8, 64], F32)
            wk8 = sbt(st, "wk8", [128, 8, 64], F32)
            for g8 in range(8):
                P.op("pool", lambda e, g8=g8: e.tensor_copy(wq8.t[:, g8, :], wqk.t[:, 0, :]), reads=[wqk.b], writes=[wq8.b])
                P.op("pool", lambda e, g8=g8: e.tensor_copy(wk8.t[:, g8, :], wqk.t[:, 2, :]), reads=[wqk.b], writes=[wk8.b])
            uTb = [sbt(st, "duT%d" % i, [128, 8, 128], BF16) for i in range(3)]
            NCH = 4
            sqd = [sbt(st, "sqd%d" % i, [128, 8, 64], F32) for i in range(NCH)]
            ssd = [sbt(st, "ssd%d" % i, [128, 8], F32) for i in range(NCH)]
            rsd = [sbt(st, "rsd%d" % i, [128, 8], F32) for i in range(NCH)]
            xn = [[sbt(st, "xn%d_%d" % (pp, i), [128, 8, 64], F32) for i in range(NCH)] for pp in range(2)]
            xbq = [[sbt(st, "xbq%d_%d" % (pp, i), [128, 8, 64], BF16) for i in range(NCH)] for pp in range(2)]
            rc = [[sbt(st, "rc%d_%d" % (pp, i), [128, 8, 16], F32) for i in range(NCH)] for pp in range(2)]
            xw = [[sbt(st, "xw%d_%d" % (pp, i), [128, 8, 16], F32) for i in range(NCH)] for pp in range(2)]
            rsn = [[sbt(st, "rsn%d_%d" % (pp, i), [128, 8, 16], F32) for i in range(NCH)] for pp in range(2)]
            kst = [sbt(st, "kst%d" % i, [128, 8, 128], BF16) for i in range(2)]
            qst = [sbt(st, "qst%d" % i, [128, 8, 128], BF16) for i in range(2)]
            vst = [sbt(st, "vst%d" % i, [128, 8, 128], BF16) for i in range(2)]
            pq = [pbank(st, "pq%d" % i) for i in range(2)]
            pk = [pbank(st, "pk%d" % i) for i in range(2)]
            pvv = [pbank(st, "pvv%d" % i) for i in range(2)]
            pTk = pbank(st, "pTk", BF16)
            pTq = pbank(st, "pTq1", BF16)
            def mk_chains(blk):
                chains = []
                for half in range(2):
                    chains.append((pk[half], wk8, pTk, half, 1024 + half * 512))
                if blk >= NC:
                    for half in range(2):
                        chains.append((pq[half], wq8, pTq, half, half * 512))
                return chains

            def d1_early(blk):
                u_ = uTb[blk % 3]
                par = blk % 2
                P.dma("sp", lambda e: e.dma_start(out=u_.t[:].rearrange("p a b -> p (a b)"), in_=UT[blk]),
                      u_.b, reads=[b_UT[blk]], writes=[u_.b])
                chains = mk_chains(blk)
                for (pb, wt, ptT, half, c0) in chains:
                    for kc in range(8):
                        P.op("pe", lambda e, kc=kc, pb=pb, c0=c0: e.matmul(pb.t[:, 0:512], lhsT=u_.t[:, kc, :], rhs=WDa.t[:, kc, c0:c0 + 512],
                                                                           start=(kc == 0), stop=(kc == 7)),
                             reads=[u_.b, WDa.b], writes=[pb.b])
                for half in range(2):
                    for kc in range(8):
                        P.op("pe", lambda e, kc=kc, half=half: e.matmul(pvv[half].t[:, 0:512], lhsT=u_.t[:, kc, :],
                                                                       rhs=WDa.t[:, kc, 2048 + half * 512:2048 + (half + 1) * 512],
                                                                       start=(kc == 0), stop=(kc == 7)),
                             reads=[u_.b, WDa.b], writes=[pvv[half].b])
                nch = len(chains)
                pvw = [ch[0].t[:, 0:512].rearrange("p (a b) -> p a b", b=64) for ch in chains]
                for ci in range(nch):
                    P.op("act", lambda e, ci=ci: e.activation(out=sqd[ci].t[:], in_=pvw[ci], func=AF.Square),
                         reads=[chains[ci][0].b], writes=[sqd[ci].b])
                for ci in range(nch):
                    P.op("dve", lambda e, ci=ci: e.tensor_reduce(out=ssd[ci].t[:], in_=sqd[ci].t[:], axis=AX.X, op=ALU.add),
                         reads=[sqd[ci].b], writes=[ssd[ci].b])
                for ci in range(nch):
                    P.op("act", lambda e, ci=ci: e.activation(out=rsd[ci].t[:], in_=ssd[ci].t[:], func=AF.Sqrt, scale=1.0 / 64, bias=eps_t.t[:, 0:1]),
                         reads=[ssd[ci].b, eps_t.b], writes=[rsd[ci].b])
                v_ = vst[par]
                for half in range(2):
                    P.op("act", lambda e, half=half: e.copy(v_.t[:, half * 4:half * 4 + 4, :].rearrange("p a b -> p (a b)"), pvv[half].t[:, 0:512]),
                         reads=[pvv[half].b], writes=[v_.b])
                P.dma("act", lambda e: e.dma_start(out=VVs_w[:, :, blk, :], in_=v_.t[:]), v_.b, reads=[v_.b], writes=[b_VVs])
                for ci in range(nch):
                    P.op("dve", lambda e, ci=ci: e.reciprocal(rsd[ci].t[:], rsd[ci].t[:]), reads=[rsd[ci].b], writes=[rsd[ci].b])
                for ci in range(nch):
                    P.op("dve", lambda e, ci=ci: e.tensor_tensor(out=xn[par][ci].t[:], in0=pvw[ci], in1=bc_last(rsd[ci].t[:], 64), op=ALU.mult),
                         reads=[chains[ci][0].b, rsd[ci].b], writes=[xn[par][ci].b])

            def d1_late(blk):
                par = blk % 2
                own = blk >= NC
                ob = blk - NC
                chains = mk_chains(blk)
                nch = len(chains)
                xn_, xb_, rc_, rsn_, xw_ = xn[par], xbq[par], rc[par], rsn[par], xw[par]
                for ci in range(nch):
                    eng = "pool"
                    wt = chains[ci][1]
                    P.op(eng, lambda e, ci=ci, wt=wt: e.tensor_tensor(out=xb_[ci].t[:], in0=xn_[ci].t[:], in1=wt.t[:], op=ALU.mult),
                         reads=[xn_[ci].b, wt.b], writes=[xb_[ci].b])
                    P.op(eng, lambda e, ci=ci, wt=wt: e.tensor_tensor(out=xw_[ci].t[:], in0=xn_[ci].t[:, :, 0:16], in1=wt.t[:, :, 0:16], op=ALU.mult),
                         reads=[xn_[ci].b, wt.b], writes=[xw_[ci].b])
                    P.op(eng, lambda e, ci=ci: e.tensor_tensor(out=rc_[ci].t[:], in0=xw_[ci].t[:],
                                                              in1=bc_mid(ropd.t[:, blk, 0:16], 8), op=ALU.mult),
                         reads=[xw_[ci].b, ropd.b], writes=[rc_[ci].b])
                    P.op(eng, lambda e, ci=ci: e.tensor_tensor(out=rsn_[ci].t[:], in0=xw_[ci].t[:],
                                                              in1=bc_mid(ropd.t[:, blk, 16:32], 8), op=ALU.mult),
                         reads=[xw_[ci].b, ropd.b], writes=[rsn_[ci].b])
                    P.op(eng, lambda e, ci=ci: e.tensor_tensor(out=xb_[ci].t[:, :, 0:8], in0=rc_[ci].t[:, :, 0:8], in1=rsn_[ci].t[:, :, 8:16], op=ALU.subtract),
                         reads=[rc_[ci].b, rsn_[ci].b], writes=[xb_[ci].b])
                    P.op(eng, lambda e, ci=ci: e.tensor_tensor(out=xb_[ci].t[:, :, 8:16], in0=rc_[ci].t[:, :, 8:16], in1=rsn_[ci].t[:, :, 0:8], op=ALU.add),
                         reads=[rc_[ci].b, rsn_[ci].b], writes=[xb_[ci].b])
                for ci in range(nch):
                    ptT, half = chains[ci][2], chains[ci][3]
                    for hh in range(4):
                        P.op("pe", lambda e, ci=ci, hh=hh, ptT=ptT, half=half: e.transpose(ptT.t[:, (half * 4 + hh) * 128:(half * 4 + hh + 1) * 128],
                                                                                          xb_[ci].t[:, 2 * hh:2 * hh + 2, :].rearrange("p a b -> p (a b)"),
                                                                                          ident.t[:]),
                             reads=[xb_[ci].b, ident.b], writes=[ptT.b])
                k_ = kst[par]
                P.op("dve", lambda e: e.tensor_copy(k_.t[:].rearrange("p a b -> p (a b)"), pTk.t[:, 0:1024]), reads=[pTk.b], writes=[k_.b])
                P.dma("act", lambda e: e.dma_start(out=KTs_w[:, :, blk, :], in_=k_.t[:]), k_.b, reads=[k_.b], writes=[b_KTs])
                if own:
                    q_ = qst[par]
                    P.op("act", lambda e: e.copy(q_.t[:].rearrange("p a b -> p (a b)"), pTq.t[:, 0:1024]), reads=[pTq.b], writes=[q_.b])
                    P.dma("act", lambda e: e.dma_start(out=QTs_w[:, :, ob, :], in_=q_.t[:]), q_.b, reads=[q_.b], writes=[b_QTs])

            d1_early(0)
            for blk in range(NB):
                if blk + 1 < NB:
                    d1_early(blk + 1)
                d1_late(blk)
            P.emit()

        with ExitStack() as st:
            KTb = [sbt(st, "KT%d" % i, [128, NB * 128], BF16) for i in range(2)]
            VVb = [sbt(st, "VV%d" % i, [128, NB, 130], BF16) for i in range(2)]
            QT2b = [sbt(st, "QT2%d" % i, [128, NO, 256], BF16) for i in range(2)]
            NPT = 6
            PT = [sbt(st, "PT%d" % i, [128, 4, 128], BF16) for i in range(NPT)]
            zz = sbt(st, "zz", [128, 2], F32)
            a1 = sbt(st, "a1", [128, 128], F32)
            aa = sbt(st, "aa", [128, 128], F32)
            asq = sbt(st, "asq", [128, 128], F32)
            ass = sbt(st, "ass", [128, 1], F32)
            ars = sbt(st, "ars", [128, 1], F32)
            dob = sbt(st, "dob", [128, 128], BF16)
            doT = [sbt(st, "doT%d" % i, [128, 128], BF16) for i in range(2)]
            pP = pbank(st, "pP")
            pTd = pbank(st, "pTd", BF16)
            pSd = [pbank(st, "pSd%d" % i) for i in range(2)]
            pO0 = [pbank(st, "pO0%d" % i) for i in range(2)]
            pO1 = [pbank(st, "pO1%d" % i) for i in range(2)]
            assert NC % 2 == 0
            for i2 in range(2):
                P.op("pool", lambda e, i2=i2: e.memset(VVb[i2].t[:], 0.0), writes=[VVb[i2].b])
                P.op("dve", lambda e, i2=i2: e.tensor_copy(VVb[i2].t[:, :, 128:129], kbias.t[:].rearrange("p (a b) -> p a b", b=1)),
                     reads=[kbias.b], writes=[VVb[i2].b])
                P.op("pool", lambda e, i2=i2: e.memset(QT2b[i2].t[:], 0.0), writes=[QT2b[i2].b])

            def load_head(h):
                kt, vv, q2 = KTb[h % 2], VVb[h % 2], QT2b[h % 2]
                P.dma("sp", lambda e: e.dma_start(out=kt.t[:], in_=KTs[h]), kt.b, reads=[b_KTs], writes=[kt.b])
                P.dma("sp", lambda e: e.dma_start(out=vv.t[:, :, 0:128], in_=VVs[h]), vv.b, reads=[b_VVs], writes=[vv.b])
                P.dma("sp", lambda e: e.dma_start(out=q2.t[0:64, :, 0:128], in_=QTs[h][0:64, :].rearrange("p (i t) -> p i t", t=128)),
                      q2.b, reads=[b_QTs], writes=[q2.b])
                P.dma("sp", lambda e: e.dma_start(out=q2.t[64:128, :, 128:256], in_=QTs[h][64:128, :].rearrange("p (i t) -> p i t", t=128)),
                      q2.b, reads=[b_QTs], writes=[q2.b])
            load_head(0)
            for h in range(8):
                if h + 1 < 8:
                    load_head(h + 1)
                KT, VV, QT2 = KTb[h % 2], VVb[h % 2], QT2b[h % 2]
                b_K = [KT.b] * NB
                b_Kv = VV.b
                b_Q = [QT2.b] * NO
                items = []
                for i in range(NO):
                    nk = NC + i + 1
                    for kb0 in range(0, nk, 2):
                        items.append((i, kb0, min(2, nk - kb0)))
                SKEW = 2
                pS3 = [pSd[0], pSd[1], pP]

                def qk_exp(n):
                    i, kb0, nb = items[n]
                    nk = NC + i + 1
                    ps = pS3[n % 3]
                    pt = PT[n % NPT]
                    for j in range(nb):
                        kb = kb0 + j
                        P.op("pe", lambda e, j=j, kb=kb: e.matmul(ps.t[:, j * 256:(j + 1) * 256], lhsT=KT.t[:, kb * 128:(kb + 1) * 128],
                                                                  rhs=QT2.t[:, i, :], start=True, stop=True),
                             reads=[b_K[kb], b_Q[i]], writes=[ps.b])
                    P.op("act", lambda e: e.activation(out=pt.t[:, 0:2 * nb, :].rearrange("p a b -> p (a b)"),
                                                       in_=ps.t[:, 0:256 * nb], func=AF.Exp, scale=0.125),
                         reads=[ps.b], writes=[pt.b])
                    if kb0 + nb == nk:
                        jl = nb - 1
                        P.op("pool", lambda e: e.tensor_tensor(out=pt.t[:, 2 * jl:2 * jl + 2, :], in0=pt.t[:, 2 * jl:2 * jl + 2, :],
                                                               in1=cmask2.t[:], op=ALU.mult),
                             reads=[pt.b, cmask2.b], writes=[pt.b])

                def pv(n):
                    i, kb0, nb = items[n]
                    nk = NC + i + 1
                    o0 = pO0[i % 2]
                    o1 = pO1[i % 2]
                    pt = PT[n % NPT]
                    for j in range(nb):
                        kb = kb0 + j
                        P.op("pe", lambda e, j=j, kb=kb: e.matmul(o0.t[:, 0:129], lhsT=pt.t[:, 2 * j, :], rhs=VV.t[:, kb, 0:129],
                                                                  start=(kb == 0), stop=(kb == nk - 1)),
                             reads=[pt.b, b_Kv], writes=[o0.b])
                        P.op("pe", lambda e, j=j, kb=kb: e.matmul(o1.t[:, 0:129], lhsT=pt.t[:, 2 * j + 1, :], rhs=VV.t[:, kb, 0:129],
                                                                  start=(kb == 0), stop=(kb == nk - 1)),
                             reads=[pt.b, b_Kv], writes=[o1.b])
                    if kb0 + nb == nk:
                        finalize(i, o0, o1)

                def finalize(i, o0, o1):
                    while pend_fin:
                        pend_fin.pop(0)[1]()
                    P.op("dve", lambda e, o0=o0: e.reciprocal(zz.t[:, 0:1], o0.t[:, 128:129]), reads=[o0.b], writes=[zz.b])
                    P.op("dve", lambda e, o1=o1: e.reciprocal(zz.t[:, 1:2], o1.t[:, 128:129]), reads=[o1.b], writes=[zz.b])
                    P.op("dve", lambda e: e.tensor_tensor(out=zz.t[:, 1:2], in0=zz.t[:, 1:2], in1=lam.t[:, 0:1], op=ALU.mult),
                         reads=[zz.b, lam.b], writes=[zz.b])
                    P.op("dve", lambda e, o1=o1: e.tensor_scalar(a1.t[:], o1.t[:, 0:128], zz.t[:, 1:2], None, op0=ALU.mult),
                         reads=[o1.b, zz.b], writes=[a1.b])
                    P.op("dve", lambda e, o0=o0: e.scalar_tensor_tensor(out=aa.t[:], in0=o0.t[:, 0:128], scalar=zz.t[:, 0:1], in1=a1.t[:],
                                                                        op0=ALU.mult, op1=ALU.subtract),
                         reads=[o0.b, zz.b, a1.b], writes=[aa.b])
                    finalize_b(i)
                    pend_fin.append((n_now[0] + 8, lambda: finalize_c(i)))

                def finalize_b(i):
                    P.op("dve", lambda e: e.tensor_tensor(out=asq.t[:], in0=aa.t[:], in1=aa.t[:], op=ALU.mult), reads=[aa.b], writes=[asq.b])
                    P.op("dve", lambda e: e.tensor_reduce(out=ass.t[:, 0:1], in_=asq.t[:], axis=AX.X, op=ALU.add), reads=[asq.b], writes=[ass.b])
                    P.op("dve", lambda e: e.tensor_scalar(ars.t[:], ass.t[:], 1.0 / 128, EPS, op0=ALU.mult, op1=ALU.add),
                         reads=[ass.b], writes=[ars.b])
                    P.op("pool", lambda e: e.tensor_tensor(out=ars.t[:], in0=ars.t[:], in1=mhalf.t[:, 0:1], op=ALU.pow),
                         reads=[ars.b, mhalf.b], writes=[ars.b])
                    P.op("dve", lambda e: e.scalar_tensor_tensor(out=dob.t[:], in0=aa.t[:], scalar=ars.t[:, 0:1], in1=sublnw.t[:],
                                                                 op0=ALU.mult, op1=ALU.mult),
                         reads=[aa.b, ars.b, sublnw.b], writes=[dob.b])

                def finalize_c(i):
                    P.op("pe", lambda e: e.transpose(pTd.t[:, 256:384], dob.t[:], ident.t[:]), reads=[dob.b, ident.b], writes=[pTd.b])
                    d_ = doT[i % 2]
                    P.op("dve", lambda e, d_=d_: e.tensor_copy(d_.t[:], pTd.t[:, 256:384]), reads=[pTd.b], writes=[d_.b])
                    P.dma("pool", lambda e, d_=d_, i=i: e.dma_start(out=DOT[i][:, h * 128:(h + 1) * 128], in_=d_.t[:]),
                          d_.b, reads=[d_.b], writes=[b_DOT[i]])
                pend_fin = []
                n_now = [0]
                for n in range(len(items) + SKEW + 10):
                    n_now[0] = n
                    if n < len(items):
                        qk_exp(n)
                    if 0 <= n - SKEW < len(items):
                        pv(n - SKEW)
                    while pend_fin and pend_fin[0][0] <= n:
                        pend_fin.pop(0)[1]()
                assert not pend_fin
                P.emit()

        with ExitStack() as st:
            Wro = sbt(st, "Wro", [128, 16, 1024], BF16)
            Wdo = sbt(st, "Wdo", [128, 8, 1024], BF16)
            Wou = sbt(st, "Wou", [128, 8, 1024], BF16)
            Wg = sbt(st, "Wg", [128, 8, 2048], BF16)
            n2w = sbt(st, "n2w", [128, D], F32)
            P.dma("sp", lambda e: e.dma_start(out=n2w.t[:], in_=bc_part(norm2_w, D)), n2w.b, writes=[n2w.b])
            for k0 in range(0, 8, 2):
                P.dma("pool", lambda e, k0=k0: e.dma_start(out=Wg.t[:, k0:k0 + 2, :], in_=w_in_v[:, k0:k0 + 2, C_GT:C_GT + 2048]), Wg.b, writes=[Wg.b])
            wro_v = w_ret_o.rearrange("(k p) n -> p k n", p=128)
            for k0 in range(0, 16, 4):
                P.dma("pool", lambda e, k0=k0: e.dma_start(out=Wro.t[:, k0:k0 + 4, :], in_=wro_v[:, k0:k0 + 4, :]), Wro.b, writes=[Wro.b])
            wdo_v = w_diff_o.rearrange("(k p) n -> p k n", p=128)
            wou_v = w_out.rearrange("(k p) n -> p k n", p=128)
            for k0 in range(0, 8, 4):
                P.dma("pool", lambda e, k0=k0: e.dma_start(out=Wdo.t[:, k0:k0 + 4, :], in_=wdo_v[:, k0:k0 + 4, :]), Wdo.b, writes=[Wdo.b])
                P.dma("pool", lambda e, k0=k0: e.dma_start(out=Wou.t[:, k0:k0 + 4, :], in_=wou_v[:, k0:k0 + 4, :]), Wou.b, writes=[Wou.b])
            gTb = [sbt(st, "mgT%d" % i, [128, 16, 128], BF16) for i in range(2)]
            dTb = [sbt(st, "mdT%d" % i, [128, 8, 128], BF16) for i in range(2)]
            uTb = [sbt(st, "muT%d" % i, [128, 8, 128], BF16) for i in range(2)]
            xb = [sbt(st, "mx%d" % i, [128, D], F32) for i in range(2)]
            sig = sbt(st, "sig", [128, 2048], F32)
            m1 = sbt(st, "m1", [128, D], F32)
            m2 = sbt(st, "m2", [128, D], F32)
            mb = [sbt(st, "mb%d" % i, [128, D], BF16) for i in range(2)]
            mT = sbt(st, "mT", [128, 8, 128], BF16)
            h2 = [sbt(st, "h2%d" % i, [128, D], F32) for i in range(2)]
            sq = sbt(st, "msq", [128, D], F32)
            ss = sbt(st, "mss", [128, 1], F32)
            rs = sbt(st, "mrs", [128, 1], F32)
            ub = sbt(st, "mub", [128, D], BF16)
            u2T = [sbt(st, "mu2T%d" % i, [128, 8, 128], BF16) for i in range(2)]
            pA = [pbank(st, "pA%d" % i) for i in range(4)]
            pB = [pbank(st, "pB%d" % i) for i in range(2)]
            pTm = [pbank(st, "pTm%d" % i, BF16) for i in range(2)]
            def MA1(ob):
                blk = NC + ob
                g_ = gTb[ob % 2]; d_ = dTb[ob % 2]; u_ = uTb[ob % 2]; x_ = xb[ob % 2]
                P.dma("sp", lambda e: e.dma_start(out=g_.t[:].rearrange("p a b -> p (a b)"), in_=GT[ob]),
                      g_.b, reads=[b_GT[ob]], writes=[g_.b])
                P.dma("sp", lambda e: e.dma_start(out=d_.t[:].rearrange("p a b -> p (a b)"), in_=DOT[ob]),
                      d_.b, reads=[b_DOT[ob]], writes=[d_.b])
                P.dma("sp", lambda e: e.dma_start(out=u_.t[:].rearrange("p a b -> p (a b)"), in_=UT[blk]),
                      u_.b, reads=[b_UT[blk]], writes=[u_.b])
                P.dma("sp", lambda e: e.dma_start(out=x_.t[:], in_=xo[ob * 128:(ob + 1) * 128, :]), x_.b, writes=[x_.b])
                for j in range(4):
                    for kc in range(8):
                        P.op("pe", lambda e, j=j, kc=kc: e.matmul(pA[j].t[:, 0:512], lhsT=u_.t[:, kc, :], rhs=Wg.t[:, kc, j * 512:(j + 1) * 512],
                                                                  start=(kc == 0), stop=(kc == 7)),
                             reads=[u_.b, Wg.b], writes=[pA[j].b])
                    P.op("act", lambda e, j=j: e.activation(out=sig.t[:, j * 512:(j + 1) * 512], in_=pA[j].t[:, 0:512], func=AF.Sigmoid),
                         reads=[pA[j].b], writes=[sig.b])

            def MA2(ob):
                g_ = gTb[ob % 2]
                for j in range(2):
                    for kc in range(16):
                        P.op("pe", lambda e, j=j, kc=kc: e.matmul(pA[j].t[:, 0:512], lhsT=g_.t[:, kc, :], rhs=Wro.t[:, kc, j * 512:(j + 1) * 512],
                                                                  start=(kc == 0), stop=(kc == 15)),
                             reads=[g_.b, Wro.b], writes=[pA[j].b])
                    P.op("dve", lambda e, j=j: e.tensor_tensor(out=m1.t[:, j * 512:(j + 1) * 512], in0=pA[j].t[:, 0:512],
                                                               in1=sig.t[:, j * 512:(j + 1) * 512], op=ALU.mult),
                         reads=[pA[j].b, sig.b], writes=[m1.b])

            def MA3(ob):
                d_ = dTb[ob % 2]
                mb_ = mb[ob % 2]
                for j in range(2):
                    for kc in range(8):
                        P.op("pe", lambda e, j=j, kc=kc: e.matmul(pA[2 + j].t[:, 0:512], lhsT=d_.t[:, kc, :], rhs=Wdo.t[:, kc, j * 512:(j + 1) * 512],
                                                                  start=(kc == 0), stop=(kc == 7)),
                             reads=[d_.b, Wdo.b], writes=[pA[2 + j].b])
                    P.op("dve", lambda e, j=j: e.tensor_tensor(out=m2.t[:, j * 512:(j + 1) * 512], in0=pA[2 + j].t[:, 0:512],
                                                               in1=sig.t[:, 1024 + j * 512:1024 + (j + 1) * 512], op=ALU.mult),
                         reads=[pA[2 + j].b, sig.b], writes=[m2.b])
                P.op("pool", lambda e: e.tensor_tensor(out=mb_.t[:], in0=m1.t[:], in1=m2.t[:], op=ALU.add), reads=[m1.b, m2.b], writes=[mb_.b])

            def MB1(ob):
                mb_ = mb[ob % 2]
                for half in range(2):
                    pt = pTm[half]
                    for j in range(4):
                        kc = half * 4 + j
                        P.op("pe", lambda e, kc=kc, j=j, pt=pt: e.transpose(pt.t[:, j * 128:(j + 1) * 128], mb_.t[:, kc * 128:(kc + 1) * 128], ident.t[:]),
                             reads=[mb_.b, ident.b], writes=[pt.b])
                    if half == 0:
                        P.op("dve", lambda e, pt=pt: e.tensor_copy(mT.t[:, 0:4, :].rearrange("p a b -> p (a b)"), pt.t[:, 0:512]),
                             reads=[pt.b], writes=[mT.b])
                    else:
                        P.op("act", lambda e, pt=pt: e.copy(mT.t[:, 4:8, :].rearrange("p a b -> p (a b)"), pt.t[:, 0:512]),
                             reads=[pt.b], writes=[mT.b])

            def MB2(ob):
                h_ = h2[ob % 2]
                x_ = xb[ob % 2]
                for j in range(2):
                    for kc in range(8):
                        P.op("pe", lambda e, j=j, kc=kc: e.matmul(pB[j].t[:, 0:512], lhsT=mT.t[:, kc, :], rhs=Wou.t[:, kc, j * 512:(j + 1) * 512],
                                                                  start=(kc == 0), stop=(kc == 7)),
                             reads=[mT.b, Wou.b], writes=[pB[j].b])
                    P.op("dve", lambda e, j=j: e.tensor_tensor(out=h_.t[:, j * 512:(j + 1) * 512], in0=pB[j].t[:, 0:512],
                                                               in1=x_.t[:, j * 512:(j + 1) * 512], op=ALU.add),
                         reads=[pB[j].b, x_.b], writes=[h_.b])
                P.dma("pool", lambda e: e.dma_start(out=H2[ob * 128:(ob + 1) * 128, :], in_=h_.t[:]),
                      h_.b, reads=[h_.b], writes=[b_H2[ob]])
                norm_part(h_, n2w, sq, ss, rs, ub, use_pow=True)

            def MB3(ob):
                t_ = u2T[ob % 2]
                tr_part(ub, pTm, t_)
                P.dma("pool", lambda e: e.dma_start(out=U2T[ob], in_=t_.t[:].rearrange("p a b -> p (a b)")),
                      t_.b, reads=[t_.b], writes=[b_U2T[ob]])

            MA1(0); MA2(0); MA3(0)
            for ob in range(NO):
                nx = ob + 1
                MB1(ob)
                if nx < NO:
                    MA1(nx)
                MB2(ob)
                if nx < NO:
                    MA2(nx)
                MB3(ob)
                if nx < NO:
                    MA3(nx)
            P.emit()

        GB = 3
        NG = (NO + GB - 1) // GB
        with ExitStack() as st:
            Wup = sbt(st, "Wup", [128, 8, 2 * FFN], BF16)
            Wdn = sbt(st, "Wdn", [128, 22, D], BF16)
            cw = sbt(st, "cw", [128, 3, 44], F32)
            cb = sbt(st, "cb", [128, 44], F32)
            wup_v = w_up.rearrange("(k p) n -> p k n", p=128)
            b_Wup = [Buf("Wup%d" % i) for i in range(4)]
            for ch in (0, 2, 1, 3):
                c0 = ch * 1408
                for kc in range(8):
                    P.dma("pool", lambda e, kc=kc, c0=c0: e.dma_start(out=Wup.t[:, kc, c0:c0 + 1408], in_=wup_v[:, kc, c0:c0 + 1408]),
                          b_Wup[ch], writes=[b_Wup[ch]])
            wdn_v = w_down.rearrange("(k p) n -> p k n", p=128)
            for k0 in range(0, 22, 2):
                P.dma("pool", lambda e, k0=k0: e.dma_start(out=Wdn.t[:, k0:k0 + 2, :], in_=wdn_v[:, k0:k0 + 2, :]), Wdn.b, writes=[Wdn.b])
            for t0 in range(0, 44, 11):
                for k in range(3):
                    P.dma("sp", lambda e, k=k, t0=t0: e.dma_start(out=cw.t[:, k, t0:t0 + 11],
                                                                  in_=conv_w[k].rearrange("(t p) -> p t", p=128)[:, t0:t0 + 11],
                                                                  allow_slow_non_contiguous=True), cw.b, writes=[cw.b])
                P.dma("sp", lambda e, t0=t0: e.dma_start(out=cb.t[:, t0:t0 + 11], in_=conv_b.rearrange("(t p) -> p t", p=128)[:, t0:t0 + 11],
                                                         allow_slow_non_contiguous=True), cb.b, writes=[cb.b])
            NT = GB * 128
            u2g = [sbt(st, "u2g%d" % i, [128, 8, 2 + NT], BF16) for i in range(2)]
            ya = [sbt(st, "ya%d" % i, [128, NT], F32) for i in range(2)]
            yb = [sbt(st, "yb%d" % i, [128, NT], F32) for i in range(2)]
            sa = [sbt(st, "sa%d" % i, [128, NT], F32) for i in range(2)]
            gTt2 = [sbt(st, "gTt%d" % i, [128, 22, NT], BF16) for i in range(2)]
            hb = [sbt(st, "fh%d" % i, [128, D], F32) for i in range(1)] * 2
            ob_ = [sbt(st, "fo%d" % i, [128, D], F32) for i in range(2)]
            pU = [pbank(st, "pU%d" % i) for i in range(4)]
            pD = [pbank(st, "pD%d" % i) for i in range(4)]
            P.op("pool", lambda e: e.memset(u2g[0].t[:], 0.0), writes=[u2g[0].b])
            P.op("pool", lambda e: e.memset(u2g[1].t[:], 0.0), writes=[u2g[1].b])
            ui = [0]

            def up_pair(gi, ft, ug, nt):
                gt = gTt2[gi % 2]
                tiles = []
                for which, fi in ((0, ft), (1, ft + 22)):
                    pu = pU[ui[0] % 4]
                    ui[0] += 1
                    for kc in range(8):
                        P.op("pe", lambda e, pu=pu, kc=kc, fi=fi: e.matmul(pu.t[:, 0:nt + 2], lhsT=Wup.t[:, kc, fi * 128:(fi + 1) * 128],
                                                                           rhs=ug.t[:, kc, 0:nt + 2], start=(kc == 0), stop=(kc == 7)),
                             reads=[b_Wup[fi // 11], ug.b], writes=[pu.b])
                    yt = (ya if which == 0 else yb)[ft % 2]
                    P.op("dve", lambda e, pu=pu, yt=yt, fi=fi: e.tensor_scalar(yt.t[:, 0:nt], pu.t[:, 2:nt + 2], cw.t[:, 2, fi:fi + 1], cb.t[:, fi:fi + 1],
                                                                               op0=ALU.mult, op1=ALU.add),
                         reads=[pu.b, cw.b, cb.b], writes=[yt.b])
                    P.op("dve", lambda e, pu=pu, yt=yt, fi=fi: e.scalar_tensor_tensor(out=yt.t[:, 0:nt], in0=pu.t[:, 1:nt + 1], scalar=cw.t[:, 1, fi:fi + 1],
                                                                                      in1=yt.t[:, 0:nt], op0=ALU.mult, op1=ALU.add),
                         reads=[pu.b, cw.b, yt.b], writes=[yt.b])
                    P.op("dve", lambda e, pu=pu, yt=yt, fi=fi: e.scalar_tensor_tensor(out=yt.t[:, 0:nt], in0=pu.t[:, 0:nt], scalar=cw.t[:, 0, fi:fi + 1],
                                                                                      in1=yt.t[:, 0:nt], op0=ALU.mult, op1=ALU.add),
                         reads=[pu.b, cw.b, yt.b], writes=[yt.b])
                    tiles.append(yt)
                s_ = sa[ft % 2]
                P.op("act", lambda e: e.activation(out=s_.t[:, 0:nt], in_=tiles[0].t[:, 0:nt], func=AF.Silu),
                     reads=[tiles[0].b], writes=[s_.b])
                P.op("pool", lambda e: e.tensor_tensor(out=gt.t[:, ft, 0:nt], in0=s_.t[:, 0:nt], in1=tiles[1].t[:, 0:nt], op=ALU.mult),
                     reads=[s_.b, tiles[1].b], writes=[gt.b])

            def down_unit(gi, j, ob, half):
                gt = gTt2[gi % 2]
                h_ = hb[ob % 2]
                o_ = ob_[ob % 2]
                if half == 0:
                    P.dma("sp", lambda e: e.dma_start(out=h_.t[:], in_=H2[ob * 128:(ob + 1) * 128, :]),
                          h_.b, reads=[b_H2[ob]], writes=[h_.b])
                pd = pD[(ob * 2 + half) % 4]
                for ft in range(22):
                    P.op("pe", lambda e, ft=ft: e.matmul(pd.t[:, 0:512], lhsT=gt.t[:, ft, j * 128:(j + 1) * 128],
                                                         rhs=Wdn.t[:, ft, half * 512:(half + 1) * 512],
                                                         start=(ft == 0), stop=(ft == 21)),
                         reads=[gt.b, Wdn.b], writes=[pd.b])
                P.op("dve", lambda e: e.tensor_tensor(out=o_.t[:, half * 512:(half + 1) * 512], in0=pd.t[:, 0:512],
                                                      in1=h_.t[:, half * 512:(half + 1) * 512], op=ALU.add),
                     reads=[pd.b, h_.b], writes=[o_.b])
                if half == 1:
                    P.dma("pool", lambda e: e.dma_start(out=y[ob * 128:(ob + 1) * 128, :], in_=o_.t[:]),
                          o_.b, reads=[o_.b], writes=[b_y])

            pending_down = []
            for gi in range(NG):
                blks = list(range(gi * GB, min(NO, (gi + 1) * GB)))
                nt = len(blks) * 128
                ug = u2g[gi % 2]
                up_ = u2g[(gi + 1) % 2]
                for j, ob in enumerate(blks):
                    P.dma("sp", lambda e, ug=ug, j=j, ob=ob: e.dma_start(out=ug.t[:, :, 2 + j * 128:2 + (j + 1) * 128],
                                                                        in_=U2T[ob].rearrange("p (a b) -> p a b", a=8)),
                          ug.b, reads=[b_U2T[ob]], writes=[ug.b])
                if gi > 0:
                    P.op("pool", lambda e, ug=ug, up_=up_: e.tensor_copy(ug.t[:, :, 0:2], up_.t[:, :, NT:NT + 2]),
                         reads=[up_.b], writes=[ug.b])
                nd = len(pending_down)
                slots = {int(round((k + 1) * 22.0 / (nd + 1))): k for k in range(nd)} if nd else {}
                for ft in range(22):
                    up_pair(gi, ft, ug, nt)
                    if (ft + 1) in slots:
                        down_unit(*pending_down[slots[ft + 1]])
                pending_down = [(gi, j, ob, half) for j, ob in enumerate(blks) for half in range(2)]
            for u in pending_down:
                down_unit(*u)
            P.wait_all("pool", [b_y])
            P.emit()
        print("n_inst", P.n_inst, "n_wait", P.n_wait, "ndsem", P.ndsem)
    return nc


def make_tables(NC, NO, p, S):
    NB = NC + NO
    L = N_META + S
    if p == 0:
        ctx_pos = np.full(NC * 128, -1, np.int64)
        own_pos = np.arange(NO * 128)
    else:
        ctx_pos = np.arange(NC * 128) - PAD
        own_pos = L - NO * 128 + np.arange(NO * 128)
    pos = np.concatenate([ctx_pos, own_pos])
    valid = pos >= 0
    posf = np.where(valid, pos, 0).astype(np.float32)
    inv_r = np.power(np.float32(10000.0), -np.arange(128, dtype=np.float32) / np.float32(128))
    ang = posf[:, None] * inv_r[None, :]
    c, s = np.cos(ang), np.sin(ang)
    rope_r = np.concatenate([c, c, s, s], axis=1).astype(np.float32)
    inv_d = np.power(np.float32(500000.0), -np.arange(8, dtype=np.float32) / np.float32(8))
    ang = posf[:, None] * inv_d[None, :]
    c, s = np.cos(ang), np.sin(ang)
    rope_d = np.concatenate([c, c, s, s], axis=1).astype(np.float32)
    kb = np.where(valid, 1.0, 0.0).astype(np.float32).reshape(NB, 128).T.copy()
    idx = np.arange(128)
    cm = (idx[:, None] <= idx[None, :]).astype(np.float32)
    rdec = np.zeros((128, 8), np.float32)
    for h in range(4):
        rdec[:, h] = GAM[h] ** (idx + 1.0)
        rdec[:, 4 + h] = (256 ** -0.5) * GAM[h] ** (127.0 - idx)
    return rope_r, rope_d, kb, cm, rdec


_NC_CACHE = {}


def run(inputs, NC, NO, debug=False, trace=False):
    x = np.asarray(inputs["x"], np.float32)
    B, S, _ = x.shape
    assert S == 128 * (NC + NO - 1)
    L = N_META + S
    meta = np.asarray(inputs["meta_tokens"], np.float32)
    key = (NC, NO, debug)
    if key not in _NC_CACHE:
        _NC_CACHE[key] = build(NC, NO, debug)
    nc = _NC_CACHE[key]
    f = lambda k: np.ascontiguousarray(np.asarray(inputs[k], np.float32)[0])
    common = {
        "w_in": f("w_in"), "w_ret_o": f("w_ret_o"), "w_diff_o": f("w_diff_o"), "w_out": f("w_out"),
        "w_up": f("w_up"), "w_down": f("w_down"), "norm1_w": f("norm1_w"), "norm2_w": f("norm2_w"),
        "qk_norm_w": np.concatenate([f("q_norm_w"), f("q_norm_w"), f("k_norm_w"), f("k_norm_w")]),
        "lambdas": np.concatenate([f("lambda_q1"), f("lambda_k1"), f("lambda_q2"), f("lambda_k2")]),
        "subln_w": f("diff_subln_w"), "conv_w": f("conv_w"), "conv_b": f("conv_b"),
    }
    tabs = [make_tables(NC, NO, p, S) for p in range(2)]
    in_maps = []
    for b in range(B):
        seq = np.concatenate([meta, x[b]], axis=0)
        for p in range(2):
            if p == 0:
                xc_ = np.zeros((NC * 128, D), np.float32)
                xo_ = seq[0:NO * 128]
            else:
                xc_ = np.concatenate([np.zeros((PAD, D), np.float32), seq[0:NC * 128 - PAD]], axis=0)
                xo_ = seq[L - NO * 128:L]
            rr, rd, kb, cm, rdec = tabs[p]
            m = dict(common)
            m.update({"xc": np.ascontiguousarray(xc_), "xo": np.ascontiguousarray(xo_), "rope_r": rr, "rope_d": rd,
                      "kbias": kb, "cmask": cm, "rdec": rdec})
            in_maps.append(m)
    res = run_bass_kernel_spmd(nc, in_maps, core_ids=list(range(len(in_maps))), trace=trace)
    out = np.empty((B, S, D), np.float32)
    split = (NO * 128 - N_META) - 64
    for b in range(B):
        y0 = res.results[2 * b]["y"]
        y1 = res.results[2 * b + 1]["y"]
        out[b, :split] = y0[N_META:N_META + split]
        off1 = L - NO * 128
        out[b, split:] = y1[N_META + split - off1:]
    return out, res


def kernel(**inputs):
    out, _ = run(inputs, 32, 33)
    return out
```

```python
import math
import numpy as np
from contextlib import ExitStack
import concourse.bass as bass
import concourse.mybir as mybir
from concourse.bass_utils import run_bass_kernel_spmd

F32 = mybir.dt.float32
BF16 = mybir.dt.bfloat16
AF = mybir.ActivationFunctionType
ALU = mybir.AluOpType
AX = mybir.AxisListType

D = 1024
N_META = 16
PAD = 112
FFN = 2816
IN_COLS = 11264
EPS = 1e-6
C_RQ, C_RK, C_RV, C_RG, C_DQ, C_DK, C_DV, C_GT = 0, 1024, 2048, 4096, 6144, 7168, 8192, 9216
NEGB = -30000.0
LAM_INIT = 0.8 - 0.6 * math.exp(-0.3 * 0)
GAM = [1.0 - 2.0 ** (-5.0 - h) for h in range(4)]

SAME_ENGINE_SYNC = True


class Buf:
    __slots__ = ("name", "last_write", "reads", "dsem", "dcount")

    def __init__(self, name=""):
        self.name = name
        self.last_write = None
        self.reads = {}
        self.dsem = None
        self.dcount = 0


class T:
    def __init__(self, t, name):
        self.t = t
        self.b = Buf(name)


class Prog:
    ENGS = ("pe", "act", "dve", "pool", "sp")
    ENGOBJ = {"pe": "tensor", "act": "scalar", "dve": "vector", "pool": "gpsimd", "sp": "sync"}

    def __init__(self, nc, stack):
        self.nc = nc
        self.stack = stack
        self.q = {e: [] for e in self.ENGS}
        self.ecount = {e: 0 for e in self.ENGS}
        self.sems = {}
        for e in self.ENGS:
            self.sems[("e", e)] = stack.enter_context(nc.semaphore("s_" + e))
        self.waited = {e: {} for e in self.ENGS}
        self.ndsem = 0
        self.n_inst = 0
        self.n_wait = 0

    def _dsem(self, buf):
        if buf.dsem is None:
            buf.dsem = ("d", self.ndsem)
            self.sems[buf.dsem] = self.stack.enter_context(self.nc.semaphore("d%d" % self.ndsem))
            self.ndsem += 1
        return buf.dsem

    def _deps(self, eng, reads, writes):
        deps = {}

        def add(t):
            if t is None:
                return
            k, v = t
            if deps.get(k, -1) < v:
                deps[k] = v
        for b in reads:
            add(b.last_write)
        for b in writes:
            add(b.last_write)
            for k, v in b.reads.items():
                add((k, v))
        out = []
        w = self.waited[eng]
        for k, v in deps.items():
            if k == ("e", eng) and (eng == "pe" or not SAME_ENGINE_SYNC):
                continue
            if w.get(k, -1) >= v:
                continue
            w[k] = v
            out.append((k, v))
        return out

    def _commit(self, tok, reads, writes):
        k, v = tok
        for b in writes:
            b.last_write = tok
            b.reads = {}
        for b in reads:
            if b.reads.get(k, -1) < v:
                b.reads[k] = v

    def op(self, eng, fn, reads=(), writes=()):
        waits = self._deps(eng, reads, writes)
        self.ecount[eng] += 1
        tok = (("e", eng), self.ecount[eng])
        self.q[eng].append((waits, fn, tok[0], 1))
        self._commit(tok, reads, writes)
        self.n_inst += 1
        self.n_wait += len(waits)
        return tok

    def dma(self, eng, fn, sb, reads=(), writes=()):
        waits = self._deps(eng, reads, writes)
        k = self._dsem(sb)
        sb.dcount += 16
        tok = (k, sb.dcount)
        self.q[eng].append((waits, fn, k, 16))
        self._commit(tok, reads, writes)
        self.n_inst += 1
        self.n_wait += len(waits)
        return tok

    def wait_all(self, eng, bufs):
        waits = self._deps(eng, bufs, bufs)
        self.q[eng].append((waits, None, None, 0))

    def emit(self):
        nc = self.nc
        sems = self.sems
        with nc.Block() as block:
            for e in self.ENGS:
                lst = self.q[e]

                def body(eo, lst=lst):
                    for waits, fn, sk, inc in lst:
                        for k, v in waits:
                            eo.wait_ge(sems[k], v)
                        if fn is not None:
                            fn(eo).then_inc(sems[sk], inc)
                getattr(block, self.ENGOBJ[e])(body)
        self.q = {e: [] for e in self.ENGS}


def bc_mid(ap, n):
    return bass.AP(ap.tensor, ap.offset, [list(ap.ap[0]), [0, n], list(ap.ap[1])])


def bc_last(ap, k):
    return bass.AP(ap.tensor, ap.offset, [list(ap.ap[0]), list(ap.ap[1]), [0, k]])


def bc_part(dram_ap_1d, n):
    return bass.AP(dram_ap_1d.tensor, dram_ap_1d.offset, [[0, 128], [1, n]])


def build(NC, NO, debug=False):
    NB = NC + NO
    nc = bass.Bass("TRN2", target_bir_lowering=False)

    def din(name, shape, dt=F32):
        return nc.dram_tensor(name, list(shape), dt, kind="ExternalInput").ap()

    okind = "ExternalOutput" if debug else "Internal"

    def dscr(name, shape, dt):
        return nc.dram_tensor(name, list(shape), dt, kind=okind).ap()

    xc = din("xc", [NC * 128, D])
    xo = din("xo", [NO * 128, D])
    w_in = din("w_in", [D, IN_COLS])
    w_ret_o = din("w_ret_o", [2048, D])
    w_diff_o = din("w_diff_o", [D, D])
    w_out = din("w_out", [D, D])
    w_up = din("w_up", [D, 2 * FFN])
    w_down = din("w_down", [FFN, D])
    norm1_w = din("norm1_w", [D])
    norm2_w = din("norm2_w", [D])
    qk_norm_w = din("qk_norm_w", [256])
    lambdas = din("lambdas", [256])
    subln_w = din("subln_w", [128])
    conv_w = din("conv_w", [3, 2 * FFN])
    conv_b = din("conv_b", [2 * FFN])
    rope_r = din("rope_r", [NB * 128, 512])
    rope_d = din("rope_d", [NB * 128, 32])
    kbias_d = din("kbias", [128, NB])
    cmask_d = din("cmask", [128, 128])
    rdec_d = din("rdec", [128, 8])
    y = nc.dram_tensor("y", [NO * 128, D], F32, kind="ExternalOutput").ap()

    UT = dscr("UT", [NB, 128, 1024], BF16)
    GT = dscr("GT", [NO, 128, 2048], BF16)
    DOT = dscr("DOT", [NO, 128, 1024], BF16)
    H2 = dscr("H2", [NO * 128, D], F32)
    U2T = dscr("U2T", [NO, 128, 1024], BF16)
    b_UT = [Buf("UT%d" % i) for i in range(NB)]
    b_GT = [Buf("GT%d" % i) for i in range(NO)]
    b_DOT = [Buf("DOT%d" % i) for i in range(NO)]
    b_H2 = [Buf("H2%d" % i) for i in range(NO)]
    b_U2T = [Buf("U2T%d" % i) for i in range(NO)]
    b_y = Buf("y")

    w_in_v = w_in.rearrange("(k p) n -> p k n", p=128)

    with ExitStack() as gst:
        P = Prog(nc, gst)

        def sbt(st, name, shape, dt):
            return T(st.enter_context(nc.sbuf_tensor("sb_" + name, list(shape), dt)), name)

        def pbank(st, name, dt=F32):
            n = 512 if dt == F32 else 1024
            return T(st.enter_context(nc.psum_tensor("ps_" + name, [128, n], dt)), name)

        ident = sbt(gst, "ident", [128, 128], BF16)
        identf = sbt(gst, "identf", [128, 128], F32)
        cmask = sbt(gst, "cmask", [128, 128], F32)
        cmask2 = sbt(gst, "cmask2", [128, 2, 128], BF16)
        kbias = sbt(gst, "kbias", [128, NB], F32)
        rdec = sbt(gst, "rdec", [128, 8], F32)
        lam = sbt(gst, "lam", [128, 4], F32)
        lamv = sbt(gst, "lamv", [128, 256], F32)
        lamt = sbt(gst, "lamt", [128, 128], F32)
        lams = sbt(gst, "lams", [128, 2], F32)
        sublnw = sbt(gst, "sublnw", [128, 128], F32)
        wqk = sbt(gst, "wqk", [128, 4, 64], F32)

        P.op("pool", lambda e: e.iota(identf.t[:], pattern=[[1, 128]], base=0, channel_multiplier=-1,
                                      allow_small_or_imprecise_dtypes=True), writes=[identf.b])
        P.op("dve", lambda e: e.tensor_scalar(ident.t[:], identf.t[:], 0.0, None, op0=ALU.is_equal),
             reads=[identf.b], writes=[ident.b])
        P.dma("sp", lambda e: e.dma_start(out=cmask.t[:], in_=cmask_d), cmask.b, writes=[cmask.b])
        P.dma("sp", lambda e: e.dma_start(out=kbias.t[:], in_=kbias_d), kbias.b, writes=[kbias.b])
        P.dma("sp", lambda e: e.dma_start(out=rdec.t[:], in_=rdec_d), rdec.b, writes=[rdec.b])
        P.dma("sp", lambda e: e.dma_start(out=lamv.t[:], in_=bc_part(lambdas, 256)), lamv.b, writes=[lamv.b])
        P.dma("sp", lambda e: e.dma_start(out=sublnw.t[:], in_=bc_part(subln_w, 128)), sublnw.b, writes=[sublnw.b])
        P.dma("sp", lambda e: e.dma_start(out=wqk.t[:].rearrange("p a b -> p (a b)"), in_=bc_part(qk_norm_w, 256)),
              wqk.b, writes=[wqk.b])
        P.op("dve", lambda e: e.tensor_copy(cmask2.t[:, 0, :], cmask.t[:]), reads=[cmask.b], writes=[cmask2.b])
        P.op("dve", lambda e: e.tensor_copy(cmask2.t[:, 1, :], cmask.t[:]), reads=[cmask.b], writes=[cmask2.b])
        P.op("dve", lambda e: e.tensor_tensor(out=lamt.t[:, 0:64], in0=lamv.t[:, 0:64], in1=lamv.t[:, 64:128], op=ALU.mult),
             reads=[lamv.b], writes=[lamt.b])
        P.op("dve", lambda e: e.tensor_tensor(out=lamt.t[:, 64:128], in0=lamv.t[:, 128:192], in1=lamv.t[:, 192:256], op=ALU.mult),
             reads=[lamv.b], writes=[lamt.b])
        P.op("dve", lambda e: e.tensor_reduce(out=lams.t[:, 0:2], in_=lamt.t[:].rearrange("p (a b) -> p a b", a=2),
                                              axis=AX.X, op=ALU.add), reads=[lamt.b], writes=[lams.b])
        P.op("act", lambda e: e.activation(out=lams.t[:], in_=lams.t[:], func=AF.Exp), reads=[lams.b], writes=[lams.b])
        P.op("dve", lambda e: e.tensor_tensor(out=lam.t[:, 0:1], in0=lams.t[:, 0:1], in1=lams.t[:, 1:2], op=ALU.subtract),
             reads=[lams.b], writes=[lam.b])
        P.op("dve", lambda e: e.tensor_scalar(lam.t[:, 0:1], lam.t[:, 0:1], LAM_INIT, None, op0=ALU.add),
             reads=[lam.b], writes=[lam.b])
        P.op("dve", lambda e: e.tensor_scalar(sublnw.t[:], sublnw.t[:], 1.0 - LAM_INIT, None, op0=ALU.mult),
             reads=[sublnw.b], writes=[sublnw.b])

        def norm_part(xt, nw, sq, ss, rs, ub, use_pow=False):
            P.op("act", lambda e: e.activation(out=sq.t[:], in_=xt.t[:], func=AF.Square, accum_out=ss.t[:, 0:1]),
                 reads=[xt.b], writes=[sq.b, ss.b])
            if use_pow:
                P.op("dve", lambda e: e.tensor_scalar(rs.t[:, 0:1], ss.t[:, 0:1], 1.0 / D, EPS, op0=ALU.mult, op1=ALU.add),
                     reads=[ss.b], writes=[rs.b])
                P.op("pool", lambda e: e.tensor_tensor(out=rs.t[:, 0:1], in0=rs.t[:, 0:1], in1=mhalf.t[:, 0:1], op=ALU.pow),
                     reads=[rs.b, mhalf.b], writes=[rs.b])
            else:
                P.op("act", lambda e: e.activation(out=rs.t[:, 0:1], in_=ss.t[:, 0:1], func=AF.Sqrt, scale=1.0 / D, bias=eps_t.t[:, 0:1]),
                     reads=[ss.b, eps_t.b], writes=[rs.b])
                P.op("dve", lambda e: e.reciprocal(rs.t[:, 0:1], rs.t[:, 0:1]), reads=[rs.b], writes=[rs.b])
            P.op("dve", lambda e: e.scalar_tensor_tensor(out=ub.t[:], in0=xt.t[:], scalar=rs.t[:, 0:1], in1=nw.t[:],
                                                         op0=ALU.mult, op1=ALU.mult),
                 reads=[xt.b, rs.b, nw.b], writes=[ub.b])

        def tr_part(ub, pT, uT):
            for half in range(2):
                pt = pT[half]
                for j in range(4):
                    kc = half * 4 + j
                    P.op("pe", lambda e, kc=kc, j=j, pt=pt: e.transpose(pt.t[:, j * 128:(j + 1) * 128],
                                                                        ub.t[:, kc * 128:(kc + 1) * 128], ident.t[:]),
                         reads=[ub.b, ident.b], writes=[pt.b])
                if half == 0:
                    P.op("dve", lambda e, pt=pt: e.tensor_copy(uT.t[:, 0:4, :].rearrange("p a b -> p (a b)"), pt.t[:, 0:512]),
                         reads=[pt.b], writes=[uT.b])
                else:
                    P.op("act", lambda e, pt=pt: e.copy(uT.t[:, 4:8, :].rearrange("p a b -> p (a b)"), pt.t[:, 0:512]),
                         reads=[pt.b], writes=[uT.b])

        def norm_transpose(xt, nw, sq, ss, rs, ub, pT, uT):
            norm_part(xt, nw, sq, ss, rs, ub)
            tr_part(ub, pT, uT)

        eps_t = sbt(gst, "eps_t", [128, 1], F32)
        P.op("pool", lambda e: e.memset(eps_t.t[:], EPS), writes=[eps_t.b])
        mhalf = sbt(gst, "mhalf", [128, 8], F32)
        P.op("pool", lambda e: e.memset(mhalf.t[:], -0.5), writes=[mhalf.b])

        with ExitStack() as st:
            n1w = sbt(st, "n1w", [128, D], F32)
            P.dma("sp", lambda e: e.dma_start(out=n1w.t[:], in_=bc_part(norm1_w, D)), n1w.b, writes=[n1w.b])
            xb = [sbt(st, "x%d" % i, [128, D], F32) for i in range(3)]
            sq = [sbt(st, "sq%d" % i, [128, D], F32) for i in range(2)]
            ss = [sbt(st, "ss%d" % i, [128, 1], F32) for i in range(2)]
            rs = [sbt(st, "rs%d" % i, [128, 1], F32) for i in range(2)]
            ub = [sbt(st, "ub%d" % i, [128, D], BF16) for i in range(2)]
            uT = [sbt(st, "uT%d" % i, [128, 8, 128], BF16) for i in range(2)]
            pT = [pbank(st, "pT%d" % i, BF16) for i in range(4)]
            def p0_x(blk):
                src = xc[blk * 128:(blk + 1) * 128, :] if blk < NC else xo[(blk - NC) * 128:(blk - NC + 1) * 128, :]
                x_ = xb[blk % 3]
                P.dma("sp", lambda e: e.dma_start(out=x_.t[:], in_=src), x_.b, writes=[x_.b])
                norm_part(x_, n1w, sq[blk % 2], ss[blk % 2], rs[blk % 2], ub[blk % 2])

            def p0_y(blk):
                u_ = uT[blk % 2]
                tr_part(ub[blk % 2], pT[(blk % 2) * 2:(blk % 2) * 2 + 2], u_)
                P.dma("pool", lambda e: e.dma_start(out=UT[blk], in_=u_.t[:].rearrange("p a b -> p (a b)")),
                      u_.b, reads=[u_.b], writes=[b_UT[blk]])
            p0_x(0)
            for blk in range(NB):
                if blk + 1 < NB:
                    p0_x(blk + 1)
                p0_y(blk)
            P.emit()

        with ExitStack() as st_pre1:
            WDa = sbt(st_pre1, "WDa", [128, 8, 3072], BF16)
            with ExitStack() as st:
                WR = [sbt(st, "WR%d" % i, [128, 8, 1536], BF16) for i in range(2)]
                uTb = [sbt(st, "ruT%d" % i, [128, 8, 128], BF16) for i in range(3)]
                RT = [sbt(st, "RT%d" % i, [128, 512], F32) for i in range(3)]
                Rf = sbt(st, "Rf", [128, 2, 512], F32)
                Rb = sbt(st, "Rb", [128, 2, 512], BF16)
                Aq = sbt(st, "Aq", [128, 256], F32)
                Bq = sbt(st, "Bq", [128, 256], F32)
                Ak = sbt(st, "Ak", [128, 256], F32)
                Bk = sbt(st, "Bk", [128, 256], F32)
                qr = [sbt(st, "qr%d" % i, [128, 256], BF16) for i in range(2)]
                kr = [sbt(st, "kr%d" % i, [128, 256], BF16) for i in range(2)]
                vb = [sbt(st, "vb%d" % i, [128, 512], BF16) for i in range(2)]
                sg = [sbt(st, "sg%d" % i, [128, 512], F32) for i in range(2)]
                qkT = sbt(st, "qkT", [128, 4, 128], BF16)
                Sm = sbt(st, "Sm", [128, 128], BF16)
                bst = sbt(st, "bst", [128, 6], F32)
                mv = sbt(st, "mv", [128, 2], F32)
                grs = sbt(st, "grs", [128, 1], F32)
                on = sbt(st, "on", [128, 512], F32)
                gtd = sbt(st, "gtd", [128, 512], BF16)
                gT = [sbt(st, "gT%d" % i, [128, 4, 128], BF16) for i in range(2)]
                pQK = pbank(st, "pQK")
                pV = pbank(st, "pV")
                pG = pbank(st, "pG")
                pTq = pbank(st, "pTq", BF16)
                pSv = pTq.t[:, 512:768].bitcast(F32)
                pTg = pbank(st, "pTg", BF16)
                pO2 = [pbank(st, "pO%d" % i) for i in range(2)]
                pR1 = pbank(st, "pR1")
                Rb2 = [sbt(st, "Rb%d" % i, [128, 2, 512], BF16) for i in range(2)]
                bst2 = [sbt(st, "bst%d" % i, [128, 6], F32) for i in range(2)]
                mv2 = [sbt(st, "mv%d" % i, [128, 2], F32) for i in range(2)]
                grs2 = [sbt(st, "grs%d" % i, [128, 1], F32) for i in range(2)]
                on2 = [sbt(st, "on%d" % i, [128, 512], F32) for i in range(2)]
                gtd2 = [sbt(st, "gtd%d" % i, [128, 512], BF16) for i in range(2)]

                def load_WR(h):
                    w = WR[h % 2]
                    for (c0, n, o0) in ((C_RQ + h * 256, 256, 0), (C_RK + h * 256, 256, 256),
                                        (C_RV + h * 512, 512, 512), (C_RG + h * 512, 512, 1024)):
                        P.dma("pool", lambda e, w=w, c0=c0, n=n, o0=o0: e.dma_start(out=w.t[:, :, o0:o0 + n],
                                                                                       in_=w_in_v[:, :, c0:c0 + n]),
                              w.b, writes=[w.b])
                load_WR(0)
                for k0 in range(0, 8, 2):
                    P.dma("pool", lambda e, k0=k0: e.dma_start(out=WDa.t[:, k0:k0 + 2, :], in_=w_in_v[:, k0:k0 + 2, C_DQ:C_DQ + 3072]),
                          WDa.b, writes=[WDa.b])
                for h in range(4):
                    if h + 1 < 4:
                        load_WR(h + 1)
                    w = WR[h % 2]
                    g = GAM[h]
                    P.op("pool", lambda e: e.memset(Rf.t[:], 0.0), writes=[Rf.b])
                    P.op("pool", lambda e: e.memset(Rb2[0].t[:], 0.0), writes=[Rb2[0].b])
                    P.op("pool", lambda e: e.memset(Rb2[1].t[:], 0.0), writes=[Rb2[1].b])

                    def A1(blk):
                        own = blk >= NC
                        u_ = uTb[blk % 3]
                        rt = RT[blk % 3]
                        k_ = kr[blk % 2]
                        q_ = qr[blk % 2]
                        P.dma("sp", lambda e: e.dma_start(out=u_.t[:].rearrange("p a b -> p (a b)"), in_=UT[blk]),
                              u_.b, reads=[b_UT[blk]], writes=[u_.b])
                        P.dma("sp", lambda e: e.dma_start(out=rt.t[:], in_=rope_r[blk * 128:(blk + 1) * 128, :]),
                              rt.b, writes=[rt.b])
                        c0 = 0 if own else 256
                        for kc in range(8):
                            P.op("pe", lambda e, kc=kc: e.matmul(pQK.t[:, c0:512], lhsT=u_.t[:, kc, :], rhs=w.t[:, kc, c0:512],
                                                                 start=(kc == 0), stop=(kc == 7)),
                                 reads=[u_.b, w.b], writes=[pQK.b])
                        P.op("dve", lambda e: e.scalar_tensor_tensor(out=Ak.t[:], in0=pQK.t[:, 256:512], scalar=rdec.t[:, 4 + h:5 + h],
                                                                     in1=rt.t[:, 0:256], op0=ALU.mult, op1=ALU.mult),
                             reads=[pQK.b, rdec.b, rt.b], writes=[Ak.b])
                        P.op("dve", lambda e: e.scalar_tensor_tensor(out=Bk.t[:], in0=pQK.t[:, 256:512], scalar=rdec.t[:, 4 + h:5 + h],
                                                                     in1=rt.t[:, 256:512], op0=ALU.mult, op1=ALU.mult),
                             reads=[pQK.b, rdec.b, rt.b], writes=[Bk.b])
                        if own:
                            P.op("dve", lambda e: e.scalar_tensor_tensor(out=Aq.t[:], in0=pQK.t[:, 0:256], scalar=rdec.t[:, h:h + 1],
                                                                         in1=rt.t[:, 0:256], op0=ALU.mult, op1=ALU.mult),
                                 reads=[pQK.b, rdec.b, rt.b], writes=[Aq.b])
                            P.op("dve", lambda e: e.scalar_tensor_tensor(out=Bq.t[:], in0=pQK.t[:, 0:256], scalar=rdec.t[:, h:h + 1],
                                                                         in1=rt.t[:, 256:512], op0=ALU.mult, op1=ALU.mult),
                                 reads=[pQK.b, rdec.b, rt.b], writes=[Bq.b])
                        P.op("pool", lambda e: e.tensor_tensor(out=k_.t[:, 0:128], in0=Ak.t[:, 0:128], in1=Bk.t[:, 128:256], op=ALU.subtract),
                             reads=[Ak.b, Bk.b], writes=[k_.b])
                        P.op("pool", lambda e: e.tensor_tensor(out=k_.t[:, 128:256], in0=Ak.t[:, 128:256], in1=Bk.t[:, 0:128], op=ALU.add),
                             reads=[Ak.b, Bk.b], writes=[k_.b])
                        if own:
                            P.op("pool", lambda e: e.tensor_tensor(out=q_.t[:, 0:128], in0=Aq.t[:, 0:128], in1=Bq.t[:, 128:256], op=ALU.subtract),
                                 reads=[Aq.b, Bq.b], writes=[q_.b])
                            P.op("pool", lambda e: e.tensor_tensor(out=q_.t[:, 128:256], in0=Aq.t[:, 128:256], in1=Bq.t[:, 0:128], op=ALU.add),
                                 reads=[Aq.b, Bq.b], writes=[q_.b])

                    def A2(blk):
                        u_ = uTb[blk % 3]
                        v_ = vb[blk % 2]
                        for kc in range(8):
                            P.op("pe", lambda e, kc=kc: e.matmul(pV.t[:, 0:512], lhsT=u_.t[:, kc, :], rhs=w.t[:, kc, 512:1024],
                                                                 start=(kc == 0), stop=(kc == 7)),
                                 reads=[u_.b, w.b], writes=[pV.b])
                        P.op("act", lambda e: e.copy(v_.t[:], pV.t[:, 0:512]), reads=[pV.b], writes=[v_.b])

                    def A3(blk):
                        if blk < NC:
                            return
                        u_ = uTb[blk % 3]
                        s_ = sg[blk % 2]
                        for kc in range(8):
                            P.op("pe", lambda e, kc=kc: e.matmul(pG.t[:, 0:512], lhsT=u_.t[:, kc, :], rhs=w.t[:, kc, 1024:1536],
                                                                 start=(kc == 0), stop=(kc == 7)),
                                 reads=[u_.b, w.b], writes=[pG.b])
                        P.op("act", lambda e: e.activation(out=s_.t[:], in_=pG.t[:, 0:512], func=AF.Silu), reads=[pG.b], writes=[s_.b])

                    def B1(blk):
                        if blk < NC:
                            return
                        k_ = kr[blk % 2]
                        q_ = qr[blk % 2]
                        for j in range(4):
                            srcT = q_ if j < 2 else k_
                            c = j % 2
                            P.op("pe", lambda e, j=j, c=c, srcT=srcT: e.transpose(pTq.t[:, j * 128:(j + 1) * 128],
                                                                                  srcT.t[:, c * 128:(c + 1) * 128], ident.t[:]),
                                 reads=[srcT.b, ident.b], writes=[pTq.b])
                        P.op("act", lambda e: e.copy(qkT.t[:].rearrange("p a b -> p (a b)"), pTq.t[:, 0:512]),
                             reads=[pTq.b], writes=[qkT.b])

                    def B2(blk):
                        if blk < NC:
                            return
                        for c in range(2):
                            P.op("pe", lambda e, c=c: e.matmul(pSv, lhsT=qkT.t[:, 2 + c, :], rhs=qkT.t[:, c, :],
                                                               start=(c == 0), stop=(c == 1)),
                                 reads=[qkT.b], writes=[pTq.b])
                        P.op("dve", lambda e: e.scalar_tensor_tensor(out=Sm.t[:], in0=pSv, scalar=float(g ** -128.0),
                                                                     in1=cmask.t[:], op0=ALU.mult, op1=ALU.mult),
                             reads=[pTq.b, cmask.b], writes=[Sm.b])

                    def B3a(blk):
                        if blk < NC:
                            return
                        v_ = vb[blk % 2]
                        po = pO2[blk % 2]
                        rb = Rb2[(blk - 1) % 2]
                        P.op("pe", lambda e: e.matmul(po.t[:, 0:512], lhsT=Sm.t[:], rhs=v_.t[:], start=True, stop=False),
                             reads=[Sm.b, v_.b], writes=[po.b])
                        for c in range(2):
                            P.op("pe", lambda e, c=c: e.matmul(po.t[:, 0:512], lhsT=qkT.t[:, c, :], rhs=rb.t[:, c, :],
                                                               start=False, stop=(c == 1)),
                                 reads=[qkT.b, rb.b], writes=[po.b])

                    def CH(blk):
                        if blk < NC:
                            return
                        par = blk % 2
                        po = pO2[par]
                        s_ = sg[par]
                        bst_, mv_, grs_, on_, gtd_ = bst2[par], mv2[par], grs2[par], on2[par], gtd2[par]
                        P.op("dve", lambda e: e.bn_stats(bst_.t[:], po.t[:, 0:512]), reads=[po.b], writes=[bst_.b])
                        P.op("dve", lambda e: e.bn_aggr(mv_.t[:], bst_.t[:]), reads=[bst_.b], writes=[mv_.b])
                        P.op("dve", lambda e: e.tensor_scalar(grs_.t[:], mv_.t[:, 1:2], EPS, None, op0=ALU.add),
                             reads=[mv_.b], writes=[grs_.b])
                        P.op("pool", lambda e: e.tensor_tensor(out=grs_.t[:], in0=grs_.t[:], in1=mhalf.t[:, 0:1], op=ALU.pow),
                             reads=[grs_.b, mhalf.b], writes=[grs_.b])
                        P.op("dve", lambda e: e.tensor_scalar(on_.t[:], po.t[:, 0:512], mv_.t[:, 0:1], grs_.t[:, 0:1],
                                                              op0=ALU.subtract, op1=ALU.mult),
                             reads=[po.b, mv_.b, grs_.b], writes=[on_.b])
                        P.op("pool", lambda e: e.tensor_tensor(out=gtd_.t[:], in0=on_.t[:], in1=s_.t[:], op=ALU.mult),
                             reads=[on_.b, s_.b], writes=[gtd_.b])

                    def ST(blk, c):
                        if blk >= NB - 1:
                            return
                        k_ = kr[blk % 2]
                        v_ = vb[blk % 2]
                        P.op("pe", lambda e: e.matmul(pR1.t[:, 0:512], lhsT=k_.t[:, c * 128:(c + 1) * 128], rhs=v_.t[:],
                                                      start=True, stop=True),
                             reads=[k_.b, v_.b], writes=[pR1.b])
                        P.op("dve", lambda e: e.scalar_tensor_tensor(out=Rf.t[:, c, :], in0=Rf.t[:, c, :], scalar=float(g ** 128.0),
                                                                     in1=pR1.t[:, 0:512], op0=ALU.mult, op1=ALU.add),
                             reads=[Rf.b, pR1.b], writes=[Rf.b])
                        if c == 1:
                            rb = Rb2[blk % 2]
                            P.op("act", lambda e: e.copy(rb.t[:].rearrange("p a b -> p (a b)"), Rf.t[:].rearrange("p a b -> p (a b)")),
                                 reads=[Rf.b], writes=[rb.b])

                    def G(blk):
                        if blk < NC or blk >= NB:
                            return
                        ob = blk - NC
                        g_ = gT[ob % 2]
                        gtd_ = gtd2[blk % 2]
                        for j in range(4):
                            P.op("pe", lambda e, j=j: e.transpose(pTg.t[:, j * 128:(j + 1) * 128],
                                                                  gtd_.t[:, j * 128:(j + 1) * 128], ident.t[:]),
                                 reads=[gtd_.b, ident.b], writes=[pTg.b])
                        P.op("act", lambda e: e.copy(g_.t[:].rearrange("p a b -> p (a b)"), pTg.t[:, 0:512]),
                             reads=[pTg.b], writes=[g_.b])
                        P.dma("pool", lambda e: e.dma_start(out=GT[ob][:, h * 512:(h + 1) * 512],
                                                            in_=g_.t[:].rearrange("p a b -> p (a b)")),
                              g_.b, reads=[g_.b], writes=[b_GT[ob]])

                    A1(0); A2(0); A3(0)
                    for blk in range(NB):
                        nx = blk + 1
                        B1(blk)
                        if nx < NB:
                            A1(nx)
                        B2(blk)
                        if blk < NC:
                            ST(blk, 0)
                            if nx < NB:
                                A2(nx)
                            ST(blk, 1)
                            if nx < NB:
                                A3(nx)
                            continue
                        if nx < NB:
                            A2(nx)
                        B3a(blk)
                        G(blk - 1)
                        ST(blk, 0)
                        if nx < NB:
                            A3(nx)
                        ST(blk, 1)
                        CH(blk)
                    G(NB - 1)
                    P.emit()

            KTs = dscr("KTs", [8, 128, NB * 128], BF16)
            QTs = dscr("QTs", [8, 128, NO * 128], BF16)
            VVs = dscr("VVs", [8, 128, NB, 128], BF16)
            b_KTs = Buf("KTs"); b_QTs = Buf("QTs"); b_VVs = Buf("VVs")
            KTs_w = KTs.rearrange("h p (b t) -> p h b t", t=128)
            QTs_w = QTs.rearrange("h p (b t) -> p h b t", t=128)
            VVs_w = VVs.rearrange("h p b e -> p h b e")
            with ExitStack() as st:
                ropd = sbt(st, "ropd", [128, NB, 32], F32)
                ropd_v = rope_d.rearrange("(b p) c -> p b c", p=128)
                for b0 in range(0, NB, 16):
                    b1 = min(NB, b0 + 16)
                    P.dma("sp", lambda e, b0=b0, b1=b1: e.dma_start(out=ropd.t[:, b0:b1, :], in_=ropd_v[:, b0:b1, :]),
                          ropd.b, writes=[ropd.b])
                wq8 = sbt(st, "wq8", [128, 8, 64], F32)
                wk8 = sbt(st, "wk8", [128, 8, 64], F32)
                for g8 in range(8):
                    P.op("pool", lambda e, g8=g8: e.tensor_copy(wq8.t[:, g8, :], wqk.t[:, 0, :]), reads=[wqk.b], writes=[wq8.b])
                    P.op("pool", lambda e, g8=g8: e.tensor_copy(wk8.t[:, g8, :], wqk.t[:, 2, :]), reads=[wqk.b], writes=[wk8.b])
                uTb = [sbt(st, "duT%d" % i, [128, 8, 128], BF16) for i in range(3)]
                NCH = 4
                sqd = [sbt(st, "sqd%d" % i, [128, 8, 64], F32) for i in range(NCH)]
                ssd = [sbt(st, "ssd%d" % i, [128, 8], F32) for i in range(NCH)]
                rsd = [sbt(st, "rsd%d" % i, [128, 8], F32) for i in range(NCH)]
                xn = [[sbt(st, "xn%d_%d" % (pp, i), [128, 8, 64], F32) for i in range(NCH)] for pp in range(2)]
                xbq = [[sbt(st, "xbq%d_%d" % (pp, i), [128, 8, 64], BF16) for i in range(NCH)] for pp in range(2)]
                rc = [[sbt(st, "rc%d_%d" % (pp, i), [128, 8, 16], F32) for i in range(NCH)] for pp in range(2)]
                xw = [[sbt(st, "xw%d_%d" % (pp, i), [128, 8, 16], F32) for i in range(NCH)] for pp in range(2)]
                rsn = [[sbt(st, "rsn%d_%d" % (pp, i), [128, 8, 16], F32) for i in range(NCH)] for pp in range(2)]
                kst = [sbt(st, "kst%d" % i, [128, 8, 128], BF16) for i in range(2)]
                qst = [sbt(st, "qst%d" % i, [128, 8, 128], BF16) for i in range(2)]
                vst = [sbt(st, "vst%d" % i, [128, 8, 128], BF16) for i in range(2)]
                pq = [pbank(st, "pq%d" % i) for i in range(2)]
                pk = [pbank(st, "pk%d" % i) for i in range(2)]
                pvv = [pbank(st, "pvv%d" % i) for i in range(2)]
                pTk = pbank(st, "pTk", BF16)
                pTq = pbank(st, "pTq1", BF16)
                def mk_chains(blk):
                    chains = []
                    for half in range(2):
                        chains.append((pk[half], wk8, pTk, half, 1024 + half * 512))
                    if blk >= NC:
                        for half in range(2):
                            chains.append((pq[half], wq8, pTq, half, half * 512))
                    return chains

                def d1_early(blk):
                    u_ = uTb[blk % 3]
                    par = blk % 2
                    P.dma("sp", lambda e: e.dma_start(out=u_.t[:].rearrange("p a b -> p (a b)"), in_=UT[blk]),
                          u_.b, reads=[b_UT[blk]], writes=[u_.b])
                    chains = mk_chains(blk)
                    for (pb, wt, ptT, half, c0) in chains:
                        for kc in range(8):
                            P.op("pe", lambda e, kc=kc, pb=pb, c0=c0: e.matmul(pb.t[:, 0:512], lhsT=u_.t[:, kc, :], rhs=WDa.t[:, kc, c0:c0 + 512],
                                                                               start=(kc == 0), stop=(kc == 7)),
                                 reads=[u_.b, WDa.b], writes=[pb.b])
                    for half in range(2):
                        for kc in range(8):
                            P.op("pe", lambda e, kc=kc, half=half: e.matmul(pvv[half].t[:, 0:512], lhsT=u_.t[:, kc, :],
                                                                           rhs=WDa.t[:, kc, 2048 + half * 512:2048 + (half + 1) * 512],
                                                                           start=(kc == 0), stop=(kc == 7)),
                                 reads=[u_.b, WDa.b], writes=[pvv[half].b])
                    nch = len(chains)
                    pvw = [ch[0].t[:, 0:512].rearrange("p (a b) -> p a b", b=64) for ch in chains]
                    for ci in range(nch):
                        P.op("act", lambda e, ci=ci: e.activation(out=sqd[ci].t[:], in_=pvw[ci], func=AF.Square),
                             reads=[chains[ci][0].b], writes=[sqd[ci].b])
                    for ci in range(nch):
                        P.op("dve", lambda e, ci=ci: e.tensor_reduce(out=ssd[ci].t[:], in_=sqd[ci].t[:], axis=AX.X, op=ALU.add),
                             reads=[sqd[ci].b], writes=[ssd[ci].b])
                    for ci in range(nch):
                        P.op("act", lambda e, ci=ci: e.activation(out=rsd[ci].t[:], in_=ssd[ci].t[:], func=AF.Sqrt, scale=1.0 / 64, bias=eps_t.t[:, 0:1]),
                             reads=[ssd[ci].b, eps_t.b], writes=[rsd[ci].b])
                    v_ = vst[par]
                    for half in range(2):
                        P.op("act", lambda e, half=half: e.copy(v_.t[:, half * 4:half * 4 + 4, :].rearrange("p a b -> p (a b)"), pvv[half].t[:, 0:512]),
                             reads=[pvv[half].b], writes=[v_.b])
                    P.dma("act", lambda e: e.dma_start(out=VVs_w[:, :, blk, :], in_=v_.t[:]), v_.b, reads=[v_.b], writes=[b_VVs])
                    for ci in range(nch):
                        P.op("dve", lambda e, ci=ci: e.reciprocal(rsd[ci].t[:], rsd[ci].t[:]), reads=[rsd[ci].b], writes=[rsd[ci].b])
                    for ci in range(nch):
                        P.op("dve", lambda e, ci=ci: e.tensor_tensor(out=xn[par][ci].t[:], in0=pvw[ci], in1=bc_last(rsd[ci].t[:], 64), op=ALU.mult),
                             reads=[chains[ci][0].b, rsd[ci].b], writes=[xn[par][ci].b])

                def d1_late(blk):
                    par = blk % 2
                    own = blk >= NC
                    ob = blk - NC
                    chains = mk_chains(blk)
                    nch = len(chains)
                    xn_, xb_, rc_, rsn_, xw_ = xn[par], xbq[par], rc[par], rsn[par], xw[par]
                    for ci in range(nch):
                        eng = "pool"
                        wt = chains[ci][1]
                        P.op(eng, lambda e, ci=ci, wt=wt: e.tensor_tensor(out=xb_[ci].t[:], in0=xn_[ci].t[:], in1=wt.t[:], op=ALU.mult),
                             reads=[xn_[ci].b, wt.b], writes=[xb_[ci].b])
                        P.op(eng, lambda e, ci=ci, wt=wt: e.tensor_tensor(out=xw_[ci].t[:], in0=xn_[ci].t[:, :, 0:16], in1=wt.t[:, :, 0:16], op=ALU.mult),
                             reads=[xn_[ci].b, wt.b], writes=[xw_[ci].b])
                        P.op(eng, lambda e, ci=ci: e.tensor_tensor(out=rc_[ci].t[:], in0=xw_[ci].t[:],
                                                                  in1=bc_mid(ropd.t[:, blk, 0:16], 8), op=ALU.mult),
                             reads=[xw_[ci].b, ropd.b], writes=[rc_[ci].b])
                        P.op(eng, lambda e, ci=ci: e.tensor_tensor(out=rsn_[ci].t[:], in0=xw_[ci].t[:],
                                                                  in1=bc_mid(ropd.t[:, blk, 16:32], 8), op=ALU.mult),
                             reads=[xw_[ci].b, ropd.b], writes=[rsn_[ci].b])
                        P.op(eng, lambda e, ci=ci: e.tensor_tensor(out=xb_[ci].t[:, :, 0:8], in0=rc_[ci].t[:, :, 0:8], in1=rsn_[ci].t[:, :, 8:16], op=ALU.subtract),
                             reads=[rc_[ci].b, rsn_[ci].b], writes=[xb_[ci].b])
                        P.op(eng, lambda e, ci=ci: e.tensor_tensor(out=xb_[ci].t[:, :, 8:16], in0=rc_[ci].t[:, :, 8:16], in1=rsn_[ci].t[:, :, 0:8], op=ALU.add),
                             reads=[rc_[ci].b, rsn_[ci].b], writes=[xb_[ci].b])
                    for ci in range(nch):
                        ptT, half = chains[ci][2], chains[ci][3]
                        for hh in range(4):
                            P.op("pe", lambda e, ci=ci, hh=hh, ptT=ptT, half=half: e.transpose(ptT.t[:, (half * 4 + hh) * 128:(half * 4 + hh + 1) * 128],
                                                                                              xb_[ci].t[:, 2 * hh:2 * hh + 2, :].rearrange("p a b -> p (a b)"),
                                                                                              ident.t[:]),
                                 reads=[xb_[ci].b, ident.b], writes=[ptT.b])
                    k_ = kst[par]
                    P.op("dve", lambda e: e.tensor_copy(k_.t[:].rearrange("p a b -> p (a b)"), pTk.t[:, 0:1024]), reads=[pTk.b], writes=[k_.b])
                    P.dma("act", lambda e: e.dma_start(out=KTs_w[:, :, blk, :], in_=k_.t[:]), k_.b, reads=[k_.b], writes=[b_KTs])
                    if own:
                        q_ = qst[par]
                        P.op("act", lambda e: e.copy(q_.t[:].rearrange("p a b -> p (a b)"), pTq.t[:, 0:1024]), reads=[pTq.b], writes=[q_.b])
                        P.dma("act", lambda e: e.dma_start(out=QTs_w[:, :, ob, :], in_=q_.t[:]), q_.b, reads=[q_.b], writes=[b_QTs])

                d1_early(0)
                for blk in range(NB):
                    if blk + 1 < NB:
                        d1_early(blk + 1)
                    d1_late(blk)
                P.emit()

        with ExitStack() as st_pre2:
            Wg = sbt(st_pre2, "Wg", [128, 8, 2048], BF16)
            Wro = sbt(st_pre2, "Wro", [128, 16, 1024], BF16)
            with ExitStack() as st:
                KTb = [sbt(st, "KT%d" % i, [128, NB * 128], BF16) for i in range(2)]
                VVb = [sbt(st, "VV%d" % i, [128, NB, 130], BF16) for i in range(2)]
                QT2b = [sbt(st, "QT2%d" % i, [128, NO, 256], BF16) for i in range(2)]
                NPT = 6
                PT = [sbt(st, "PT%d" % i, [128, 4, 128], BF16) for i in range(NPT)]
                zz = sbt(st, "zz", [128, 2], F32)
                a1 = sbt(st, "a1", [128, 128], F32)
                aa = sbt(st, "aa", [128, 128], F32)
                asq = sbt(st, "asq", [128, 128], F32)
                ass = sbt(st, "ass", [128, 1], F32)
                ars = sbt(st, "ars", [128, 1], F32)
                dob = sbt(st, "dob", [128, 128], BF16)
                doT = [sbt(st, "doT%d" % i, [128, 128], BF16) for i in range(2)]
                pP = pbank(st, "pP")
                pTd = pbank(st, "pTd", BF16)
                pSd = [pbank(st, "pSd%d" % i) for i in range(2)]
                pO0 = [pbank(st, "pO0%d" % i) for i in range(2)]
                pO1 = [pbank(st, "pO1%d" % i) for i in range(2)]
                assert NC % 2 == 0
                for i2 in range(2):
                    P.op("pool", lambda e, i2=i2: e.memset(VVb[i2].t[:], 0.0), writes=[VVb[i2].b])
                    P.op("dve", lambda e, i2=i2: e.tensor_copy(VVb[i2].t[:, :, 128:129], kbias.t[:].rearrange("p (a b) -> p a b", b=1)),
                         reads=[kbias.b], writes=[VVb[i2].b])
                    P.op("pool", lambda e, i2=i2: e.memset(QT2b[i2].t[:], 0.0), writes=[QT2b[i2].b])

                def load_head(h):
                    kt, vv, q2 = KTb[h % 2], VVb[h % 2], QT2b[h % 2]
                    P.dma("sp", lambda e: e.dma_start(out=kt.t[:], in_=KTs[h]), kt.b, reads=[b_KTs], writes=[kt.b])
                    P.dma("sp", lambda e: e.dma_start(out=vv.t[:, :, 0:128], in_=VVs[h]), vv.b, reads=[b_VVs], writes=[vv.b])
                    P.dma("sp", lambda e: e.dma_start(out=q2.t[0:64, :, 0:128], in_=QTs[h][0:64, :].rearrange("p (i t) -> p i t", t=128)),
                          q2.b, reads=[b_QTs], writes=[q2.b])
                    P.dma("sp", lambda e: e.dma_start(out=q2.t[64:128, :, 128:256], in_=QTs[h][64:128, :].rearrange("p (i t) -> p i t", t=128)),
                          q2.b, reads=[b_QTs], writes=[q2.b])
                load_head(0)
                wro_pv = w_ret_o.rearrange("(k p) n -> p k n", p=128)
                for k0 in range(0, 8, 2):
                    P.dma("pool", lambda e, k0=k0: e.dma_start(out=Wg.t[:, k0:k0 + 2, :], in_=w_in_v[:, k0:k0 + 2, C_GT:C_GT + 2048]), Wg.b, writes=[Wg.b])
                for k0 in range(0, 16, 4):
                    P.dma("pool", lambda e, k0=k0: e.dma_start(out=Wro.t[:, k0:k0 + 4, :], in_=wro_pv[:, k0:k0 + 4, :]), Wro.b, writes=[Wro.b])
                for h in range(8):
                    if h + 1 < 8:
                        load_head(h + 1)
                    KT, VV, QT2 = KTb[h % 2], VVb[h % 2], QT2b[h % 2]
                    b_K = [KT.b] * NB
                    b_Kv = VV.b
                    b_Q = [QT2.b] * NO
                    items = []
                    for i in range(NO):
                        nk = NC + i + 1
                        for kb0 in range(0, nk, 2):
                            items.append((i, kb0, min(2, nk - kb0)))
                    SKEW = 2
                    pS3 = [pSd[0], pSd[1], pP]

                    def qk_exp(n):
                        i, kb0, nb = items[n]
                        nk = NC + i + 1
                        ps = pS3[n % 3]
                        pt = PT[n % NPT]
                        for j in range(nb):
                            kb = kb0 + j
                            P.op("pe", lambda e, j=j, kb=kb: e.matmul(ps.t[:, j * 256:(j + 1) * 256], lhsT=KT.t[:, kb * 128:(kb + 1) * 128],
                                                                      rhs=QT2.t[:, i, :], start=True, stop=True),
                                 reads=[b_K[kb], b_Q[i]], writes=[ps.b])
                        P.op("act", lambda e: e.activation(out=pt.t[:, 0:2 * nb, :].rearrange("p a b -> p (a b)"),
                                                           in_=ps.t[:, 0:256 * nb], func=AF.Exp, scale=0.125),
                             reads=[ps.b], writes=[pt.b])
                        if kb0 + nb == nk:
                            jl = nb - 1
                            P.op("pool", lambda e: e.tensor_tensor(out=pt.t[:, 2 * jl:2 * jl + 2, :], in0=pt.t[:, 2 * jl:2 * jl + 2, :],
                                                                   in1=cmask2.t[:], op=ALU.mult),
                                 reads=[pt.b, cmask2.b], writes=[pt.b])

                    def pv(n):
                        i, kb0, nb = items[n]
                        nk = NC + i + 1
                        o0 = pO0[i % 2]
                        o1 = pO1[i % 2]
                        pt = PT[n % NPT]
                        for j in range(nb):
                            kb = kb0 + j
                            P.op("pe", lambda e, j=j, kb=kb: e.matmul(o0.t[:, 0:129], lhsT=pt.t[:, 2 * j, :], rhs=VV.t[:, kb, 0:129],
                                                                      start=(kb == 0), stop=(kb == nk - 1)),
                                 reads=[pt.b, b_Kv], writes=[o0.b])
                            P.op("pe", lambda e, j=j, kb=kb: e.matmul(o1.t[:, 0:129], lhsT=pt.t[:, 2 * j + 1, :], rhs=VV.t[:, kb, 0:129],
                                                                      start=(kb == 0), stop=(kb == nk - 1)),
                                 reads=[pt.b, b_Kv], writes=[o1.b])
                        if kb0 + nb == nk:
                            finalize(i, o0, o1)

                    def finalize(i, o0, o1):
                        while pend_fin:
                            pend_fin.pop(0)[1]()
                        P.op("dve", lambda e, o0=o0: e.reciprocal(zz.t[:, 0:1], o0.t[:, 128:129]), reads=[o0.b], writes=[zz.b])
                        P.op("dve", lambda e, o1=o1: e.reciprocal(zz.t[:, 1:2], o1.t[:, 128:129]), reads=[o1.b], writes=[zz.b])
                        P.op("dve", lambda e: e.tensor_tensor(out=zz.t[:, 1:2], in0=zz.t[:, 1:2], in1=lam.t[:, 0:1], op=ALU.mult),
                             reads=[zz.b, lam.b], writes=[zz.b])
                        P.op("dve", lambda e, o1=o1: e.tensor_scalar(a1.t[:], o1.t[:, 0:128], zz.t[:, 1:2], None, op0=ALU.mult),
                             reads=[o1.b, zz.b], writes=[a1.b])
                        P.op("dve", lambda e, o0=o0: e.scalar_tensor_tensor(out=aa.t[:], in0=o0.t[:, 0:128], scalar=zz.t[:, 0:1], in1=a1.t[:],
                                                                            op0=ALU.mult, op1=ALU.subtract),
                             reads=[o0.b, zz.b, a1.b], writes=[aa.b])
                        finalize_b(i)
                        pend_fin.append((n_now[0] + 8, lambda: finalize_c(i)))

                    def finalize_b(i):
                        P.op("dve", lambda e: e.tensor_tensor(out=asq.t[:], in0=aa.t[:], in1=aa.t[:], op=ALU.mult), reads=[aa.b], writes=[asq.b])
                        P.op("dve", lambda e: e.tensor_reduce(out=ass.t[:, 0:1], in_=asq.t[:], axis=AX.X, op=ALU.add), reads=[asq.b], writes=[ass.b])
                        P.op("dve", lambda e: e.tensor_scalar(ars.t[:], ass.t[:], 1.0 / 128, EPS, op0=ALU.mult, op1=ALU.add),
                             reads=[ass.b], writes=[ars.b])
                        P.op("pool", lambda e: e.tensor_tensor(out=ars.t[:], in0=ars.t[:], in1=mhalf.t[:, 0:1], op=ALU.pow),
                             reads=[ars.b, mhalf.b], writes=[ars.b])
                        P.op("dve", lambda e: e.scalar_tensor_tensor(out=dob.t[:], in0=aa.t[:], scalar=ars.t[:, 0:1], in1=sublnw.t[:],
                                                                     op0=ALU.mult, op1=ALU.mult),
                             reads=[aa.b, ars.b, sublnw.b], writes=[dob.b])

                    def finalize_c(i):
                        P.op("pe", lambda e: e.transpose(pTd.t[:, 256:384], dob.t[:], ident.t[:]), reads=[dob.b, ident.b], writes=[pTd.b])
                        d_ = doT[i % 2]
                        P.op("dve", lambda e, d_=d_: e.tensor_copy(d_.t[:], pTd.t[:, 256:384]), reads=[pTd.b], writes=[d_.b])
                        P.dma("pool", lambda e, d_=d_, i=i: e.dma_start(out=DOT[i][:, h * 128:(h + 1) * 128], in_=d_.t[:]),
                              d_.b, reads=[d_.b], writes=[b_DOT[i]])
                    pend_fin = []
                    n_now = [0]
                    for n in range(len(items) + SKEW + 10):
                        n_now[0] = n
                        if n < len(items):
                            qk_exp(n)
                        if 0 <= n - SKEW < len(items):
                            pv(n - SKEW)
                        while pend_fin and pend_fin[0][0] <= n:
                            pend_fin.pop(0)[1]()
                    assert not pend_fin
                    P.emit()

            with ExitStack() as st:
                Wdo = sbt(st, "Wdo", [128, 8, 1024], BF16)
                Wou = sbt(st, "Wou", [128, 8, 1024], BF16)
                n2w = sbt(st, "n2w", [128, D], F32)
                P.dma("sp", lambda e: e.dma_start(out=n2w.t[:], in_=bc_part(norm2_w, D)), n2w.b, writes=[n2w.b])
                wro_v = w_ret_o.rearrange("(k p) n -> p k n", p=128)
                wdo_v = w_diff_o.rearrange("(k p) n -> p k n", p=128)
                wou_v = w_out.rearrange("(k p) n -> p k n", p=128)
                for k0 in range(0, 8, 4):
                    P.dma("pool", lambda e, k0=k0: e.dma_start(out=Wdo.t[:, k0:k0 + 4, :], in_=wdo_v[:, k0:k0 + 4, :]), Wdo.b, writes=[Wdo.b])
                    P.dma("pool", lambda e, k0=k0: e.dma_start(out=Wou.t[:, k0:k0 + 4, :], in_=wou_v[:, k0:k0 + 4, :]), Wou.b, writes=[Wou.b])
                gTb = [sbt(st, "mgT%d" % i, [128, 16, 128], BF16) for i in range(2)]
                dTb = [sbt(st, "mdT%d" % i, [128, 8, 128], BF16) for i in range(2)]
                uTb = [sbt(st, "muT%d" % i, [128, 8, 128], BF16) for i in range(2)]
                xb = [sbt(st, "mx%d" % i, [128, D], F32) for i in range(2)]
                sig = sbt(st, "sig", [128, 2048], F32)
                m1 = sbt(st, "m1", [128, D], F32)
                m2 = sbt(st, "m2", [128, D], F32)
                mb = [sbt(st, "mb%d" % i, [128, D], BF16) for i in range(2)]
                mT = sbt(st, "mT", [128, 8, 128], BF16)
                h2 = [sbt(st, "h2%d" % i, [128, D], F32) for i in range(2)]
                sq = sbt(st, "msq", [128, D], F32)
                ss = sbt(st, "mss", [128, 1], F32)
                rs = sbt(st, "mrs", [128, 1], F32)
                ub = sbt(st, "mub", [128, D], BF16)
                u2T = [sbt(st, "mu2T%d" % i, [128, 8, 128], BF16) for i in range(2)]
                pA = [pbank(st, "pA%d" % i) for i in range(4)]
                pB = [pbank(st, "pB%d" % i) for i in range(2)]
                pTm = [pbank(st, "pTm%d" % i, BF16) for i in range(2)]
                def MA1(ob):
                    blk = NC + ob
                    g_ = gTb[ob % 2]; d_ = dTb[ob % 2]; u_ = uTb[ob % 2]; x_ = xb[ob % 2]
                    P.dma("sp", lambda e: e.dma_start(out=g_.t[:].rearrange("p a b -> p (a b)"), in_=GT[ob]),
                          g_.b, reads=[b_GT[ob]], writes=[g_.b])
                    P.dma("sp", lambda e: e.dma_start(out=d_.t[:].rearrange("p a b -> p (a b)"), in_=DOT[ob]),
                          d_.b, reads=[b_DOT[ob]], writes=[d_.b])
                    P.dma("sp", lambda e: e.dma_start(out=u_.t[:].rearrange("p a b -> p (a b)"), in_=UT[blk]),
                          u_.b, reads=[b_UT[blk]], writes=[u_.b])
                    P.dma("sp", lambda e: e.dma_start(out=x_.t[:], in_=xo[ob * 128:(ob + 1) * 128, :]), x_.b, writes=[x_.b])
                    for j in range(4):
                        for kc in range(8):
                            P.op("pe", lambda e, j=j, kc=kc: e.matmul(pA[j].t[:, 0:512], lhsT=u_.t[:, kc, :], rhs=Wg.t[:, kc, j * 512:(j + 1) * 512],
                                                                      start=(kc == 0), stop=(kc == 7)),
                                 reads=[u_.b, Wg.b], writes=[pA[j].b])
                        P.op("act", lambda e, j=j: e.activation(out=sig.t[:, j * 512:(j + 1) * 512], in_=pA[j].t[:, 0:512], func=AF.Sigmoid),
                             reads=[pA[j].b], writes=[sig.b])

                def MA2(ob):
                    g_ = gTb[ob % 2]
                    for j in range(2):
                        for kc in range(16):
                            P.op("pe", lambda e, j=j, kc=kc: e.matmul(pA[j].t[:, 0:512], lhsT=g_.t[:, kc, :], rhs=Wro.t[:, kc, j * 512:(j + 1) * 512],
                                                                      start=(kc == 0), stop=(kc == 15)),
                                 reads=[g_.b, Wro.b], writes=[pA[j].b])
                        P.op("dve", lambda e, j=j: e.tensor_tensor(out=m1.t[:, j * 512:(j + 1) * 512], in0=pA[j].t[:, 0:512],
                                                                   in1=sig.t[:, j * 512:(j + 1) * 512], op=ALU.mult),
                             reads=[pA[j].b, sig.b], writes=[m1.b])

                def MA3(ob):
                    d_ = dTb[ob % 2]
                    mb_ = mb[ob % 2]
                    for j in range(2):
                        for kc in range(8):
                            P.op("pe", lambda e, j=j, kc=kc: e.matmul(pA[2 + j].t[:, 0:512], lhsT=d_.t[:, kc, :], rhs=Wdo.t[:, kc, j * 512:(j + 1) * 512],
                                                                      start=(kc == 0), stop=(kc == 7)),
                                 reads=[d_.b, Wdo.b], writes=[pA[2 + j].b])
                        P.op("dve", lambda e, j=j: e.tensor_tensor(out=m2.t[:, j * 512:(j + 1) * 512], in0=pA[2 + j].t[:, 0:512],
                                                                   in1=sig.t[:, 1024 + j * 512:1024 + (j + 1) * 512], op=ALU.mult),
                             reads=[pA[2 + j].b, sig.b], writes=[m2.b])
                    P.op("pool", lambda e: e.tensor_tensor(out=mb_.t[:], in0=m1.t[:], in1=m2.t[:], op=ALU.add), reads=[m1.b, m2.b], writes=[mb_.b])

                def MB1(ob):
                    mb_ = mb[ob % 2]
                    for half in range(2):
                        pt = pTm[half]
                        for j in range(4):
                            kc = half * 4 + j
                            P.op("pe", lambda e, kc=kc, j=j, pt=pt: e.transpose(pt.t[:, j * 128:(j + 1) * 128], mb_.t[:, kc * 128:(kc + 1) * 128], ident.t[:]),
                                 reads=[mb_.b, ident.b], writes=[pt.b])
                        if half == 0:
                            P.op("dve", lambda e, pt=pt: e.tensor_copy(mT.t[:, 0:4, :].rearrange("p a b -> p (a b)"), pt.t[:, 0:512]),
                                 reads=[pt.b], writes=[mT.b])
                        else:
                            P.op("act", lambda e, pt=pt: e.copy(mT.t[:, 4:8, :].rearrange("p a b -> p (a b)"), pt.t[:, 0:512]),
                                 reads=[pt.b], writes=[mT.b])

                def MB2(ob):
                    h_ = h2[ob % 2]
                    x_ = xb[ob % 2]
                    for j in range(2):
                        for kc in range(8):
                            P.op("pe", lambda e, j=j, kc=kc: e.matmul(pB[j].t[:, 0:512], lhsT=mT.t[:, kc, :], rhs=Wou.t[:, kc, j * 512:(j + 1) * 512],
                                                                      start=(kc == 0), stop=(kc == 7)),
                                 reads=[mT.b, Wou.b], writes=[pB[j].b])
                        P.op("dve", lambda e, j=j: e.tensor_tensor(out=h_.t[:, j * 512:(j + 1) * 512], in0=pB[j].t[:, 0:512],
                                                                   in1=x_.t[:, j * 512:(j + 1) * 512], op=ALU.add),
                             reads=[pB[j].b, x_.b], writes=[h_.b])
                    P.dma("pool", lambda e: e.dma_start(out=H2[ob * 128:(ob + 1) * 128, :], in_=h_.t[:]),
                          h_.b, reads=[h_.b], writes=[b_H2[ob]])
                    norm_part(h_, n2w, sq, ss, rs, ub, use_pow=True)

                def MB3(ob):
                    t_ = u2T[ob % 2]
                    tr_part(ub, pTm, t_)
                    P.dma("pool", lambda e: e.dma_start(out=U2T[ob], in_=t_.t[:].rearrange("p a b -> p (a b)")),
                          t_.b, reads=[t_.b], writes=[b_U2T[ob]])

                MA1(0); MA2(0); MA3(0)
                for ob in range(NO):
                    nx = ob + 1
                    MB1(ob)
                    if nx < NO:
                        MA1(nx)
                    MB2(ob)
                    if nx < NO:
                        MA2(nx)
                    MB3(ob)
                    if nx < NO:
                        MA3(nx)
                P.emit()

        GB = 3
        NG = (NO + GB - 1) // GB
        with ExitStack() as st:
            Wup = sbt(st, "Wup", [128, 8, 2 * FFN], BF16)
            Wdn = sbt(st, "Wdn", [128, 22, D], BF16)
            cw = sbt(st, "cw", [128, 3, 44], F32)
            cb = sbt(st, "cb", [128, 44], F32)
            wup_v = w_up.rearrange("(k p) n -> p k n", p=128)
            b_Wup = [Buf("Wup%d" % i) for i in range(4)]
            for ch in (0, 2, 1, 3):
                c0 = ch * 1408
                for kc in range(8):
                    P.dma("pool", lambda e, kc=kc, c0=c0: e.dma_start(out=Wup.t[:, kc, c0:c0 + 1408], in_=wup_v[:, kc, c0:c0 + 1408]),
                          b_Wup[ch], writes=[b_Wup[ch]])
            wdn_v = w_down.rearrange("(k p) n -> p k n", p=128)
            for k0 in range(0, 22, 2):
                P.dma("pool", lambda e, k0=k0: e.dma_start(out=Wdn.t[:, k0:k0 + 2, :], in_=wdn_v[:, k0:k0 + 2, :]), Wdn.b, writes=[Wdn.b])
            for t0 in range(0, 44, 11):
                for k in range(3):
                    P.dma("sp", lambda e, k=k, t0=t0: e.dma_start(out=cw.t[:, k, t0:t0 + 11],
                                                                  in_=conv_w[k].rearrange("(t p) -> p t", p=128)[:, t0:t0 + 11],
                                                                  allow_slow_non_contiguous=True), cw.b, writes=[cw.b])
                P.dma("sp", lambda e, t0=t0: e.dma_start(out=cb.t[:, t0:t0 + 11], in_=conv_b.rearrange("(t p) -> p t", p=128)[:, t0:t0 + 11],
                                                         allow_slow_non_contiguous=True), cb.b, writes=[cb.b])
            NT = GB * 128
            u2g = [sbt(st, "u2g%d" % i, [128, 8, 2 + NT], BF16) for i in range(2)]
            ya = [sbt(st, "ya%d" % i, [128, NT], F32) for i in range(2)]
            yb = [sbt(st, "yb%d" % i, [128, NT], F32) for i in range(2)]
            sa = [sbt(st, "sa%d" % i, [128, NT], F32) for i in range(2)]
            gTt2 = [sbt(st, "gTt%d" % i, [128, 22, NT], BF16) for i in range(2)]
            hb = [sbt(st, "fh%d" % i, [128, D], F32) for i in range(1)] * 2
            ob_ = [sbt(st, "fo%d" % i, [128, D], F32) for i in range(2)]
            pU = [pbank(st, "pU%d" % i) for i in range(4)]
            pD = [pbank(st, "pD%d" % i) for i in range(4)]
            P.op("pool", lambda e: e.memset(u2g[0].t[:], 0.0), writes=[u2g[0].b])
            P.op("pool", lambda e: e.memset(u2g[1].t[:], 0.0), writes=[u2g[1].b])
            ui = [0]

            def up_pair(gi, ft, ug, nt):
                gt = gTt2[gi % 2]
                tiles = []
                for which, fi in ((0, ft), (1, ft + 22)):
                    pu = pU[ui[0] % 4]
                    ui[0] += 1
                    for kc in range(8):
                        P.op("pe", lambda e, pu=pu, kc=kc, fi=fi: e.matmul(pu.t[:, 0:nt + 2], lhsT=Wup.t[:, kc, fi * 128:(fi + 1) * 128],
                                                                           rhs=ug.t[:, kc, 0:nt + 2], start=(kc == 0), stop=(kc == 7)),
                             reads=[b_Wup[fi // 11], ug.b], writes=[pu.b])
                    yt = (ya if which == 0 else yb)[ft % 2]
                    P.op("dve", lambda e, pu=pu, yt=yt, fi=fi: e.tensor_scalar(yt.t[:, 0:nt], pu.t[:, 2:nt + 2], cw.t[:, 2, fi:fi + 1], cb.t[:, fi:fi + 1],
                                                                               op0=ALU.mult, op1=ALU.add),
                         reads=[pu.b, cw.b, cb.b], writes=[yt.b])
                    P.op("dve", lambda e, pu=pu, yt=yt, fi=fi: e.scalar_tensor_tensor(out=yt.t[:, 0:nt], in0=pu.t[:, 1:nt + 1], scalar=cw.t[:, 1, fi:fi + 1],
                                                                                      in1=yt.t[:, 0:nt], op0=ALU.mult, op1=ALU.add),
                         reads=[pu.b, cw.b, yt.b], writes=[yt.b])
                    P.op("dve", lambda e, pu=pu, yt=yt, fi=fi: e.scalar_tensor_tensor(out=yt.t[:, 0:nt], in0=pu.t[:, 0:nt], scalar=cw.t[:, 0, fi:fi + 1],
                                                                                      in1=yt.t[:, 0:nt], op0=ALU.mult, op1=ALU.add),
                         reads=[pu.b, cw.b, yt.b], writes=[yt.b])
                    tiles.append(yt)
                s_ = sa[ft % 2]
                P.op("act", lambda e: e.activation(out=s_.t[:, 0:nt], in_=tiles[0].t[:, 0:nt], func=AF.Silu),
                     reads=[tiles[0].b], writes=[s_.b])
                P.op("pool", lambda e: e.tensor_tensor(out=gt.t[:, ft, 0:nt], in0=s_.t[:, 0:nt], in1=tiles[1].t[:, 0:nt], op=ALU.mult),
                     reads=[s_.b, tiles[1].b], writes=[gt.b])

            def down_unit(gi, j, ob, half):
                gt = gTt2[gi % 2]
                h_ = hb[ob % 2]
                o_ = ob_[ob % 2]
                if half == 0:
                    P.dma("sp", lambda e: e.dma_start(out=h_.t[:], in_=H2[ob * 128:(ob + 1) * 128, :]),
                          h_.b, reads=[b_H2[ob]], writes=[h_.b])
                pd = pD[(ob * 2 + half) % 4]
                for ft in range(22):
                    P.op("pe", lambda e, ft=ft: e.matmul(pd.t[:, 0:512], lhsT=gt.t[:, ft, j * 128:(j + 1) * 128],
                                                         rhs=Wdn.t[:, ft, half * 512:(half + 1) * 512],
                                                         start=(ft == 0), stop=(ft == 21)),
                         reads=[gt.b, Wdn.b], writes=[pd.b])
                P.op("dve", lambda e: e.tensor_tensor(out=o_.t[:, half * 512:(half + 1) * 512], in0=pd.t[:, 0:512],
                                                      in1=h_.t[:, half * 512:(half + 1) * 512], op=ALU.add),
                     reads=[pd.b, h_.b], writes=[o_.b])
                if half == 1:
                    P.dma("pool", lambda e: e.dma_start(out=y[ob * 128:(ob + 1) * 128, :], in_=o_.t[:]),
                          o_.b, reads=[o_.b], writes=[b_y])

            pending_down = []
            for gi in range(NG):
                blks = list(range(gi * GB, min(NO, (gi + 1) * GB)))
                nt = len(blks) * 128
                ug = u2g[gi % 2]
                up_ = u2g[(gi + 1) % 2]
                for j, ob in enumerate(blks):
                    P.dma("sp", lambda e, ug=ug, j=j, ob=ob: e.dma_start(out=ug.t[:, :, 2 + j * 128:2 + (j + 1) * 128],
                                                                        in_=U2T[ob].rearrange("p (a b) -> p a b", a=8)),
                          ug.b, reads=[b_U2T[ob]], writes=[ug.b])
                if gi > 0:
                    P.op("pool", lambda e, ug=ug, up_=up_: e.tensor_copy(ug.t[:, :, 0:2], up_.t[:, :, NT:NT + 2]),
                         reads=[up_.b], writes=[ug.b])
                nd = len(pending_down)
                slots = {int(round((k + 1) * 22.0 / (nd + 1))): k for k in range(nd)} if nd else {}
                for ft in range(22):
                    up_pair(gi, ft, ug, nt)
                    if (ft + 1) in slots:
                        down_unit(*pending_down[slots[ft + 1]])
                pending_down = [(gi, j, ob, half) for j, ob in enumerate(blks) for half in range(2)]
            for u in pending_down:
                down_unit(*u)
            P.wait_all("pool", [b_y])
            P.emit()
        print("n_inst", P.n_inst, "n_wait", P.n_wait, "ndsem", P.ndsem)
    return nc


def make_tables(NC, NO, p, S):
    NB = NC + NO
    L = N_META + S
    if p == 0:
        ctx_pos = np.full(NC * 128, -1, np.int64)
        own_pos = np.arange(NO * 128)
    else:
        ctx_pos = np.arange(NC * 128) - PAD
        own_pos = L - NO * 128 + np.arange(NO * 128)
    pos = np.concatenate([ctx_pos, own_pos])
    valid = pos >= 0
    posf = np.where(valid, pos, 0).astype(np.float32)
    inv_r = np.power(np.float32(10000.0), -np.arange(128, dtype=np.float32) / np.float32(128))
    ang = posf[:, None] * inv_r[None, :]
    c, s = np.cos(ang), np.sin(ang)
    rope_r = np.concatenate([c, c, s, s], axis=1).astype(np.float32)
    inv_d = np.power(np.float32(500000.0), -np.arange(8, dtype=np.float32) / np.float32(8))
    ang = posf[:, None] * inv_d[None, :]
    c, s = np.cos(ang), np.sin(ang)
    rope_d = np.concatenate([c, c, s, s], axis=1).astype(np.float32)
    kb = np.where(valid, 1.0, 0.0).astype(np.float32).reshape(NB, 128).T.copy()
    idx = np.arange(128)
    cm = (idx[:, None] <= idx[None, :]).astype(np.float32)
    rdec = np.zeros((128, 8), np.float32)
    for h in range(4):
        rdec[:, h] = GAM[h] ** (idx + 1.0)
        rdec[:, 4 + h] = (256 ** -0.5) * GAM[h] ** (127.0 - idx)
    return rope_r, rope_d, kb, cm, rdec


_NC_CACHE = {}


def run(inputs, NC, NO, debug=False, trace=False):
    x = np.asarray(inputs["x"], np.float32)
    B, S, _ = x.shape
    assert S == 128 * (NC + NO - 1)
    L = N_META + S
    meta = np.asarray(inputs["meta_tokens"], np.float32)
    key = (NC, NO, debug)
    if key not in _NC_CACHE:
        _NC_CACHE[key] = build(NC, NO, debug)
    nc = _NC_CACHE[key]
    f = lambda k: np.ascontiguousarray(np.asarray(inputs[k], np.float32)[0])
    common = {
        "w_in": f("w_in"), "w_ret_o": f("w_ret_o"), "w_diff_o": f("w_diff_o"), "w_out": f("w_out"),
        "w_up": f("w_up"), "w_down": f("w_down"), "norm1_w": f("norm1_w"), "norm2_w": f("norm2_w"),
        "qk_norm_w": np.concatenate([f("q_norm_w"), f("q_norm_w"), f("k_norm_w"), f("k_norm_w")]),
        "lambdas": np.concatenate([f("lambda_q1"), f("lambda_k1"), f("lambda_q2"), f("lambda_k2")]),
        "subln_w": f("diff_subln_w"), "conv_w": f("conv_w"), "conv_b": f("conv_b"),
    }
    tabs = [make_tables(NC, NO, p, S) for p in range(2)]
    in_maps = []
    for b in range(B):
        seq = np.concatenate([meta, x[b]], axis=0)
        for p in range(2):
            if p == 0:
                xc_ = np.zeros((NC * 128, D), np.float32)
                xo_ = seq[0:NO * 128]
            else:
                xc_ = np.concatenate([np.zeros((PAD, D), np.float32), seq[0:NC * 128 - PAD]], axis=0)
                xo_ = seq[L - NO * 128:L]
            rr, rd, kb, cm, rdec = tabs[p]
            m = dict(common)
            m.update({"xc": np.ascontiguousarray(xc_), "xo": np.ascontiguousarray(xo_), "rope_r": rr, "rope_d": rd,
                      "kbias": kb, "cmask": cm, "rdec": rdec})
            in_maps.append(m)
    res = run_bass_kernel_spmd(nc, in_maps, core_ids=list(range(len(in_maps))), trace=trace)
    out = np.empty((B, S, D), np.float32)
    split = (NO * 128 - N_META) - 64
    for b in range(B):
        y0 = res.results[2 * b]["y"]
        y1 = res.results[2 * b + 1]["y"]
        out[b, :split] = y0[N_META:N_META + split]
        off1 = L - NO * 128
        out[b, split:] = y1[N_META + split - off1:]
    return out, res


def kernel(**inputs):
    out, _ = run(inputs, 32, 33)
    return out
```

```python
import math
import numpy as np
from contextlib import ExitStack
import concourse.bass as bass
import concourse.mybir as mybir
from concourse.bass_utils import run_bass_kernel_spmd

F32 = mybir.dt.float32
BF16 = mybir.dt.bfloat16
AF = mybir.ActivationFunctionType
ALU = mybir.AluOpType
AX = mybir.AxisListType

D = 1024
N_META = 16
PAD = 112
FFN = 2816
IN_COLS = 11264
EPS = 1e-6
C_RQ, C_RK, C_RV, C_RG, C_DQ, C_DK, C_DV, C_GT = 0, 1024, 2048, 4096, 6144, 7168, 8192, 9216
NEGB = -30000.0
LAM_INIT = 0.8 - 0.6 * math.exp(-0.3 * 0)
GAM = [1.0 - 2.0 ** (-5.0 - h) for h in range(4)]

SAME_ENGINE_SYNC = True


class Buf:
    __slots__ = ("name", "last_write", "reads", "dsem", "dcount")

    def __init__(self, name=""):
        self.name = name
        self.last_write = None
        self.reads = {}
        self.dsem = None
        self.dcount = 0


class T:
    def __init__(self, t, name):
        self.t = t
        self.b = Buf(name)


class Prog:
    ENGS = ("pe", "act", "dve", "pool", "sp")
    ENGOBJ = {"pe": "tensor", "act": "scalar", "dve": "vector", "pool": "gpsimd", "sp": "sync"}

    def __init__(self, nc, stack):
        self.nc = nc
        self.stack = stack
        self.q = {e: [] for e in self.ENGS}
        self.ecount = {e: 0 for e in self.ENGS}
        self.sems = {}
        for e in self.ENGS:
            self.sems[("e", e)] = stack.enter_context(nc.semaphore("s_" + e))
        self.waited = {e: {} for e in self.ENGS}
        self.ndsem = 0
        self.n_inst = 0
        self.n_wait = 0

    def _dsem(self, buf):
        if buf.dsem is None:
            buf.dsem = ("d", self.ndsem)
            self.sems[buf.dsem] = self.stack.enter_context(self.nc.semaphore("d%d" % self.ndsem))
            self.ndsem += 1
        return buf.dsem

    def _deps(self, eng, reads, writes):
        deps = {}

        def add(t):
            if t is None:
                return
            k, v = t
            if deps.get(k, -1) < v:
                deps[k] = v
        for b in reads:
            add(b.last_write)
        for b in writes:
            add(b.last_write)
            for k, v in b.reads.items():
                add((k, v))
        out = []
        w = self.waited[eng]
        for k, v in deps.items():
            if k == ("e", eng) and (eng == "pe" or not SAME_ENGINE_SYNC):
                continue
            if w.get(k, -1) >= v:
                continue
            w[k] = v
            out.append((k, v))
        return out

    def _commit(self, tok, reads, writes):
        k, v = tok
        for b in writes:
            b.last_write = tok
            b.reads = {}
        for b in reads:
            if b.reads.get(k, -1) < v:
                b.reads[k] = v

    def op(self, eng, fn, reads=(), writes=()):
        waits = self._deps(eng, reads, writes)
        self.ecount[eng] += 1
        tok = (("e", eng), self.ecount[eng])
        self.q[eng].append((waits, fn, tok[0], 1))
        self._commit(tok, reads, writes)
        self.n_inst += 1
        self.n_wait += len(waits)
        return tok

    def dma(self, eng, fn, sb, reads=(), writes=()):
        waits = self._deps(eng, reads, writes)
        k = self._dsem(sb)
        sb.dcount += 16
        tok = (k, sb.dcount)
        self.q[eng].append((waits, fn, k, 16))
        self._commit(tok, reads, writes)
        self.n_inst += 1
        self.n_wait += len(waits)
        return tok

    def wait_all(self, eng, bufs):
        waits = self._deps(eng, bufs, bufs)
        self.q[eng].append((waits, None, None, 0))

    def emit(self):
        nc = self.nc
        sems = self.sems
        with nc.Block() as block:
            for e in self.ENGS:
                lst = self.q[e]

                def body(eo, lst=lst):
                    for waits, fn, sk, inc in lst:
                        for k, v in waits:
                            eo.wait_ge(sems[k], v)
                        if fn is not None:
                            fn(eo).then_inc(sems[sk], inc)
                getattr(block, self.ENGOBJ[e])(body)
        self.q = {e: [] for e in self.ENGS}


def bc_mid(ap, n):
    return bass.AP(ap.tensor, ap.offset, [list(ap.ap[0]), [0, n], list(ap.ap[1])])


def bc_last(ap, k):
    return bass.AP(ap.tensor, ap.offset, [list(ap.ap[0]), list(ap.ap[1]), [0, k]])


def bc_part(dram_ap_1d, n):
    return bass.AP(dram_ap_1d.tensor, dram_ap_1d.offset, [[0, 128], [1, n]])


def build(NC, NO, debug=False):
    NB = NC + NO
    nc = bass.Bass("TRN2", target_bir_lowering=False)

    def din(name, shape, dt=F32):
        return nc.dram_tensor(name, list(shape), dt, kind="ExternalInput").ap()

    okind = "ExternalOutput" if debug else "Internal"

    def dscr(name, shape, dt):
        return nc.dram_tensor(name, list(shape), dt, kind=okind).ap()

    xc = din("xc", [NC * 128, D])
    xo = din("xo", [NO * 128, D])
    w_in = din("w_in", [D, IN_COLS])
    w_ret_o = din("w_ret_o", [2048, D])
    w_diff_o = din("w_diff_o", [D, D])
    w_out = din("w_out", [D, D])
    w_up = din("w_up", [D, 2 * FFN])
    w_down = din("w_down", [FFN, D])
    norm1_w = din("norm1_w", [D])
    norm2_w = din("norm2_w", [D])
    qk_norm_w = din("qk_norm_w", [256])
    lambdas = din("lambdas", [256])
    subln_w = din("subln_w", [128])
    conv_w = din("conv_w", [3, 2 * FFN])
    conv_b = din("conv_b", [2 * FFN])
    rope_r = din("rope_r", [NB * 128, 512])
    rope_d = din("rope_d", [NB * 128, 32])
    kbias_d = din("kbias", [128, NB])
    cmask_d = din("cmask", [128, 128])
    rdec_d = din("rdec", [128, 8])
    y = nc.dram_tensor("y", [NO * 128, D], F32, kind="ExternalOutput").ap()

    UT = dscr("UT", [NB, 128, 1024], BF16)
    GT = dscr("GT", [NO, 128, 2048], BF16)
    DOT = dscr("DOT", [NO, 128, 1024], BF16)
    H2 = dscr("H2", [NO * 128, D], F32)
    U2T = dscr("U2T", [NO, 128, 1024], BF16)
    b_UT = [Buf("UT%d" % i) for i in range(NB)]
    b_GT = [Buf("GT%d" % i) for i in range(NO)]
    b_DOT = [Buf("DOT%d" % i) for i in range(NO)]
    b_H2 = [Buf("H2%d" % i) for i in range(NO)]
    b_U2T = [Buf("U2T%d" % i) for i in range(NO)]
    b_y = Buf("y")

    w_in_v = w_in.rearrange("(k p) n -> p k n", p=128)

    with ExitStack() as gst:
        P = Prog(nc, gst)

        def sbt(st, name, shape, dt):
            return T(st.enter_context(nc.sbuf_tensor("sb_" + name, list(shape), dt)), name)

        def pbank(st, name, dt=F32):
            n = 512 if dt == F32 else 1024
            return T(st.enter_context(nc.psum_tensor("ps_" + name, [128, n], dt)), name)

        ident = sbt(gst, "ident", [128, 128], BF16)
        identf = sbt(gst, "identf", [128, 128], F32)
        cmask = sbt(gst, "cmask", [128, 128], F32)
        cmask2 = sbt(gst, "cmask2", [128, 2, 128], BF16)
        kbias = sbt(gst, "kbias", [128, NB], F32)
        rdec = sbt(gst, "rdec", [128, 8], F32)
        lam = sbt(gst, "lam", [128, 4], F32)
        lamv = sbt(gst, "lamv", [128, 256], F32)
        lamt = sbt(gst, "lamt", [128, 128], F32)
        lams = sbt(gst, "lams", [128, 2], F32)
        sublnw = sbt(gst, "sublnw", [128, 128], F32)
        wqk = sbt(gst, "wqk", [128, 4, 64], F32)

        P.op("pool", lambda e: e.iota(identf.t[:], pattern=[[1, 128]], base=0, channel_multiplier=-1,
                                      allow_small_or_imprecise_dtypes=True), writes=[identf.b])
        P.op("dve", lambda e: e.tensor_scalar(ident.t[:], identf.t[:], 0.0, None, op0=ALU.is_equal),
             reads=[identf.b], writes=[ident.b])
        P.dma("sp", lambda e: e.dma_start(out=cmask.t[:], in_=cmask_d), cmask.b, writes=[cmask.b])
        P.dma("sp", lambda e: e.dma_start(out=kbias.t[:], in_=kbias_d), kbias.b, writes=[kbias.b])
        P.dma("sp", lambda e: e.dma_start(out=rdec.t[:], in_=rdec_d), rdec.b, writes=[rdec.b])
        P.dma("sp", lambda e: e.dma_start(out=lamv.t[:], in_=bc_part(lambdas, 256)), lamv.b, writes=[lamv.b])
        P.dma("sp", lambda e: e.dma_start(out=sublnw.t[:], in_=bc_part(subln_w, 128)), sublnw.b, writes=[sublnw.b])
        P.dma("sp", lambda e: e.dma_start(out=wqk.t[:].rearrange("p a b -> p (a b)"), in_=bc_part(qk_norm_w, 256)),
              wqk.b, writes=[wqk.b])
        P.op("dve", lambda e: e.tensor_copy(cmask2.t[:, 0, :], cmask.t[:]), reads=[cmask.b], writes=[cmask2.b])
        P.op("dve", lambda e: e.tensor_copy(cmask2.t[:, 1, :], cmask.t[:]), reads=[cmask.b], writes=[cmask2.b])
        P.op("dve", lambda e: e.tensor_tensor(out=lamt.t[:, 0:64], in0=lamv.t[:, 0:64], in1=lamv.t[:, 64:128], op=ALU.mult),
             reads=[lamv.b], writes=[lamt.b])
        P.op("dve", lambda e: e.tensor_tensor(out=lamt.t[:, 64:128], in0=lamv.t[:, 128:192], in1=lamv.t[:, 192:256], op=ALU.mult),
             reads=[lamv.b], writes=[lamt.b])
        P.op("dve", lambda e: e.tensor_reduce(out=lams.t[:, 0:2], in_=lamt.t[:].rearrange("p (a b) -> p a b", a=2),
                                              axis=AX.X, op=ALU.add), reads=[lamt.b], writes=[lams.b])
        P.op("act", lambda e: e.activation(out=lams.t[:], in_=lams.t[:], func=AF.Exp), reads=[lams.b], writes=[lams.b])
        P.op("dve", lambda e: e.tensor_tensor(out=lam.t[:, 0:1], in0=lams.t[:, 0:1], in1=lams.t[:, 1:2], op=ALU.subtract),
             reads=[lams.b], writes=[lam.b])
        P.op("dve", lambda e: e.tensor_scalar(lam.t[:, 0:1], lam.t[:, 0:1], LAM_INIT, None, op0=ALU.add),
             reads=[lam.b], writes=[lam.b])
        P.op("dve", lambda e: e.tensor_scalar(sublnw.t[:], sublnw.t[:], 1.0 - LAM_INIT, None, op0=ALU.mult),
             reads=[sublnw.b], writes=[sublnw.b])

        def norm_part(xt, nw, sq, ss, rs, ub, use_pow=False):
            P.op("act", lambda e: e.activation(out=sq.t[:], in_=xt.t[:], func=AF.Square, accum_out=ss.t[:, 0:1]),
                 reads=[xt.b], writes=[sq.b, ss.b])
            if use_pow:
                P.op("dve", lambda e: e.tensor_scalar(rs.t[:, 0:1], ss.t[:, 0:1], 1.0 / D, EPS, op0=ALU.mult, op1=ALU.add),
                     reads=[ss.b], writes=[rs.b])
                P.op("pool", lambda e: e.tensor_tensor(out=rs.t[:, 0:1], in0=rs.t[:, 0:1], in1=mhalf.t[:, 0:1], op=ALU.pow),
                     reads=[rs.b, mhalf.b], writes=[rs.b])
            else:
                P.op("act", lambda e: e.activation(out=rs.t[:, 0:1], in_=ss.t[:, 0:1], func=AF.Sqrt, scale=1.0 / D, bias=eps_t.t[:, 0:1]),
                     reads=[ss.b, eps_t.b], writes=[rs.b])
                P.op("dve", lambda e: e.reciprocal(rs.t[:, 0:1], rs.t[:, 0:1]), reads=[rs.b], writes=[rs.b])
            P.op("dve", lambda e: e.scalar_tensor_tensor(out=ub.t[:], in0=xt.t[:], scalar=rs.t[:, 0:1], in1=nw.t[:],
                                                         op0=ALU.mult, op1=ALU.mult),
                 reads=[xt.b, rs.b, nw.b], writes=[ub.b])

        def tr_part(ub, pT, uT):
            for half in range(2):
                pt = pT[half]
                for j in range(4):
                    kc = half * 4 + j
                    P.op("pe", lambda e, kc=kc, j=j, pt=pt: e.transpose(pt.t[:, j * 128:(j + 1) * 128],
                                                                        ub.t[:, kc * 128:(kc + 1) * 128], ident.t[:]),
                         reads=[ub.b, ident.b], writes=[pt.b])
                if half == 0:
                    P.op("dve", lambda e, pt=pt: e.tensor_copy(uT.t[:, 0:4, :].rearrange("p a b -> p (a b)"), pt.t[:, 0:512]),
                         reads=[pt.b], writes=[uT.b])
                else:
                    P.op("act", lambda e, pt=pt: e.copy(uT.t[:, 4:8, :].rearrange("p a b -> p (a b)"), pt.t[:, 0:512]),
                         reads=[pt.b], writes=[uT.b])

        def norm_transpose(xt, nw, sq, ss, rs, ub, pT, uT):
            norm_part(xt, nw, sq, ss, rs, ub)
            tr_part(ub, pT, uT)

        eps_t = sbt(gst, "eps_t", [128, 1], F32)
        P.op("pool", lambda e: e.memset(eps_t.t[:], EPS), writes=[eps_t.b])
        mhalf = sbt(gst, "mhalf", [128, 8], F32)
        P.op("pool", lambda e: e.memset(mhalf.t[:], -0.5), writes=[mhalf.b])

        with ExitStack() as st:
            n1w = sbt(st, "n1w", [128, D], F32)
            P.dma("sp", lambda e: e.dma_start(out=n1w.t[:], in_=bc_part(norm1_w, D)), n1w.b, writes=[n1w.b])
            xb = [sbt(st, "x%d" % i, [128, D], F32) for i in range(3)]
            sq = [sbt(st, "sq%d" % i, [128, D], F32) for i in range(2)]
            ss = [sbt(st, "ss%d" % i, [128, 1], F32) for i in range(2)]
            rs = [sbt(st, "rs%d" % i, [128, 1], F32) for i in range(2)]
            ub = [sbt(st, "ub%d" % i, [128, D], BF16) for i in range(2)]
            uT = [sbt(st, "uT%d" % i, [128, 8, 128], BF16) for i in range(2)]
            pT = [pbank(st, "pT%d" % i, BF16) for i in range(4)]
            def p0_x(blk):
                src = xc[blk * 128:(blk + 1) * 128, :] if blk < NC else xo[(blk - NC) * 128:(blk - NC + 1) * 128, :]
                x_ = xb[blk % 3]
                P.dma("sp", lambda e: e.dma_start(out=x_.t[:], in_=src), x_.b, writes=[x_.b])
                norm_part(x_, n1w, sq[blk % 2], ss[blk % 2], rs[blk % 2], ub[blk % 2])

            def p0_y(blk):
                u_ = uT[blk % 2]
                tr_part(ub[blk % 2], pT[(blk % 2) * 2:(blk % 2) * 2 + 2], u_)
                P.dma("pool", lambda e: e.dma_start(out=UT[blk], in_=u_.t[:].rearrange("p a b -> p (a b)")),
                      u_.b, reads=[u_.b], writes=[b_UT[blk]])
            p0_x(0)
            for blk in range(NB):
                if blk + 1 < NB:
                    p0_x(blk + 1)
                p0_y(blk)
            P.emit()

        with ExitStack() as st_pre1:
            WDa = sbt(st_pre1, "WDa", [128, 8, 3072], BF16)
            with ExitStack() as st:
                WR = [sbt(st, "WR%d" % i, [128, 8, 1536], BF16) for i in range(2)]
                uTb = [sbt(st, "ruT%d" % i, [128, 8, 128], BF16) for i in range(3)]
                RT = [sbt(st, "RT%d" % i, [128, 512], F32) for i in range(3)]
                Rf = sbt(st, "Rf", [128, 2, 512], F32)
                Rb = sbt(st, "Rb", [128, 2, 512], BF16)
                Aq = sbt(st, "Aq", [128, 256], F32)
                Bq = sbt(st, "Bq", [128, 256], F32)
                Ak = sbt(st, "Ak", [128, 256], F32)
                Bk = sbt(st, "Bk", [128, 256], F32)
                qr = [sbt(st, "qr%d" % i, [128, 256], BF16) for i in range(2)]
                kr = [sbt(st, "kr%d" % i, [128, 256], BF16) for i in range(2)]
                vb = [sbt(st, "vb%d" % i, [128, 512], BF16) for i in range(2)]
                sg = [sbt(st, "sg%d" % i, [128, 512], F32) for i in range(2)]
                qkT = sbt(st, "qkT", [128, 4, 128], BF16)
                Sm = sbt(st, "Sm", [128, 128], BF16)
                bst = sbt(st, "bst", [128, 6], F32)
                mv = sbt(st, "mv", [128, 2], F32)
                grs = sbt(st, "grs", [128, 1], F32)
                on = sbt(st, "on", [128, 512], F32)
                gtd = sbt(st, "gtd", [128, 512], BF16)
                gT = [sbt(st, "gT%d" % i, [128, 4, 128], BF16) for i in range(2)]
                pQK = pbank(st, "pQK")
                pV = pbank(st, "pV")
                pG = pbank(st, "pG")
                pTq = pbank(st, "pTq", BF16)
                pSv = pTq.t[:, 512:768].bitcast(F32)
                pTg = pbank(st, "pTg", BF16)
                pO2 = [pbank(st, "pO%d" % i) for i in range(2)]
                pR1 = pbank(st, "pR1")
                Rb2 = [sbt(st, "Rb%d" % i, [128, 2, 512], BF16) for i in range(2)]
                bst2 = [sbt(st, "bst%d" % i, [128, 6], F32) for i in range(2)]
                mv2 = [sbt(st, "mv%d" % i, [128, 2], F32) for i in range(2)]
                grs2 = [sbt(st, "grs%d" % i, [128, 1], F32) for i in range(2)]
                on2 = [sbt(st, "on%d" % i, [128, 512], F32) for i in range(2)]
                gtd2 = [sbt(st, "gtd%d" % i, [128, 512], BF16) for i in range(2)]

                def load_WR(h):
                    w = WR[h % 2]
                    for (c0, n, o0) in ((C_RQ + h * 256, 256, 0), (C_RK + h * 256, 256, 256),
                                        (C_RV + h * 512, 512, 512), (C_RG + h * 512, 512, 1024)):
                        P.dma("pool", lambda e, w=w, c0=c0, n=n, o0=o0: e.dma_start(out=w.t[:, :, o0:o0 + n],
                                                                                       in_=w_in_v[:, :, c0:c0 + n]),
                              w.b, writes=[w.b])
                load_WR(0)
                for k0 in range(0, 8, 2):
                    P.dma("pool", lambda e, k0=k0: e.dma_start(out=WDa.t[:, k0:k0 + 2, :], in_=w_in_v[:, k0:k0 + 2, C_DQ:C_DQ + 3072]),
                          WDa.b, writes=[WDa.b])
                for h in range(4):
                    if h + 1 < 4:
                        load_WR(h + 1)
                    w = WR[h % 2]
                    g = GAM[h]
                    P.op("dve", lambda e: e.memset(Rf.t[:], 0.0), writes=[Rf.b])
                    P.op("dve", lambda e: e.memset(Rb2[0].t[:], 0.0), writes=[Rb2[0].b])
                    P.op("dve", lambda e: e.memset(Rb2[1].t[:], 0.0), writes=[Rb2[1].b])

                    def A1(blk):
                        own = blk >= NC
                        u_ = uTb[blk % 3]
                        rt = RT[blk % 3]
                        k_ = kr[blk % 2]
                        q_ = qr[blk % 2]
                        P.dma("sp", lambda e: e.dma_start(out=u_.t[:].rearrange("p a b -> p (a b)"), in_=UT[blk]),
                              u_.b, reads=[b_UT[blk]], writes=[u_.b])
                        P.dma("sp", lambda e: e.dma_start(out=rt.t[:], in_=rope_r[blk * 128:(blk + 1) * 128, :]),
                              rt.b, writes=[rt.b])
                        c0 = 0 if own else 256
                        for kc in range(8):
                            P.op("pe", lambda e, kc=kc: e.matmul(pQK.t[:, c0:512], lhsT=u_.t[:, kc, :], rhs=w.t[:, kc, c0:512],
                                                                 start=(kc == 0), stop=(kc == 7)),
                                 reads=[u_.b, w.b], writes=[pQK.b])
                        P.op("dve", lambda e: e.scalar_tensor_tensor(out=Ak.t[:], in0=pQK.t[:, 256:512], scalar=rdec.t[:, 4 + h:5 + h],
                                                                     in1=rt.t[:, 0:256], op0=ALU.mult, op1=ALU.mult),
                             reads=[pQK.b, rdec.b, rt.b], writes=[Ak.b])
                        P.op("dve", lambda e: e.scalar_tensor_tensor(out=Bk.t[:], in0=pQK.t[:, 256:512], scalar=rdec.t[:, 4 + h:5 + h],
                                                                     in1=rt.t[:, 256:512], op0=ALU.mult, op1=ALU.mult),
                             reads=[pQK.b, rdec.b, rt.b], writes=[Bk.b])
                        if own:
                            P.op("dve", lambda e: e.scalar_tensor_tensor(out=Aq.t[:], in0=pQK.t[:, 0:256], scalar=rdec.t[:, h:h + 1],
                                                                         in1=rt.t[:, 0:256], op0=ALU.mult, op1=ALU.mult),
                                 reads=[pQK.b, rdec.b, rt.b], writes=[Aq.b])
                            P.op("dve", lambda e: e.scalar_tensor_tensor(out=Bq.t[:], in0=pQK.t[:, 0:256], scalar=rdec.t[:, h:h + 1],
                                                                         in1=rt.t[:, 256:512], op0=ALU.mult, op1=ALU.mult),
                                 reads=[pQK.b, rdec.b, rt.b], writes=[Bq.b])
                        P.op("pool", lambda e: e.tensor_tensor(out=k_.t[:, 0:128], in0=Ak.t[:, 0:128], in1=Bk.t[:, 128:256], op=ALU.subtract),
                             reads=[Ak.b, Bk.b], writes=[k_.b])
                        P.op("pool", lambda e: e.tensor_tensor(out=k_.t[:, 128:256], in0=Ak.t[:, 128:256], in1=Bk.t[:, 0:128], op=ALU.add),
                             reads=[Ak.b, Bk.b], writes=[k_.b])
                        if own:
                            P.op("pool", lambda e: e.tensor_tensor(out=q_.t[:, 0:128], in0=Aq.t[:, 0:128], in1=Bq.t[:, 128:256], op=ALU.subtract),
                                 reads=[Aq.b, Bq.b], writes=[q_.b])
                            P.op("pool", lambda e: e.tensor_tensor(out=q_.t[:, 128:256], in0=Aq.t[:, 128:256], in1=Bq.t[:, 0:128], op=ALU.add),
                                 reads=[Aq.b, Bq.b], writes=[q_.b])

                    def A2(blk):
                        u_ = uTb[blk % 3]
                        v_ = vb[blk % 2]
                        for kc in range(8):
                            P.op("pe", lambda e, kc=kc: e.matmul(pV.t[:, 0:512], lhsT=u_.t[:, kc, :], rhs=w.t[:, kc, 512:1024],
                                                                 start=(kc == 0), stop=(kc == 7)),
                                 reads=[u_.b, w.b], writes=[pV.b])
                        P.op("act", lambda e: e.copy(v_.t[:], pV.t[:, 0:512]), reads=[pV.b], writes=[v_.b])

                    def A3(blk):
                        if blk < NC:
                            return
                        u_ = uTb[blk % 3]
                        s_ = sg[blk % 2]
                        for kc in range(8):
                            P.op("pe", lambda e, kc=kc: e.matmul(pG.t[:, 0:512], lhsT=u_.t[:, kc, :], rhs=w.t[:, kc, 1024:1536],
                                                                 start=(kc == 0), stop=(kc == 7)),
                                 reads=[u_.b, w.b], writes=[pG.b])
                        P.op("act", lambda e: e.activation(out=s_.t[:], in_=pG.t[:, 0:512], func=AF.Silu), reads=[pG.b], writes=[s_.b])

                    def B1(blk):
                        if blk < NC:
                            return
                        k_ = kr[blk % 2]
                        q_ = qr[blk % 2]
                        for j in range(4):
                            srcT = q_ if j < 2 else k_
                            c = j % 2
                            P.op("pe", lambda e, j=j, c=c, srcT=srcT: e.transpose(pTq.t[:, j * 128:(j + 1) * 128],
                                                                                  srcT.t[:, c * 128:(c + 1) * 128], ident.t[:]),
                                 reads=[srcT.b, ident.b], writes=[pTq.b])
                        P.op("act", lambda e: e.copy(qkT.t[:].rearrange("p a b -> p (a b)"), pTq.t[:, 0:512]),
                             reads=[pTq.b], writes=[qkT.b])

                    def B2(blk):
                        if blk < NC:
                            return
                        for c in range(2):
                            P.op("pe", lambda e, c=c: e.matmul(pSv, lhsT=qkT.t[:, 2 + c, :], rhs=qkT.t[:, c, :],
                                                               start=(c == 0), stop=(c == 1)),
                                 reads=[qkT.b], writes=[pTq.b])
                        P.op("dve", lambda e: e.scalar_tensor_tensor(out=Sm.t[:], in0=pSv, scalar=float(g ** -128.0),
                                                                     in1=cmask.t[:], op0=ALU.mult, op1=ALU.mult),
                             reads=[pTq.b, cmask.b], writes=[Sm.b])

                    def B3a(blk):
                        if blk < NC:
                            return
                        v_ = vb[blk % 2]
                        po = pO2[blk % 2]
                        rb = Rb2[(blk - 1) % 2]
                        P.op("pe", lambda e: e.matmul(po.t[:, 0:512], lhsT=Sm.t[:], rhs=v_.t[:], start=True, stop=False),
                             reads=[Sm.b, v_.b], writes=[po.b])
                        for c in range(2):
                            P.op("pe", lambda e, c=c: e.matmul(po.t[:, 0:512], lhsT=qkT.t[:, c, :], rhs=rb.t[:, c, :],
                                                               start=False, stop=(c == 1)),
                                 reads=[qkT.b, rb.b], writes=[po.b])

                    def CH(blk):
                        if blk < NC:
                            return
                        par = blk % 2
                        po = pO2[par]
                        s_ = sg[par]
                        bst_, mv_, grs_, on_, gtd_ = bst2[par], mv2[par], grs2[par], on2[par], gtd2[par]
                        P.op("dve", lambda e: e.bn_stats(bst_.t[:], po.t[:, 0:512]), reads=[po.b], writes=[bst_.b])
                        P.op("dve", lambda e: e.bn_aggr(mv_.t[:], bst_.t[:]), reads=[bst_.b], writes=[mv_.b])
                        P.op("dve", lambda e: e.tensor_scalar(grs_.t[:], mv_.t[:, 1:2], EPS, None, op0=ALU.add),
                             reads=[mv_.b], writes=[grs_.b])
                        P.op("pool", lambda e: e.tensor_tensor(out=grs_.t[:], in0=grs_.t[:], in1=mhalf.t[:, 0:1], op=ALU.pow),
                             reads=[grs_.b, mhalf.b], writes=[grs_.b])
                        P.op("dve", lambda e: e.tensor_scalar(on_.t[:], po.t[:, 0:512], mv_.t[:, 0:1], grs_.t[:, 0:1],
                                                              op0=ALU.subtract, op1=ALU.mult),
                             reads=[po.b, mv_.b, grs_.b], writes=[on_.b])
                        P.op("pool", lambda e: e.tensor_tensor(out=gtd_.t[:], in0=on_.t[:], in1=s_.t[:], op=ALU.mult),
                             reads=[on_.b, s_.b], writes=[gtd_.b])

                    def ST(blk, c):
                        if blk >= NB - 1:
                            return
                        k_ = kr[blk % 2]
                        v_ = vb[blk % 2]
                        P.op("pe", lambda e: e.matmul(pR1.t[:, 0:512], lhsT=k_.t[:, c * 128:(c + 1) * 128], rhs=v_.t[:],
                                                      start=True, stop=True),
                             reads=[k_.b, v_.b], writes=[pR1.b])
                        P.op("dve", lambda e: e.scalar_tensor_tensor(out=Rf.t[:, c, :], in0=Rf.t[:, c, :], scalar=float(g ** 128.0),
                                                                     in1=pR1.t[:, 0:512], op0=ALU.mult, op1=ALU.add),
                             reads=[Rf.b, pR1.b], writes=[Rf.b])
                        if c == 1:
                            rb = Rb2[blk % 2]
                            P.op("act", lambda e: e.copy(rb.t[:].rearrange("p a b -> p (a b)"), Rf.t[:].rearrange("p a b -> p (a b)")),
                                 reads=[Rf.b], writes=[rb.b])

                    def G(blk):
                        if blk < NC or blk >= NB:
                            return
                        ob = blk - NC
                        g_ = gT[ob % 2]
                        gtd_ = gtd2[blk % 2]
                        for j in range(4):
                            P.op("pe", lambda e, j=j: e.transpose(pTg.t[:, j * 128:(j + 1) * 128],
                                                                  gtd_.t[:, j * 128:(j + 1) * 128], ident.t[:]),
                                 reads=[gtd_.b, ident.b], writes=[pTg.b])
                        P.op("act", lambda e: e.copy(g_.t[:].rearrange("p a b -> p (a b)"), pTg.t[:, 0:512]),
                             reads=[pTg.b], writes=[g_.b])
                        P.dma("pool", lambda e: e.dma_start(out=GT[ob][:, h * 512:(h + 1) * 512],
                                                            in_=g_.t[:].rearrange("p a b -> p (a b)")),
                              g_.b, reads=[g_.b], writes=[b_GT[ob]])

                    A1(0); A2(0); A3(0)
                    for blk in range(NB):
                        nx = blk + 1
                        B1(blk)
                        if nx < NB:
                            A1(nx)
                        B2(blk)
                        if blk < NC:
                            ST(blk, 0)
                            if nx < NB:
                                A2(nx)
                            ST(blk, 1)
                            if nx < NB:
                                A3(nx)
                            continue
                        if nx < NB:
                            A2(nx)
                        B3a(blk)
                        G(blk - 1)
                        ST(blk, 0)
                        if nx < NB:
                            A3(nx)
                        ST(blk, 1)
                        CH(blk)
                    G(NB - 1)
                    P.emit()

            KTs = dscr("KTs", [8, 128, NB * 128], BF16)
            QTs = dscr("QTs", [8, 128, NO * 128], BF16)
            VVs = dscr("VVs", [8, 128, NB, 128], BF16)
            b_KTs = Buf("KTs"); b_QTs = Buf("QTs"); b_VVs = Buf("VVs")
            KTs_w = KTs.rearrange("h p (b t) -> p h b t", t=128)
            QTs_w = QTs.rearrange("h p (b t) -> p h b t", t=128)
            VVs_w = VVs.rearrange("h p b e -> p h b e")
            with ExitStack() as st:
                ropd = sbt(st, "ropd", [128, NB, 32], F32)
                ropd_v = rope_d.rearrange("(b p) c -> p b c", p=128)
                for b0 in range(0, NB, 16):
                    b1 = min(NB, b0 + 16)
                    P.dma("sp", lambda e, b0=b0, b1=b1: e.dma_start(out=ropd.t[:, b0:b1, :], in_=ropd_v[:, b0:b1, :]),
                          ropd.b, writes=[ropd.b])
                wq8 = sbt(st, "wq8", [128, 8, 64], F32)
                wk8 = sbt(st, "wk8", [128, 8, 64], F32)
                for g8 in range(8):
                    P.op("pool", lambda e, g8=g8: e.tensor_copy(wq8.t[:, g8, :], wqk.t[:, 0, :]), reads=[wqk.b], writes=[wq8.b])
                    P.op("pool", lambda e, g8=g8: e.tensor_copy(wk8.t[:, g8, :], wqk.t[:, 2, :]), reads=[wqk.b], writes=[wk8.b])
                uTb = [sbt(st, "duT%d" % i, [128, 8, 128], BF16) for i in range(3)]
                NCH = 4
                sqd = [sbt(st, "sqd%d" % i, [128, 8, 64], F32) for i in range(NCH)]
                ssd = [sbt(st, "ssd%d" % i, [128, 8], F32) for i in range(NCH)]
                rsd = [sbt(st, "rsd%d" % i, [128, 8], F32) for i in range(NCH)]
                xn = [[sbt(st, "xn%d_%d" % (pp, i), [128, 8, 64], F32) for i in range(NCH)] for pp in range(2)]
                xbq = [[sbt(st, "xbq%d_%d" % (pp, i), [128, 8, 64], BF16) for i in range(NCH)] for pp in range(2)]
                rc = [[sbt(st, "rc%d_%d" % (pp, i), [128, 8, 16], F32) for i in range(NCH)] for pp in range(2)]
                xw = [[sbt(st, "xw%d_%d" % (pp, i), [128, 8, 16], F32) for i in range(NCH)] for pp in range(2)]
                rsn = [[sbt(st, "rsn%d_%d" % (pp, i), [128, 8, 16], F32) for i in range(NCH)] for pp in range(2)]
                kst = [sbt(st, "kst%d" % i, [128, 8, 128], BF16) for i in range(2)]
                qst = [sbt(st, "qst%d" % i, [128, 8, 128], BF16) for i in range(2)]
                vst = [sbt(st, "vst%d" % i, [128, 8, 128], BF16) for i in range(2)]
                pq = [pbank(st, "pq%d" % i) for i in range(2)]
                pk = [pbank(st, "pk%d" % i) for i in range(2)]
                pvv = [pbank(st, "pvv%d" % i) for i in range(2)]
                pTk = pbank(st, "pTk", BF16)
                pTq = pbank(st, "pTq1", BF16)
                def mk_chains(blk):
                    chains = []
                    for half in range(2):
                        chains.append((pk[half], wk8, pTk, half, 1024 + half * 512))
                    if blk >= NC:
                        for half in range(2):
                            chains.append((pq[half], wq8, pTq, half, half * 512))
                    return chains

                def d1_early(blk):
                    u_ = uTb[blk % 3]
                    par = blk % 2
                    P.dma("sp", lambda e: e.dma_start(out=u_.t[:].rearrange("p a b -> p (a b)"), in_=UT[blk]),
                          u_.b, reads=[b_UT[blk]], writes=[u_.b])
                    chains = mk_chains(blk)
                    for (pb, wt, ptT, half, c0) in chains:
                        for kc in range(8):
                            P.op("pe", lambda e, kc=kc, pb=pb, c0=c0: e.matmul(pb.t[:, 0:512], lhsT=u_.t[:, kc, :], rhs=WDa.t[:, kc, c0:c0 + 512],
                                                                               start=(kc == 0), stop=(kc == 7)),
                                 reads=[u_.b, WDa.b], writes=[pb.b])
                    for half in range(2):
                        for kc in range(8):
                            P.op("pe", lambda e, kc=kc, half=half: e.matmul(pvv[half].t[:, 0:512], lhsT=u_.t[:, kc, :],
                                                                           rhs=WDa.t[:, kc, 2048 + half * 512:2048 + (half + 1) * 512],
                                                                           start=(kc == 0), stop=(kc == 7)),
                                 reads=[u_.b, WDa.b], writes=[pvv[half].b])
                    nch = len(chains)
                    pvw = [ch[0].t[:, 0:512].rearrange("p (a b) -> p a b", b=64) for ch in chains]
                    for ci in range(nch):
                        P.op("act", lambda e, ci=ci: e.activation(out=sqd[ci].t[:], in_=pvw[ci], func=AF.Square),
                             reads=[chains[ci][0].b], writes=[sqd[ci].b])
                    for ci in range(nch):
                        P.op("dve", lambda e, ci=ci: e.tensor_reduce(out=ssd[ci].t[:], in_=sqd[ci].t[:], axis=AX.X, op=ALU.add),
                             reads=[sqd[ci].b], writes=[ssd[ci].b])
                    for ci in range(nch):
                        P.op("act", lambda e, ci=ci: e.activation(out=rsd[ci].t[:], in_=ssd[ci].t[:], func=AF.Sqrt, scale=1.0 / 64, bias=eps_t.t[:, 0:1]),
                             reads=[ssd[ci].b, eps_t.b], writes=[rsd[ci].b])
                    v_ = vst[par]
                    for half in range(2):
                        P.op("act", lambda e, half=half: e.copy(v_.t[:, half * 4:half * 4 + 4, :].rearrange("p a b -> p (a b)"), pvv[half].t[:, 0:512]),
                             reads=[pvv[half].b], writes=[v_.b])
                    P.dma("act", lambda e: e.dma_start(out=VVs_w[:, :, blk, :], in_=v_.t[:]), v_.b, reads=[v_.b], writes=[b_VVs])
                    for ci in range(nch):
                        P.op("dve", lambda e, ci=ci: e.reciprocal(rsd[ci].t[:], rsd[ci].t[:]), reads=[rsd[ci].b], writes=[rsd[ci].b])
                    for ci in range(nch):
                        P.op("dve", lambda e, ci=ci: e.tensor_tensor(out=xn[par][ci].t[:], in0=pvw[ci], in1=bc_last(rsd[ci].t[:], 64), op=ALU.mult),
                             reads=[chains[ci][0].b, rsd[ci].b], writes=[xn[par][ci].b])
                    for ci in range(nch):
                        P.op("dve", lambda e, ci=ci: e.tensor_tensor(out=xbq[par][ci].t[:], in0=xn[par][ci].t[:], in1=chains[ci][1].t[:], op=ALU.mult),
                             reads=[xn[par][ci].b, chains[ci][1].b], writes=[xbq[par][ci].b])

                def d1_late(blk):
                    par = blk % 2
                    own = blk >= NC
                    ob = blk - NC
                    chains = mk_chains(blk)
                    nch = len(chains)
                    xn_, xb_, rc_, rsn_, xw_ = xn[par], xbq[par], rc[par], rsn[par], xw[par]
                    for ci in range(nch):
                        eng = "pool"
                        wt = chains[ci][1]
                        P.op(eng, lambda e, ci=ci, wt=wt: e.tensor_tensor(out=xw_[ci].t[:], in0=xn_[ci].t[:, :, 0:16], in1=wt.t[:, :, 0:16], op=ALU.mult),
                             reads=[xn_[ci].b, wt.b], writes=[xw_[ci].b])
                        P.op(eng, lambda e, ci=ci: e.tensor_tensor(out=rc_[ci].t[:], in0=xw_[ci].t[:],
                                                                  in1=bc_mid(ropd.t[:, blk, 0:16], 8), op=ALU.mult),
                             reads=[xw_[ci].b, ropd.b], writes=[rc_[ci].b])
                        P.op(eng, lambda e, ci=ci: e.tensor_tensor(out=rsn_[ci].t[:], in0=xw_[ci].t[:],
                                                                  in1=bc_mid(ropd.t[:, blk, 16:32], 8), op=ALU.mult),
                             reads=[xw_[ci].b, ropd.b], writes=[rsn_[ci].b])
                        P.op(eng, lambda e, ci=ci: e.tensor_tensor(out=xb_[ci].t[:, :, 0:8], in0=rc_[ci].t[:, :, 0:8], in1=rsn_[ci].t[:, :, 8:16], op=ALU.subtract),
                             reads=[rc_[ci].b, rsn_[ci].b], writes=[xb_[ci].b])
                        P.op(eng, lambda e, ci=ci: e.tensor_tensor(out=xb_[ci].t[:, :, 8:16], in0=rc_[ci].t[:, :, 8:16], in1=rsn_[ci].t[:, :, 0:8], op=ALU.add),
                             reads=[rc_[ci].b, rsn_[ci].b], writes=[xb_[ci].b])
                    for ci in range(nch):
                        ptT, half = chains[ci][2], chains[ci][3]
                        for hh in range(4):
                            P.op("pe", lambda e, ci=ci, hh=hh, ptT=ptT, half=half: e.transpose(ptT.t[:, (half * 4 + hh) * 128:(half * 4 + hh + 1) * 128],
                                                                                              xb_[ci].t[:, 2 * hh:2 * hh + 2, :].rearrange("p a b -> p (a b)"),
                                                                                              ident.t[:]),
                                 reads=[xb_[ci].b, ident.b], writes=[ptT.b])
                    k_ = kst[par]
                    P.op("dve", lambda e: e.tensor_copy(k_.t[:].rearrange("p a b -> p (a b)"), pTk.t[:, 0:1024]), reads=[pTk.b], writes=[k_.b])
                    P.dma("act", lambda e: e.dma_start(out=KTs_w[:, :, blk, :], in_=k_.t[:]), k_.b, reads=[k_.b], writes=[b_KTs])
                    if own:
                        q_ = qst[par]
                        P.op("act", lambda e: e.copy(q_.t[:].rearrange("p a b -> p (a b)"), pTq.t[:, 0:1024]), reads=[pTq.b], writes=[q_.b])
                        P.dma("act", lambda e: e.dma_start(out=QTs_w[:, :, ob, :], in_=q_.t[:]), q_.b, reads=[q_.b], writes=[b_QTs])

                d1_early(0)
                for blk in range(NB):
                    if blk + 1 < NB:
                        d1_early(blk + 1)
                    d1_late(blk)
                P.emit()

        with ExitStack() as st_pre2:
            Wg = sbt(st_pre2, "Wg", [128, 8, 2048], BF16)
            Wro = sbt(st_pre2, "Wro", [128, 16, 1024], BF16)
            with ExitStack() as st:
                KTb = [sbt(st, "KT%d" % i, [128, NB * 128], BF16) for i in range(2)]
                VVb = [sbt(st, "VV%d" % i, [128, NB, 130], BF16) for i in range(2)]
                QT2b = [sbt(st, "QT2%d" % i, [128, NO, 256], BF16) for i in range(2)]
                NPT = 6
                PT = [sbt(st, "PT%d" % i, [128, 4, 128], BF16) for i in range(NPT)]
                zz = sbt(st, "zz", [128, 2], F32)
                a1 = sbt(st, "a1", [128, 128], F32)
                aa = sbt(st, "aa", [128, 128], F32)
                asq = sbt(st, "asq", [128, 128], F32)
                ass = sbt(st, "ass", [128, 1], F32)
                ars = sbt(st, "ars", [128, 1], F32)
                dob = sbt(st, "dob", [128, 128], BF16)
                doT = [sbt(st, "doT%d" % i, [128, 128], BF16) for i in range(2)]
                pP = pbank(st, "pP")
                pTd = pbank(st, "pTd", BF16)
                pSd = [pbank(st, "pSd%d" % i) for i in range(2)]
                pO0 = [pbank(st, "pO0%d" % i) for i in range(2)]
                pO1 = [pbank(st, "pO1%d" % i) for i in range(2)]
                assert NC % 2 == 0
                for i2 in range(2):
                    P.op("pool", lambda e, i2=i2: e.memset(VVb[i2].t[:], 0.0), writes=[VVb[i2].b])
                    P.op("dve", lambda e, i2=i2: e.tensor_copy(VVb[i2].t[:, :, 128:129], kbias.t[:].rearrange("p (a b) -> p a b", b=1)),
                         reads=[kbias.b], writes=[VVb[i2].b])
                    P.op("pool", lambda e, i2=i2: e.memset(QT2b[i2].t[:], 0.0), writes=[QT2b[i2].b])

                def load_head(h):
                    kt, vv, q2 = KTb[h % 2], VVb[h % 2], QT2b[h % 2]
                    P.dma("sp", lambda e: e.dma_start(out=kt.t[:], in_=KTs[h]), kt.b, reads=[b_KTs], writes=[kt.b])
                    P.dma("sp", lambda e: e.dma_start(out=vv.t[:, :, 0:128], in_=VVs[h]), vv.b, reads=[b_VVs], writes=[vv.b])
                    P.dma("sp", lambda e: e.dma_start(out=q2.t[0:64, :, 0:128], in_=QTs[h][0:64, :].rearrange("p (i t) -> p i t", t=128)),
                          q2.b, reads=[b_QTs], writes=[q2.b])
                    P.dma("sp", lambda e: e.dma_start(out=q2.t[64:128, :, 128:256], in_=QTs[h][64:128, :].rearrange("p (i t) -> p i t", t=128)),
                          q2.b, reads=[b_QTs], writes=[q2.b])
                load_head(0)
                wro_pv = w_ret_o.rearrange("(k p) n -> p k n", p=128)
                for k0 in range(0, 8, 2):
                    P.dma("pool", lambda e, k0=k0: e.dma_start(out=Wg.t[:, k0:k0 + 2, :], in_=w_in_v[:, k0:k0 + 2, C_GT:C_GT + 2048]), Wg.b, writes=[Wg.b])
                for k0 in range(0, 16, 4):
                    P.dma("pool", lambda e, k0=k0: e.dma_start(out=Wro.t[:, k0:k0 + 4, :], in_=wro_pv[:, k0:k0 + 4, :]), Wro.b, writes=[Wro.b])
                for h in range(8):
                    if h + 1 < 8:
                        load_head(h + 1)
                    KT, VV, QT2 = KTb[h % 2], VVb[h % 2], QT2b[h % 2]
                    b_K = [KT.b] * NB
                    b_Kv = VV.b
                    b_Q = [QT2.b] * NO
                    items = []
                    for i in range(NO):
                        nk = NC + i + 1
                        for kb0 in range(0, nk, 2):
                            items.append((i, kb0, min(2, nk - kb0)))
                    SKEW = 2
                    pS3 = [pSd[0], pSd[1], pP]

                    def qk_exp(n):
                        i, kb0, nb = items[n]
                        nk = NC + i + 1
                        ps = pS3[n % 3]
                        pt = PT[n % NPT]
                        for j in range(nb):
                            kb = kb0 + j
                            P.op("pe", lambda e, j=j, kb=kb: e.matmul(ps.t[:, j * 256:(j + 1) * 256], lhsT=KT.t[:, kb * 128:(kb + 1) * 128],
                                                                      rhs=QT2.t[:, i, :], start=True, stop=True),
                                 reads=[b_K[kb], b_Q[i]], writes=[ps.b])
                        P.op("act", lambda e: e.activation(out=pt.t[:, 0:2 * nb, :].rearrange("p a b -> p (a b)"),
                                                           in_=ps.t[:, 0:256 * nb], func=AF.Exp, scale=0.125),
                             reads=[ps.b], writes=[pt.b])
                        if kb0 + nb == nk:
                            jl = nb - 1
                            P.op("pool", lambda e: e.tensor_tensor(out=pt.t[:, 2 * jl:2 * jl + 2, :], in0=pt.t[:, 2 * jl:2 * jl + 2, :],
                                                                   in1=cmask2.t[:], op=ALU.mult),
                                 reads=[pt.b, cmask2.b], writes=[pt.b])

                    def pv(n):
                        i, kb0, nb = items[n]
                        nk = NC + i + 1
                        o0 = pO0[i % 2]
                        o1 = pO1[i % 2]
                        pt = PT[n % NPT]
                        for j in range(nb):
                            kb = kb0 + j
                            P.op("pe", lambda e, j=j, kb=kb: e.matmul(o0.t[:, 0:129], lhsT=pt.t[:, 2 * j, :], rhs=VV.t[:, kb, 0:129],
                                                                      start=(kb == 0), stop=(kb == nk - 1)),
                                 reads=[pt.b, b_Kv], writes=[o0.b])
                            P.op("pe", lambda e, j=j, kb=kb: e.matmul(o1.t[:, 0:129], lhsT=pt.t[:, 2 * j + 1, :], rhs=VV.t[:, kb, 0:129],
                                                                      start=(kb == 0), stop=(kb == nk - 1)),
                                 reads=[pt.b, b_Kv], writes=[o1.b])
                        if kb0 + nb == nk:
                            finalize(i, o0, o1)

                    def finalize(i, o0, o1):
                        while pend_fin:
                            pend_fin.pop(0)[1]()
                        P.op("dve", lambda e, o0=o0: e.reciprocal(zz.t[:, 0:1], o0.t[:, 128:129]), reads=[o0.b], writes=[zz.b])
                        P.op("dve", lambda e, o1=o1: e.reciprocal(zz.t[:, 1:2], o1.t[:, 128:129]), reads=[o1.b], writes=[zz.b])
                        P.op("dve", lambda e: e.tensor_tensor(out=zz.t[:, 1:2], in0=zz.t[:, 1:2], in1=lam.t[:, 0:1], op=ALU.mult),
                             reads=[zz.b, lam.b], writes=[zz.b])
                        P.op("dve", lambda e, o1=o1: e.tensor_scalar(a1.t[:], o1.t[:, 0:128], zz.t[:, 1:2], None, op0=ALU.mult),
                             reads=[o1.b, zz.b], writes=[a1.b])
                        P.op("dve", lambda e, o0=o0: e.scalar_tensor_tensor(out=aa.t[:], in0=o0.t[:, 0:128], scalar=zz.t[:, 0:1], in1=a1.t[:],
                                                                            op0=ALU.mult, op1=ALU.subtract),
                             reads=[o0.b, zz.b, a1.b], writes=[aa.b])
                        finalize_b(i)
                        pend_fin.append((n_now[0] + 8, lambda: finalize_c(i)))

                    def finalize_b(i):
                        P.op("dve", lambda e: e.tensor_tensor(out=asq.t[:], in0=aa.t[:], in1=aa.t[:], op=ALU.mult), reads=[aa.b], writes=[asq.b])
                        P.op("dve", lambda e: e.tensor_reduce(out=ass.t[:, 0:1], in_=asq.t[:], axis=AX.X, op=ALU.add), reads=[asq.b], writes=[ass.b])
                        P.op("dve", lambda e: e.tensor_scalar(ars.t[:], ass.t[:], 1.0 / 128, EPS, op0=ALU.mult, op1=ALU.add),
                             reads=[ass.b], writes=[ars.b])
                        P.op("pool", lambda e: e.tensor_tensor(out=ars.t[:], in0=ars.t[:], in1=mhalf.t[:, 0:1], op=ALU.pow),
                             reads=[ars.b, mhalf.b], writes=[ars.b])
                        P.op("dve", lambda e: e.scalar_tensor_tensor(out=dob.t[:], in0=aa.t[:], scalar=ars.t[:, 0:1], in1=sublnw.t[:],
                                                                     op0=ALU.mult, op1=ALU.mult),
                             reads=[aa.b, ars.b, sublnw.b], writes=[dob.b])

                    def finalize_c(i):
                        P.op("pe", lambda e: e.transpose(pTd.t[:, 256:384], dob.t[:], ident.t[:]), reads=[dob.b, ident.b], writes=[pTd.b])
                        d_ = doT[i % 2]
                        P.op("dve", lambda e, d_=d_: e.tensor_copy(d_.t[:], pTd.t[:, 256:384]), reads=[pTd.b], writes=[d_.b])
                        P.dma("pool", lambda e, d_=d_, i=i: e.dma_start(out=DOT[i][:, h * 128:(h + 1) * 128], in_=d_.t[:]),
                              d_.b, reads=[d_.b], writes=[b_DOT[i]])
                    pend_fin = []
                    n_now = [0]
                    for n in range(len(items) + SKEW + 10):
                        n_now[0] = n
                        if n < len(items):
                            qk_exp(n)
                        if 0 <= n - SKEW < len(items):
                            pv(n - SKEW)
                        while pend_fin and pend_fin[0][0] <= n:
                            pend_fin.pop(0)[1]()
                    assert not pend_fin
                    P.emit()

            with ExitStack() as st:
                Wdo = sbt(st, "Wdo", [128, 8, 1024], BF16)
                Wou = sbt(st, "Wou", [128, 8, 1024], BF16)
                n2w = sbt(st, "n2w", [128, D], F32)
                P.dma("sp", lambda e: e.dma_start(out=n2w.t[:], in_=bc_part(norm2_w, D)), n2w.b, writes=[n2w.b])
                wro_v = w_ret_o.rearrange("(k p) n -> p k n", p=128)
                wdo_v = w_diff_o.rearrange("(k p) n -> p k n", p=128)
                wou_v = w_out.rearrange("(k p) n -> p k n", p=128)
                for k0 in range(0, 8, 4):
                    P.dma("pool", lambda e, k0=k0: e.dma_start(out=Wdo.t[:, k0:k0 + 4, :], in_=wdo_v[:, k0:k0 + 4, :]), Wdo.b, writes=[Wdo.b])
                    P.dma("pool", lambda e, k0=k0: e.dma_start(out=Wou.t[:, k0:k0 + 4, :], in_=wou_v[:, k0:k0 + 4, :]), Wou.b, writes=[Wou.b])
                gTb = [sbt(st, "mgT%d" % i, [128, 16, 128], BF16) for i in range(2)]
                dTb = [sbt(st, "mdT%d" % i, [128, 8, 128], BF16) for i in range(2)]
                uTb = [sbt(st, "muT%d" % i, [128, 8, 128], BF16) for i in range(2)]
                xb = [sbt(st, "mx%d" % i, [128, D], F32) for i in range(2)]
                sig = sbt(st, "sig", [128, 2048], F32)
                m1 = sbt(st, "m1", [128, D], F32)
                m2 = sbt(st, "m2", [128, D], F32)
                mb = [sbt(st, "mb%d" % i, [128, D], BF16) for i in range(2)]
                mT = sbt(st, "mT", [128, 8, 128], BF16)
                h2 = [sbt(st, "h2%d" % i, [128, D], F32) for i in range(2)]
                sq = sbt(st, "msq", [128, D], F32)
                ss = sbt(st, "mss", [128, 1], F32)
                rs = sbt(st, "mrs", [128, 1], F32)
                ub = sbt(st, "mub", [128, D], BF16)
                u2T = [sbt(st, "mu2T%d" % i, [128, 8, 128], BF16) for i in range(2)]
                pA = [pbank(st, "pA%d" % i) for i in range(4)]
                pB = [pbank(st, "pB%d" % i) for i in range(2)]
                pTm = [pbank(st, "pTm%d" % i, BF16) for i in range(2)]
                def MA1(ob):
                    blk = NC + ob
                    g_ = gTb[ob % 2]; d_ = dTb[ob % 2]; u_ = uTb[ob % 2]; x_ = xb[ob % 2]
                    P.dma("sp", lambda e: e.dma_start(out=g_.t[:].rearrange("p a b -> p (a b)"), in_=GT[ob]),
                          g_.b, reads=[b_GT[ob]], writes=[g_.b])
                    P.dma("sp", lambda e: e.dma_start(out=d_.t[:].rearrange("p a b -> p (a b)"), in_=DOT[ob]),
                          d_.b, reads=[b_DOT[ob]], writes=[d_.b])
                    P.dma("sp", lambda e: e.dma_start(out=u_.t[:].rearrange("p a b -> p (a b)"), in_=UT[blk]),
                          u_.b, reads=[b_UT[blk]], writes=[u_.b])
                    P.dma("sp", lambda e: e.dma_start(out=x_.t[:], in_=xo[ob * 128:(ob + 1) * 128, :]), x_.b, writes=[x_.b])
                    for j in range(4):
                        for kc in range(8):
                            P.op("pe", lambda e, j=j, kc=kc: e.matmul(pA[j].t[:, 0:512], lhsT=u_.t[:, kc, :], rhs=Wg.t[:, kc, j * 512:(j + 1) * 512],
                                                                      start=(kc == 0), stop=(kc == 7)),
                                 reads=[u_.b, Wg.b], writes=[pA[j].b])
                        P.op("act", lambda e, j=j: e.activation(out=sig.t[:, j * 512:(j + 1) * 512], in_=pA[j].t[:, 0:512], func=AF.Sigmoid),
                             reads=[pA[j].b], writes=[sig.b])

                def MA2(ob):
                    g_ = gTb[ob % 2]
                    for j in range(2):
                        for kc in range(16):
                            P.op("pe", lambda e, j=j, kc=kc: e.matmul(pA[j].t[:, 0:512], lhsT=g_.t[:, kc, :], rhs=Wro.t[:, kc, j * 512:(j + 1) * 512],
                                                                      start=(kc == 0), stop=(kc == 15)),
                                 reads=[g_.b, Wro.b], writes=[pA[j].b])
                        P.op("dve", lambda e, j=j: e.tensor_tensor(out=m1.t[:, j * 512:(j + 1) * 512], in0=pA[j].t[:, 0:512],
                                                                   in1=sig.t[:, j * 512:(j + 1) * 512], op=ALU.mult),
                             reads=[pA[j].b, sig.b], writes=[m1.b])

                def MA3(ob):
                    d_ = dTb[ob % 2]
                    mb_ = mb[ob % 2]
                    for j in range(2):
                        for kc in range(8):
                            P.op("pe", lambda e, j=j, kc=kc: e.matmul(pA[2 + j].t[:, 0:512], lhsT=d_.t[:, kc, :], rhs=Wdo.t[:, kc, j * 512:(j + 1) * 512],
                                                                      start=(kc == 0), stop=(kc == 7)),
                                 reads=[d_.b, Wdo.b], writes=[pA[2 + j].b])
                        P.op("dve", lambda e, j=j: e.tensor_tensor(out=m2.t[:, j * 512:(j + 1) * 512], in0=pA[2 + j].t[:, 0:512],
                                                                   in1=sig.t[:, 1024 + j * 512:1024 + (j + 1) * 512], op=ALU.mult),
                             reads=[pA[2 + j].b, sig.b], writes=[m2.b])
                    P.op("pool", lambda e: e.tensor_tensor(out=mb_.t[:], in0=m1.t[:], in1=m2.t[:], op=ALU.add), reads=[m1.b, m2.b], writes=[mb_.b])

                def MB1(ob):
                    mb_ = mb[ob % 2]
                    for half in range(2):
                        pt = pTm[half]
                        for j in range(4):
                            kc = half * 4 + j
                            P.op("pe", lambda e, kc=kc, j=j, pt=pt: e.transpose(pt.t[:, j * 128:(j + 1) * 128], mb_.t[:, kc * 128:(kc + 1) * 128], ident.t[:]),
                                 reads=[mb_.b, ident.b], writes=[pt.b])
                        if half == 0:
                            P.op("dve", lambda e, pt=pt: e.tensor_copy(mT.t[:, 0:4, :].rearrange("p a b -> p (a b)"), pt.t[:, 0:512]),
                                 reads=[pt.b], writes=[mT.b])
                        else:
                            P.op("act", lambda e, pt=pt: e.copy(mT.t[:, 4:8, :].rearrange("p a b -> p (a b)"), pt.t[:, 0:512]),
                                 reads=[pt.b], writes=[mT.b])

                def MB2(ob):
                    h_ = h2[ob % 2]
                    x_ = xb[ob % 2]
                    for j in range(2):
                        for kc in range(8):
                            P.op("pe", lambda e, j=j, kc=kc: e.matmul(pB[j].t[:, 0:512], lhsT=mT.t[:, kc, :], rhs=Wou.t[:, kc, j * 512:(j + 1) * 512],
                                                                      start=(kc == 0), stop=(kc == 7)),
                                 reads=[mT.b, Wou.b], writes=[pB[j].b])
                        P.op("dve", lambda e, j=j: e.tensor_tensor(out=h_.t[:, j * 512:(j + 1) * 512], in0=pB[j].t[:, 0:512],
                                                                   in1=x_.t[:, j * 512:(j + 1) * 512], op=ALU.add),
                             reads=[pB[j].b, x_.b], writes=[h_.b])
                    P.dma("pool", lambda e: e.dma_start(out=H2[ob * 128:(ob + 1) * 128, :], in_=h_.t[:]),
                          h_.b, reads=[h_.b], writes=[b_H2[ob]])
                    norm_part(h_, n2w, sq, ss, rs, ub, use_pow=True)

                def MB3(ob):
                    t_ = u2T[ob % 2]
                    tr_part(ub, pTm, t_)
                    P.dma("pool", lambda e: e.dma_start(out=U2T[ob], in_=t_.t[:].rearrange("p a b -> p (a b)")),
                          t_.b, reads=[t_.b], writes=[b_U2T[ob]])

                MA1(0); MA2(0); MA3(0)
                for ob in range(NO):
                    nx = ob + 1
                    MB1(ob)
                    if nx < NO:
                        MA1(nx)
                    MB2(ob)
                    if nx < NO:
                        MA2(nx)
                    MB3(ob)
                    if nx < NO:
                        MA3(nx)
                P.emit()

        GB = 3
        NG = (NO + GB - 1) // GB
        with ExitStack() as st:
            Wup = sbt(st, "Wup", [128, 8, 2 * FFN], BF16)
            Wdn = sbt(st, "Wdn", [128, 22, D], BF16)
            cw = sbt(st, "cw", [128, 3, 44], F32)
            cb = sbt(st, "cb", [128, 44], F32)
            wup_v = w_up.rearrange("(k p) n -> p k n", p=128)
            b_Wup = [Buf("Wup%d" % i) for i in range(4)]
            for ch in (0, 2, 1, 3):
                c0 = ch * 1408
                for kc in range(8):
                    P.dma("pool", lambda e, kc=kc, c0=c0: e.dma_start(out=Wup.t[:, kc, c0:c0 + 1408], in_=wup_v[:, kc, c0:c0 + 1408]),
                          b_Wup[ch], writes=[b_Wup[ch]])
            wdn_v = w_down.rearrange("(k p) n -> p k n", p=128)
            for k0 in range(0, 22, 2):
                P.dma("pool", lambda e, k0=k0: e.dma_start(out=Wdn.t[:, k0:k0 + 2, :], in_=wdn_v[:, k0:k0 + 2, :]), Wdn.b, writes=[Wdn.b])
            for t0 in range(0, 44, 11):
                for k in range(3):
                    P.dma("sp", lambda e, k=k, t0=t0: e.dma_start(out=cw.t[:, k, t0:t0 + 11],
                                                                  in_=conv_w[k].rearrange("(t p) -> p t", p=128)[:, t0:t0 + 11],
                                                                  allow_slow_non_contiguous=True), cw.b, writes=[cw.b])
                P.dma("sp", lambda e, t0=t0: e.dma_start(out=cb.t[:, t0:t0 + 11], in_=conv_b.rearrange("(t p) -> p t", p=128)[:, t0:t0 + 11],
                                                         allow_slow_non_contiguous=True), cb.b, writes=[cb.b])
            NT = GB * 128
            u2g = [sbt(st, "u2g%d" % i, [128, 8, 2 + NT], BF16) for i in range(2)]
            ya = [sbt(st, "ya%d" % i, [128, NT], F32) for i in range(2)]
            yb = [sbt(st, "yb%d" % i, [128, NT], F32) for i in range(2)]
            sa = [sbt(st, "sa%d" % i, [128, NT], F32) for i in range(2)]
            gTt2 = [sbt(st, "gTt%d" % i, [128, 22, NT], BF16) for i in range(2)]
            hb = [sbt(st, "fh%d" % i, [128, D], F32) for i in range(1)] * 2
            ob_ = [sbt(st, "fo%d" % i, [128, D], F32) for i in range(2)]
            pU = [pbank(st, "pU%d" % i) for i in range(4)]
            pD = [pbank(st, "pD%d" % i) for i in range(4)]
            P.op("dve", lambda e: e.memset(u2g[0].t[:], 0.0), writes=[u2g[0].b])
            P.op("dve", lambda e: e.memset(u2g[1].t[:], 0.0), writes=[u2g[1].b])
            ui = [0]

            def up_pair(gi, ft, ug, nt):
                gt = gTt2[gi % 2]
                tiles = []
                for which, fi in ((0, ft), (1, ft + 22)):
                    pu = pU[ui[0] % 4]
                    ui[0] += 1
                    for kc in range(8):
                        P.op("pe", lambda e, pu=pu, kc=kc, fi=fi: e.matmul(pu.t[:, 0:nt + 2], lhsT=Wup.t[:, kc, fi * 128:(fi + 1) * 128],
                                                                           rhs=ug.t[:, kc, 0:nt + 2], start=(kc == 0), stop=(kc == 7)),
                             reads=[b_Wup[fi // 11], ug.b], writes=[pu.b])
                    yt = (ya if which == 0 else yb)[ft % 2]
                    P.op("dve", lambda e, pu=pu, yt=yt, fi=fi: e.tensor_scalar(yt.t[:, 0:nt], pu.t[:, 2:nt + 2], cw.t[:, 2, fi:fi + 1], cb.t[:, fi:fi + 1],
                                                                               op0=ALU.mult, op1=ALU.add),
                         reads=[pu.b, cw.b, cb.b], writes=[yt.b])
                    P.op("dve", lambda e, pu=pu, yt=yt, fi=fi: e.scalar_tensor_tensor(out=yt.t[:, 0:nt], in0=pu.t[:, 1:nt + 1], scalar=cw.t[:, 1, fi:fi + 1],
                                                                                      in1=yt.t[:, 0:nt], op0=ALU.mult, op1=ALU.add),
                         reads=[pu.b, cw.b, yt.b], writes=[yt.b])
                    P.op("dve", lambda e, pu=pu, yt=yt, fi=fi: e.scalar_tensor_tensor(out=yt.t[:, 0:nt], in0=pu.t[:, 0:nt], scalar=cw.t[:, 0, fi:fi + 1],
                                                                                      in1=yt.t[:, 0:nt], op0=ALU.mult, op1=ALU.add),
                         reads=[pu.b, cw.b, yt.b], writes=[yt.b])
                    tiles.append(yt)
                s_ = sa[ft % 2]
                P.op("act", lambda e: e.activation(out=s_.t[:, 0:nt], in_=tiles[0].t[:, 0:nt], func=AF.Silu),
                     reads=[tiles[0].b], writes=[s_.b])
                P.op("dve" if gi == 0 else "pool",
                     lambda e: e.tensor_tensor(out=gt.t[:, ft, 0:nt], in0=s_.t[:, 0:nt], in1=tiles[1].t[:, 0:nt], op=ALU.mult),
                     reads=[s_.b, tiles[1].b], writes=[gt.b])

            def down_unit(gi, j, ob, half):
                gt = gTt2[gi % 2]
                h_ = hb[ob % 2]
                o_ = ob_[ob % 2]
                if half == 0:
                    P.dma("sp", lambda e: e.dma_start(out=h_.t[:], in_=H2[ob * 128:(ob + 1) * 128, :]),
                          h_.b, reads=[b_H2[ob]], writes=[h_.b])
                pd = pD[(ob * 2 + half) % 4]
                for ft in range(22):
                    P.op("pe", lambda e, ft=ft: e.matmul(pd.t[:, 0:512], lhsT=gt.t[:, ft, j * 128:(j + 1) * 128],
                                                         rhs=Wdn.t[:, ft, half * 512:(half + 1) * 512],
                                                         start=(ft == 0), stop=(ft == 21)),
                         reads=[gt.b, Wdn.b], writes=[pd.b])
                P.op("dve", lambda e: e.tensor_tensor(out=o_.t[:, half * 512:(half + 1) * 512], in0=pd.t[:, 0:512],
                                                      in1=h_.t[:, half * 512:(half + 1) * 512], op=ALU.add),
                     reads=[pd.b, h_.b], writes=[o_.b])
                if half == 1:
                    P.dma("pool", lambda e: e.dma_start(out=y[ob * 128:(ob + 1) * 128, :], in_=o_.t[:]),
                          o_.b, reads=[o_.b], writes=[b_y])

            pending_down = []
            for gi in range(NG):
                blks = list(range(gi * GB, min(NO, (gi + 1) * GB)))
                nt = len(blks) * 128
                ug = u2g[gi % 2]
                up_ = u2g[(gi + 1) % 2]
                for j, ob in enumerate(blks):
                    P.dma("sp", lambda e, ug=ug, j=j, ob=ob: e.dma_start(out=ug.t[:, :, 2 + j * 128:2 + (j + 1) * 128],
                                                                        in_=U2T[ob].rearrange("p (a b) -> p a b", a=8)),
                          ug.b, reads=[b_U2T[ob]], writes=[ug.b])
                if gi > 0:
                    P.op("pool", lambda e, ug=ug, up_=up_: e.tensor_copy(ug.t[:, :, 0:2], up_.t[:, :, NT:NT + 2]),
                         reads=[up_.b], writes=[ug.b])
                nd = len(pending_down)
                slots = {int(round((k + 1) * 22.0 / (nd + 1))): k for k in range(nd)} if nd else {}
                for ft in range(22):
                    up_pair(gi, ft, ug, nt)
                    if (ft + 1) in slots:
                        down_unit(*pending_down[slots[ft + 1]])
                pending_down = [(gi, j, ob, half) for j, ob in enumerate(blks) for half in range(2)]
            for u in pending_down:
                down_unit(*u)
            P.wait_all("pool", [b_y])
            P.emit()
        print("n_inst", P.n_inst, "n_wait", P.n_wait, "ndsem", P.ndsem)
    return nc


def make_tables(NC, NO, p, S):
    NB = NC + NO
    L = N_META + S
    if p == 0:
        ctx_pos = np.full(NC * 128, -1, np.int64)
        own_pos = np.arange(NO * 128)
    else:
        ctx_pos = np.arange(NC * 128) - PAD
        own_pos = L - NO * 128 + np.arange(NO * 128)
    pos = np.concatenate([ctx_pos, own_pos])
    valid = pos >= 0
    posf = np.where(valid, pos, 0).astype(np.float32)
    inv_r = np.power(np.float32(10000.0), -np.arange(128, dtype=np.float32) / np.float32(128))
    ang = posf[:, None] * inv_r[None, :]
    c, s = np.cos(ang), np.sin(ang)
    rope_r = np.concatenate([c, c, s, s], axis=1).astype(np.float32)
    inv_d = np.power(np.float32(500000.0), -np.arange(8, dtype=np.float32) / np.float32(8))
    ang = posf[:, None] * inv_d[None, :]
    c, s = np.cos(ang), np.sin(ang)
    rope_d = np.concatenate([c, c, s, s], axis=1).astype(np.float32)
    kb = np.where(valid, 1.0, 0.0).astype(np.float32).reshape(NB, 128).T.copy()
    idx = np.arange(128)
    cm = (idx[:, None] <= idx[None, :]).astype(np.float32)
    rdec = np.zeros((128, 8), np.float32)
    for h in range(4):
        rdec[:, h] = GAM[h] ** (idx + 1.0)
        rdec[:, 4 + h] = (256 ** -0.5) * GAM[h] ** (127.0 - idx)
    return rope_r, rope_d, kb, cm, rdec


_NC_CACHE = {}


def run(inputs, NC, NO, debug=False, trace=False):
    x = np.asarray(inputs["x"], np.float32)
    B, S, _ = x.shape
    assert S == 128 * (NC + NO - 1)
    L = N_META + S
    meta = np.asarray(inputs["meta_tokens"], np.float32)
    key = (NC, NO, debug)
    if key not in _NC_CACHE:
        _NC_CACHE[key] = build(NC, NO, debug)
    nc = _NC_CACHE[key]
    f = lambda k: np.ascontiguousarray(np.asarray(inputs[k], np.float32)[0])
    common = {
        "w_in": f("w_in"), "w_ret_o": f("w_ret_o"), "w_diff_o": f("w_diff_o"), "w_out": f("w_out"),
        "w_up": f("w_up"), "w_down": f("w_down"), "norm1_w": f("norm1_w"), "norm2_w": f("norm2_w"),
        "qk_norm_w": np.concatenate([f("q_norm_w"), f("q_norm_w"), f("k_norm_w"), f("k_norm_w")]),
        "lambdas": np.concatenate([f("lambda_q1"), f("lambda_k1"), f("lambda_q2"), f("lambda_k2")]),
        "subln_w": f("diff_subln_w"), "conv_w": f("conv_w"), "conv_b": f("conv_b"),
    }
    tabs = [make_tables(NC, NO, p, S) for p in range(2)]
    in_maps = []
    for b in range(B):
        seq = np.concatenate([meta, x[b]], axis=0)
        for p in range(2):
            if p == 0:
                xc_ = np.zeros((NC * 128, D), np.float32)
                xo_ = seq[0:NO * 128]
            else:
                xc_ = np.concatenate([np.zeros((PAD, D), np.float32), seq[0:NC * 128 - PAD]], axis=0)
                xo_ = seq[L - NO * 128:L]
            rr, rd, kb, cm, rdec = tabs[p]
            m = dict(common)
            m.update({"xc": np.ascontiguousarray(xc_), "xo": np.ascontiguousarray(xo_), "rope_r": rr, "rope_d": rd,
                      "kbias": kb, "cmask": cm, "rdec": rdec})
            in_maps.append(m)
    res = run_bass_kernel_spmd(nc, in_maps, core_ids=list(range(len(in_maps))), trace=trace)
    out = np.empty((B, S, D), np.float32)
    split = (NO * 128 - N_META) - 64
    for b in range(B):
        y0 = res.results[2 * b]["y"]
        y1 = res.results[2 * b + 1]["y"]
        out[b, :split] = y0[N_META:N_META + split]
        off1 = L - NO * 128
        out[b, split:] = y1[N_META + split - off1:]
    return out, res


def kernel(**inputs):
    out, _ = run(inputs, 32, 33)
    return out
```
